# Optimizing a Trainium2 kernel written in Bass

```python
import math
import jax
import jax.numpy as jnp
from jax import lax
import numpy as np

D_MODEL = 2048
BATCH = 4
SEQ = 4096
DEPTH = 2

HEAD_DIM = 128
N_A = DEPTH // 2
N_B = DEPTH - N_A
N_HEADS_A = 12
N_KV_A = 2
HPG_A = N_HEADS_A // N_KV_A
CMP_LEN = 32
CMP_STRIDE = 16
CMP_HIDDEN = 256
SLC_BLK = 64
N_SEL = 16
WIN_A = 512
Q_BLK_A = 64
DIL_CONFIGS = ((128, 1), (512, 4), (2048, 16))
N_DIL_GROUPS = len(DIL_CONFIGS)
DIL_HEADS = 4
N_KV_B = DIL_HEADS
Q_BLK_B = 128
N_MEM = 256
N_MEM_HEADS = 4
D_FF = -(-(8 * D_MODEL) // (3 * 256)) * 256
ROPE_THETA = 10000.0
EPS = 1e-6
NEG_INF = -1e30
TINY = 1e-30

A_Q = N_HEADS_A * HEAD_DIM
A_KV = 6 * N_KV_A * HEAD_DIM
A_GATE = 3 * N_HEADS_A
MEM_Q = N_MEM_HEADS * HEAD_DIM
A_IN = A_Q + A_KV + A_GATE + MEM_Q
A_OUT_IN = A_Q + MEM_Q
B_Q = N_DIL_GROUPS * DIL_HEADS * HEAD_DIM
B_IN = B_Q + MEM_Q
B_OUT_IN = DIL_HEADS * HEAD_DIM + MEM_Q

kernel_name = "yoco_nsa_dilated_mem_swiglu"


def rms_norm(x, g):
    x32 = x.astype(jnp.float32)
    y = x32 * lax.rsqrt(jnp.mean(x32 * x32, axis=-1, keepdims=True) + EPS)
    return (y * g.astype(jnp.float32)).astype(x.dtype)


def rope_tables(seq):
    inv = 1.0 / (ROPE_THETA ** (jnp.arange(0, HEAD_DIM, 2, dtype=jnp.float32) / HEAD_DIM))
    ang = jnp.arange(seq, dtype=jnp.float32)[:, None] * inv[None, :]
    return jnp.cos(ang), jnp.sin(ang)


def apply_rope(x, cos, sin):
    x32 = x.astype(jnp.float32)
    x1, x2 = jnp.split(x32, 2, axis=-1)
    c = cos[None, :, None, :]
    s = sin[None, :, None, :]
    return jnp.concatenate([x1 * c - x2 * s, x2 * c + x1 * s], axis=-1).astype(x.dtype)


def masked_probs(s, mask):
    s = jnp.where(mask, s, NEG_INF)
    m = jnp.max(s, axis=-1, keepdims=True)
    e = jnp.where(mask, jnp.exp(s - m), 0.0)
    den = jnp.sum(e, axis=-1, keepdims=True)
    return e / jnp.maximum(den, TINY), m + jnp.log(jnp.maximum(den, TINY))


def swiglu(h, w_gate, w_up, w_down):
    return (jax.nn.silu(h @ w_gate) * (h @ w_up)) @ w_down


def compress(x_raw, pe, w1, w2):
    b, s, g, d = x_raw.shape
    n_cmp = (s - CMP_LEN) // CMP_STRIDE + 1
    idx = np.arange(n_cmp)[:, None] * CMP_STRIDE + np.arange(CMP_LEN)[None, :]
    blocks = x_raw[:, idx] + pe[None, None, :, None, :].astype(x_raw.dtype)
    blocks = jnp.transpose(blocks, (0, 1, 3, 2, 4)).reshape(b, n_cmp, g, CMP_LEN * d)
    return jax.nn.silu(blocks @ w1) @ w2


def cmp_to_slc_matrix(seq):
    n_cmp = (seq - CMP_LEN) // CMP_STRIDE + 1
    n_slc = seq // SLC_BLK
    c0 = np.arange(n_cmp) * CMP_STRIDE
    s0 = np.arange(n_slc) * SLC_BLK
    ov = (c0[None, :] < s0[:, None] + SLC_BLK) & (c0[None, :] + CMP_LEN > s0[:, None])
    return jnp.asarray(ov.astype(np.float32))


def nsa_attention(q, k_cmp_raw, v_cmp_raw, k_slc, v_slc, k_win, v_win, gates,
                  pe_k, w1_k, w2_k, pe_v, w1_v, w2_v):
    b, s, h, d = q.shape
    g = N_KV_A
    scale = HEAD_DIM ** -0.5
    kc = compress(k_cmp_raw, pe_k, w1_k, w2_k)
    vc = compress(v_cmp_raw, pe_v, w1_v, w2_v)
    n_cmp = kc.shape[1]
    cmp_end = jnp.arange(n_cmp) * CMP_STRIDE + CMP_LEN - 1
    n_slc = s // SLC_BLK
    n_top = min(N_SEL, n_slc)
    m_map = cmp_to_slc_matrix(s)
    kb = k_slc.reshape(b, n_slc, SLC_BLK, g, d).transpose(0, 3, 1, 2, 4)
    vb = v_slc.reshape(b, n_slc, SLC_BLK, g, d).transpose(0, 3, 1, 2, 4)
    kw = jnp.pad(k_win, ((0, 0), (WIN_A, 0), (0, 0), (0, 0)))
    vw = jnp.pad(v_win, ((0, 0), (WIN_A, 0), (0, 0), (0, 0)))
    qg = q.reshape(b, s, g, HPG_A, d)
    gg = gates.reshape(b, s, g, HPG_A, 3)
    bi = jnp.arange(b)[:, None, None, None]
    gi = jnp.arange(g)[None, :, None, None]
    j_blk = jnp.arange(n_slc)
    lb = jnp.arange(SLC_BLK)

    def block(i):
        s0 = i * Q_BLK_A
        t = s0 + jnp.arange(Q_BLK_A)
        qb = lax.dynamic_slice_in_dim(qg, s0, Q_BLK_A, axis=1)
        gb = lax.dynamic_slice_in_dim(gg, s0, Q_BLK_A, axis=1)
        sc = jnp.einsum('bqgpd,bcgd->bgpqc', qb, kc).astype(jnp.float32) * scale
        p_cmp, _ = masked_probs(sc, cmp_end[None, :] <= t[:, None])
        o_cmp = jnp.einsum('bgpqc,bcgd->bqgpd', p_cmp.astype(vc.dtype), vc)
        imp = jnp.einsum('bgpqc,sc->bgqs', p_cmp, m_map)
        cur = t // SLC_BLK
        forced = (j_blk[None] == 0) | (j_blk[None] == cur[:, None]) | (j_blk[None] == cur[:, None] - 1)
        eligible = j_blk[None] * SLC_BLK <= t[:, None]
        score = jnp.where(forced, 1e9, jnp.where(eligible, imp, -1e9))
        _, sel = lax.top_k(score, n_top)
        ks = kb[bi, gi, sel]
        vs = vb[bi, gi, sel]
        ss = jnp.einsum('bqgpd,bgqnkd->bgpqnk', qb, ks).astype(jnp.float32) * scale
        ss = ss.reshape(b, g, HPG_A, Q_BLK_A, n_top * SLC_BLK)
        tok = sel[..., None] * SLC_BLK + lb
        smask = (tok <= t[None, None, :, None, None]).reshape(b, g, 1, Q_BLK_A, n_top * SLC_BLK)
        p_slc, _ = masked_probs(ss, smask)
        o_slc = jnp.einsum('bgpqm,bgqmd->bqgpd', p_slc.astype(vs.dtype),
                           vs.reshape(b, g, Q_BLK_A, n_top * SLC_BLK, d))
        kwb = lax.dynamic_slice_in_dim(kw, s0, Q_BLK_A + WIN_A, axis=1)
        vwb = lax.dynamic_slice_in_dim(vw, s0, Q_BLK_A + WIN_A, axis=1)
        pos = s0 - WIN_A + jnp.arange(Q_BLK_A + WIN_A)
        dist = t[:, None] - pos[None, :]
        wmask = (dist >= 0) & (dist < WIN_A) & (pos[None, :] >= 0)
        sw = jnp.einsum('bqgpd,bkgd->bgpqk', qb, kwb).astype(jnp.float32) * scale
        p_win, _ = masked_probs(sw, wmask)
        o_win = jnp.einsum('bgpqk,bkgd->bqgpd', p_win.astype(vwb.dtype), vwb)
        o = gb[..., 0:1] * o_cmp + gb[..., 1:2] * o_slc + gb[..., 2:3] * o_win
        return o.reshape(b, Q_BLK_A, h, d)

    out = lax.map(block, jnp.arange(s // Q_BLK_A))
    return jnp.transpose(out, (1, 0, 2, 3, 4)).reshape(b, s, h, d)


def dilated_attention(q, k, v):
    b, s, _, hg, d = q.shape
    scale = HEAD_DIM ** -0.5

    def block(i):
        s0 = i * Q_BLK_B
        t = s0 + jnp.arange(Q_BLK_B)
        qb = lax.dynamic_slice_in_dim(q, s0, Q_BLK_B, axis=1)
        outs, lses = [], []
        for gidx, (w, r) in enumerate(DIL_CONFIGS):
            n_k = w // r + 1
            pos = t[:, None] - r * jnp.arange(n_k)[None, :]
            valid = pos >= 0
            posc = jnp.maximum(pos, 0)
            kg = k[:, posc]
            vg = v[:, posc]
            sg = jnp.einsum('bqhd,bqkhd->bhqk', qb[:, :, gidx], kg).astype(jnp.float32) * scale
            p, lse = masked_probs(sg, valid[None, None])
            og = jnp.einsum('bhqk,bqkhd->bhqd', p.astype(vg.dtype), vg)
            outs.append(og.astype(jnp.float32))
            lses.append(lse[..., 0])
        alpha = jax.nn.softmax(jnp.stack(lses, 0), axis=0)
        o = jnp.sum(alpha[..., None] * jnp.stack(outs, 0), axis=0)
        return jnp.transpose(o, (0, 2, 1, 3)).astype(q.dtype)

    out = lax.map(block, jnp.arange(s // Q_BLK_B))
    return jnp.transpose(out, (1, 0, 2, 3, 4)).reshape(b, s, hg, d)


def memory_attention(qm, mem, norm_mem, w_mem_kv):
    b, m, _ = mem.shape
    mkv = (rms_norm(mem, norm_mem) @ w_mem_kv).reshape(b, m, 2, N_MEM_HEADS, HEAD_DIM)
    mk, mv = mkv[:, :, 0], mkv[:, :, 1]
    sm = jnp.einsum('bshd,bmhd->bhsm', qm, mk).astype(jnp.float32) * (HEAD_DIM ** -0.5)
    p = jax.nn.softmax(sm, axis=-1)
    return jnp.einsum('bhsm,bmhd->bshd', p.astype(mv.dtype), mv)


def layer_a(h, mem, cos, sin, norm_attn, w_in, gate_bias, pe_k, w1_k, w2_k, pe_v, w1_v, w2_v,
            norm_mem, w_mem_kv, w_out, norm_ffn, w_gate, w_up, w_down):
    b, s, _ = h.shape
    z = rms_norm(h, norm_attn) @ w_in
    zq, zkv, zg, zm = jnp.split(z, [A_Q, A_Q + A_KV, A_Q + A_KV + A_GATE], axis=-1)
    q = apply_rope(zq.reshape(b, s, N_HEADS_A, HEAD_DIM), cos, sin)
    kv = zkv.reshape(b, s, 6, N_KV_A, HEAD_DIM)
    k_cmp = apply_rope(kv[:, :, 0], cos, sin)
    k_slc = apply_rope(kv[:, :, 2], cos, sin)
    k_win = apply_rope(kv[:, :, 4], cos, sin)
    gates = jax.nn.sigmoid(zg + gate_bias).reshape(b, s, N_HEADS_A, 3)
    o_nsa = nsa_attention(q, k_cmp, kv[:, :, 1], k_slc, kv[:, :, 3], k_win, kv[:, :, 5], gates,
                          pe_k, w1_k, w2_k, pe_v, w1_v, w2_v)
    o_mem = memory_attention(zm.reshape(b, s, N_MEM_HEADS, HEAD_DIM), mem, norm_mem, w_mem_kv)
    o = jnp.concatenate([o_nsa.reshape(b, s, A_Q), o_mem.reshape(b, s, MEM_Q)], axis=-1) @ w_out
    h = h + o
    return h + swiglu(rms_norm(h, norm_ffn), w_gate, w_up, w_down)


def layer_b(h, mem, cos, sin, k_sh, v_sh, norm_attn, w_in, norm_mem, w_mem_kv, w_out,
            norm_ffn, w_gate, w_up, w_down):
    b, s, _ = h.shape
    z = rms_norm(h, norm_attn) @ w_in
    zq, zm = jnp.split(z, [B_Q], axis=-1)
    q = apply_rope(zq.reshape(b, s, N_DIL_GROUPS * DIL_HEADS, HEAD_DIM), cos, sin)
    q = q.reshape(b, s, N_DIL_GROUPS, DIL_HEADS, HEAD_DIM)
    o_dil = dilated_attention(q, k_sh, v_sh)
    o_mem = memory_attention(zm.reshape(b, s, N_MEM_HEADS, HEAD_DIM), mem, norm_mem, w_mem_kv)
    o = jnp.concatenate([o_dil.reshape(b, s, DIL_HEADS * HEAD_DIM),
                         o_mem.reshape(b, s, MEM_Q)], axis=-1) @ w_out
    h = h + o
    return h + swiglu(rms_norm(h, norm_ffn), w_gate, w_up, w_down)


def setup_inputs(seed: int = 0) -> dict:
    key = jax.random.key(seed)
    ks = jax.random.split(key, 40)
    f32 = jnp.float32

    def w(k, shape, fan_in):
        return jax.random.normal(k, shape, f32) * (fan_in ** -0.5)

    def gain(k, shape):
        return 1.0 + 0.02 * jax.random.normal(k, shape, f32)

    return {
        "x": jax.random.normal(ks[0], (BATCH, SEQ, D_MODEL), f32),
        "mem": jax.random.normal(ks[1], (BATCH, N_MEM, D_MODEL), f32),
        "a_norm_attn": gain(ks[2], (N_A, D_MODEL)),
        "a_w_in": w(ks[3], (N_A, D_MODEL, A_IN), D_MODEL),
        "a_gate_bias": 0.01 * jax.random.normal(ks[4], (N_A, A_GATE), f32),
        "a_cmp_pe_k": 0.1 * jax.random.normal(ks[5], (N_A, CMP_LEN, HEAD_DIM), f32),
        "a_cmp_w1_k": w(ks[6], (N_A, CMP_LEN * HEAD_DIM, CMP_HIDDEN), CMP_LEN * HEAD_DIM),
        "a_cmp_w2_k": w(ks[7], (N_A, CMP_HIDDEN, HEAD_DIM), CMP_HIDDEN),
        "a_cmp_pe_v": 0.1 * jax.random.normal(ks[8], (N_A, CMP_LEN, HEAD_DIM), f32),
        "a_cmp_w1_v": w(ks[9], (N_A, CMP_LEN * HEAD_DIM, CMP_HIDDEN), CMP_LEN * HEAD_DIM),
        "a_cmp_w2_v": w(ks[10], (N_A, CMP_HIDDEN, HEAD_DIM), CMP_HIDDEN),
        "a_norm_mem": gain(ks[11], (N_A, D_MODEL)),
        "a_w_mem_kv": w(ks[12], (N_A, D_MODEL, 2 * MEM_Q), D_MODEL),
        "a_w_out": w(ks[13], (N_A, A_OUT_IN, D_MODEL), A_OUT_IN),
        "a_norm_ffn": gain(ks[14], (N_A, D_MODEL)),
        "a_w_gate": w(ks[15], (N_A, D_MODEL, D_FF), D_MODEL),
        "a_w_up": w(ks[16], (N_A, D_MODEL, D_FF), D_MODEL),
        "a_w_down": w(ks[17], (N_A, D_FF, D_MODEL), D_FF),
        "kv_norm": gain(ks[18], (D_MODEL,)),
        "w_kv_shared": w(ks[19], (D_MODEL, 2 * N_KV_B * HEAD_DIM), D_MODEL),
        "b_norm_attn": gain(ks[20], (N_B, D_MODEL)),
        "b_w_in": w(ks[21], (N_B, D_MODEL, B_IN), D_MODEL),
        "b_norm_mem": gain(ks[22], (N_B, D_MODEL)),
        "b_w_mem_kv": w(ks[23], (N_B, D_MODEL, 2 * MEM_Q), D_MODEL),
        "b_w_out": w(ks[24], (N_B, B_OUT_IN, D_MODEL), B_OUT_IN),
        "b_norm_ffn": gain(ks[25], (N_B, D_MODEL)),
        "b_w_gate": w(ks[26], (N_B, D_MODEL, D_FF), D_MODEL),
        "b_w_up": w(ks[27], (N_B, D_MODEL, D_FF), D_MODEL),
        "b_w_down": w(ks[28], (N_B, D_FF, D_MODEL), D_FF),
        "final_norm": gain(ks[29], (D_MODEL,)),
    }


def reference(x, mem, a_norm_attn, a_w_in, a_gate_bias, a_cmp_pe_k, a_cmp_w1_k, a_cmp_w2_k,
              a_cmp_pe_v, a_cmp_w1_v, a_cmp_w2_v, a_norm_mem, a_w_mem_kv, a_w_out, a_norm_ffn,
              a_w_gate, a_w_up, a_w_down, kv_norm, w_kv_shared, b_norm_attn, b_w_in,
              b_norm_mem, b_w_mem_kv, b_w_out, b_norm_ffn, b_w_gate, b_w_up, b_w_down,
              final_norm):
    b, s, _ = x.shape
    cos, sin = rope_tables(s)
    h = x
    k_sh = v_sh = None
    for layer in range(DEPTH):
        if layer < N_A:
            l = layer
            h = layer_a(h, mem, cos, sin, a_norm_attn[l], a_w_in[l], a_gate_bias[l],
                        a_cmp_pe_k[l], a_cmp_w1_k[l], a_cmp_w2_k[l],
                        a_cmp_pe_v[l], a_cmp_w1_v[l], a_cmp_w2_v[l],
                        a_norm_mem[l], a_w_mem_kv[l], a_w_out[l], a_norm_ffn[l],
                        a_w_gate[l], a_w_up[l], a_w_down[l])
        else:
            if layer == N_A:
                kv = (rms_norm(h, kv_norm) @ w_kv_shared).reshape(b, s, 2, N_KV_B, HEAD_DIM)
                k_sh = apply_rope(kv[:, :, 0], cos, sin)
                v_sh = kv[:, :, 1]
            l = layer - N_A
            h = layer_b(h, mem, cos, sin, k_sh, v_sh, b_norm_attn[l], b_w_in[l], b_norm_mem[l],
                        b_w_mem_kv[l], b_w_out[l], b_norm_ffn[l], b_w_gate[l], b_w_up[l],
                        b_w_down[l])
    return rms_norm(h, final_norm)
```

```python
import numpy as np
import concourse.bass as bass
import concourse.mybir as mybir
from concourse.bass_utils import run_bass_kernel_spmd

F32 = mybir.dt.float32
BF16 = mybir.dt.bfloat16
AF = mybir.ActivationFunctionType
ALU = mybir.AluOpType
AX = mybir.AxisListType


ENGS = ("pe", "act", "dve", "pool", "sp")


class Buf:
    __slots__ = ("name", "w", "r", "sem", "ndma", "lo", "hi", "space")

    def __init__(self, name, space="sb", lo=0, hi=0):
        self.name = name
        self.w = []
        self.r = []
        self.sem = None
        self.ndma = 0
        self.lo = lo
        self.hi = hi
        self.space = space


class DSem:
    __slots__ = ("handle", "ndma", "idx", "inc")

    def __init__(self, idx, inc=16):
        self.handle = None
        self.ndma = 0
        self.idx = idx
        self.inc = inc


class Op:
    __slots__ = ("eng", "idx", "fn", "waits", "flagged", "rank", "dma_buf", "pe_group")

    def __init__(self, eng, idx, fn):
        self.eng = eng
        self.idx = idx
        self.fn = fn
        self.waits = {}
        self.flagged = False
        self.rank = 0
        self.dma_buf = None


class Sched:
    def __init__(self, nc, arena_bytes=200 * 1024):
        self.nc = nc
        self.ops = {e: [] for e in ENGS}
        self.waited = {e: {} for e in ENGS}
        self.arena_bytes = arena_bytes
        self.arena = nc.alloc_sbuf_tensor("arena", [128, arena_bytes // 4], F32)
        self.arena_top = 0
        self.live = []
        self.retired = []
        self.psum = [nc.alloc_psum_tensor("ps%d" % i, [128, 512], F32) for i in range(8)]
        self.psbuf = [Buf("ps%d" % i, "ps") for i in range(8)]
        self.nsem = 0
        self.eng_sem = {}
        self.dma_rr = 0
        self.NPOOL = 90
        self.pool = [DSem(i) for i in range(self.NPOOL)]
        self.cc_sem = DSem(1000, inc=1)

    def alloc(self, name, nbytes, dtype=F32):
        req = nbytes
        nbytes = (nbytes + 31) // 32 * 32
        lo = self.arena_top
        hi = lo + nbytes
        assert hi <= self.arena_bytes, "arena overflow %s: %d > %d" % (name, hi, self.arena_bytes)
        self.arena_top = hi
        b = Buf(name, "sb", lo, hi)
        keep = []
        for rb in self.retired:
            if rb.lo < hi and lo < rb.hi:
                b.r.extend(rb.w)
                b.r.extend(rb.r)
                if rb.lo < lo or rb.hi > hi:
                    keep.append(rb)
            else:
                keep.append(rb)
        self.retired = keep
        dd = {}
        for dep in b.r:
            if dep[0] == "e":
                k = ("e", dep[1].eng)
                if k not in dd or dd[k][1].idx < dep[1].idx:
                    dd[k] = dep
            else:
                dd[("d", dep[1].idx)] = dep
        b.r = list(dd.values())
        self.live.append(b)
        ap = self.arena[:, lo // 4:hi // 4]
        if dtype != F32:
            ap = ap.bitcast(dtype)
            ap = ap[:, 0:req // 2]
        else:
            ap = ap[:, 0:req // 4]
        return ap, b

    def mark(self):
        return (self.arena_top, len(self.live))

    def release(self, mark):
        top, n = mark
        for b in self.live[n:]:
            self.retired.append(b)
        self.live = self.live[:n]
        self.arena_top = top

    def _dep_of(self, op):
        if op.dma_buf is not None:
            return ("d", op.dma_buf)
        return ("e", op)

    def _add_wait(self, op, dep):
        if dep[0] == "e":
            p = dep[1]
            if p.eng == "pe" and op.eng == "pe":
                return
            key = ("e", p.eng)
            cur = op.waits.get(key)
            if cur is None or cur.idx < p.idx:
                op.waits[key] = p
        else:
            b = dep[1]
            key = ("d", b.idx)
            op.waits[key] = (b, b.ndma * b.inc)

    def _collect(self, op, reads, writes, pwrites):
        for b in reads:
            for d in b.w:
                self._add_wait(op, d)
        for b in writes:
            for d in b.w:
                self._add_wait(op, d)
            for d in b.r:
                self._add_wait(op, d)
        for b in pwrites:
            for d in b.r:
                self._add_wait(op, d)

    @staticmethod
    def _same(d, me):
        if d[0] != me[0]:
            return False
        if me[0] == "e":
            return d[1].eng == me[1].eng
        return d[1] is me[1]

    def _register(self, me, reads, writes, pwrites):
        for b in writes:
            b.w = [me]
            b.r = []
        for b in pwrites:
            b.w = [d for d in b.w if not self._same(d, me)]
            b.w.append(me)
        for b in reads:
            if b in writes or b in pwrites:
                continue
            b.r = [d for d in b.r if not self._same(d, me)]
            b.r.append(me)

    def add(self, eng, fn, reads=(), writes=(), pwrites=()):
        op = Op(eng, len(self.ops[eng]), fn)
        self.ops[eng].append(op)
        reads, writes, pwrites = list(reads), list(writes), list(pwrites)
        self._collect(op, reads, writes, pwrites)
        self._register(("e", op), reads, writes, pwrites)
        return op

    def dma(self, out_ap, in_ap, sem_buf, reads=(), writes=(), pwrites=(), q=None):
        if q is None:
            q = "sp"
        op = Op(q, len(self.ops[q]), lambda e, o=out_ap, i=in_ap: e.dma_start(out=o, in_=i))
        self.ops[q].append(op)
        reads, writes, pwrites = list(reads), list(writes), list(pwrites)
        self._collect(op, reads, writes, pwrites)
        if sem_buf.sem is None:
            sem_buf.sem = self.pool[self.dma_rr % self.NPOOL]
            self.dma_rr += 1
        ds = sem_buf.sem
        op.dma_buf = ds
        ds.ndma += 1
        self._register(("d", ds), reads, writes, pwrites)
        return op

    def collective(self, fn, reads=(), writes=()):
        op = Op("pool", len(self.ops["pool"]), fn)
        self.ops["pool"].append(op)
        reads, writes = list(reads), list(writes)
        self._collect(op, reads, writes, [])
        ds = self.cc_sem
        op.dma_buf = ds
        ds.ndma += 1
        self._register(("d", ds), reads, writes, [])
        return op

    def finalize(self, final_bufs=()):
        nc = self.nc
        fin = Op("sp", len(self.ops["sp"]), None)
        for ds in self.pool + [self.cc_sem]:
            if ds.ndma > 0:
                fin.waits[("d", ds.idx)] = (ds, ds.ndma * ds.inc)
        self.ops["sp"].append(fin)
        for e in ENGS:
            for op in self.ops[e]:
                for k, v in op.waits.items():
                    if k[0] == "e":
                        v.flagged = True
        for e in ENGS:
            r = 0
            for op in self.ops[e]:
                if op.flagged:
                    r += 1
                    op.rank = r
        for e in ENGS:
            if e != "sp":
                self.eng_sem[e] = nc.alloc_semaphore("sem_" + e)
        n = 0
        for ds in self.pool + [self.cc_sem]:
            if ds.ndma > 0:
                ds.handle = nc.alloc_semaphore("dsem_%d" % ds.idx)
                n += 1
        self.n_dma_sems = n
        sched = self

        def emit(e, eng):
            waited = {}
            for op in sched.ops[e]:
                for k, v in op.waits.items():
                    if k[0] == "e":
                        sem = sched.eng_sem[v.eng]
                        val = v.rank
                        wk = ("e", v.eng)
                    else:
                        sem = v[0].handle
                        val = v[1]
                        wk = k
                    if waited.get(wk, 0) >= val:
                        continue
                    waited[wk] = val
                    eng.wait_ge(sem, val)
                if op.fn is None:
                    continue
                ins = op.fn(eng)
                if op.dma_buf is not None:
                    ins.then_inc(op.dma_buf.handle, op.dma_buf.inc)
                elif op.flagged:
                    ins.then_inc(sched.eng_sem[e], 1)

        with nc.Block() as block:
            @block.tensor
            def _(eng):
                emit("pe", eng)

            @block.scalar
            def _(eng):
                emit("act", eng)

            @block.vector
            def _(eng):
                emit("dve", eng)

            @block.gpsimd
            def _(eng):
                emit("pool", eng)

            @block.sync
            def _(eng):
                emit("sp", eng)

NEG = -30000.0
EPS = 1e-6
TINY = 1e-30
SV = 4096
SO = 2048
DM = 2048
DFF = 5632
SCALE = 128 ** -0.5


def sub3(a, off, s1, n1, s2, n2):
    return bass.AP(a.tensor, a.offset + off, [list(a.ap[0]), [s1, n1], [s2, n2]])


def dview(d, r0, nr, c0, nc_):
    return d[r0:r0 + nr, c0:c0 + nc_].rearrange("(k p) n -> p k n", p=128)


class KB:
    def __init__(self, nc):
        self.nc = nc
        self.s = Sched(nc, arena_bytes=198 * 1024)
        self.d = {}
        self.dbufs = {}
        self.bank_rr = 0
        self.outs = []

    def inp(self, name, shape, dt=F32):
        self.d[name] = self.nc.dram_tensor(name, list(shape), dt, kind="ExternalInput").ap()
        return self.d[name]

    def scr(self, name, shape, dt):
        self.d[name] = self.nc.dram_tensor(name, list(shape), dt).ap()
        return self.d[name]

    def outp(self, name, shape, dt=F32):
        self.d[name] = self.nc.dram_tensor(name, list(shape), dt, kind="ExternalOutput").ap()
        return self.d[name]

    def db(self, name, i=0):
        k = (name, i)
        if k not in self.dbufs:
            self.dbufs[k] = Buf("d_%s_%s" % (name, i), "dram")
        return self.dbufs[k]

    def rd(self, name, i=0):
        if name is None:
            return []
        return [self.db(name, i)]

    def consts(self, ngain):
        s = self.s
        self.ident, self.ident_b = s.alloc("ident", 128 * 2, BF16)
        self.ones, self.ones_b = s.alloc("ones", 128 * 2, BF16)
        self.gains, self.gains_b = s.alloc("gains", ngain * 16 * 4)
        s.dma(self.ident, self.d["c_ident"], self.ident_b, writes=[self.ident_b], q="pool")
        s.dma(self.ones, self.d["c_ones"], self.ones_b, writes=[self.ones_b], q="pool")
        s.dma(self.gains, self.d["gains"], self.gains_b, writes=[self.gains_b])

    def stage_norm(self, src, srcname, ntok, gidx, dsts, src_tt0=0):
        s = self.s
        mk = s.mark()
        hb = [s.alloc("nh%d" % i, 2048 * 4) for i in range(4)]
        yb = [s.alloc("ny%d" % i, 2048 * 2, BF16) for i in range(4)]
        junk_ap, junk_b = s.alloc("njunk", 2048 * 2, BF16)
        st = [s.alloc("nst%d" % i, 12 * 4) for i in range(2)]
        ob = [[s.alloc("no%d_%d" % (g, i), 16 * 512 * 2, BF16) for i in range(2)] for g in range(len(gidx))]
        psb = [s.psum[i][:, :].bitcast(BF16) for i in range(8)]
        ident, ident_b = self.ident, self.ident_b
        for tt in range(ntok // 512):
            st_ap, st_b = st[tt % 2]
            for sub in range(4):
                h_ap, h_b = hb[sub]
                r0 = tt * 512 + sub * 128
                s.dma(h_ap, src[r0:r0 + 128, :], h_b, reads=self.rd(srcname, src_tt0 + tt), writes=[h_b])
                s.add("act", lambda e, h=h_ap, o=st_ap[:, sub:sub + 1]: e.activation(junk_ap, h, AF.Square, accum_out=o),
                      reads=[h_b], writes=[junk_b], pwrites=[st_b])
            s.add("dve", lambda e, a=st_ap: e.tensor_scalar(a[:, 4:8], a[:, 0:4], 1.0 / 2048, EPS, ALU.mult, ALU.add),
                  reads=[st_b], pwrites=[st_b])
            s.add("act", lambda e, a=st_ap: e.sqrt(a[:, 4:8], a[:, 4:8]), reads=[st_b], pwrites=[st_b])
            s.add("dve", lambda e, a=st_ap: e.reciprocal(a[:, 8:12], a[:, 4:8]), reads=[st_b], pwrites=[st_b])
            for sub in range(4):
                h_ap, h_b = hb[sub]
                y_ap, y_b = yb[sub]
                s.add("act", lambda e, y=y_ap, h=h_ap, sc=st_ap[:, 8 + sub:9 + sub]: e.activation(y, h, AF.Copy, scale=sc),
                      reads=[h_b, st_b], writes=[y_b])
                for half in range(2):
                    bank = self.bank_rr % 8
                    self.bank_rr += 1
                    for k8 in range(8):
                        kc = half * 8 + k8
                        s.add("pe", lambda e, o=psb[bank][:, k8 * 128:(k8 + 1) * 128], i=y_ap[:, kc * 128:(kc + 1) * 128]:
                              e.transpose(o, i, ident), reads=[y_b, ident_b], writes=[s.psbuf[bank]])
                    for gi, g in enumerate(gidx):
                        o_ap, o_b = ob[gi][tt % 2]
                        out3 = sub3(o_ap, half * 8 * 512 + sub * 128, 512, 8, 1, 128)
                        in0 = sub3(psb[bank], 0, 128, 8, 1, 128)
                        ga = self.gains[:, g * 16 + half * 8:g * 16 + half * 8 + 8]
                        in1 = sub3(ga, 0, 1, 8, 0, 128)
                        s.add("dve", lambda e, o=out3, a=in0, b=in1: e.tensor_tensor(o, a, b, ALU.mult),
                              reads=[s.psbuf[bank], self.gains_b], pwrites=[o_b])
            for gi in range(len(gidx)):
                dst, dname, dtt0 = dsts[gi]
                o_ap, o_b = ob[gi][tt % 2]
                for q4 in range(4):
                    s.dma(dview(dst, q4 * 512, 512, (dtt0 + tt) * 512, 512),
                          sub3(o_ap, q4 * 4 * 512, 512, 4, 1, 512), o_b, reads=[o_b], pwrites=[self.db(dname, dtt0 + tt)])
        s.release(mk)

    def load_panel(self, p_ap, p_b, segs, KC, kgrp=4):
        s = self.s
        po = 0
        for (W, c0, n) in segs:
            for q in range(0, KC, kgrp):
                kn = min(kgrp, KC - q)
                s.dma(sub3(p_ap, q * 512 + po, 512, kn, 1, n), dview(W, q * 128, kn * 128, c0, n), p_b,
                      pwrites=[p_b], q="pool")
            po += n

    def stage_lfm(self, xT, xname, tok0, ntok, KC, panels, setup):
        s = self.s
        mk = s.mark()
        nt = ntok // 512
        xs = [s.alloc("lx%d" % i, KC * 512 * 2, BF16) for i in range(nt)]
        for tt in range(nt):
            x_ap, x_b = xs[tt]
            for q in range(0, KC, 4):
                s.dma(sub3(x_ap, q * 512, 512, 4, 1, 512), dview(xT, q * 128, 512, tok0 + tt * 512, 512), x_b,
                      reads=self.rd(xname, tok0 // 512 + tt), pwrites=[x_b])
        pr = [s.alloc("lp%d" % i, KC * 512 * 2, BF16) for i in range(3)]
        ctx = setup(s)
        npan = len(panels)
        for i in range(min(2, npan)):
            self.load_panel(pr[i % 3][0], pr[i % 3][1], panels[i]["segs"], KC)
        for i, pan in enumerate(panels):
            p_ap, p_b = pr[i % 3]
            for job in pan["jobs"]:
                nb = len(job["cols"])
                for tt in range(nt):
                    x_ap, x_b = xs[tt]
                    banks = []
                    for (off, n) in job["cols"]:
                        bank = self.bank_rr % 8
                        self.bank_rr += 1
                        banks.append(bank)
                        for kc in range(KC):
                            s.add("pe", lambda e, o=s.psum[bank][0:n, :], l=p_ap[:, kc * 512 + off:kc * 512 + off + n],
                                  r=x_ap[:, kc * 512:(kc + 1) * 512], st=(kc == 0), sp=(kc == KC - 1):
                                  e.matmul(o, l, r, start=st, stop=sp),
                                  reads=[p_b, x_b], writes=[s.psbuf[bank]])
                    job["epi"](ctx, job, tt, banks)
            if i + 2 < npan:
                self.load_panel(pr[(i + 2) % 3][0], pr[(i + 2) % 3][1], panels[i + 2]["segs"], KC)
        s.release(mk)

    def stage_ltm(self, aT, aname, KC, tok0, ntok, TB, panels, setup, epi):
        s = self.s
        mk = s.mark()
        ntb = TB // 512
        as_ = [s.alloc("ta%d" % i, KC * 512 * 2, BF16) for i in range(ntb)]
        pr = [s.alloc("tp%d" % i, KC * 512 * 2, BF16) for i in range(2)]
        ctx = setup(s)
        for tb in range(ntok // TB):
            for tt in range(ntb):
                a_ap, a_b = as_[tt]
                t0 = tok0 + tb * TB + tt * 512
                for q in range(0, KC, 4):
                    s.dma(sub3(a_ap, q * 512, 512, 4, 1, 512), dview(aT, q * 128, 512, t0, 512), a_b,
                          reads=self.rd(aname, t0 // 512), pwrites=[a_b])
            self.load_panel(pr[0][0], pr[0][1], panels[0], KC)
            for pi, segs in enumerate(panels):
                p_ap, p_b = pr[pi % 2]
                if pi + 1 < len(panels):
                    self.load_panel(pr[(pi + 1) % 2][0], pr[(pi + 1) % 2][1], panels[pi + 1], KC)
                ncol = sum(n for (_, _, n) in segs)
                for tt in range(ntb):
                    a_ap, a_b = as_[tt]
                    for t4 in range(4):
                        bank = self.bank_rr % 8
                        self.bank_rr += 1
                        for kc in range(KC):
                            s.add("pe", lambda e, o=s.psum[bank][:, 0:ncol], l=a_ap[:, kc * 512 + t4 * 128:kc * 512 + t4 * 128 + 128],
                                  r=p_ap[:, kc * 512:kc * 512 + ncol], st=(kc == 0), sp=(kc == KC - 1):
                                  e.matmul(o, l, r, start=st, stop=sp),
                                  reads=[p_b, a_b], writes=[s.psbuf[bank]])
                        epi(ctx, tok0 + tb * TB + tt * 512 + t4 * 128, pi, bank, ncol)
        s.release(mk)

    def epi_plain_fm(self, dst, dname, row0fn, tok0):
        kb = self

        def epi(ctx, job, tt, banks):
            s = kb.s
            n = job["cols"][0][1]
            o_ap, o_b = ctx["oring"][ctx["oi"] % len(ctx["oring"])]
            ctx["oi"] += 1
            bank = banks[0]
            s.add("act", lambda e, o=o_ap[0:n, :], i=s.psum[bank][0:n, :]: e.copy(o, i), reads=[s.psbuf[bank]], writes=[o_b])
            r0 = job["row0"]
            s.dma(dst[r0:r0 + n, tok0 + tt * 512:tok0 + tt * 512 + 512], o_ap[0:n, :], o_b, reads=[o_b],
                  pwrites=[kb.db(dname, (tok0 // 512) + tt)])
        return epi

    def epi_rope_fm(self, dst, dname, tok0):
        kb = self

        def epi(ctx, job, tt, banks):
            s = kb.s
            bz, br = banks
            t1, t1b = ctx["t1"][ctx["oi"] % 2]
            t2, t2b = ctx["t2"][ctx["oi"] % 2]
            o_ap, o_b = ctx["oring"][ctx["oi"] % len(ctx["oring"])]
            ctx["oi"] += 1
            cs, csb = ctx["cos"]
            sn, snb = ctx["sin"]
            s.add("dve", lambda e, o=t1, a=s.psum[bz][:, :], b=cs[:, tt * 512:(tt + 1) * 512]: e.tensor_tensor(o, a, b, ALU.mult),
                  reads=[s.psbuf[bz], csb], writes=[t1b])
            s.add("dve", lambda e, o=t2, a=s.psum[br][:, :], b=sn[:, tt * 512:(tt + 1) * 512]: e.tensor_tensor(o, a, b, ALU.mult),
                  reads=[s.psbuf[br], snb], writes=[t2b])
            s.add("pool", lambda e, o=o_ap, a=t1, b=t2: e.tensor_tensor(o, a, b, ALU.add), reads=[t1b, t2b], writes=[o_b])
            r0 = job["row0"]
            s.dma(job["dst"][r0:r0 + 128, tok0 + tt * 512:tok0 + tt * 512 + 512], o_ap, o_b, reads=[o_b],
                  pwrites=[kb.db(job["dname"], (tok0 // 512) + tt)])
        return epi

    def rope_setup(self, tok0, ntok, extra=None):
        kb = self

        def setup(s):
            ctx = {"oi": 0}
            ctx["oring"] = [s.alloc("eo%d" % i, 512 * 2, BF16) for i in range(4)]
            ctx["t1"] = [s.alloc("et1%d" % i, 512 * 4) for i in range(2)]
            ctx["t2"] = [s.alloc("et2%d" % i, 512 * 4) for i in range(2)]
            ctx["cos"] = s.alloc("ecos", ntok * 4)
            ctx["sin"] = s.alloc("esin", ntok * 4)
            s.dma(ctx["cos"][0], kb.d["cosT"][:, tok0:tok0 + ntok], ctx["cos"][1], writes=[ctx["cos"][1]])
            s.dma(ctx["sin"][0], kb.d["sinT"][:, tok0:tok0 + ntok], ctx["sin"][1], writes=[ctx["sin"][1]])
            if extra is not None:
                extra(s, ctx)
            return ctx
        return setup

    def stage_ffn(self, hnT, hnname, wg, wu, wd, h_in, h_in_name, h_out, h_out_name):
        kb = self
        hidT = self.d["hidT"]

        def setup(s):
            ctx = {"oi": 0}
            ctx["sg"] = [s.alloc("fsg%d" % i, 512 * 4) for i in range(3)]
            ctx["oring"] = [s.alloc("fo%d" % i, 512 * 2, BF16) for i in range(4)]
            return ctx

        def epi(ctx, job, tt, banks):
            s = kb.s
            bg, bu = banks
            sg, sgb = ctx["sg"][ctx["oi"] % 3]
            o_ap, o_b = ctx["oring"][ctx["oi"] % 4]
            ctx["oi"] += 1
            s.add("act", lambda e, o=sg, i=s.psum[bg][:, :]: e.activation(o, i, AF.Silu), reads=[s.psbuf[bg]], writes=[sgb])
            s.add("dve", lambda e, o=o_ap, a=s.psum[bu][:, :], b=sg: e.tensor_tensor(o, a, b, ALU.mult),
                  reads=[s.psbuf[bu], sgb], writes=[o_b])
            r0 = job["row0"]
            s.dma(hidT[r0:r0 + 128, tt * 512:(tt + 1) * 512], o_ap, o_b, reads=[o_b], pwrites=[kb.db("hidT", tt)])

        panels = []
        for pc in range(DFF // 256):
            segs = [(wg, pc * 256, 256), (wu, pc * 256, 256)]
            jobs = [dict(cols=[(j * 128, 128), (256 + j * 128, 128)], epi=epi, row0=pc * 256 + j * 128) for j in range(2)]
            panels.append(dict(segs=segs, jobs=jobs))
        self.stage_lfm(hnT, hnname, 0, SO, 16, panels, setup)
        self.stage_down(self.d["hidT"], "hidT", DFF // 128, wd, h_in, h_in_name, h_out, h_out_name, TB=1024)

    def stage_down(self, aT, aname, KC, W, h_in, h_in_name, h_out, h_out_name, TB):
        kb = self

        def setup(s):
            ctx = {"oi": 0}
            ctx["hin"] = [s.alloc("dh%d" % i, 512 * 4) for i in range(3)]
            ctx["oring"] = [s.alloc("do%d" % i, 512 * 4) for i in range(3)]
            return ctx

        def epi(ctx, tok, pi, bank, ncol):
            s = kb.s
            hi, hib = ctx["hin"][ctx["oi"] % 3]
            o_ap, o_b = ctx["oring"][ctx["oi"] % 3]
            ctx["oi"] += 1
            s.dma(hi, h_in[tok:tok + 128, pi * 512:(pi + 1) * 512], hib, reads=kb.rd(h_in_name, tok // 512), writes=[hib])
            s.add("dve", lambda e, o=o_ap, a=s.psum[bank][:, :], b=hi: e.tensor_tensor(o, a, b, ALU.add),
                  reads=[s.psbuf[bank], hib], writes=[o_b])
            s.dma(h_out[tok:tok + 128, pi * 512:(pi + 1) * 512], o_ap, o_b, reads=[o_b], pwrites=[kb.db(h_out_name, tok // 512)])
            if h_out_name == "out":
                kb.outs.append(o_b)

        panels = [[(W, pi * 512, 512)] for pi in range(4)]
        self.stage_ltm(aT, aname, KC, 0, SO, TB, panels, setup, epi)

    def stage_tm_bf16(self, aT, aname, KC, tok0, ntok, segs, dst, dname):
        kb = self

        def setup(s):
            return {"oi": 0, "oring": [s.alloc("vo%d" % i, 512 * 2, BF16) for i in range(4)]}

        def epi(ctx, tok, pi, bank, ncol):
            s = kb.s
            o_ap, o_b = ctx["oring"][ctx["oi"] % 4]
            ctx["oi"] += 1
            s.add("act", lambda e, o=o_ap[:, 0:ncol], i=s.psum[bank][:, 0:ncol]: e.copy(o, i), reads=[s.psbuf[bank]], writes=[o_b])
            s.dma(dst[tok:tok + 128, 0:ncol], o_ap[:, 0:ncol], o_b, reads=[o_b], pwrites=[kb.db(dname, tok // 512)])

        self.stage_ltm(aT, aname, KC, tok0, ntok, min(ntok, 2048), [segs], setup, epi)

    def attn_unit(self, ctx, klhs, krd, qrhs, qrd, extras, vlhs, vrd, bacc, bden, first, last, n=512, np_=128):
        s = self.s
        bS = ctx["sbanks"][ctx["si"] % len(ctx["sbanks"])]
        ctx["si"] += 1
        pt, ptb = ctx["pt"][ctx["pi"] % len(ctx["pt"])]
        ctx["pi"] += 1
        ne = len(extras)
        s.add("pe", lambda e, o=s.psum[bS][0:np_, 0:n], l=klhs, r=qrhs, sp=(ne == 0): e.matmul(o, l, r, start=True, stop=sp),
              reads=krd + qrd, writes=[s.psbuf[bS]])
        for i, (l, r, rds) in enumerate(extras):
            s.add("pe", lambda e, o=s.psum[bS][0:np_, 0:n], l=l, r=r, sp=(i == ne - 1): e.matmul(o, l, r, start=False, stop=sp),
                  reads=rds, writes=[s.psbuf[bS]])
        s.add("act", lambda e, o=pt[0:np_, 0:n], i=s.psum[bS][0:np_, 0:n]: e.activation(o, i, AF.Exp, scale=SCALE),
              reads=[s.psbuf[bS]], writes=[ptb])
        if vlhs is not None:
            s.add("pe", lambda e, o=s.psum[bacc][:, 0:n], l=vlhs, r=pt[0:np_, 0:n], st=first, sp=last: e.matmul(o, l, r, start=st, stop=sp),
                  reads=vrd + [ptb], writes=[s.psbuf[bacc]])
        s.add("pe", lambda e, o=s.psum[bden][:, 0:n], l=self.ones[0:np_, :], r=pt[0:np_, 0:n], st=first, sp=last: e.matmul(o, l, r, start=st, stop=sp),
              reads=[self.ones_b, ptb], writes=[s.psbuf[bden]])
        return pt, ptb

    def recip_den(self, ctx, bden, n=512):
        s = self.s
        r, rb = ctx["rd"][ctx["ri"] % len(ctx["rd"])]
        ctx["ri"] += 1
        s.add("dve", lambda e, o=r[:, 0:n], i=s.psum[bden][:, 0:n]: e.tensor_scalar(o, i, TINY, None, ALU.max),
              reads=[s.psbuf[bden]], writes=[rb])
        s.add("dve", lambda e, o=r[:, 0:n]: e.reciprocal(o, o), reads=[rb], writes=[rb])
        return r, rb

    def stage_mem_attn(self, qmT, qmname, mkT, mv, oT, oname, chunk0):
        s = self.s
        mk = s.mark()
        ctx = dict(si=0, pi=0, ri=0, sbanks=[0, 1, 2], pt=[s.alloc("mpt%d" % i, 512 * 2, BF16) for i in range(4)],
                   rd=[s.alloc("mrd%d" % i, 512 * 4) for i in range(2)])
        k_ap, k_b = s.alloc("mk", 4 * 256 * 2, BF16)
        v_ap, v_b = s.alloc("mv", 2 * 512 * 2, BF16)
        q_ap, q_b = s.alloc("mq", 4 * SO * 2, BF16)
        oring = [s.alloc("mo%d" % i, 512 * 2, BF16) for i in range(3)]
        s.dma(sub3(k_ap, 0, 256, 4, 1, 256), dview(mkT, 0, 512, 0, 256), k_b, reads=self.rd("mkT"), writes=[k_b])
        s.dma(sub3(v_ap, 0, 512, 2, 1, 512), dview(mv, 0, 256, 0, 512), v_b, reads=self.rd("mv"), writes=[v_b])
        for h in range(4):
            s.dma(q_ap[:, h * SO:(h + 1) * SO], qmT[h * 128:(h + 1) * 128, :], q_b,
                  reads=[self.db(qmname, i) for i in range(4)], pwrites=[q_b])
        oi = 0
        for h in range(4):
            for qt in range(4):
                bacc, bden = (3, 4) if (oi % 2 == 0) else (5, 6)
                for mt in range(2):
                    self.attn_unit(ctx, k_ap[:, h * 256 + mt * 128:h * 256 + mt * 128 + 128], [k_b],
                                   q_ap[:, h * SO + qt * 512:h * SO + qt * 512 + 512], [q_b], [],
                                   v_ap[:, mt * 512 + h * 128:mt * 512 + h * 128 + 128], [v_b], bacc, bden, mt == 0, mt == 1)
                r, rb = self.recip_den(ctx, bden)
                o_ap, o_b = oring[oi % 3]
                oi += 1
                s.add("dve", lambda e, o=o_ap, a=s.psum[bacc][:, :], b=r: e.tensor_tensor(o, a, b, ALU.mult),
                      reads=[s.psbuf[bacc], rb], writes=[o_b])
                s.dma(oT[(chunk0 + h) * 128:(chunk0 + h + 1) * 128, qt * 512:(qt + 1) * 512], o_ap, o_b, reads=[o_b],
                      pwrites=[self.db(oname, qt)])
        s.release(mk)

    def stage_memkv(self, mem, gi, wkv):
        kb = self
        s = self.s
        mk = s.mark()
        hb = [s.alloc("kh%d" % i, 2048 * 4) for i in range(2)]
        yb = [s.alloc("ky%d" % i, 2048 * 2, BF16) for i in range(2)]
        junk_ap, junk_b = s.alloc("kjunk", 2048 * 2, BF16)
        st_ap, st_b = s.alloc("kst", 12 * 4)
        o_ap, o_b = s.alloc("ko", 16 * 256 * 2, BF16)
        psb = [s.psum[i][:, :].bitcast(BF16) for i in range(8)]
        for sub in range(2):
            h_ap, h_b = hb[sub]
            s.dma(h_ap, mem[sub * 128:(sub + 1) * 128, :], h_b, writes=[h_b])
            s.add("act", lambda e, h=h_ap, o=st_ap[:, sub:sub + 1]: e.activation(junk_ap, h, AF.Square, accum_out=o),
                  reads=[h_b], writes=[junk_b], pwrites=[st_b])
        s.add("dve", lambda e, a=st_ap: e.tensor_scalar(a[:, 4:6], a[:, 0:2], 1.0 / 2048, EPS, ALU.mult, ALU.add), reads=[st_b], pwrites=[st_b])
        s.add("act", lambda e, a=st_ap: e.sqrt(a[:, 4:6], a[:, 4:6]), reads=[st_b], pwrites=[st_b])
        s.add("dve", lambda e, a=st_ap: e.reciprocal(a[:, 8:10], a[:, 4:6]), reads=[st_b], pwrites=[st_b])
        for sub in range(2):
            h_ap, h_b = hb[sub]
            y_ap, y_b = yb[sub]
            s.add("act", lambda e, y=y_ap, h=h_ap, sc=st_ap[:, 8 + sub:9 + sub]: e.activation(y, h, AF.Copy, scale=sc),
                  reads=[h_b, st_b], writes=[y_b])
            for half in range(2):
                bank = self.bank_rr % 8
                self.bank_rr += 1
                for k8 in range(8):
                    kc = half * 8 + k8
                    s.add("pe", lambda e, o=psb[bank][:, k8 * 128:(k8 + 1) * 128], i=y_ap[:, kc * 128:(kc + 1) * 128]:
                          e.transpose(o, i, kb.ident), reads=[y_b, kb.ident_b], writes=[s.psbuf[bank]])
                out3 = sub3(o_ap, half * 8 * 256 + sub * 128, 256, 8, 1, 128)
                in0 = sub3(psb[bank], 0, 128, 8, 1, 128)
                ga = self.gains[:, gi * 16 + half * 8:gi * 16 + half * 8 + 8]
                in1 = sub3(ga, 0, 1, 8, 0, 128)
                s.add("dve", lambda e, o=out3, a=in0, b=in1: e.tensor_tensor(o, a, b, ALU.mult),
                      reads=[s.psbuf[bank], self.gains_b], pwrites=[o_b])
        mkT = self.d["mkT"]
        mv = self.d["mv"]
        pr = [s.alloc("kp%d" % i, 16 * 512 * 2, BF16) for i in range(2)]
        oring = [s.alloc("kor%d" % i, 512 * 2, BF16) for i in range(3)]
        oi = 0
        for half in range(2):
            p_ap, p_b = pr[half]
            self.load_panel(p_ap, p_b, [(wkv, half * 512, 512)], 16)
        p_ap, p_b = pr[0]
        for h in range(4):
            bank = self.bank_rr % 8
            self.bank_rr += 1
            for kc in range(16):
                s.add("pe", lambda e, o=s.psum[bank][:, 0:256], l=p_ap[:, kc * 512 + h * 128:kc * 512 + h * 128 + 128],
                      r=o_ap[:, kc * 256:(kc + 1) * 256], st=(kc == 0), sp=(kc == 15): e.matmul(o, l, r, start=st, stop=sp),
                      reads=[p_b, o_b], writes=[s.psbuf[bank]])
            oo, oob = oring[oi % 3]
            oi += 1
            s.add("act", lambda e, o=oo[:, 0:256], i=s.psum[bank][:, 0:256]: e.copy(o, i), reads=[s.psbuf[bank]], writes=[oob])
            s.dma(mkT[h * 128:(h + 1) * 128, :], oo[:, 0:256], oob, reads=[oob], pwrites=[self.db("mkT")])
        p_ap, p_b = pr[1]
        for mt in range(2):
            bank = self.bank_rr % 8
            self.bank_rr += 1
            for kc in range(16):
                s.add("pe", lambda e, o=s.psum[bank][:, :], l=o_ap[:, kc * 256 + mt * 128:kc * 256 + mt * 128 + 128],
                      r=p_ap[:, kc * 512:(kc + 1) * 512], st=(kc == 0), sp=(kc == 15): e.matmul(o, l, r, start=st, stop=sp),
                      reads=[p_b, o_b], writes=[s.psbuf[bank]])
            oo, oob = oring[oi % 3]
            oi += 1
            s.add("act", lambda e, o=oo, i=s.psum[bank][:, :]: e.copy(o, i), reads=[s.psbuf[bank]], writes=[oob])
            s.dma(mv[mt * 128:(mt + 1) * 128, :], oo, oob, reads=[oob], pwrites=[self.db("mv")])
        s.release(mk)

    def stage_inproj_a(self, w_in, w_rot, gbias):
        kb = self
        d = self.d
        xT = d["xnT"]
        for tok0, own in ((0, False), (SO, True)):
            epi_rope = self.epi_rope_fm(None, None, tok0)
            epi_plain = self.epi_plain_fm(None, None, None, tok0)

            def mkplain(dst, dname):
                return kb.epi_plain_fm(dst, dname, None, tok0)

            def gate_extra(s, ctx):
                ctx["gb"] = s.alloc("egb", 4)
                s.dma(ctx["gb"][0][0:36, :], gbias, ctx["gb"][1], writes=[ctx["gb"][1]])
                ctx["go"] = [s.alloc("ego%d" % i, 512 * 4) for i in range(2)]

            def epi_gate(ctx, job, tt, banks):
                s = kb.s
                o_ap, o_b = ctx["go"][tt % 2]
                gb, gbb = ctx["gb"]
                bank = banks[0]
                s.add("act", lambda e, o=o_ap[0:36, :], i=s.psum[bank][0:36, :], b=gb[0:36, 0:1]: e.activation(o, i, AF.Sigmoid, bias=b),
                      reads=[s.psbuf[bank], gbb], writes=[o_b])
                s.dma(d["gatesT"][:, tt * 512:(tt + 1) * 512], o_ap[0:36, :], o_b, reads=[o_b], pwrites=[kb.db("gatesT", tt)])

            panels = []
            otok = tok0 - SO

            def ropejob(off, roff, dst, dname, row0):
                return dict(cols=[(off, 128), (roff, 128)], epi=kb.epi_rope_fm(None, None, tok0 if dst is not d["qT"] else 0),
                            dst=dst, dname=dname, row0=row0)
            if own:
                for hp in range(6):
                    segs = [(w_in, hp * 256, 256), (w_rot, hp * 256, 256)]
                    jobs = []
                    for j in range(2):
                        jb = dict(cols=[(j * 128, 128), (256 + j * 128, 128)], dst=d["qT"], dname="qT", row0=(hp * 2 + j) * 128)
                        jb["epi"] = self._rope_epi_own()
                        jobs.append(jb)
                    panels.append(dict(segs=segs, jobs=jobs))
            for (kcol, rcol, dst, dname) in ((1536, 1536, d["kcmpT"], "kcmpT"), (2048, 1792, d["kslcT"], "kslcT"),
                                             (2560, 2048, d["kwinT"], "kwinT")):
                segs = [(w_in, kcol, 256), (w_rot, rcol, 256)]
                jobs = []
                for g in range(2):
                    jobs.append(dict(cols=[(g * 128, 128), (256 + g * 128, 128)], dst=dst, dname=dname, row0=g * 128,
                                     epi=self._rope_epi_all(tok0)))
                panels.append(dict(segs=segs, jobs=jobs))
            segs = [(w_in, 1792, 256)]
            jobs = [dict(cols=[(g * 128, 128)], row0=g * 128, epi=mkplain(d["vcmpT"], "vcmpT")) for g in range(2)]
            panels.append(dict(segs=segs, jobs=jobs))
            if own:
                segs = [(w_in, 3072, 36), (w_in, 3108, 256)]
                jobs = [dict(cols=[(0, 36)], epi=epi_gate)]
                for j in range(2):
                    jobs.append(dict(cols=[(36 + j * 128, 128)], row0=j * 128, epi=kb.epi_plain_fm(d["qmT"], "qmT", None, 0)))
                panels.append(dict(segs=segs, jobs=jobs))
                segs = [(w_in, 3108 + 256, 256)]
                jobs = []
                for j in range(2):
                    jobs.append(dict(cols=[(j * 128, 128)], row0=(2 + j) * 128, epi=kb.epi_plain_fm(d["qmT"], "qmT", None, 0)))
                panels.append(dict(segs=segs, jobs=jobs))
            self.cur_tok0 = tok0
            self.stage_lfm(xT, "xnT", tok0, SO, 16, panels, self.rope_setup(tok0, SO, gate_extra if own else None))

    def _rope_epi_all(self, tok0):
        kb = self

        def epi(ctx, job, tt, banks):
            kb._rope_core(ctx, job, tt, banks, tok0 + tt * 512, (tok0 // 512) + tt)
        return epi

    def _rope_epi_own(self):
        kb = self

        def epi(ctx, job, tt, banks):
            kb._rope_core(ctx, job, tt, banks, tt * 512, tt)
        return epi

    def _rope_core(self, ctx, job, tt, banks, col0, dbi):
        s = self.s
        bz, br = banks
        t1, t1b = ctx["t1"][ctx["oi"] % 2]
        t2, t2b = ctx["t2"][ctx["oi"] % 2]
        o_ap, o_b = ctx["oring"][ctx["oi"] % len(ctx["oring"])]
        ctx["oi"] += 1
        cs, csb = ctx["cos"]
        sn, snb = ctx["sin"]
        s.add("dve", lambda e, o=t1, a=s.psum[bz][:, :], b=cs[:, tt * 512:(tt + 1) * 512]: e.tensor_tensor(o, a, b, ALU.mult),
              reads=[s.psbuf[bz], csb], writes=[t1b])
        s.add("dve", lambda e, o=t2, a=s.psum[br][:, :], b=sn[:, tt * 512:(tt + 1) * 512]: e.tensor_tensor(o, a, b, ALU.mult),
              reads=[s.psbuf[br], snb], writes=[t2b])
        s.add("pool", lambda e, o=o_ap, a=t1, b=t2: e.tensor_tensor(o, a, b, ALU.add), reads=[t1b, t2b], writes=[o_b])
        r0 = job["row0"]
        s.dma(job["dst"][r0:r0 + 128, col0:col0 + 512], o_ap, o_b, reads=[o_b], pwrites=[self.db(job["dname"], dbi)])

    def stage_cmp(self, w1k, w2k, pek, w1v, w2v, pev):
        s = self.s
        d = self.d
        for kv, (w1, w2, peT, srcT, sname) in enumerate(((w1k, w2k, pek, d["kcmpT"], "kcmpT"), (w1v, w2v, pev, d["vcmpT"], "vcmpT"))):
            mk = s.mark()
            w1_ap, w1_b = s.alloc("cw1", 32 * 256 * 2, BF16)
            for q in range(0, 32, 8):
                s.dma(sub3(w1_ap, q * 256, 256, 8, 1, 256), dview(w1, q * 128, 1024, 0, 256), w1_b, pwrites=[w1_b], q="pool")
            w2_ap, w2_b = s.alloc("cw2", 2 * 128 * 2, BF16)
            s.dma(sub3(w2_ap, 0, 128, 2, 1, 128), dview(w2, 0, 256, 0, 128), w2_b, writes=[w2_b], q="pool")
            pe_ap, pe_b = s.alloc("cpe", 32 * 2, BF16)
            s.dma(pe_ap, peT, pe_b, writes=[pe_b], q="pool")
            bias_ap, bias_b = s.alloc("cbias", 2 * 4)
            for hc in range(2):
                bank = self.bank_rr % 8
                self.bank_rr += 1
                for l in range(32):
                    s.add("pe", lambda e, o=s.psum[bank][:, 0:1], lh=w1_ap[:, l * 256 + hc * 128:l * 256 + hc * 128 + 128], r=pe_ap[:, l:l + 1],
                          st=(l == 0), sp=(l == 31): e.matmul(o, lh, r, start=st, stop=sp), reads=[w1_b, pe_b], writes=[s.psbuf[bank]])
                s.add("dve", lambda e, o=bias_ap[:, hc:hc + 1], i=s.psum[bank][:, 0:1]: e.tensor_copy(o, i), reads=[s.psbuf[bank]], pwrites=[bias_b])
            for g in range(2):
                k_ap, k_b = s.alloc("ck%d" % g, SV * 2, BF16)
                s.dma(k_ap, srcT[g * 128:(g + 1) * 128, :], k_b, reads=[self.db(sname, i) for i in range(8)], writes=[k_b])
                hs_ap, hs_b = s.alloc("chs%d" % g, 2 * 256 * 2, BF16)
                for hc in range(2):
                    bank = self.bank_rr % 8
                    self.bank_rr += 1
                    for l in range(32):
                        s.add("pe", lambda e, o=s.psum[bank][:, 0:255], lh=w1_ap[:, l * 256 + hc * 128:l * 256 + hc * 128 + 128],
                              r=k_ap[:, l:l + 16 * 254 + 1:16], st=(l == 0), sp=(l == 31): e.matmul(o, lh, r, start=st, stop=sp),
                              reads=[w1_b, k_b], writes=[s.psbuf[bank]])
                    s.add("act", lambda e, o=hs_ap[:, hc * 256:hc * 256 + 255], i=s.psum[bank][:, 0:255], b=bias_ap[:, hc:hc + 1]:
                          e.activation(o, i, AF.Silu, bias=b), reads=[s.psbuf[bank], bias_b], pwrites=[hs_b])
                o_ap, o_b = s.alloc("cout%d" % g, 256 * 2, BF16)
                if kv == 0:
                    bank = self.bank_rr % 8
                    self.bank_rr += 1
                    for hc in range(2):
                        s.add("pe", lambda e, o=s.psum[bank][:, 0:255], lh=w2_ap[:, hc * 128:(hc + 1) * 128], r=hs_ap[:, hc * 256:hc * 256 + 255],
                              st=(hc == 0), sp=(hc == 1): e.matmul(o, lh, r, start=st, stop=sp), reads=[w2_b, hs_b], writes=[s.psbuf[bank]])
                    s.add("pool", lambda e, o=o_ap: e.memset(o, 0.0), writes=[o_b])
                    s.add("act", lambda e, o=o_ap[:, 0:255], i=s.psum[bank][:, 0:255]: e.copy(o, i), reads=[s.psbuf[bank]], pwrites=[o_b])
                    s.dma(d["kcT"][g * 128:(g + 1) * 128, :], o_ap, o_b, reads=[o_b], pwrites=[self.db("kcT")])
                else:
                    s.add("pool", lambda e, o=o_ap: e.memset(o, 0.0), writes=[o_b])
                    for ct in range(2):
                        ncn = 128 if ct == 0 else 127
                        bank = self.bank_rr % 8
                        self.bank_rr += 1
                        for hc in range(2):
                            s.add("pe", lambda e, o=s.psum[bank][0:ncn, 0:128], lh=hs_ap[:, hc * 256 + ct * 128:hc * 256 + ct * 128 + ncn],
                                  r=w2_ap[:, hc * 128:(hc + 1) * 128], st=(hc == 0), sp=(hc == 1): e.matmul(o, lh, r, start=st, stop=sp),
                                  reads=[w2_b, hs_b], writes=[s.psbuf[bank]])
                        s.add("act", lambda e, o=o_ap[0:ncn, ct * 128:(ct + 1) * 128], i=s.psum[bank][0:ncn, 0:128]: e.copy(o, i),
                              reads=[s.psbuf[bank]], pwrites=[o_b])
                    s.dma(dview(d["vc"], g * 256, 256, 0, 128), sub3(o_ap, 0, 128, 2, 1, 128), o_b, reads=[o_b], pwrites=[self.db("vc")])
            s.release(mk)

    def stage_attn_a(self):
        s = self.s
        d = self.d
        mk0 = s.mark()
        def ld(name, src, nbytes, dt, q="sp", parts=128):
            ap, b = s.alloc(name, nbytes, dt)
            s.dma(ap[0:parts, :], src, b, writes=[b], q=q)
            return ap, b
        mcmp, mcmp_b = ld("mcmp", d["m_cmp"], 8 * 512 * 2, BF16, "pool")
        mwin, mwin_b = ld("mwin", d["m_win"], 8 * 512 * 2, BF16, "pool")
        mwin0, mwin0_b = ld("mwin0", d["m_win0"], 4 * 512 * 2, BF16, "pool")
        E, E_b = ld("E", d["c_E"], SV * 2, BF16, "pool", 64)
        mmap, mmap_b = ld("mmap", d["c_mmap"], 2 * 64 * 4, F32)
        ph = [s.alloc("aph%d" % i, 512 * 4) for i in range(2)]
        selM, selM_b = ld("selM", d["selM"], 4 * 256 * 4, F32)
        selA, selA_b = ld("selA", d["selA"], 4 * 256 * 4, F32)
        selmat, selmat_b = ld("selmat", d["c_selmat"], 36 * 128 * 4, F32, "sp", 36)
        gat, gat_b = s.alloc("gat", SO * 4)
        s.dma(gat[0:36, :], d["gatesT"], gat_b, reads=[self.db("gatesT", i) for i in range(4)], writes=[gat_b])
        ctx = dict(si=0, pi=0, ri=0, sbanks=[0, 1, 2], pt=[s.alloc("apt%d" % i, 512 * 2, BF16) for i in range(4)],
                   rd=[s.alloc("ard%d" % i, 512 * 4) for i in range(3)])
        pn = [s.alloc("apn%d" % i, 512 * 2, BF16) for i in range(4)]
        Gs = [s.alloc("aG%d" % i, 512 * 4) for i in range(3)]
        ocs = [s.alloc("aocs%d" % i, 512 * 4) for i in range(6)]
        tb = [s.alloc("atb%d" % i, 512 * 4) for i in range(4)]
        fb = [s.alloc("afb%d" % i, 512 * 4) for i in range(2)]
        oring = [s.alloc("aor%d" % i, 512 * 2, BF16) for i in range(3)]
        sc_ap, sc_b = s.alloc("asc", 256 * 4)
        m16, m16_b = s.alloc("am16", 4 * 16 * 4)
        wk, wk_b = s.alloc("awk", 256 * 4)
        selb, selb_b = s.alloc("aselb", 256 * 2, BF16)
        selbT, selbT_b = s.alloc("aselbT", 512 * 2, BF16)
        psb = [s.psum[i][:, :].bitcast(BF16) for i in range(8)]
        gi_ = 0
        oi = 0
        for g in range(2):
            mk = s.mark()
            kc_ap, kc_b = s.alloc("akc", 256 * 2, BF16)
            s.dma(kc_ap, d["kcT"][g * 128:(g + 1) * 128, :], kc_b, reads=self.rd("kcT"), writes=[kc_b])
            vc_ap, vc_b = s.alloc("avc", 256 * 2, BF16)
            s.dma(sub3(vc_ap, 0, 128, 2, 1, 128), dview(d["vc"], g * 256, 256, 0, 128), vc_b, reads=self.rd("vc"), writes=[vc_b])
            ks_ap, ks_b = s.alloc("aks", SV * 2, BF16)
            kw_ap, kw_b = s.alloc("akw", SV * 2, BF16)
            s.dma(ks_ap, d["kslcT"][g * 128:(g + 1) * 128, :], ks_b, reads=[self.db("kslcT", i) for i in range(8)], writes=[ks_b])
            s.dma(kw_ap, d["kwinT"][g * 128:(g + 1) * 128, :], kw_b, reads=[self.db("kwinT", i) for i in range(8)], writes=[kw_b])
            vs_ap, vs_b = s.alloc("avs", SV * 2, BF16)
            vw_ap, vw_b = s.alloc("avw", SV * 2, BF16)
            for q in range(4):
                s.dma(sub3(vs_ap, q * 8 * 128, 128, 8, 1, 128), dview(d["vsw"], q * 1024, 1024, g * 128, 128), vs_b,
                      reads=[self.db("vsw", i) for i in range(8)], pwrites=[vs_b])
                s.dma(sub3(vw_ap, q * 8 * 128, 128, 8, 1, 128), dview(d["vsw"], q * 1024, 1024, 256 + g * 128, 128), vw_b,
                      reads=[self.db("vsw", i) for i in range(8)], pwrites=[vw_b])
            q_ap, q_b = s.alloc("aq", 6 * SO * 2, BF16)
            for p in range(6):
                s.dma(q_ap[:, p * SO:(p + 1) * SO], d["qT"][(g * 6 + p) * 128:(g * 6 + p + 1) * 128, :], q_b,
                      reads=[self.db("qT", i) for i in range(4)], pwrites=[q_b])
            for qt in range(4):
                t0v = SO + qt * 512
                njt = (t0v + 512) // 128
                bI = 5
                for p in range(6):
                    qr = q_ap[:, p * SO + qt * 512:p * SO + qt * 512 + 512]
                    pts = []
                    for ct in range(2):
                        pt, ptb = self.attn_unit(ctx, kc_ap[:, ct * 128:(ct + 1) * 128], [kc_b], qr, [q_b],
                                                 [(self.ident, mcmp[:, (qt * 2 + ct) * 512:(qt * 2 + ct + 1) * 512], [self.ident_b, mcmp_b])],
                                                 None, [], None, 3, ct == 0, ct == 1)
                        pts.append((pt, ptb))
                    r, rb = self.recip_den(ctx, 3)
                    pns = []
                    for ct in range(2):
                        pa, pb = pn[(p * 2 + ct) % 4]
                        s.add("pool", lambda e, o=pa, a=pts[ct][0], b=r: e.tensor_tensor(o, a, b, ALU.mult),
                              reads=[pts[ct][1], rb], writes=[pb])
                        pns.append((pa, pb))
                    for ct in range(2):
                        s.add("pe", lambda e, o=s.psum[4][:, :], l=vc_ap[:, ct * 128:(ct + 1) * 128], r_=pns[ct][0], st=(ct == 0), sp=(ct == 1):
                              e.matmul(o, l, r_, start=st, stop=sp), reads=[vc_b, pns[ct][1]], writes=[s.psbuf[4]])
                    for ct in range(2):
                        if p == 0:
                            s.add("pool", lambda e, o=ph[ct][0], a=pns[ct][0]: e.tensor_copy(o, a), reads=[pns[ct][1]], writes=[ph[ct][1]])
                        else:
                            s.add("pool", lambda e, o=ph[ct][0], a=pns[ct][0]: e.tensor_tensor(o, o, a, ALU.add),
                                  reads=[pns[ct][1], ph[ct][1]], writes=[ph[ct][1]])
                    hh = g * 6 + p
                    G, Gb = Gs[gi_ % 3]
                    gi_ += 1
                    bG = 6 + (gi_ % 2)
                    s.add("pe", lambda e, o=s.psum[bG][:, :], l=selmat[0:36, (hh * 3) * 128:(hh * 3 + 1) * 128], r_=gat[0:36, qt * 512:(qt + 1) * 512]:
                          e.matmul(o, l, r_, start=True, stop=True), reads=[selmat_b, gat_b], writes=[s.psbuf[bG]])
                    s.add("act", lambda e, o=G, i=s.psum[bG][:, :]: e.copy(o, i), reads=[s.psbuf[bG]], writes=[Gb])
                    s.add("dve", lambda e, o=ocs[p][0], a=s.psum[4][:, :], b=G: e.tensor_tensor(o, a, b, ALU.mult),
                          reads=[s.psbuf[4], Gb], writes=[ocs[p][1]])
                for qs in range(4):
                    for ct in range(2):
                        s.add("pe", lambda e, o=s.psum[bI][:, qs * 64:(qs + 1) * 64], l=ph[ct][0][:, qs * 128:(qs + 1) * 128],
                              r_=mmap[:, ct * 64:(ct + 1) * 64], st=(ct == 0), sp=(ct == 1):
                              e.matmul(o, l, r_, start=st, stop=sp), reads=[ph[ct][1], mmap_b], writes=[s.psbuf[bI]])
                s.add("dve", lambda e, o=sc_ap, a=s.psum[bI][:, 0:256], b=selM[:, qt * 256:(qt + 1) * 256]: e.tensor_tensor(o, a, b, ALU.mult),
                      reads=[s.psbuf[bI], selM_b], writes=[sc_b])
                s.add("dve", lambda e, o=sc_ap, b=selA[:, qt * 256:(qt + 1) * 256]: e.tensor_tensor(o, o, b, ALU.add),
                      reads=[sc_b, selA_b], writes=[sc_b])
                for qs in range(4):
                    scq = sc_ap[:, qs * 64:(qs + 1) * 64]
                    mm = m16[:, qs * 16:(qs + 1) * 16]
                    s.add("dve", lambda e, o=mm[:, 0:8], i=scq: e.max(o, i), reads=[sc_b], pwrites=[m16_b])
                    s.add("dve", lambda e, o=wk[:, qs * 64:(qs + 1) * 64], m=mm[:, 0:8], i=scq: e.match_replace(o, m, i, -3e9),
                          reads=[sc_b, m16_b], pwrites=[wk_b])
                    s.add("dve", lambda e, o=mm[:, 8:16], i=wk[:, qs * 64:(qs + 1) * 64]: e.max(o, i), reads=[wk_b, m16_b], pwrites=[m16_b])
                    s.add("dve", lambda e, o=mm[:, 15:16]: e.tensor_scalar(o, o, -5e8, None, ALU.max), reads=[m16_b], pwrites=[m16_b])
                    s.add("dve", lambda e, o=selb[:, qs * 64:(qs + 1) * 64], i=scq, t=mm[:, 15:16]: e.tensor_scalar(o, i, t, NEG, ALU.is_lt, ALU.mult),
                          reads=[sc_b, m16_b], pwrites=[selb_b])
                for qs in range(4):
                    s.add("pe", lambda e, o=psb[7][0:64, qs * 128:(qs + 1) * 128], i=selb[:, qs * 64:(qs + 1) * 64]: e.transpose(o, i, self.ident),
                          reads=[selb_b, self.ident_b], writes=[s.psbuf[7]])
                s.add("act", lambda e, o=selbT[0:64, :], i=psb[7][0:64, 0:512]: e.copy(o, i), reads=[s.psbuf[7]], writes=[selbT_b])
                for p in range(6):
                    qr = q_ap[:, p * SO + qt * 512:p * SO + qt * 512 + 512]
                    hh = g * 6 + p
                    for jt in range(njt):
                        ex = [(E[0:64, jt * 128:(jt + 1) * 128], selbT[0:64, :], [E_b, selbT_b])]
                        o_ = jt * 128 - t0v
                        if o_ >= 0:
                            mi = (o_ + 512) // 128
                            ex.append((self.ident, mwin[:, mi * 512:(mi + 1) * 512], [self.ident_b, mwin_b]))
                        self.attn_unit(ctx, ks_ap[:, jt * 128:(jt + 1) * 128], [ks_b], qr, [q_b], ex,
                                       vs_ap[:, jt * 128:(jt + 1) * 128], [vs_b], 3, 4, jt == 0, jt == njt - 1)
                    jt0 = (t0v - 512) // 128
                    for jt in range(jt0, njt):
                        o_ = jt * 128 - t0v
                        mi = (o_ + 512) // 128
                        if qt == 0 and o_ < 0:
                            mt_ap, mt_b = mwin0[:, mi * 512:(mi + 1) * 512], mwin0_b
                        else:
                            mt_ap, mt_b = mwin[:, mi * 512:(mi + 1) * 512], mwin_b
                        self.attn_unit(ctx, kw_ap[:, jt * 128:(jt + 1) * 128], [kw_b], qr, [q_b], [(self.ident, mt_ap, [self.ident_b, mt_b])],
                                       vw_ap[:, jt * 128:(jt + 1) * 128], [vw_b], 5, 6, jt == jt0, jt == njt - 1)
                    ts = []
                    for bi, (bacc, bden) in enumerate(((3, 4), (5, 6))):
                        G, Gb = Gs[gi_ % 3]
                        gi_ += 1
                        s.add("pe", lambda e, o=s.psum[7][:, :], l=selmat[0:36, (hh * 3 + 1 + bi) * 128:(hh * 3 + 2 + bi) * 128],
                              r_=gat[0:36, qt * 512:(qt + 1) * 512]: e.matmul(o, l, r_, start=True, stop=True),
                              reads=[selmat_b, gat_b], writes=[s.psbuf[7]])
                        s.add("act", lambda e, o=G, i=s.psum[7][:, :]: e.copy(o, i), reads=[s.psbuf[7]], writes=[Gb])
                        r, rb = self.recip_den(ctx, bden)
                        f, fbb = fb[bi]
                        s.add("pool", lambda e, o=f, a=r, b=G: e.tensor_tensor(o, a, b, ALU.mult), reads=[rb, Gb], writes=[fbb])
                        t, tbb = tb[(oi * 2 + bi) % 4]
                        s.add("dve", lambda e, o=t, a=s.psum[bacc][:, :], b=f: e.tensor_tensor(o, a, b, ALU.mult),
                              reads=[s.psbuf[bacc], fbb], writes=[tbb])
                        ts.append((t, tbb))
                    o_ap, o_b = oring[oi % 3]
                    oi += 1
                    if "dbg_br" in d:
                        for bi_, (ap_, b_) in enumerate((ocs[p], ts[0], ts[1])):
                            s.dma(d["dbg_br"][bi_ * 1536 + hh * 128:bi_ * 1536 + (hh + 1) * 128, qt * 512:(qt + 1) * 512], ap_, b_, reads=[b_])
                    s.add("pool", lambda e, o=ts[0][0], a=ts[0][0], b=ocs[p][0]: e.tensor_tensor(o, a, b, ALU.add),
                          reads=[ocs[p][1], ts[0][1]], writes=[ts[0][1]])
                    s.add("pool", lambda e, o=o_ap, a=ts[0][0], b=ts[1][0]: e.tensor_tensor(o, a, b, ALU.add),
                          reads=[ts[0][1], ts[1][1]], writes=[o_b])
                    s.dma(d["oT"][hh * 128:(hh + 1) * 128, qt * 512:(qt + 1) * 512], o_ap, o_b, reads=[o_b], pwrites=[self.db("oT", qt)])
            s.release(mk)
        s.release(mk0)

    def stage_kvshared(self, w_kv, w_kv_rot):
        kb = self
        d = self.d
        panels = []
        for hp in range(2):
            segs = [(w_kv, hp * 256, 256), (w_kv_rot, hp * 256, 256)]
            jobs = [dict(cols=[(j * 128, 128), (256 + j * 128, 128)], dst=d["kshT"], dname="kshT", row0=(hp * 2 + j) * 128,
                         epi=self._rope_epi_own()) for j in range(2)]
            panels.append(dict(segs=segs, jobs=jobs))
        self.stage_lfm(d["kvnT"], "kvnT", 0, SO, 16, panels, self.rope_setup(SO, SO))
        self.stage_tm_bf16(d["kvnT"], "kvnT", 16, 0, SO, [(w_kv, 512, 512)], d["vsh"], "vsh")


NG = 8


def build_phase_a(debug=()):
    nc = bass.Bass("TRN2", target_bir_lowering=False)
    kb = KB(nc)
    I = kb.inp
    xv = I("xv", [SV, DM])
    memb = I("memb", [256, DM])
    I("cosT", [128, SV]); I("sinT", [128, SV]); I("gains", [128, NG * 16])
    I("c_ident", [128, 128]); I("c_ones", [128, 128])
    I("m_cmp", [128, 8 * 512]); I("m_win", [128, 8 * 512]); I("m_win0", [128, 4 * 512])
    I("c_E", [64, SV]); I("c_mmap", [128, 128]); I("selM", [128, 1024]); I("selA", [128, 1024]); I("c_selmat", [36, 36 * 128])
    w_in = I("a_w_in", [DM, 3620]); w_rot = I("a_w_rot", [DM, 2304]); gbias = I("a_gbias", [36, 1])
    w1k = I("a_w1k", [4096, 256]); w2k = I("a_w2k", [256, 128]); pek = I("a_pekT", [128, 32])
    w1v = I("a_w1v", [4096, 256]); w2v = I("a_w2v", [256, 128]); pev = I("a_pevT", [128, 32])
    wmkv = I("a_w_mem_kv", [DM, 1024]); wout = I("a_w_out", [DM, DM])
    wg = I("a_w_gate", [DM, DFF]); wu = I("a_w_up", [DM, DFF]); wd = I("a_w_down", [DFF, DM])
    wkv = I("w_kv", [DM, 1024]); wkvr = I("w_kv_rot", [DM, 512])

    def S(name, shape, dt):
        if name in debug:
            return kb.outp(name, shape, dt)
        return kb.scr(name, shape, dt)
    S("xnT", [DM, SV], BF16)
    S("qT", [1536, SO], BF16); S("kcmpT", [256, SV], BF16); S("vcmpT", [256, SV], BF16)
    S("kslcT", [256, SV], BF16); S("kwinT", [256, SV], BF16); S("vsw", [SV, 512], BF16)
    S("gatesT", [36, SO], F32); S("qmT", [512, SO], BF16)
    S("kcT", [256, 256], BF16); S("vc", [512, 128], BF16)
    S("mkT", [512, 256], BF16); S("mv", [256, 512], BF16)
    S("oT", [DM, SO], BF16); S("h1", [SO, DM], F32); S("hnT", [DM, SO], BF16); S("hidT", [DFF, SO], BF16)
    kb.outp("h2", [SO, DM], F32)
    S("kvnT", [DM, SO], BF16)
    kb.outp("kshT", [512, SO], BF16); kb.outp("vsh", [SO, 512], BF16)
    if "dbg_br" in debug:
        kb.outp("dbg_br", [3 * 1536, SO], F32)
    d = kb.d
    kb.consts(NG)
    stop = kb.stop_after if hasattr(kb, "stop_after") else None
    kb.stage_norm(xv, None, SV, [0], [(d["xnT"], "xnT", 0)])
    kb.stage_inproj_a(w_in, w_rot, gbias)
    kb.stage_tm_bf16(d["xnT"], "xnT", 16, 0, SV, [(w_in, 2304, 256), (w_in, 2816, 256)], d["vsw"], "vsw")
    kb.stage_cmp(w1k, w2k, pek, w1v, w2v, pev)
    kb.stage_memkv(memb, 1, wmkv)
    kb.stage_attn_a()
    kb.stage_mem_attn(d["qmT"], "qmT", d["mkT"], d["mv"], d["oT"], "oT", 12)
    kb.stage_down(d["oT"], "oT", 16, wout, xv[SO:SV, :], None, d["h1"], "h1", TB=2048)
    kb.stage_norm(d["h1"], "h1", SO, [2], [(d["hnT"], "hnT", 0)])
    kb.stage_ffn(d["hnT"], "hnT", wg, wu, wd, d["h1"], "h1", d["h2"], "h2")
    kb.stage_norm(d["h2"], "h2", SO, [3], [(d["kvnT"], "kvnT", 0)])
    kb.stage_kvshared(wkv, wkvr)
    kb.s.finalize()
    return nc, kb


def rope_tabs(half):
    inv = (1.0 / (10000.0 ** (np.arange(0, 128, 2, dtype=np.float32) / 128))).astype(np.float32)
    pos = np.arange(SV, dtype=np.float32) - (0 if half == 1 else SO)
    pos = np.maximum(pos, 0).astype(np.float32)
    ang = (pos[:, None] * inv[None, :]).astype(np.float32)
    c = np.cos(ang).astype(np.float32).T
    sn = np.sin(ang).astype(np.float32).T
    cosT = np.concatenate([c, c], 0)
    sinT = np.concatenate([-sn, sn], 0)
    return np.ascontiguousarray(cosT), np.ascontiguousarray(sinT)


def rot_cols(w, heads):
    outs = []
    for c0 in heads:
        outs.append(w[:, c0 + 64:c0 + 128])
        outs.append(w[:, c0:c0 + 64])
    return np.ascontiguousarray(np.concatenate(outs, 1))


def gain_arr(gs):
    return np.ascontiguousarray(np.concatenate([g.reshape(16, 128).T for g in gs], 1).astype(np.float32))


def band_mask(o, w, prevmask):
    jj = np.arange(128)[:, None]
    qq = np.arange(512)[None, :]
    dist = qq - jj - o
    m = np.where((dist >= 0) & (dist <= w), 0.0, NEG).astype(np.float32)
    if prevmask:
        m[:] = NEG
    return m


def attn_consts(half):
    out = {}
    m_cmp = np.zeros((128, 8, 512), np.float32)
    for qt in range(4):
        for ct in range(2):
            c = ct * 128 + np.arange(128)[:, None]
            t = SO + qt * 512 + np.arange(512)[None, :]
            valid = (16 * c + 31 <= t) & (c <= 254)
            if half == 0:
                valid &= (c >= 128)
            m_cmp[:, qt * 2 + ct, :] = np.where(valid, 0.0, NEG)
    out["m_cmp"] = m_cmp.reshape(128, -1)
    mw = np.zeros((128, 8, 512), np.float32)
    for mi in range(8):
        mw[:, mi, :] = band_mask(mi * 128 - 512, 511, False)
    out["m_win"] = mw.reshape(128, -1)
    mw0 = np.zeros((128, 4, 512), np.float32)
    for mi in range(4):
        mw0[:, mi, :] = band_mask(mi * 128 - 512, 511, half == 0)
    out["m_win0"] = mw0.reshape(128, -1)
    selM = np.zeros((128, 4, 4, 64), np.float32)
    selA = np.zeros((128, 4, 4, 64), np.float32)
    sblk = np.arange(64)[None, :]
    first = 0 if half == 1 else 32
    for qt in range(4):
        for qs in range(4):
            t = SO + qt * 512 + qs * 128 + np.arange(128)[:, None]
            cur = t // 64
            elig = (sblk * 64 <= t) & (sblk >= first)
            f0 = (sblk == first) & elig
            f1 = (sblk == cur)
            f2 = (sblk == cur - 1) & (sblk >= first)
            A = np.where(elig, 0.0, -1e9)
            A = np.where(f0, 1e9, A)
            A = np.where(f2, 2e9, A)
            A = np.where(f1, 3e9, A)
            M = (elig & ~f0 & ~f1 & ~f2).astype(np.float32)
            selM[:, qt, qs, :] = M
            selA[:, qt, qs, :] = A
    out["selM"] = selM.reshape(128, -1)
    out["selA"] = selA.reshape(128, -1)
    return out


def shared_consts():
    out = {}
    out["c_ident"] = np.eye(128, dtype=np.float32)
    out["c_ones"] = np.ones((128, 128), np.float32)
    E = np.zeros((64, SV), np.float32)
    E[np.arange(SV) // 64, np.arange(SV)] = 1.0
    out["c_E"] = E
    mm = np.zeros((2, 128, 64), np.float32)
    for c in range(255):
        for sb in range(64):
            if (16 * c < 64 * sb + 64) and (16 * c + 32 > 64 * sb):
                mm[c // 128, c % 128, sb] = 1.0
    out["c_mmap"] = np.ascontiguousarray(mm.transpose(1, 0, 2).reshape(128, 128))
    sm = np.zeros((36, 36, 128), np.float32)
    for i in range(36):
        sm[i, i, :] = 1.0
    out["c_selmat"] = sm.reshape(36, -1)
    return out


def phase_a_inputs(inp, b, half, sc):
    f = np.float32
    x = inp["x"][b]
    if half == 1:
        xv = x
    else:
        xv = np.concatenate([np.zeros((SO, DM), f), x[:SO]], 0)
    cosT, sinT = rope_tabs(half)
    m = dict(sc)
    m.update(attn_consts(half))
    m["xv"] = np.ascontiguousarray(xv)
    m["memb"] = np.ascontiguousarray(inp["mem"][b])
    m["cosT"] = cosT
    m["sinT"] = sinT
    return m


def weights_a(inp):
    w = {}
    w_in = inp["a_w_in"][0]
    w["a_w_in"] = w_in
    heads = [h * 128 for h in range(12)] + [1536 + (i * 2 + g) * 128 for i in (0, 2, 4) for g in range(2)]
    w["a_w_rot"] = rot_cols(w_in, heads)
    w["a_gbias"] = np.ascontiguousarray(inp["a_gate_bias"][0].reshape(36, 1))
    w["a_w1k"] = inp["a_cmp_w1_k"][0]; w["a_w2k"] = inp["a_cmp_w2_k"][0]
    w["a_pekT"] = np.ascontiguousarray(inp["a_cmp_pe_k"][0].T)
    w["a_w1v"] = inp["a_cmp_w1_v"][0]; w["a_w2v"] = inp["a_cmp_w2_v"][0]
    w["a_pevT"] = np.ascontiguousarray(inp["a_cmp_pe_v"][0].T)
    w["a_w_mem_kv"] = inp["a_w_mem_kv"][0]; w["a_w_out"] = inp["a_w_out"][0]
    w["a_w_gate"] = inp["a_w_gate"][0]; w["a_w_up"] = inp["a_w_up"][0]; w["a_w_down"] = inp["a_w_down"][0]
    w["w_kv"] = inp["w_kv_shared"]
    w["w_kv_rot"] = rot_cols(inp["w_kv_shared"], [h * 128 for h in range(4)])
    w["gains"] = gain_arr([inp["a_norm_attn"][0], inp["a_norm_mem"][0], inp["a_norm_ffn"][0], inp["kv_norm"],
                           inp["b_norm_attn"][0], inp["b_norm_mem"][0], inp["b_norm_ffn"][0], inp["final_norm"]])
    return {k: np.ascontiguousarray(np.asarray(v, dtype=np.float32)) for k, v in w.items()}


def _kb_stage_attn_b(self):
    s = self.s
    d = self.d
    mk0 = s.mark()
    mdil, mdil_b = s.alloc("mdil", 5 * 512 * 2, BF16)
    s.dma(mdil, d["m_dil"], mdil_b, writes=[mdil_b], q="pool")
    mdil0, mdil0_b = s.alloc("mdil0", 512 * 2, BF16)
    s.dma(mdil0, d["m_dil0"], mdil0_b, writes=[mdil0_b], q="pool")
    ctx = dict(si=0, pi=0, ri=0, sbanks=[0, 1, 2], pt=[s.alloc("bpt%d" % i, 512 * 2, BF16) for i in range(4)],
               rd=[s.alloc("brd%d" % i, 512 * 4) for i in range(2)])
    oring = [s.alloc("bor%d" % i, 512 * 2, BF16) for i in range(3)]
    oi = 0
    ui = 0
    for hh in range(4):
        mk = s.mark()
        k_ap, k_b = s.alloc("bk", SV * 2, BF16)
        kown = [self.db("kshT", i) for i in range(4)]
        vown = [self.db("vsh", i) for i in range(4)]
        s.dma(k_ap[:, 0:SO], d["kg"][hh * 128:(hh + 1) * 128, :], k_b, reads=self.rd("kg"), pwrites=[k_b])
        s.dma(k_ap[:, SO:SV], d["kshT"][hh * 128:(hh + 1) * 128, :], k_b, reads=kown, pwrites=[k_b])
        v1, v1_b = s.alloc("bv1", SV * 2, BF16)
        v4, v4_b = s.alloc("bv4", SV * 2, BF16)
        v16, v16_b = s.alloc("bv16", SV * 2, BF16)
        for pi_, (vsrc, rds) in enumerate(((d["vg"][0:SO, :], self.rd("vg")), (d["vsh"], vown))):
            vcol = vsrc[:, hh * 128:(hh + 1) * 128]
            for q in range(2):
                s.dma(sub3(v1, (pi_ * 16 + q * 8) * 128, 128, 8, 1, 128), dview(vsrc, q * 1024, 1024, hh * 128, 128), v1_b,
                      reads=rds, pwrites=[v1_b])
            r4 = vcol.rearrange("(jt p r) c -> r p jt c", p=128, r=4)
            for rho in range(4):
                s.dma(sub3(v4, rho * 1024 + pi_ * 4 * 128, 128, 4, 1, 128), r4[rho], v4_b, reads=rds, pwrites=[v4_b])
            r16 = vcol.rearrange("(jt p r) c -> r p jt c", p=128, r=16)
            for rho in range(16):
                s.dma(sub3(v16, rho * 256 + pi_ * 128, 128, 1, 1, 128), r16[rho], v16_b, reads=rds, pwrites=[v16_b])
        q_ap, q_b = s.alloc("bq", 3 * SO * 2, BF16)
        for g in range(3):
            s.dma(q_ap[:, g * SO:(g + 1) * SO], d["qbT"][(g * 4 + hh) * 128:(g * 4 + hh + 1) * 128, :], q_b,
                  reads=[self.db("qbT", i) for i in range(4)], pwrites=[q_b])
        accS, accS_b = s.alloc("bacc", SO * 4)
        denS, denS_b = s.alloc("bden", SO * 4)

        def flush(bacc, bden, n, oa, od, first):
            if first:
                s.add("dve", lambda e, o=oa, i=s.psum[bacc][:, 0:n]: e.tensor_copy(o, i), reads=[s.psbuf[bacc]], pwrites=[accS_b])
                s.add("dve", lambda e, o=od, i=s.psum[bden][:, 0:n]: e.tensor_copy(o, i), reads=[s.psbuf[bden]], pwrites=[denS_b])
            else:
                s.add("dve", lambda e, o=oa, i=s.psum[bacc][:, 0:n]: e.tensor_tensor(o, o, i, ALU.add), reads=[s.psbuf[bacc], accS_b], pwrites=[accS_b])
                s.add("dve", lambda e, o=od, i=s.psum[bden][:, 0:n]: e.tensor_tensor(o, o, i, ALU.add), reads=[s.psbuf[bden], denS_b], pwrites=[denS_b])

        for qt in range(4):
            t0v = SO + qt * 512
            bacc, bden = (3, 4) if (ui % 2 == 0) else (5, 6)
            ui += 1
            offs = [-128, 0, 128, 256, 384]
            for i, o_ in enumerate(offs):
                jt = (t0v + o_) // 128
                if qt == 0 and o_ < 0:
                    m_ap, m_b = mdil0, mdil0_b
                else:
                    m_ap, m_b = mdil[:, i * 512:(i + 1) * 512], mdil_b
                self.attn_unit(ctx, k_ap[:, jt * 128:(jt + 1) * 128], [k_b], q_ap[:, qt * 512:(qt + 1) * 512], [q_b],
                               [(self.ident, m_ap, [self.ident_b, m_b])], v1[:, jt * 128:(jt + 1) * 128], [v1_b], bacc, bden, i == 0, i == 4)
            flush(bacc, bden, 512, accS[:, qt * 512:(qt + 1) * 512], denS[:, qt * 512:(qt + 1) * 512], True)
        for rho in range(4):
            bacc, bden = (3, 4) if (ui % 2 == 0) else (5, 6)
            ui += 1
            qr = q_ap[:, SO + rho:SO + SO:4]
            offs = [-128, 0, 128, 256, 384]
            for i, o_ in enumerate(offs):
                ju0 = 512 + o_
                if o_ < 0:
                    m_ap, m_b = mdil0, mdil0_b
                else:
                    m_ap, m_b = mdil[:, i * 512:(i + 1) * 512], mdil_b
                kl = k_ap[:, rho + 4 * ju0:rho + 4 * (ju0 + 127) + 1:4]
                jtu = ju0 // 128
                self.attn_unit(ctx, kl, [k_b], qr, [q_b], [(self.ident, m_ap, [self.ident_b, m_b])],
                               v4[:, rho * 1024 + jtu * 128:rho * 1024 + (jtu + 1) * 128], [v4_b], bacc, bden, i == 0, i == 4)
            flush(bacc, bden, 512, accS[:, rho:SO:4], denS[:, rho:SO:4], False)
        for rho in range(16):
            bacc, bden = (3, 4) if (ui % 2 == 0) else (5, 6)
            ui += 1
            qr = q_ap[:, 2 * SO + rho:3 * SO:16]
            for i, o_ in enumerate([-128, 0]):
                ju0 = 128 + o_
                if o_ < 0:
                    m_ap, m_b = mdil0[:, 0:128], mdil0_b
                else:
                    m_ap, m_b = mdil[:, 512:512 + 128], mdil_b
                kl = k_ap[:, rho + 16 * ju0:rho + 16 * (ju0 + 127) + 1:16]
                jtu = ju0 // 128
                self.attn_unit(ctx, kl, [k_b], qr, [q_b], [(self.ident, m_ap, [self.ident_b, m_b])],
                               v16[:, rho * 256 + jtu * 128:rho * 256 + (jtu + 1) * 128], [v16_b], bacc, bden, i == 0, i == 1, n=128)
            flush(bacc, bden, 128, accS[:, rho:SO:16], denS[:, rho:SO:16], False)
        s.add("dve", lambda e: e.tensor_scalar(denS, denS, TINY, None, ALU.max), reads=[denS_b], writes=[denS_b])
        s.add("dve", lambda e: e.reciprocal(denS, denS), reads=[denS_b], writes=[denS_b])
        for qt in range(4):
            o_ap, o_b = oring[oi % 3]
            oi += 1
            s.add("dve", lambda e, o=o_ap, a=accS[:, qt * 512:(qt + 1) * 512], b=denS[:, qt * 512:(qt + 1) * 512]: e.tensor_tensor(o, a, b, ALU.mult),
                  reads=[accS_b, denS_b], writes=[o_b])
            s.dma(d["oT"][hh * 128:(hh + 1) * 128, qt * 512:(qt + 1) * 512], o_ap, o_b, reads=[o_b], pwrites=[self.db("oT", qt)])
        s.release(mk)
    s.release(mk0)


KB.stage_attn_b = _kb_stage_attn_b


def _kb_stage_inproj_b(self, w_in, w_rot):
    kb = self
    d = self.d
    panels = []
    for hp in range(6):
        segs = [(w_in, hp * 256, 256), (w_rot, hp * 256, 256)]
        jobs = [dict(cols=[(j * 128, 128), (256 + j * 128, 128)], dst=d["qbT"], dname="qbT", row0=(hp * 2 + j) * 128,
                     epi=self._rope_epi_own()) for j in range(2)]
        panels.append(dict(segs=segs, jobs=jobs))
    segs = [(w_in, 1536, 512)]
    jobs = [dict(cols=[(j * 128, 128)], row0=j * 128, epi=kb.epi_plain_fm(d["qmT"], "qmT", None, 0)) for j in range(4)]
    panels.append(dict(segs=segs, jobs=jobs))
    self.stage_lfm(d["bnT"], "bnT", 0, SO, 16, panels, self.rope_setup(SO, SO))


KB.stage_inproj_b = _kb_stage_inproj_b


def _kb_stage_final_norm(self, src, srcname, dst):
    s = self.s
    mk = s.mark()
    fg, fg_b = s.alloc("fg", DM * 4)
    s.dma(fg, self.d["fgain"], fg_b, writes=[fg_b])
    hb = [s.alloc("fh%d" % i, DM * 4) for i in range(3)]
    ob = [s.alloc("fo%d" % i, DM * 4) for i in range(2)]
    junk_ap, junk_b = s.alloc("fjunk", DM * 2, BF16)
    st = [s.alloc("fst%d" % i, 4 * 4) for i in range(3)]
    for it in range(SO // 128):
        h_ap, h_b = hb[it % 3]
        st_ap, st_b = st[it % 3]
        o_ap, o_b = ob[it % 2]
        s.dma(h_ap, src[it * 128:(it + 1) * 128, :], h_b, reads=self.rd(srcname, it // 4), writes=[h_b])
        s.add("act", lambda e, h=h_ap, o=st_ap[:, 0:1]: e.activation(junk_ap, h, AF.Square, accum_out=o), reads=[h_b], pwrites=[junk_b, st_b])
        s.add("dve", lambda e, a=st_ap: e.tensor_scalar(a[:, 1:2], a[:, 0:1], 1.0 / 2048, EPS, ALU.mult, ALU.add), reads=[st_b], pwrites=[st_b])
        s.add("act", lambda e, a=st_ap: e.sqrt(a[:, 1:2], a[:, 1:2]), reads=[st_b], pwrites=[st_b])
        s.add("dve", lambda e, a=st_ap: e.reciprocal(a[:, 2:3], a[:, 1:2]), reads=[st_b], pwrites=[st_b])
        s.add("act", lambda e, o=o_ap, h=h_ap, sc=st_ap[:, 2:3]: e.activation(o, h, AF.Copy, scale=sc), reads=[h_b, st_b], writes=[o_b])
        s.add("dve", lambda e, o=o_ap: e.tensor_tensor(o, o, fg, ALU.mult), reads=[o_b, fg_b], writes=[o_b])
        s.dma(dst[it * 128:(it + 1) * 128, :], o_ap, o_b, reads=[o_b])
    s.release(mk)


KB.stage_final_norm = _kb_stage_final_norm


def build_phase_b(debug=()):
    nc = bass.Bass("TRN2", target_bir_lowering=False)
    kb = KB(nc)
    I = kb.inp
    h2 = I("h2in", [SO, DM])
    memb = I("memb", [256, DM])
    I("kshTv", [512, SV], BF16); I("vshv", [SV, 512], BF16)
    I("cosT", [128, SV]); I("sinT", [128, SV]); I("gains", [128, NG * 16]); I("fgain", [128, DM])
    I("c_ident", [128, 128]); I("c_ones", [128, 128])
    I("m_dil", [128, 5 * 512]); I("m_dil0", [128, 512])
    w_in = I("b_w_in", [DM, 2048]); w_rot = I("b_w_rot", [DM, 1536])
    wmkv = I("b_w_mem_kv", [DM, 1024]); wout = I("b_w_out", [1024, DM])
    wg = I("b_w_gate", [DM, DFF]); wu = I("b_w_up", [DM, DFF]); wd = I("b_w_down", [DFF, DM])

    def S(name, shape, dt):
        if name in debug:
            return kb.outp(name, shape, dt)
        return kb.scr(name, shape, dt)
    S("bnT", [DM, SO], BF16); S("qbT", [1536, SO], BF16); S("qmT", [512, SO], BF16)
    S("mkT", [512, 256], BF16); S("mv", [256, 512], BF16)
    S("oT", [1024, SO], BF16); S("h3", [SO, DM], F32); S("hnT", [DM, SO], BF16); S("hidT", [DFF, SO], BF16)
    S("h4", [SO, DM], F32)
    kb.outp("out", [SO, DM], F32)
    d = kb.d
    kb.consts(NG)
    kb.stage_norm(h2, None, SO, [4], [(d["bnT"], "bnT", 0)])
    kb.stage_inproj_b(w_in, w_rot)
    kb.stage_memkv(memb, 5, wmkv)
    kb.stage_attn_b()
    kb.stage_mem_attn(d["qmT"], "qmT", d["mkT"], d["mv"], d["oT"], "oT", 4)
    kb.stage_down(d["oT"], "oT", 8, wout, h2, None, d["h3"], "h3", TB=2048)
    kb.stage_norm(d["h3"], "h3", SO, [6], [(d["hnT"], "hnT", 0)])
    kb.stage_ffn(d["hnT"], "hnT", wg, wu, wd, d["h3"], "h3", d["h4"], "h4")
    kb.stage_final_norm(d["h4"], "h4", d["out"])
    kb.s.finalize()
    return nc, kb


def dil_consts(half):
    out = {}
    md = np.zeros((128, 5, 512), np.float32)
    for i, o_ in enumerate([-128, 0, 128, 256, 384]):
        md[:, i, :] = band_mask(o_, 128, False)
    out["m_dil"] = md.reshape(128, -1)
    out["m_dil0"] = band_mask(-128, 128, half == 0)
    return out


def weights_b(inp):
    w = {}
    w_in = inp["b_w_in"][0]
    w["b_w_in"] = w_in
    w["b_w_rot"] = rot_cols(w_in, [h * 128 for h in range(12)])
    w["b_w_mem_kv"] = inp["b_w_mem_kv"][0]; w["b_w_out"] = inp["b_w_out"][0]
    w["b_w_gate"] = inp["b_w_gate"][0]; w["b_w_up"] = inp["b_w_up"][0]; w["b_w_down"] = inp["b_w_down"][0]
    w["fgain"] = np.broadcast_to(inp["final_norm"][None, :], (128, DM))
    return {k: np.ascontiguousarray(np.asarray(v, dtype=np.float32)) for k, v in w.items()}


_PROG = {}


def kernel(**inputs):
    inp = {k: np.asarray(v) for k, v in inputs.items()}
    import ml_dtypes
    bf = ml_dtypes.bfloat16
    if "a" not in _PROG:
        _PROG["a"] = build_phase_a()
        _PROG["b"] = build_phase_b()
    nca, _ = _PROG["a"]
    ncb, _ = _PROG["b"]
    sc = shared_consts()
    wa = weights_a(inp)
    maps = []
    for c in range(8):
        m = phase_a_inputs(inp, c // 2, c % 2, sc)
        m.update(wa)
        maps.append(m)
    ra = run_bass_kernel_spmd(nca, maps, core_ids=list(range(8))).results
    del maps
    wb = weights_b(inp)
    maps = []
    for c in range(8):
        b, half = c // 2, c % 2
        m = {}
        m["h2in"] = np.ascontiguousarray(ra[c]["h2"])
        ksh = np.asarray(ra[c]["kshT"])
        vsh = np.asarray(ra[c]["vsh"])
        if half == 1:
            kprev = np.asarray(ra[c - 1]["kshT"]); vprev = np.asarray(ra[c - 1]["vsh"])
        else:
            kprev = np.zeros_like(ksh); vprev = np.zeros_like(vsh)
        m["kshTv"] = np.ascontiguousarray(np.concatenate([kprev, ksh], 1))
        m["vshv"] = np.ascontiguousarray(np.concatenate([vprev, vsh], 0))
        m["memb"] = np.ascontiguousarray(inp["mem"][b])
        cosT, sinT = rope_tabs(half)
        m["cosT"] = cosT; m["sinT"] = sinT
        m["gains"] = wa["gains"]
        m["c_ident"] = sc["c_ident"]; m["c_ones"] = sc["c_ones"]
        m.update(dil_consts(half))
        m.update(wb)
        maps.append(m)
    rb = run_bass_kernel_spmd(ncb, maps, core_ids=list(range(8))).results
    out = np.zeros((4, 4096, DM), np.float32)
    for c in range(8):
        b, half = c // 2, c % 2
        out[b, half * SO:(half + 1) * SO, :] = rb[c]["out"]
    return out


def build_fused(debug=()):
    nc = bass.Bass("TRN2", target_bir_lowering=False, num_devices=8)
    kb = KB(nc)
    I = kb.inp
    xv = I("xv", [SV, DM])
    memb = I("memb", [256, DM])
    I("cosT", [128, SV]); I("sinT", [128, SV]); I("gains", [128, NG * 16]); I("fgain", [128, DM])
    I("c_ident", [128, 128]); I("c_ones", [128, 128])
    I("m_cmp", [128, 8 * 512]); I("m_win", [128, 8 * 512]); I("m_win0", [128, 4 * 512])
    I("c_E", [64, SV]); I("c_mmap", [128, 128]); I("selM", [128, 1024]); I("selA", [128, 1024]); I("c_selmat", [36, 36 * 128])
    I("m_dil", [128, 5 * 512]); I("m_dil0", [128, 512])
    w_in = I("a_w_in", [DM, 3620]); w_rot = I("a_w_rot", [DM, 2304]); gbias = I("a_gbias", [36, 1])
    w1k = I("a_w1k", [4096, 256]); w2k = I("a_w2k", [256, 128]); pek = I("a_pekT", [128, 32])
    w1v = I("a_w1v", [4096, 256]); w2v = I("a_w2v", [256, 128]); pev = I("a_pevT", [128, 32])
    wmkv = I("a_w_mem_kv", [DM, 1024]); wout = I("a_w_out", [DM, DM])
    wg = I("a_w_gate", [DM, DFF]); wu = I("a_w_up", [DM, DFF]); wd = I("a_w_down", [DFF, DM])
    wkv = I("w_kv", [DM, 1024]); wkvr = I("w_kv_rot", [DM, 512])
    bw_in = I("b_w_in", [DM, 2048]); bw_rot = I("b_w_rot", [DM, 1536])
    bwmkv = I("b_w_mem_kv", [DM, 1024]); bwout = I("b_w_out", [1024, DM])
    bwg = I("b_w_gate", [DM, DFF]); bwu = I("b_w_up", [DM, DFF]); bwd = I("b_w_down", [DFF, DM])
    S = kb.scr
    S("xnT", [DM, SV], BF16)
    S("qT", [1536, SO], BF16); S("kcmpT", [256, SV], BF16); S("vcmpT", [256, SV], BF16)
    S("kslcT", [256, SV], BF16); S("kwinT", [256, SV], BF16); S("vsw", [SV, 512], BF16)
    S("gatesT", [36, SO], F32); S("qmT", [512, SO], BF16)
    S("kcT", [256, 256], BF16); S("vc", [512, 128], BF16)
    S("mkT", [512, 256], BF16); S("mv", [256, 512], BF16)
    S("oT", [DM, SO], BF16); S("h1", [SO, DM], F32); S("hnT", [DM, SO], BF16); S("hidT", [DFF, SO], BF16)
    S("h2", [SO, DM], F32)
    S("kvnT", [DM, SO], BF16); S("bnT", [DM, SO], BF16)
    S("kshT", [512, SO], BF16); S("vsh", [SO, 512], BF16)
    S("kg", [1024, SO], BF16); S("vg", [2 * SO, 512], BF16)
    S("qbT", [1536, SO], BF16); S("h3", [SO, DM], F32); S("h4", [SO, DM], F32)
    kb.outp("out", [SO, DM], F32)
    d = kb.d
    kb.consts(NG)
    kb.stage_norm(xv, None, SV, [0], [(d["xnT"], "xnT", 0)])
    kb.stage_inproj_a(w_in, w_rot, gbias)
    kb.stage_tm_bf16(d["xnT"], "xnT", 16, 0, SV, [(w_in, 2304, 256), (w_in, 2816, 256)], d["vsw"], "vsw")
    kb.stage_cmp(w1k, w2k, pek, w1v, w2v, pev)
    kb.stage_memkv(memb, 1, wmkv)
    kb.stage_attn_a()
    kb.stage_mem_attn(d["qmT"], "qmT", d["mkT"], d["mv"], d["oT"], "oT", 12)
    kb.stage_down(d["oT"], "oT", 16, wout, xv[SO:SV, :], None, d["h1"], "h1", TB=2048)
    kb.stage_norm(d["h1"], "h1", SO, [2], [(d["hnT"], "hnT", 0)])
    kb.stage_ffn(d["hnT"], "hnT", wg, wu, wd, d["h1"], "h1", d["h2"], "h2")
    kb.stage_norm(d["h2"], "h2", SO, [3, 4], [(d["kvnT"], "kvnT", 0), (d["bnT"], "bnT", 0)])
    kb.stage_kvshared(wkv, wkvr)
    groups = [[0, 1], [2, 3], [4, 5], [6, 7]]
    kb.s.collective(lambda e: e.collective_compute("AllGather", ALU.bypass, replica_groups=groups, ins=[d["kshT"]], outs=[d["kg"]]),
                    reads=[kb.db("kshT", i) for i in range(4)], writes=[kb.db("kg")])
    kb.s.collective(lambda e: e.collective_compute("AllGather", ALU.bypass, replica_groups=groups, ins=[d["vsh"]], outs=[d["vg"]]),
                    reads=[kb.db("vsh", i) for i in range(4)], writes=[kb.db("vg")])
    kb.stage_inproj_b(bw_in, bw_rot)
    kb.stage_memkv(memb, 5, bwmkv)
    kb.stage_attn_b()
    kb.stage_mem_attn(d["qmT"], "qmT", d["mkT"], d["mv"], d["oT"], "oT", 4)
    kb.stage_down(d["oT"], "oT", 8, bwout, d["h2"], "h2", d["h3"], "h3", TB=2048)
    kb.stage_norm(d["h3"], "h3", SO, [6], [(d["hnT"], "hnT", 0)])
    kb.stage_ffn(d["hnT"], "hnT", bwg, bwu, bwd, d["h3"], "h3", d["h4"], "h4")
    kb.stage_final_norm(d["h4"], "h4", d["out"])
    kb.s.finalize()
    return nc, kb


def kernel(**inputs):
    inp = {k: np.asarray(v) for k, v in inputs.items()}
    if "f" not in _PROG:
        _PROG["f"] = build_fused()
    nc, _ = _PROG["f"]
    sc = shared_consts()
    wa = weights_a(inp)
    wb = weights_b(inp)
    maps = []
    for c in range(8):
        b, half = c // 2, c % 2
        m = phase_a_inputs(inp, b, half, sc)
        m.update(dil_consts(half))
        m.update(wa)
        m.update(wb)
        maps.append(m)
    res = run_bass_kernel_spmd(nc, maps, core_ids=list(range(8))).results
    out = np.zeros((4, 4096, DM), np.float32)
    for c in range(8):
        b, half = c // 2, c % 2
        out[b, half * SO:(half + 1) * SO, :] = res[c]["out"]
    return out
```

```python
import numpy as np
import concourse.bass as bass
import concourse.mybir as mybir
from concourse.bass_utils import run_bass_kernel_spmd

F32 = mybir.dt.float32
BF16 = mybir.dt.bfloat16
AF = mybir.ActivationFunctionType
ALU = mybir.AluOpType
AX = mybir.AxisListType


ENGS = ("pe", "act", "dve", "pool", "sp")


class Buf:
    __slots__ = ("name", "w", "r", "sem", "ndma", "lo", "hi", "space")

    def __init__(self, name, space="sb", lo=0, hi=0):
        self.name = name
        self.w = []
        self.r = []
        self.sem = None
        self.ndma = 0
        self.lo = lo
        self.hi = hi
        self.space = space


class DSem:
    __slots__ = ("handle", "ndma", "idx", "inc")

    def __init__(self, idx, inc=16):
        self.handle = None
        self.ndma = 0
        self.idx = idx
        self.inc = inc


class Op:
    __slots__ = ("eng", "idx", "fn", "waits", "flagged", "rank", "dma_buf", "pe_group")

    def __init__(self, eng, idx, fn):
        self.eng = eng
        self.idx = idx
        self.fn = fn
        self.waits = {}
        self.flagged = False
        self.rank = 0
        self.dma_buf = None


class Sched:
    def __init__(self, nc, arena_bytes=200 * 1024):
        self.nc = nc
        self.ops = {e: [] for e in ENGS}
        self.waited = {e: {} for e in ENGS}
        self.arena_bytes = arena_bytes
        self.arena = nc.alloc_sbuf_tensor("arena", [128, arena_bytes // 4], F32)
        self.arena_top = 0
        self.live = []
        self.retired = []
        self.psum = [nc.alloc_psum_tensor("ps%d" % i, [128, 512], F32) for i in range(8)]
        self.psbuf = [Buf("ps%d" % i, "ps") for i in range(8)]
        self.nsem = 0
        self.eng_sem = {}
        self.dma_rr = 0
        self.NPOOL = 90
        self.pool = [DSem(i) for i in range(self.NPOOL)]
        self.cc_sem = DSem(1000, inc=1)

    def alloc(self, name, nbytes, dtype=F32):
        req = nbytes
        nbytes = (nbytes + 31) // 32 * 32
        lo = self.arena_top
        hi = lo + nbytes
        assert hi <= self.arena_bytes, "arena overflow %s: %d > %d" % (name, hi, self.arena_bytes)
        self.arena_top = hi
        b = Buf(name, "sb", lo, hi)
        keep = []
        for rb in self.retired:
            if rb.lo < hi and lo < rb.hi:
                b.r.extend(rb.w)
                b.r.extend(rb.r)
                if rb.lo < lo or rb.hi > hi:
                    keep.append(rb)
            else:
                keep.append(rb)
        self.retired = keep
        dd = {}
        for dep in b.r:
            if dep[0] == "e":
                k = ("e", dep[1].eng)
                if k not in dd or dd[k][1].idx < dep[1].idx:
                    dd[k] = dep
            else:
                dd[("d", dep[1].idx)] = dep
        b.r = list(dd.values())
        self.live.append(b)
        ap = self.arena[:, lo // 4:hi // 4]
        if dtype != F32:
            ap = ap.bitcast(dtype)
            ap = ap[:, 0:req // 2]
        else:
            ap = ap[:, 0:req // 4]
        return ap, b

    def mark(self):
        return (self.arena_top, len(self.live))

    def release(self, mark):
        top, n = mark
        for b in self.live[n:]:
            self.retired.append(b)
        self.live = self.live[:n]
        self.arena_top = top

    def _dep_of(self, op):
        if op.dma_buf is not None:
            return ("d", op.dma_buf)
        return ("e", op)

    def _add_wait(self, op, dep):
        if dep[0] == "e":
            p = dep[1]
            if p.eng == "pe" and op.eng == "pe":
                return
            key = ("e", p.eng)
            cur = op.waits.get(key)
            if cur is None or cur.idx < p.idx:
                op.waits[key] = p
        else:
            b = dep[1]
            key = ("d", b.idx)
            op.waits[key] = (b, b.ndma * b.inc)

    def _collect(self, op, reads, writes, pwrites):
        for b in reads:
            for d in b.w:
                self._add_wait(op, d)
        for b in writes:
            for d in b.w:
                self._add_wait(op, d)
            for d in b.r:
                self._add_wait(op, d)
        for b in pwrites:
            for d in b.r:
                self._add_wait(op, d)

    @staticmethod
    def _same(d, me):
        if d[0] != me[0]:
            return False
        if me[0] == "e":
            return d[1].eng == me[1].eng
        return d[1] is me[1]

    def _register(self, me, reads, writes, pwrites):
        for b in writes:
            b.w = [me]
            b.r = []
        for b in pwrites:
            b.w = [d for d in b.w if not self._same(d, me)]
            b.w.append(me)
        for b in reads:
            if b in writes or b in pwrites:
                continue
            b.r = [d for d in b.r if not self._same(d, me)]
            b.r.append(me)

    def add(self, eng, fn, reads=(), writes=(), pwrites=()):
        op = Op(eng, len(self.ops[eng]), fn)
        self.ops[eng].append(op)
        reads, writes, pwrites = list(reads), list(writes), list(pwrites)
        self._collect(op, reads, writes, pwrites)
        self._register(("e", op), reads, writes, pwrites)
        return op

    def dma(self, out_ap, in_ap, sem_buf, reads=(), writes=(), pwrites=(), q=None):
        if q is None:
            q = "sp"
        op = Op(q, len(self.ops[q]), lambda e, o=out_ap, i=in_ap: e.dma_start(out=o, in_=i))
        self.ops[q].append(op)
        reads, writes, pwrites = list(reads), list(writes), list(pwrites)
        self._collect(op, reads, writes, pwrites)
        if sem_buf.sem is None:
            sem_buf.sem = self.pool[self.dma_rr % self.NPOOL]
            self.dma_rr += 1
        ds = sem_buf.sem
        op.dma_buf = ds
        ds.ndma += 1
        self._register(("d", ds), reads, writes, pwrites)
        return op

    def collective(self, fn, reads=(), writes=()):
        op = Op("pool", len(self.ops["pool"]), fn)
        self.ops["pool"].append(op)
        reads, writes = list(reads), list(writes)
        self._collect(op, reads, writes, [])
        ds = self.cc_sem
        op.dma_buf = ds
        ds.ndma += 1
        self._register(("d", ds), reads, writes, [])
        return op

    def finalize(self, final_bufs=()):
        nc = self.nc
        fin = Op("sp", len(self.ops["sp"]), None)
        for ds in self.pool + [self.cc_sem]:
            if ds.ndma > 0:
                fin.waits[("d", ds.idx)] = (ds, ds.ndma * ds.inc)
        self.ops["sp"].append(fin)
        for e in ENGS:
            for op in self.ops[e]:
                for k, v in op.waits.items():
                    if k[0] == "e":
                        v.flagged = True
        for e in ENGS:
            r = 0
            for op in self.ops[e]:
                if op.flagged:
                    r += 1
                    op.rank = r
        for e in ENGS:
            if e != "sp":
                self.eng_sem[e] = nc.alloc_semaphore("sem_" + e)
        n = 0
        for ds in self.pool + [self.cc_sem]:
            if ds.ndma > 0:
                ds.handle = nc.alloc_semaphore("dsem_%d" % ds.idx)
                n += 1
        self.n_dma_sems = n
        sched = self

        def emit(e, eng):
            waited = {}
            for op in sched.ops[e]:
                for k, v in op.waits.items():
                    if k[0] == "e":
                        sem = sched.eng_sem[v.eng]
                        val = v.rank
                        wk = ("e", v.eng)
                    else:
                        sem = v[0].handle
                        val = v[1]
                        wk = k
                    if waited.get(wk, 0) >= val:
                        continue
                    waited[wk] = val
                    eng.wait_ge(sem, val)
                if op.fn is None:
                    continue
                ins = op.fn(eng)
                if op.dma_buf is not None:
                    ins.then_inc(op.dma_buf.handle, op.dma_buf.inc)
                elif op.flagged:
                    ins.then_inc(sched.eng_sem[e], 1)

        with nc.Block() as block:
            @block.tensor
            def _(eng):
                emit("pe", eng)

            @block.scalar
            def _(eng):
                emit("act", eng)

            @block.vector
            def _(eng):
                emit("dve", eng)

            @block.gpsimd
            def _(eng):
                emit("pool", eng)

            @block.sync
            def _(eng):
                emit("sp", eng)

NEG = -30000.0
EPS = 1e-6
TINY = 1e-30
SV = 4096
SO = 2048
DM = 2048
DFF = 5632
SCALE = 128 ** -0.5


def sub3(a, off, s1, n1, s2, n2):
    return bass.AP(a.tensor, a.offset + off, [list(a.ap[0]), [s1, n1], [s2, n2]])


def dview(d, r0, nr, c0, nc_):
    return d[r0:r0 + nr, c0:c0 + nc_].rearrange("(k p) n -> p k n", p=128)


class KB:
    def __init__(self, nc):
        self.nc = nc
        self.s = Sched(nc, arena_bytes=198 * 1024)
        self.d = {}
        self.dbufs = {}
        self.bank_rr = 0
        self.outs = []

    def inp(self, name, shape, dt=F32):
        self.d[name] = self.nc.dram_tensor(name, list(shape), dt, kind="ExternalInput").ap()
        return self.d[name]

    def scr(self, name, shape, dt):
        self.d[name] = self.nc.dram_tensor(name, list(shape), dt).ap()
        return self.d[name]

    def outp(self, name, shape, dt=F32):
        self.d[name] = self.nc.dram_tensor(name, list(shape), dt, kind="ExternalOutput").ap()
        return self.d[name]

    def db(self, name, i=0):
        k = (name, i)
        if k not in self.dbufs:
            self.dbufs[k] = Buf("d_%s_%s" % (name, i), "dram")
        return self.dbufs[k]

    def rd(self, name, i=0):
        if name is None:
            return []
        return [self.db(name, i)]

    def consts(self, ngain):
        s = self.s
        self.ident, self.ident_b = s.alloc("ident", 128 * 2, BF16)
        self.ones, self.ones_b = s.alloc("ones", 128 * 2, BF16)
        self.gains, self.gains_b = s.alloc("gains", ngain * 16 * 4)
        s.dma(self.ident, self.d["c_ident"], self.ident_b, writes=[self.ident_b], q="pool")
        s.dma(self.ones, self.d["c_ones"], self.ones_b, writes=[self.ones_b], q="pool")
        s.dma(self.gains, self.d["gains"], self.gains_b, writes=[self.gains_b])

    def stage_norm(self, src, srcname, ntok, gidx, dsts, src_tt0=0):
        s = self.s
        mk = s.mark()
        hb = [s.alloc("nh%d" % i, 2048 * 4) for i in range(4)]
        yb = [s.alloc("ny%d" % i, 2048 * 2, BF16) for i in range(4)]
        junk_ap, junk_b = s.alloc("njunk", 2048 * 2, BF16)
        st = [s.alloc("nst%d" % i, 12 * 4) for i in range(2)]
        ob = [[s.alloc("no%d_%d" % (g, i), 16 * 512 * 2, BF16) for i in range(2)] for g in range(len(gidx))]
        psb = [s.psum[i][:, :].bitcast(BF16) for i in range(8)]
        ident, ident_b = self.ident, self.ident_b
        for tt in range(ntok // 512):
            st_ap, st_b = st[tt % 2]
            for sub in range(4):
                h_ap, h_b = hb[sub]
                r0 = tt * 512 + sub * 128
                s.dma(h_ap, src[r0:r0 + 128, :], h_b, reads=self.rd(srcname, src_tt0 + tt), writes=[h_b])
                s.add("act", lambda e, h=h_ap, o=st_ap[:, sub:sub + 1]: e.activation(junk_ap, h, AF.Square, accum_out=o),
                      reads=[h_b], writes=[junk_b], pwrites=[st_b])
            s.add("dve", lambda e, a=st_ap: e.tensor_scalar(a[:, 4:8], a[:, 0:4], 1.0 / 2048, EPS, ALU.mult, ALU.add),
                  reads=[st_b], pwrites=[st_b])
            s.add("act", lambda e, a=st_ap: e.sqrt(a[:, 4:8], a[:, 4:8]), reads=[st_b], pwrites=[st_b])
            s.add("dve", lambda e, a=st_ap: e.reciprocal(a[:, 8:12], a[:, 4:8]), reads=[st_b], pwrites=[st_b])
            for sub in range(4):
                h_ap, h_b = hb[sub]
                y_ap, y_b = yb[sub]
                s.add("act", lambda e, y=y_ap, h=h_ap, sc=st_ap[:, 8 + sub:9 + sub]: e.activation(y, h, AF.Copy, scale=sc),
                      reads=[h_b, st_b], writes=[y_b])
                for half in range(2):
                    bank = self.bank_rr % 8
                    self.bank_rr += 1
                    for k8 in range(8):
                        kc = half * 8 + k8
                        s.add("pe", lambda e, o=psb[bank][:, k8 * 128:(k8 + 1) * 128], i=y_ap[:, kc * 128:(kc + 1) * 128]:
                              e.transpose(o, i, ident), reads=[y_b, ident_b], writes=[s.psbuf[bank]])
                    for gi, g in enumerate(gidx):
                        o_ap, o_b = ob[gi][tt % 2]
                        out3 = sub3(o_ap, half * 8 * 512 + sub * 128, 512, 8, 1, 128)
                        in0 = sub3(psb[bank], 0, 128, 8, 1, 128)
                        ga = self.gains[:, g * 16 + half * 8:g * 16 + half * 8 + 8]
                        in1 = sub3(ga, 0, 1, 8, 0, 128)
                        s.add("dve", lambda e, o=out3, a=in0, b=in1: e.tensor_tensor(o, a, b, ALU.mult),
                              reads=[s.psbuf[bank], self.gains_b], pwrites=[o_b])
            for gi in range(len(gidx)):
                dst, dname, dtt0 = dsts[gi]
                o_ap, o_b = ob[gi][tt % 2]
                for q4 in range(4):
                    s.dma(dview(dst, q4 * 512, 512, (dtt0 + tt) * 512, 512),
                          sub3(o_ap, q4 * 4 * 512, 512, 4, 1, 512), o_b, reads=[o_b], pwrites=[self.db(dname, dtt0 + tt)])
        s.release(mk)

    def load_panel(self, p_ap, p_b, segs, KC, kgrp=4):
        s = self.s
        po = 0
        for (W, c0, n) in segs:
            for q in range(0, KC, kgrp):
                kn = min(kgrp, KC - q)
                s.dma(sub3(p_ap, q * 512 + po, 512, kn, 1, n), dview(W, q * 128, kn * 128, c0, n), p_b,
                      pwrites=[p_b], q="pool")
            po += n

    def stage_lfm(self, xT, xname, tok0, ntok, KC, panels, setup):
        s = self.s
        mk = s.mark()
        nt = ntok // 512
        xs = [s.alloc("lx%d" % i, KC * 512 * 2, BF16) for i in range(nt)]
        for tt in range(nt):
            x_ap, x_b = xs[tt]
            for q in range(0, KC, 4):
                s.dma(sub3(x_ap, q * 512, 512, 4, 1, 512), dview(xT, q * 128, 512, tok0 + tt * 512, 512), x_b,
                      reads=self.rd(xname, tok0 // 512 + tt), pwrites=[x_b])
        pr = [s.alloc("lp%d" % i, KC * 512 * 2, BF16) for i in range(3)]
        ctx = setup(s)
        npan = len(panels)
        for i in range(min(2, npan)):
            self.load_panel(pr[i % 3][0], pr[i % 3][1], panels[i]["segs"], KC)
        for i, pan in enumerate(panels):
            p_ap, p_b = pr[i % 3]
            for job in pan["jobs"]:
                nb = len(job["cols"])
                for tt in range(nt):
                    x_ap, x_b = xs[tt]
                    banks = []
                    for (off, n) in job["cols"]:
                        bank = self.bank_rr % 8
                        self.bank_rr += 1
                        banks.append(bank)
                        for kc in range(KC):
                            s.add("pe", lambda e, o=s.psum[bank][0:n, :], l=p_ap[:, kc * 512 + off:kc * 512 + off + n],
                                  r=x_ap[:, kc * 512:(kc + 1) * 512], st=(kc == 0), sp=(kc == KC - 1):
                                  e.matmul(o, l, r, start=st, stop=sp),
                                  reads=[p_b, x_b], writes=[s.psbuf[bank]])
                    job["epi"](ctx, job, tt, banks)
            if i + 2 < npan:
                self.load_panel(pr[(i + 2) % 3][0], pr[(i + 2) % 3][1], panels[i + 2]["segs"], KC)
        s.release(mk)

    def stage_ltm(self, aT, aname, KC, tok0, ntok, TB, panels, setup, epi):
        s = self.s
        mk = s.mark()
        ntb = TB // 512
        as_ = [s.alloc("ta%d" % i, KC * 512 * 2, BF16) for i in range(ntb)]
        pr = [s.alloc("tp%d" % i, KC * 512 * 2, BF16) for i in range(2)]
        ctx = setup(s)
        for tb in range(ntok // TB):
            for tt in range(ntb):
                a_ap, a_b = as_[tt]
                t0 = tok0 + tb * TB + tt * 512
                for q in range(0, KC, 4):
                    s.dma(sub3(a_ap, q * 512, 512, 4, 1, 512), dview(aT, q * 128, 512, t0, 512), a_b,
                          reads=self.rd(aname, t0 // 512), pwrites=[a_b])
            self.load_panel(pr[0][0], pr[0][1], panels[0], KC)
            for pi, segs in enumerate(panels):
                p_ap, p_b = pr[pi % 2]
                if pi + 1 < len(panels):
                    self.load_panel(pr[(pi + 1) % 2][0], pr[(pi + 1) % 2][1], panels[pi + 1], KC)
                ncol = sum(n for (_, _, n) in segs)
                for tt in range(ntb):
                    a_ap, a_b = as_[tt]
                    for t4 in range(4):
                        bank = self.bank_rr % 8
                        self.bank_rr += 1
                        for kc in range(KC):
                            s.add("pe", lambda e, o=s.psum[bank][:, 0:ncol], l=a_ap[:, kc * 512 + t4 * 128:kc * 512 + t4 * 128 + 128],
                                  r=p_ap[:, kc * 512:kc * 512 + ncol], st=(kc == 0), sp=(kc == KC - 1):
                                  e.matmul(o, l, r, start=st, stop=sp),
                                  reads=[p_b, a_b], writes=[s.psbuf[bank]])
                        epi(ctx, tok0 + tb * TB + tt * 512 + t4 * 128, pi, bank, ncol)
        s.release(mk)

    def epi_plain_fm(self, dst, dname, row0fn, tok0):
        kb = self

        def epi(ctx, job, tt, banks):
            s = kb.s
            n = job["cols"][0][1]
            o_ap, o_b = ctx["oring"][ctx["oi"] % len(ctx["oring"])]
            ctx["oi"] += 1
            bank = banks[0]
            s.add("act", lambda e, o=o_ap[0:n, :], i=s.psum[bank][0:n, :]: e.copy(o, i), reads=[s.psbuf[bank]], writes=[o_b])
            r0 = job["row0"]
            s.dma(dst[r0:r0 + n, tok0 + tt * 512:tok0 + tt * 512 + 512], o_ap[0:n, :], o_b, reads=[o_b],
                  pwrites=[kb.db(dname, (tok0 // 512) + tt)])
        return epi

    def epi_rope_fm(self, dst, dname, tok0):
        kb = self

        def epi(ctx, job, tt, banks):
            s = kb.s
            bz, br = banks
            t1, t1b = ctx["t1"][ctx["oi"] % 2]
            t2, t2b = ctx["t2"][ctx["oi"] % 2]
            o_ap, o_b = ctx["oring"][ctx["oi"] % len(ctx["oring"])]
            ctx["oi"] += 1
            cs, csb = ctx["cos"]
            sn, snb = ctx["sin"]
            s.add("dve", lambda e, o=t1, a=s.psum[bz][:, :], b=cs[:, tt * 512:(tt + 1) * 512]: e.tensor_tensor(o, a, b, ALU.mult),
                  reads=[s.psbuf[bz], csb], writes=[t1b])
            s.add("dve", lambda e, o=t2, a=s.psum[br][:, :], b=sn[:, tt * 512:(tt + 1) * 512]: e.tensor_tensor(o, a, b, ALU.mult),
                  reads=[s.psbuf[br], snb], writes=[t2b])
            s.add("pool", lambda e, o=o_ap, a=t1, b=t2: e.tensor_tensor(o, a, b, ALU.add), reads=[t1b, t2b], writes=[o_b])
            r0 = job["row0"]
            s.dma(job["dst"][r0:r0 + 128, tok0 + tt * 512:tok0 + tt * 512 + 512], o_ap, o_b, reads=[o_b],
                  pwrites=[kb.db(job["dname"], (tok0 // 512) + tt)])
        return epi

    def rope_setup(self, tok0, ntok, extra=None):
        kb = self

        def setup(s):
            ctx = {"oi": 0}
            ctx["oring"] = [s.alloc("eo%d" % i, 512 * 2, BF16) for i in range(4)]
            ctx["t1"] = [s.alloc("et1%d" % i, 512 * 4) for i in range(2)]
            ctx["t2"] = [s.alloc("et2%d" % i, 512 * 4) for i in range(2)]
            ctx["cos"] = s.alloc("ecos", ntok * 4)
            ctx["sin"] = s.alloc("esin", ntok * 4)
            s.dma(ctx["cos"][0], kb.d["cosT"][:, tok0:tok0 + ntok], ctx["cos"][1], writes=[ctx["cos"][1]])
            s.dma(ctx["sin"][0], kb.d["sinT"][:, tok0:tok0 + ntok], ctx["sin"][1], writes=[ctx["sin"][1]])
            if extra is not None:
                extra(s, ctx)
            return ctx
        return setup

    def stage_ffn(self, hnT, hnname, wg, wu, wd, h_in, h_in_name, h_out, h_out_name):
        kb = self
        hidT = self.d["hidT"]

        def setup(s):
            ctx = {"oi": 0}
            ctx["sg"] = [s.alloc("fsg%d" % i, 512 * 4) for i in range(3)]
            ctx["oring"] = [s.alloc("fo%d" % i, 512 * 2, BF16) for i in range(4)]
            return ctx

        def epi(ctx, job, tt, banks):
            s = kb.s
            bg, bu = banks
            sg, sgb = ctx["sg"][ctx["oi"] % 3]
            o_ap, o_b = ctx["oring"][ctx["oi"] % 4]
            ctx["oi"] += 1
            s.add("act", lambda e, o=sg, i=s.psum[bg][:, :]: e.activation(o, i, AF.Silu), reads=[s.psbuf[bg]], writes=[sgb])
            s.add("dve", lambda e, o=o_ap, a=s.psum[bu][:, :], b=sg: e.tensor_tensor(o, a, b, ALU.mult),
                  reads=[s.psbuf[bu], sgb], writes=[o_b])
            r0 = job["row0"]
            s.dma(hidT[r0:r0 + 128, tt * 512:(tt + 1) * 512], o_ap, o_b, reads=[o_b], pwrites=[kb.db("hidT", tt)])

        panels = []
        for pc in range(DFF // 256):
            segs = [(wg, pc * 256, 256), (wu, pc * 256, 256)]
            jobs = [dict(cols=[(j * 128, 128), (256 + j * 128, 128)], epi=epi, row0=pc * 256 + j * 128) for j in range(2)]
            panels.append(dict(segs=segs, jobs=jobs))
        self.stage_lfm(hnT, hnname, 0, SO, 16, panels, setup)
        self.stage_down(self.d["hidT"], "hidT", DFF // 128, wd, h_in, h_in_name, h_out, h_out_name, TB=1024)

    def stage_down(self, aT, aname, KC, W, h_in, h_in_name, h_out, h_out_name, TB):
        kb = self

        def setup(s):
            ctx = {"oi": 0}
            ctx["hin"] = [s.alloc("dh%d" % i, 512 * 4) for i in range(3)]
            ctx["oring"] = [s.alloc("do%d" % i, 512 * 4) for i in range(3)]
            return ctx

        def epi(ctx, tok, pi, bank, ncol):
            s = kb.s
            hi, hib = ctx["hin"][ctx["oi"] % 3]
            o_ap, o_b = ctx["oring"][ctx["oi"] % 3]
            ctx["oi"] += 1
            s.dma(hi, h_in[tok:tok + 128, pi * 512:(pi + 1) * 512], hib, reads=kb.rd(h_in_name, tok // 512), writes=[hib])
            s.add("dve", lambda e, o=o_ap, a=s.psum[bank][:, :], b=hi: e.tensor_tensor(o, a, b, ALU.add),
                  reads=[s.psbuf[bank], hib], writes=[o_b])
            s.dma(h_out[tok:tok + 128, pi * 512:(pi + 1) * 512], o_ap, o_b, reads=[o_b], pwrites=[kb.db(h_out_name, tok // 512)])
            if h_out_name == "out":
                kb.outs.append(o_b)

        panels = [[(W, pi * 512, 512)] for pi in range(4)]
        self.stage_ltm(aT, aname, KC, 0, SO, TB, panels, setup, epi)

    def stage_tm_bf16(self, aT, aname, KC, tok0, ntok, segs, dst, dname):
        kb = self

        def setup(s):
            return {"oi": 0, "oring": [s.alloc("vo%d" % i, 512 * 2, BF16) for i in range(4)]}

        def epi(ctx, tok, pi, bank, ncol):
            s = kb.s
            o_ap, o_b = ctx["oring"][ctx["oi"] % 4]
            ctx["oi"] += 1
            s.add("act", lambda e, o=o_ap[:, 0:ncol], i=s.psum[bank][:, 0:ncol]: e.copy(o, i), reads=[s.psbuf[bank]], writes=[o_b])
            s.dma(dst[tok:tok + 128, 0:ncol], o_ap[:, 0:ncol], o_b, reads=[o_b], pwrites=[kb.db(dname, tok // 512)])

        self.stage_ltm(aT, aname, KC, tok0, ntok, min(ntok, 2048), [segs], setup, epi)

    def unit_s(self, ctx, u):
        s = self.s
        n = u.get("n", 512)
        np_ = u.get("np_", 128)
        bS = ctx["sbanks"][ctx["si"] % len(ctx["sbanks"])]
        ctx["si"] += 1
        pt, ptb = ctx["pt"][ctx["pi"] % len(ctx["pt"])]
        ctx["pi"] += 1
        extras = u["extras"]
        ne = len(extras)
        s.add("pe", lambda e, o=s.psum[bS][0:np_, 0:n], l=u["klhs"], r=u["qrhs"], sp=(ne == 0): e.matmul(o, l, r, start=True, stop=sp),
              reads=u["krd"] + u["qrd"], writes=[s.psbuf[bS]])
        for i, (l, r, rds) in enumerate(extras):
            s.add("pe", lambda e, o=s.psum[bS][0:np_, 0:n], l=l, r=r, sp=(i == ne - 1): e.matmul(o, l, r, start=False, stop=sp),
                  reads=rds, writes=[s.psbuf[bS]])
        s.add("act", lambda e, o=pt[0:np_, 0:n], i=s.psum[bS][0:np_, 0:n]: e.activation(o, i, AF.Exp, scale=SCALE),
              reads=[s.psbuf[bS]], writes=[ptb])
        return pt, ptb

    def unit_pv(self, ctx, u, rec):
        s = self.s
        n = u.get("n", 512)
        np_ = u.get("np_", 128)
        pt, ptb = rec
        first, last = u["first"], u["last"]
        if u["vlhs"] is not None:
            s.add("pe", lambda e, o=s.psum[u["bacc"]][:, 0:n], l=u["vlhs"], r=pt[0:np_, 0:n], st=first, sp=last: e.matmul(o, l, r, start=st, stop=sp),
                  reads=u["vrd"] + [ptb], writes=[s.psbuf[u["bacc"]]])
        s.add("pe", lambda e, o=s.psum[u["bden"]][:, 0:n], l=self.ones[0:np_, :], r=pt[0:np_, 0:n], st=first, sp=last: e.matmul(o, l, r, start=st, stop=sp),
              reads=[self.ones_b, ptb], writes=[s.psbuf[u["bden"]]])
        if u.get("post") is not None:
            u["post"]()

    def run_units(self, ctx, units, skew=2):
        recs = []
        nu = len(units)
        for i in range(nu + skew):
            if i < nu:
                recs.append(self.unit_s(ctx, units[i]))
            j = i - skew
            if j >= 0:
                self.unit_pv(ctx, units[j], recs[j])
        return recs

    def attn_unit(self, ctx, klhs, krd, qrhs, qrd, extras, vlhs, vrd, bacc, bden, first, last, n=512, np_=128):
        u = dict(klhs=klhs, krd=krd, qrhs=qrhs, qrd=qrd, extras=extras, vlhs=vlhs, vrd=vrd, bacc=bacc, bden=bden,
                 first=first, last=last, n=n, np_=np_)
        rec = self.unit_s(ctx, u)
        self.unit_pv(ctx, u, rec)
        return rec

    def recip_den(self, ctx, bden, n=512):
        s = self.s
        r, rb = ctx["rd"][ctx["ri"] % len(ctx["rd"])]
        ctx["ri"] += 1
        s.add("dve", lambda e, o=r[:, 0:n], i=s.psum[bden][:, 0:n]: e.tensor_scalar(o, i, TINY, None, ALU.max),
              reads=[s.psbuf[bden]], writes=[rb])
        s.add("dve", lambda e, o=r[:, 0:n]: e.reciprocal(o, o), reads=[rb], writes=[rb])
        return r, rb

    def stage_mem_attn(self, qmT, qmname, mkT, mv, oT, oname, chunk0):
        s = self.s
        mk = s.mark()
        ctx = dict(si=0, pi=0, ri=0, sbanks=[0, 1, 2], pt=[s.alloc("mpt%d" % i, 512 * 2, BF16) for i in range(4)],
                   rd=[s.alloc("mrd%d" % i, 512 * 4) for i in range(2)])
        k_ap, k_b = s.alloc("mk", 4 * 256 * 2, BF16)
        v_ap, v_b = s.alloc("mv", 2 * 512 * 2, BF16)
        q_ap, q_b = s.alloc("mq", 4 * SO * 2, BF16)
        oring = [s.alloc("mo%d" % i, 512 * 2, BF16) for i in range(3)]
        s.dma(sub3(k_ap, 0, 256, 4, 1, 256), dview(mkT, 0, 512, 0, 256), k_b, reads=self.rd("mkT"), writes=[k_b])
        s.dma(sub3(v_ap, 0, 512, 2, 1, 512), dview(mv, 0, 256, 0, 512), v_b, reads=self.rd("mv"), writes=[v_b])
        for h in range(4):
            s.dma(q_ap[:, h * SO:(h + 1) * SO], qmT[h * 128:(h + 1) * 128, :], q_b,
                  reads=[self.db(qmname, i) for i in range(4)], pwrites=[q_b])
        units = []
        oi = 0
        for h in range(4):
            for qt in range(4):
                bacc, bden = (3, 4) if (oi % 2 == 0) else (5, 6)
                o_ap, o_b = oring[oi % 3]
                oi += 1

                def post(bacc=bacc, bden=bden, o_ap=o_ap, o_b=o_b, h=h, qt=qt):
                    r, rb = self.recip_den(ctx, bden)
                    s.add("dve", lambda e, o=o_ap, a=s.psum[bacc][:, :], b=r: e.tensor_tensor(o, a, b, ALU.mult),
                          reads=[s.psbuf[bacc], rb], writes=[o_b])
                    s.dma(oT[(chunk0 + h) * 128:(chunk0 + h + 1) * 128, qt * 512:(qt + 1) * 512], o_ap, o_b, reads=[o_b],
                          pwrites=[self.db(oname, qt)])
                for mt in range(2):
                    units.append(dict(klhs=k_ap[:, h * 256 + mt * 128:h * 256 + mt * 128 + 128], krd=[k_b],
                                      qrhs=q_ap[:, h * SO + qt * 512:h * SO + qt * 512 + 512], qrd=[q_b], extras=[],
                                      vlhs=v_ap[:, mt * 512 + h * 128:mt * 512 + h * 128 + 128], vrd=[v_b], bacc=bacc, bden=bden,
                                      first=(mt == 0), last=(mt == 1), post=(post if mt == 1 else None)))
        self.run_units(ctx, units)
        s.release(mk)

    def stage_memkv(self, mem, gi, wkv):
        kb = self
        s = self.s
        mk = s.mark()
        hb = [s.alloc("kh%d" % i, 2048 * 4) for i in range(2)]
        yb = [s.alloc("ky%d" % i, 2048 * 2, BF16) for i in range(2)]
        junk_ap, junk_b = s.alloc("kjunk", 2048 * 2, BF16)
        st_ap, st_b = s.alloc("kst", 12 * 4)
        o_ap, o_b = s.alloc("ko", 16 * 256 * 2, BF16)
        psb = [s.psum[i][:, :].bitcast(BF16) for i in range(8)]
        for sub in range(2):
            h_ap, h_b = hb[sub]
            s.dma(h_ap, mem[sub * 128:(sub + 1) * 128, :], h_b, writes=[h_b])
            s.add("act", lambda e, h=h_ap, o=st_ap[:, sub:sub + 1]: e.activation(junk_ap, h, AF.Square, accum_out=o),
                  reads=[h_b], writes=[junk_b], pwrites=[st_b])
        s.add("dve", lambda e, a=st_ap: e.tensor_scalar(a[:, 4:6], a[:, 0:2], 1.0 / 2048, EPS, ALU.mult, ALU.add), reads=[st_b], pwrites=[st_b])
        s.add("act", lambda e, a=st_ap: e.sqrt(a[:, 4:6], a[:, 4:6]), reads=[st_b], pwrites=[st_b])
        s.add("dve", lambda e, a=st_ap: e.reciprocal(a[:, 8:10], a[:, 4:6]), reads=[st_b], pwrites=[st_b])
        for sub in range(2):
            h_ap, h_b = hb[sub]
            y_ap, y_b = yb[sub]
            s.add("act", lambda e, y=y_ap, h=h_ap, sc=st_ap[:, 8 + sub:9 + sub]: e.activation(y, h, AF.Copy, scale=sc),
                  reads=[h_b, st_b], writes=[y_b])
            for half in range(2):
                bank = self.bank_rr % 8
                self.bank_rr += 1
                for k8 in range(8):
                    kc = half * 8 + k8
                    s.add("pe", lambda e, o=psb[bank][:, k8 * 128:(k8 + 1) * 128], i=y_ap[:, kc * 128:(kc + 1) * 128]:
                          e.transpose(o, i, kb.ident), reads=[y_b, kb.ident_b], writes=[s.psbuf[bank]])
                out3 = sub3(o_ap, half * 8 * 256 + sub * 128, 256, 8, 1, 128)
                in0 = sub3(psb[bank], 0, 128, 8, 1, 128)
                ga = self.gains[:, gi * 16 + half * 8:gi * 16 + half * 8 + 8]
                in1 = sub3(ga, 0, 1, 8, 0, 128)
                s.add("dve", lambda e, o=out3, a=in0, b=in1: e.tensor_tensor(o, a, b, ALU.mult),
                      reads=[s.psbuf[bank], self.gains_b], pwrites=[o_b])
        mkT = self.d["mkT"]
        mv = self.d["mv"]
        pr = [s.alloc("kp%d" % i, 16 * 512 * 2, BF16) for i in range(2)]
        oring = [s.alloc("kor%d" % i, 512 * 2, BF16) for i in range(3)]
        oi = 0
        for half in range(2):
            p_ap, p_b = pr[half]
            self.load_panel(p_ap, p_b, [(wkv, half * 512, 512)], 16)
        p_ap, p_b = pr[0]
        for h in range(4):
            bank = self.bank_rr % 8
            self.bank_rr += 1
            for kc in range(16):
                s.add("pe", lambda e, o=s.psum[bank][:, 0:256], l=p_ap[:, kc * 512 + h * 128:kc * 512 + h * 128 + 128],
                      r=o_ap[:, kc * 256:(kc + 1) * 256], st=(kc == 0), sp=(kc == 15): e.matmul(o, l, r, start=st, stop=sp),
                      reads=[p_b, o_b], writes=[s.psbuf[bank]])
            oo, oob = oring[oi % 3]
            oi += 1
            s.add("act", lambda e, o=oo[:, 0:256], i=s.psum[bank][:, 0:256]: e.copy(o, i), reads=[s.psbuf[bank]], writes=[oob])
            s.dma(mkT[h * 128:(h + 1) * 128, :], oo[:, 0:256], oob, reads=[oob], pwrites=[self.db("mkT")])
        p_ap, p_b = pr[1]
        for mt in range(2):
            bank = self.bank_rr % 8
            self.bank_rr += 1
            for kc in range(16):
                s.add("pe", lambda e, o=s.psum[bank][:, :], l=o_ap[:, kc * 256 + mt * 128:kc * 256 + mt * 128 + 128],
                      r=p_ap[:, kc * 512:(kc + 1) * 512], st=(kc == 0), sp=(kc == 15): e.matmul(o, l, r, start=st, stop=sp),
                      reads=[p_b, o_b], writes=[s.psbuf[bank]])
            oo, oob = oring[oi % 3]
            oi += 1
            s.add("act", lambda e, o=oo, i=s.psum[bank][:, :]: e.copy(o, i), reads=[s.psbuf[bank]], writes=[oob])
            s.dma(mv[mt * 128:(mt + 1) * 128, :], oo, oob, reads=[oob], pwrites=[self.db("mv")])
        s.release(mk)

    def stage_inproj_a(self, w_in, w_rot, gbias):
        kb = self
        d = self.d
        xT = d["xnT"]
        for tok0, own in ((0, False), (SO, True)):
            epi_rope = self.epi_rope_fm(None, None, tok0)
            epi_plain = self.epi_plain_fm(None, None, None, tok0)

            def mkplain(dst, dname):
                return kb.epi_plain_fm(dst, dname, None, tok0)

            def gate_extra(s, ctx):
                ctx["gb"] = s.alloc("egb", 4)
                s.dma(ctx["gb"][0][0:36, :], gbias, ctx["gb"][1], writes=[ctx["gb"][1]])
                ctx["go"] = [s.alloc("ego%d" % i, 512 * 4) for i in range(2)]

            def epi_gate(ctx, job, tt, banks):
                s = kb.s
                o_ap, o_b = ctx["go"][tt % 2]
                gb, gbb = ctx["gb"]
                bank = banks[0]
                s.add("act", lambda e, o=o_ap[0:36, :], i=s.psum[bank][0:36, :], b=gb[0:36, 0:1]: e.activation(o, i, AF.Sigmoid, bias=b),
                      reads=[s.psbuf[bank], gbb], writes=[o_b])
                s.dma(d["gatesT"][:, tt * 512:(tt + 1) * 512], o_ap[0:36, :], o_b, reads=[o_b], pwrites=[kb.db("gatesT", tt)])

            panels = []
            otok = tok0 - SO

            def ropejob(off, roff, dst, dname, row0):
                return dict(cols=[(off, 128), (roff, 128)], epi=kb.epi_rope_fm(None, None, tok0 if dst is not d["qT"] else 0),
                            dst=dst, dname=dname, row0=row0)
            if own:
                for hp in range(6):
                    segs = [(w_in, hp * 256, 256), (w_rot, hp * 256, 256)]
                    jobs = []
                    for j in range(2):
                        jb = dict(cols=[(j * 128, 128), (256 + j * 128, 128)], dst=d["qT"], dname="qT", row0=(hp * 2 + j) * 128)
                        jb["epi"] = self._rope_epi_own()
                        jobs.append(jb)
                    panels.append(dict(segs=segs, jobs=jobs))
            for (kcol, rcol, dst, dname) in ((1536, 1536, d["kcmpT"], "kcmpT"), (2048, 1792, d["kslcT"], "kslcT"),
                                             (2560, 2048, d["kwinT"], "kwinT")):
                segs = [(w_in, kcol, 256), (w_rot, rcol, 256)]
                jobs = []
                for g in range(2):
                    jobs.append(dict(cols=[(g * 128, 128), (256 + g * 128, 128)], dst=dst, dname=dname, row0=g * 128,
                                     epi=self._rope_epi_all(tok0)))
                panels.append(dict(segs=segs, jobs=jobs))
            segs = [(w_in, 1792, 256)]
            jobs = [dict(cols=[(g * 128, 128)], row0=g * 128, epi=mkplain(d["vcmpT"], "vcmpT")) for g in range(2)]
            panels.append(dict(segs=segs, jobs=jobs))
            if own:
                segs = [(w_in, 3072, 36), (w_in, 3108, 256)]
                jobs = [dict(cols=[(0, 36)], epi=epi_gate)]
                for j in range(2):
                    jobs.append(dict(cols=[(36 + j * 128, 128)], row0=j * 128, epi=kb.epi_plain_fm(d["qmT"], "qmT", None, 0)))
                panels.append(dict(segs=segs, jobs=jobs))
                segs = [(w_in, 3108 + 256, 256)]
                jobs = []
                for j in range(2):
                    jobs.append(dict(cols=[(j * 128, 128)], row0=(2 + j) * 128, epi=kb.epi_plain_fm(d["qmT"], "qmT", None, 0)))
                panels.append(dict(segs=segs, jobs=jobs))
            self.cur_tok0 = tok0
            self.stage_lfm(xT, "xnT", tok0, SO, 16, panels, self.rope_setup(tok0, SO, gate_extra if own else None))

    def _rope_epi_all(self, tok0):
        kb = self

        def epi(ctx, job, tt, banks):
            kb._rope_core(ctx, job, tt, banks, tok0 + tt * 512, (tok0 // 512) + tt)
        return epi

    def _rope_epi_own(self):
        kb = self

        def epi(ctx, job, tt, banks):
            kb._rope_core(ctx, job, tt, banks, tt * 512, tt)
        return epi

    def _rope_core(self, ctx, job, tt, banks, col0, dbi):
        s = self.s
        bz, br = banks
        t1, t1b = ctx["t1"][ctx["oi"] % 2]
        t2, t2b = ctx["t2"][ctx["oi"] % 2]
        o_ap, o_b = ctx["oring"][ctx["oi"] % len(ctx["oring"])]
        ctx["oi"] += 1
        cs, csb = ctx["cos"]
        sn, snb = ctx["sin"]
        s.add("dve", lambda e, o=t1, a=s.psum[bz][:, :], b=cs[:, tt * 512:(tt + 1) * 512]: e.tensor_tensor(o, a, b, ALU.mult),
              reads=[s.psbuf[bz], csb], writes=[t1b])
        s.add("dve", lambda e, o=t2, a=s.psum[br][:, :], b=sn[:, tt * 512:(tt + 1) * 512]: e.tensor_tensor(o, a, b, ALU.mult),
              reads=[s.psbuf[br], snb], writes=[t2b])
        s.add("pool", lambda e, o=o_ap, a=t1, b=t2: e.tensor_tensor(o, a, b, ALU.add), reads=[t1b, t2b], writes=[o_b])
        r0 = job["row0"]
        s.dma(job["dst"][r0:r0 + 128, col0:col0 + 512], o_ap, o_b, reads=[o_b], pwrites=[self.db(job["dname"], dbi)])

    def stage_cmp(self, w1k, w2k, pek, w1v, w2v, pev):
        s = self.s
        d = self.d
        for kv, (w1, w2, peT, srcT, sname) in enumerate(((w1k, w2k, pek, d["kcmpT"], "kcmpT"), (w1v, w2v, pev, d["vcmpT"], "vcmpT"))):
            mk = s.mark()
            w1_ap, w1_b = s.alloc("cw1", 32 * 256 * 2, BF16)
            for q in range(0, 32, 8):
                s.dma(sub3(w1_ap, q * 256, 256, 8, 1, 256), dview(w1, q * 128, 1024, 0, 256), w1_b, pwrites=[w1_b], q="pool")
            w2_ap, w2_b = s.alloc("cw2", 2 * 128 * 2, BF16)
            s.dma(sub3(w2_ap, 0, 128, 2, 1, 128), dview(w2, 0, 256, 0, 128), w2_b, writes=[w2_b], q="pool")
            pe_ap, pe_b = s.alloc("cpe", 32 * 2, BF16)
            s.dma(pe_ap, peT, pe_b, writes=[pe_b], q="pool")
            bias_ap, bias_b = s.alloc("cbias", 2 * 4)
            for hc in range(2):
                bank = self.bank_rr % 8
                self.bank_rr += 1
                for l in range(32):
                    s.add("pe", lambda e, o=s.psum[bank][:, 0:1], lh=w1_ap[:, l * 256 + hc * 128:l * 256 + hc * 128 + 128], r=pe_ap[:, l:l + 1],
                          st=(l == 0), sp=(l == 31): e.matmul(o, lh, r, start=st, stop=sp), reads=[w1_b, pe_b], writes=[s.psbuf[bank]])
                s.add("dve", lambda e, o=bias_ap[:, hc:hc + 1], i=s.psum[bank][:, 0:1]: e.tensor_copy(o, i), reads=[s.psbuf[bank]], pwrites=[bias_b])
            for g in range(2):
                k_ap, k_b = s.alloc("ck%d" % g, SV * 2, BF16)
                s.dma(k_ap, srcT[g * 128:(g + 1) * 128, :], k_b, reads=[self.db(sname, i) for i in range(8)], writes=[k_b])
                hs_ap, hs_b = s.alloc("chs%d" % g, 2 * 256 * 2, BF16)
                for hc in range(2):
                    bank = self.bank_rr % 8
                    self.bank_rr += 1
                    for l in range(32):
                        s.add("pe", lambda e, o=s.psum[bank][:, 0:255], lh=w1_ap[:, l * 256 + hc * 128:l * 256 + hc * 128 + 128],
                              r=k_ap[:, l:l + 16 * 254 + 1:16], st=(l == 0), sp=(l == 31): e.matmul(o, lh, r, start=st, stop=sp),
                              reads=[w1_b, k_b], writes=[s.psbuf[bank]])
                    s.add("act", lambda e, o=hs_ap[:, hc * 256:hc * 256 + 255], i=s.psum[bank][:, 0:255], b=bias_ap[:, hc:hc + 1]:
                          e.activation(o, i, AF.Silu, bias=b), reads=[s.psbuf[bank], bias_b], pwrites=[hs_b])
                o_ap, o_b = s.alloc("cout%d" % g, 256 * 2, BF16)
                if kv == 0:
                    bank = self.bank_rr % 8
                    self.bank_rr += 1
                    for hc in range(2):
                        s.add("pe", lambda e, o=s.psum[bank][:, 0:255], lh=w2_ap[:, hc * 128:(hc + 1) * 128], r=hs_ap[:, hc * 256:hc * 256 + 255],
                              st=(hc == 0), sp=(hc == 1): e.matmul(o, lh, r, start=st, stop=sp), reads=[w2_b, hs_b], writes=[s.psbuf[bank]])
                    s.add("pool", lambda e, o=o_ap: e.memset(o, 0.0), writes=[o_b])
                    s.add("act", lambda e, o=o_ap[:, 0:255], i=s.psum[bank][:, 0:255]: e.copy(o, i), reads=[s.psbuf[bank]], pwrites=[o_b])
                    s.dma(d["kcT"][g * 128:(g + 1) * 128, :], o_ap, o_b, reads=[o_b], pwrites=[self.db("kcT")])
                else:
                    s.add("pool", lambda e, o=o_ap: e.memset(o, 0.0), writes=[o_b])
                    for ct in range(2):
                        ncn = 128 if ct == 0 else 127
                        bank = self.bank_rr % 8
                        self.bank_rr += 1
                        for hc in range(2):
                            s.add("pe", lambda e, o=s.psum[bank][0:ncn, 0:128], lh=hs_ap[:, hc * 256 + ct * 128:hc * 256 + ct * 128 + ncn],
                                  r=w2_ap[:, hc * 128:(hc + 1) * 128], st=(hc == 0), sp=(hc == 1): e.matmul(o, lh, r, start=st, stop=sp),
                                  reads=[w2_b, hs_b], writes=[s.psbuf[bank]])
                        s.add("act", lambda e, o=o_ap[0:ncn, ct * 128:(ct + 1) * 128], i=s.psum[bank][0:ncn, 0:128]: e.copy(o, i),
                              reads=[s.psbuf[bank]], pwrites=[o_b])
                    s.dma(dview(d["vc"], g * 256, 256, 0, 128), sub3(o_ap, 0, 128, 2, 1, 128), o_b, reads=[o_b], pwrites=[self.db("vc")])
            s.release(mk)

    def stage_attn_a(self):
        s = self.s
        d = self.d
        mk0 = s.mark()
        def ld(name, src, nbytes, dt, q="sp", parts=128):
            ap, b = s.alloc(name, nbytes, dt)
            s.dma(ap[0:parts, :], src, b, writes=[b], q=q)
            return ap, b
        mcmp, mcmp_b = ld("mcmp", d["m_cmp"], 8 * 512 * 2, BF16, "pool")
        mwin, mwin_b = ld("mwin", d["m_win"], 8 * 512 * 2, BF16, "pool")
        mwin0, mwin0_b = ld("mwin0", d["m_win0"], 4 * 512 * 2, BF16, "pool")
        E, E_b = ld("E", d["c_E"], SV * 2, BF16, "pool", 64)
        mmap, mmap_b = ld("mmap", d["c_mmap"], 2 * 64 * 4, F32)
        ph = [s.alloc("aph%d" % i, 512 * 4) for i in range(2)]
        selM, selM_b = ld("selM", d["selM"], 4 * 256 * 4, F32)
        selA, selA_b = ld("selA", d["selA"], 4 * 256 * 4, F32)
        selmat, selmat_b = ld("selmat", d["c_selmat"], 36 * 128 * 4, F32, "sp", 36)
        gat, gat_b = s.alloc("gat", SO * 4)
        s.dma(gat[0:36, :], d["gatesT"], gat_b, reads=[self.db("gatesT", i) for i in range(4)], writes=[gat_b])
        ctx = dict(si=0, pi=0, ri=0, sbanks=[0, 1, 2], pt=[s.alloc("apt%d" % i, 512 * 2, BF16) for i in range(4)],
                   rd=[s.alloc("ard%d" % i, 512 * 4) for i in range(3)])
        pn = [s.alloc("apn%d" % i, 512 * 2, BF16) for i in range(4)]
        Gs = [s.alloc("aG%d" % i, 512 * 4) for i in range(3)]
        ocs = [s.alloc("aocs%d" % i, 512 * 4) for i in range(6)]
        tb = [s.alloc("atb%d" % i, 512 * 4) for i in range(4)]
        fb = [s.alloc("afb%d" % i, 512 * 4) for i in range(2)]
        oring = [s.alloc("aor%d" % i, 512 * 2, BF16) for i in range(3)]
        sc_ap, sc_b = s.alloc("asc", 256 * 4)
        m16, m16_b = s.alloc("am16", 4 * 16 * 4)
        wk, wk_b = s.alloc("awk", 256 * 4)
        selb, selb_b = s.alloc("aselb", 256 * 2, BF16)
        selbT, selbT_b = s.alloc("aselbT", 512 * 2, BF16)
        psb = [s.psum[i][:, :].bitcast(BF16) for i in range(8)]
        gi_ = 0
        oi = 0
        for g in range(2):
            mk = s.mark()
            kc_ap, kc_b = s.alloc("akc", 256 * 2, BF16)
            s.dma(kc_ap, d["kcT"][g * 128:(g + 1) * 128, :], kc_b, reads=self.rd("kcT"), writes=[kc_b])
            vc_ap, vc_b = s.alloc("avc", 256 * 2, BF16)
            s.dma(sub3(vc_ap, 0, 128, 2, 1, 128), dview(d["vc"], g * 256, 256, 0, 128), vc_b, reads=self.rd("vc"), writes=[vc_b])
            ks_ap, ks_b = s.alloc("aks", SV * 2, BF16)
            kw_ap, kw_b = s.alloc("akw", SV * 2, BF16)
            s.dma(ks_ap, d["kslcT"][g * 128:(g + 1) * 128, :], ks_b, reads=[self.db("kslcT", i) for i in range(8)], writes=[ks_b])
            s.dma(kw_ap, d["kwinT"][g * 128:(g + 1) * 128, :], kw_b, reads=[self.db("kwinT", i) for i in range(8)], writes=[kw_b])
            vs_ap, vs_b = s.alloc("avs", SV * 2, BF16)
            vw_ap, vw_b = s.alloc("avw", SV * 2, BF16)
            for q in range(4):
                s.dma(sub3(vs_ap, q * 8 * 128, 128, 8, 1, 128), dview(d["vsw"], q * 1024, 1024, g * 128, 128), vs_b,
                      reads=[self.db("vsw", i) for i in range(8)], pwrites=[vs_b])
                s.dma(sub3(vw_ap, q * 8 * 128, 128, 8, 1, 128), dview(d["vsw"], q * 1024, 1024, 256 + g * 128, 128), vw_b,
                      reads=[self.db("vsw", i) for i in range(8)], pwrites=[vw_b])
            q_ap, q_b = s.alloc("aq", 6 * SO * 2, BF16)
            for p in range(6):
                s.dma(q_ap[:, p * SO:(p + 1) * SO], d["qT"][(g * 6 + p) * 128:(g * 6 + p + 1) * 128, :], q_b,
                      reads=[self.db("qT", i) for i in range(4)], pwrites=[q_b])
            for qt in range(4):
                t0v = SO + qt * 512
                njt = (t0v + 512) // 128
                bI = 5
                for p in range(6):
                    qr = q_ap[:, p * SO + qt * 512:p * SO + qt * 512 + 512]
                    pts = []
                    for ct in range(2):
                        pt, ptb = self.attn_unit(ctx, kc_ap[:, ct * 128:(ct + 1) * 128], [kc_b], qr, [q_b],
                                                 [(self.ident, mcmp[:, (qt * 2 + ct) * 512:(qt * 2 + ct + 1) * 512], [self.ident_b, mcmp_b])],
                                                 None, [], None, 3, ct == 0, ct == 1)
                        pts.append((pt, ptb))
                    r, rb = self.recip_den(ctx, 3)
                    pns = []
                    for ct in range(2):
                        pa, pb = pn[(p * 2 + ct) % 4]
                        s.add("pool", lambda e, o=pa, a=pts[ct][0], b=r: e.tensor_tensor(o, a, b, ALU.mult),
                              reads=[pts[ct][1], rb], writes=[pb])
                        pns.append((pa, pb))
                    for ct in range(2):
                        s.add("pe", lambda e, o=s.psum[4][:, :], l=vc_ap[:, ct * 128:(ct + 1) * 128], r_=pns[ct][0], st=(ct == 0), sp=(ct == 1):
                              e.matmul(o, l, r_, start=st, stop=sp), reads=[vc_b, pns[ct][1]], writes=[s.psbuf[4]])
                    for ct in range(2):
                        if p == 0:
                            s.add("pool", lambda e, o=ph[ct][0], a=pns[ct][0]: e.tensor_copy(o, a), reads=[pns[ct][1]], writes=[ph[ct][1]])
                        else:
                            s.add("pool", lambda e, o=ph[ct][0], a=pns[ct][0]: e.tensor_tensor(o, o, a, ALU.add),
                                  reads=[pns[ct][1], ph[ct][1]], writes=[ph[ct][1]])
                    hh = g * 6 + p
                    G, Gb = Gs[gi_ % 3]
                    gi_ += 1
                    bG = 6 + (gi_ % 2)
                    s.add("pe", lambda e, o=s.psum[bG][:, :], l=selmat[0:36, (hh * 3) * 128:(hh * 3 + 1) * 128], r_=gat[0:36, qt * 512:(qt + 1) * 512]:
                          e.matmul(o, l, r_, start=True, stop=True), reads=[selmat_b, gat_b], writes=[s.psbuf[bG]])
                    s.add("act", lambda e, o=G, i=s.psum[bG][:, :]: e.copy(o, i), reads=[s.psbuf[bG]], writes=[Gb])
                    s.add("dve", lambda e, o=ocs[p][0], a=s.psum[4][:, :], b=G: e.tensor_tensor(o, a, b, ALU.mult),
                          reads=[s.psbuf[4], Gb], writes=[ocs[p][1]])
                for qs in range(4):
                    for ct in range(2):
                        s.add("pe", lambda e, o=s.psum[bI][:, qs * 64:(qs + 1) * 64], l=ph[ct][0][:, qs * 128:(qs + 1) * 128],
                              r_=mmap[:, ct * 64:(ct + 1) * 64], st=(ct == 0), sp=(ct == 1):
                              e.matmul(o, l, r_, start=st, stop=sp), reads=[ph[ct][1], mmap_b], writes=[s.psbuf[bI]])
                s.add("dve", lambda e, o=sc_ap, a=s.psum[bI][:, 0:256], b=selM[:, qt * 256:(qt + 1) * 256]: e.tensor_tensor(o, a, b, ALU.mult),
                      reads=[s.psbuf[bI], selM_b], writes=[sc_b])
                s.add("dve", lambda e, o=sc_ap, b=selA[:, qt * 256:(qt + 1) * 256]: e.tensor_tensor(o, o, b, ALU.add),
                      reads=[sc_b, selA_b], writes=[sc_b])
                for qs in range(4):
                    scq = sc_ap[:, qs * 64:(qs + 1) * 64]
                    mm = m16[:, qs * 16:(qs + 1) * 16]
                    s.add("dve", lambda e, o=mm[:, 0:8], i=scq: e.max(o, i), reads=[sc_b], pwrites=[m16_b])
                    s.add("dve", lambda e, o=wk[:, qs * 64:(qs + 1) * 64], m=mm[:, 0:8], i=scq: e.match_replace(o, m, i, -3e9),
                          reads=[sc_b, m16_b], pwrites=[wk_b])
                    s.add("dve", lambda e, o=mm[:, 8:16], i=wk[:, qs * 64:(qs + 1) * 64]: e.max(o, i), reads=[wk_b, m16_b], pwrites=[m16_b])
                    s.add("dve", lambda e, o=mm[:, 15:16]: e.tensor_scalar(o, o, -5e8, None, ALU.max), reads=[m16_b], pwrites=[m16_b])
                    s.add("dve", lambda e, o=selb[:, qs * 64:(qs + 1) * 64], i=scq, t=mm[:, 15:16]: e.tensor_scalar(o, i, t, NEG, ALU.is_lt, ALU.mult),
                          reads=[sc_b, m16_b], pwrites=[selb_b])
                for qs in range(4):
                    s.add("pe", lambda e, o=psb[7][0:64, qs * 128:(qs + 1) * 128], i=selb[:, qs * 64:(qs + 1) * 64]: e.transpose(o, i, self.ident),
                          reads=[selb_b, self.ident_b], writes=[s.psbuf[7]])
                s.add("act", lambda e, o=selbT[0:64, :], i=psb[7][0:64, 0:512]: e.copy(o, i), reads=[s.psbuf[7]], writes=[selbT_b])
                for p in range(6):
                    qr = q_ap[:, p * SO + qt * 512:p * SO + qt * 512 + 512]
                    hh = g * 6 + p
                    units = []
                    for jt in range(njt):
                        ex = [(E[0:64, jt * 128:(jt + 1) * 128], selbT[0:64, :], [E_b, selbT_b])]
                        o_ = jt * 128 - t0v
                        if o_ >= 0:
                            mi = (o_ + 512) // 128
                            ex.append((self.ident, mwin[:, mi * 512:(mi + 1) * 512], [self.ident_b, mwin_b]))
                        units.append(dict(klhs=ks_ap[:, jt * 128:(jt + 1) * 128], krd=[ks_b], qrhs=qr, qrd=[q_b], extras=ex,
                                          vlhs=vs_ap[:, jt * 128:(jt + 1) * 128], vrd=[vs_b], bacc=3, bden=4, first=(jt == 0), last=(jt == njt - 1)))
                    jt0 = (t0v - 512) // 128
                    for jt in range(jt0, njt):
                        o_ = jt * 128 - t0v
                        mi = (o_ + 512) // 128
                        if qt == 0 and o_ < 0:
                            mt_ap, mt_b = mwin0[:, mi * 512:(mi + 1) * 512], mwin0_b
                        else:
                            mt_ap, mt_b = mwin[:, mi * 512:(mi + 1) * 512], mwin_b
                        units.append(dict(klhs=kw_ap[:, jt * 128:(jt + 1) * 128], krd=[kw_b], qrhs=qr, qrd=[q_b],
                                          extras=[(self.ident, mt_ap, [self.ident_b, mt_b])],
                                          vlhs=vw_ap[:, jt * 128:(jt + 1) * 128], vrd=[vw_b], bacc=5, bden=6, first=(jt == jt0), last=(jt == njt - 1)))
                    self.run_units(ctx, units)
                    ts = []
                    for bi, (bacc, bden) in enumerate(((3, 4), (5, 6))):
                        G, Gb = Gs[gi_ % 3]
                        gi_ += 1
                        s.add("pe", lambda e, o=s.psum[7][:, :], l=selmat[0:36, (hh * 3 + 1 + bi) * 128:(hh * 3 + 2 + bi) * 128],
                              r_=gat[0:36, qt * 512:(qt + 1) * 512]: e.matmul(o, l, r_, start=True, stop=True),
                              reads=[selmat_b, gat_b], writes=[s.psbuf[7]])
                        s.add("act", lambda e, o=G, i=s.psum[7][:, :]: e.copy(o, i), reads=[s.psbuf[7]], writes=[Gb])
                        r, rb = self.recip_den(ctx, bden)
                        f, fbb = fb[bi]
                        s.add("pool", lambda e, o=f, a=r, b=G: e.tensor_tensor(o, a, b, ALU.mult), reads=[rb, Gb], writes=[fbb])
                        t, tbb = tb[(oi * 2 + bi) % 4]
                        s.add("dve", lambda e, o=t, a=s.psum[bacc][:, :], b=f: e.tensor_tensor(o, a, b, ALU.mult),
                              reads=[s.psbuf[bacc], fbb], writes=[tbb])
                        ts.append((t, tbb))
                    o_ap, o_b = oring[oi % 3]
                    oi += 1
                    if "dbg_br" in d:
                        for bi_, (ap_, b_) in enumerate((ocs[p], ts[0], ts[1])):
                            s.dma(d["dbg_br"][bi_ * 1536 + hh * 128:bi_ * 1536 + (hh + 1) * 128, qt * 512:(qt + 1) * 512], ap_, b_, reads=[b_])
                    s.add("pool", lambda e, o=ts[0][0], a=ts[0][0], b=ocs[p][0]: e.tensor_tensor(o, a, b, ALU.add),
                          reads=[ocs[p][1], ts[0][1]], writes=[ts[0][1]])
                    s.add("pool", lambda e, o=o_ap, a=ts[0][0], b=ts[1][0]: e.tensor_tensor(o, a, b, ALU.add),
                          reads=[ts[0][1], ts[1][1]], writes=[o_b])
                    s.dma(d["oT"][hh * 128:(hh + 1) * 128, qt * 512:(qt + 1) * 512], o_ap, o_b, reads=[o_b], pwrites=[self.db("oT", qt)])
            s.release(mk)
        s.release(mk0)

    def stage_kvshared(self, w_kv, w_kv_rot):
        kb = self
        d = self.d
        panels = []
        for hp in range(2):
            segs = [(w_kv, hp * 256, 256), (w_kv_rot, hp * 256, 256)]
            jobs = [dict(cols=[(j * 128, 128), (256 + j * 128, 128)], dst=d["kshT"], dname="kshT", row0=(hp * 2 + j) * 128,
                         epi=self._rope_epi_own()) for j in range(2)]
            panels.append(dict(segs=segs, jobs=jobs))
        self.stage_lfm(d["kvnT"], "kvnT", 0, SO, 16, panels, self.rope_setup(SO, SO))
        self.stage_tm_bf16(d["kvnT"], "kvnT", 16, 0, SO, [(w_kv, 512, 512)], d["vsh"], "vsh")


NG = 8


def build_phase_a(debug=()):
    nc = bass.Bass("TRN2", target_bir_lowering=False)
    kb = KB(nc)
    I = kb.inp
    xv = I("xv", [SV, DM])
    memb = I("memb", [256, DM])
    I("cosT", [128, SV]); I("sinT", [128, SV]); I("gains", [128, NG * 16])
    I("c_ident", [128, 128]); I("c_ones", [128, 128])
    I("m_cmp", [128, 8 * 512]); I("m_win", [128, 8 * 512]); I("m_win0", [128, 4 * 512])
    I("c_E", [64, SV]); I("c_mmap", [128, 128]); I("selM", [128, 1024]); I("selA", [128, 1024]); I("c_selmat", [36, 36 * 128])
    w_in = I("a_w_in", [DM, 3620]); w_rot = I("a_w_rot", [DM, 2304]); gbias = I("a_gbias", [36, 1])
    w1k = I("a_w1k", [4096, 256]); w2k = I("a_w2k", [256, 128]); pek = I("a_pekT", [128, 32])
    w1v = I("a_w1v", [4096, 256]); w2v = I("a_w2v", [256, 128]); pev = I("a_pevT", [128, 32])
    wmkv = I("a_w_mem_kv", [DM, 1024]); wout = I("a_w_out", [DM, DM])
    wg = I("a_w_gate", [DM, DFF]); wu = I("a_w_up", [DM, DFF]); wd = I("a_w_down", [DFF, DM])
    wkv = I("w_kv", [DM, 1024]); wkvr = I("w_kv_rot", [DM, 512])

    def S(name, shape, dt):
        if name in debug:
            return kb.outp(name, shape, dt)
        return kb.scr(name, shape, dt)
    S("xnT", [DM, SV], BF16)
    S("qT", [1536, SO], BF16); S("kcmpT", [256, SV], BF16); S("vcmpT", [256, SV], BF16)
    S("kslcT", [256, SV], BF16); S("kwinT", [256, SV], BF16); S("vsw", [SV, 512], BF16)
    S("gatesT", [36, SO], F32); S("qmT", [512, SO], BF16)
    S("kcT", [256, 256], BF16); S("vc", [512, 128], BF16)
    S("mkT", [512, 256], BF16); S("mv", [256, 512], BF16)
    S("oT", [DM, SO], BF16); S("h1", [SO, DM], F32); S("hnT", [DM, SO], BF16); S("hidT", [DFF, SO], BF16)
    kb.outp("h2", [SO, DM], F32)
    S("kvnT", [DM, SO], BF16)
    kb.outp("kshT", [512, SO], BF16); kb.outp("vsh", [SO, 512], BF16)
    if "dbg_br" in debug:
        kb.outp("dbg_br", [3 * 1536, SO], F32)
    d = kb.d
    kb.consts(NG)
    stop = kb.stop_after if hasattr(kb, "stop_after") else None
    kb.stage_norm(xv, None, SV, [0], [(d["xnT"], "xnT", 0)])
    kb.stage_inproj_a(w_in, w_rot, gbias)
    kb.stage_tm_bf16(d["xnT"], "xnT", 16, 0, SV, [(w_in, 2304, 256), (w_in, 2816, 256)], d["vsw"], "vsw")
    kb.stage_cmp(w1k, w2k, pek, w1v, w2v, pev)
    kb.stage_memkv(memb, 1, wmkv)
    kb.stage_attn_a()
    kb.stage_mem_attn(d["qmT"], "qmT", d["mkT"], d["mv"], d["oT"], "oT", 12)
    kb.stage_down(d["oT"], "oT", 16, wout, xv[SO:SV, :], None, d["h1"], "h1", TB=2048)
    kb.stage_norm(d["h1"], "h1", SO, [2], [(d["hnT"], "hnT", 0)])
    kb.stage_ffn(d["hnT"], "hnT", wg, wu, wd, d["h1"], "h1", d["h2"], "h2")
    kb.stage_norm(d["h2"], "h2", SO, [3], [(d["kvnT"], "kvnT", 0)])
    kb.stage_kvshared(wkv, wkvr)
    kb.s.finalize()
    return nc, kb


def rope_tabs(half):
    inv = (1.0 / (10000.0 ** (np.arange(0, 128, 2, dtype=np.float32) / 128))).astype(np.float32)
    pos = np.arange(SV, dtype=np.float32) - (0 if half == 1 else SO)
    pos = np.maximum(pos, 0).astype(np.float32)
    ang = (pos[:, None] * inv[None, :]).astype(np.float32)
    c = np.cos(ang).astype(np.float32).T
    sn = np.sin(ang).astype(np.float32).T
    cosT = np.concatenate([c, c], 0)
    sinT = np.concatenate([-sn, sn], 0)
    return np.ascontiguousarray(cosT), np.ascontiguousarray(sinT)


def rot_cols(w, heads):
    outs = []
    for c0 in heads:
        outs.append(w[:, c0 + 64:c0 + 128])
        outs.append(w[:, c0:c0 + 64])
    return np.ascontiguousarray(np.concatenate(outs, 1))


def gain_arr(gs):
    return np.ascontiguousarray(np.concatenate([g.reshape(16, 128).T for g in gs], 1).astype(np.float32))


def band_mask(o, w, prevmask):
    jj = np.arange(128)[:, None]
    qq = np.arange(512)[None, :]
    dist = qq - jj - o
    m = np.where((dist >= 0) & (dist <= w), 0.0, NEG).astype(np.float32)
    if prevmask:
        m[:] = NEG
    return m


def attn_consts(half):
    out = {}
    m_cmp = np.zeros((128, 8, 512), np.float32)
    for qt in range(4):
        for ct in range(2):
            c = ct * 128 + np.arange(128)[:, None]
            t = SO + qt * 512 + np.arange(512)[None, :]
            valid = (16 * c + 31 <= t) & (c <= 254)
            if half == 0:
                valid &= (c >= 128)
            m_cmp[:, qt * 2 + ct, :] = np.where(valid, 0.0, NEG)
    out["m_cmp"] = m_cmp.reshape(128, -1)
    mw = np.zeros((128, 8, 512), np.float32)
    for mi in range(8):
        mw[:, mi, :] = band_mask(mi * 128 - 512, 511, False)
    out["m_win"] = mw.reshape(128, -1)
    mw0 = np.zeros((128, 4, 512), np.float32)
    for mi in range(4):
        mw0[:, mi, :] = band_mask(mi * 128 - 512, 511, half == 0)
    out["m_win0"] = mw0.reshape(128, -1)
    selM = np.zeros((128, 4, 4, 64), np.float32)
    selA = np.zeros((128, 4, 4, 64), np.float32)
    sblk = np.arange(64)[None, :]
    first = 0 if half == 1 else 32
    for qt in range(4):
        for qs in range(4):
            t = SO + qt * 512 + qs * 128 + np.arange(128)[:, None]
            cur = t // 64
            elig = (sblk * 64 <= t) & (sblk >= first)
            f0 = (sblk == first) & elig
            f1 = (sblk == cur)
            f2 = (sblk == cur - 1) & (sblk >= first)
            A = np.where(elig, 0.0, -1e9)
            A = np.where(f0, 1e9, A)
            A = np.where(f2, 2e9, A)
            A = np.where(f1, 3e9, A)
            M = (elig & ~f0 & ~f1 & ~f2).astype(np.float32)
            selM[:, qt, qs, :] = M
            selA[:, qt, qs, :] = A
    out["selM"] = selM.reshape(128, -1)
    out["selA"] = selA.reshape(128, -1)
    return out


def shared_consts():
    out = {}
    out["c_ident"] = np.eye(128, dtype=np.float32)
    out["c_ones"] = np.ones((128, 128), np.float32)
    E = np.zeros((64, SV), np.float32)
    E[np.arange(SV) // 64, np.arange(SV)] = 1.0
    out["c_E"] = E
    mm = np.zeros((2, 128, 64), np.float32)
    for c in range(255):
        for sb in range(64):
            if (16 * c < 64 * sb + 64) and (16 * c + 32 > 64 * sb):
                mm[c // 128, c % 128, sb] = 1.0
    out["c_mmap"] = np.ascontiguousarray(mm.transpose(1, 0, 2).reshape(128, 128))
    sm = np.zeros((36, 36, 128), np.float32)
    for i in range(36):
        sm[i, i, :] = 1.0
    out["c_selmat"] = sm.reshape(36, -1)
    return out


def phase_a_inputs(inp, b, half, sc):
    f = np.float32
    x = inp["x"][b]
    if half == 1:
        xv = x
    else:
        xv = np.concatenate([np.zeros((SO, DM), f), x[:SO]], 0)
    cosT, sinT = rope_tabs(half)
    m = dict(sc)
    m.update(attn_consts(half))
    m["xv"] = np.ascontiguousarray(xv)
    m["memb"] = np.ascontiguousarray(inp["mem"][b])
    m["cosT"] = cosT
    m["sinT"] = sinT
    return m


def weights_a(inp):
    w = {}
    w_in = inp["a_w_in"][0]
    w["a_w_in"] = w_in
    heads = [h * 128 for h in range(12)] + [1536 + (i * 2 + g) * 128 for i in (0, 2, 4) for g in range(2)]
    w["a_w_rot"] = rot_cols(w_in, heads)
    w["a_gbias"] = np.ascontiguousarray(inp["a_gate_bias"][0].reshape(36, 1))
    w["a_w1k"] = inp["a_cmp_w1_k"][0]; w["a_w2k"] = inp["a_cmp_w2_k"][0]
    w["a_pekT"] = np.ascontiguousarray(inp["a_cmp_pe_k"][0].T)
    w["a_w1v"] = inp["a_cmp_w1_v"][0]; w["a_w2v"] = inp["a_cmp_w2_v"][0]
    w["a_pevT"] = np.ascontiguousarray(inp["a_cmp_pe_v"][0].T)
    w["a_w_mem_kv"] = inp["a_w_mem_kv"][0]; w["a_w_out"] = inp["a_w_out"][0]
    w["a_w_gate"] = inp["a_w_gate"][0]; w["a_w_up"] = inp["a_w_up"][0]; w["a_w_down"] = inp["a_w_down"][0]
    w["w_kv"] = inp["w_kv_shared"]
    w["w_kv_rot"] = rot_cols(inp["w_kv_shared"], [h * 128 for h in range(4)])
    w["gains"] = gain_arr([inp["a_norm_attn"][0], inp["a_norm_mem"][0], inp["a_norm_ffn"][0], inp["kv_norm"],
                           inp["b_norm_attn"][0], inp["b_norm_mem"][0], inp["b_norm_ffn"][0], inp["final_norm"]])
    return {k: np.ascontiguousarray(np.asarray(v, dtype=np.float32)) for k, v in w.items()}


def _kb_stage_attn_b(self):
    s = self.s
    d = self.d
    mk0 = s.mark()
    mdil, mdil_b = s.alloc("mdil", 5 * 512 * 2, BF16)
    s.dma(mdil, d["m_dil"], mdil_b, writes=[mdil_b], q="pool")
    mdil0, mdil0_b = s.alloc("mdil0", 512 * 2, BF16)
    s.dma(mdil0, d["m_dil0"], mdil0_b, writes=[mdil0_b], q="pool")
    ctx = dict(si=0, pi=0, ri=0, sbanks=[0, 1, 2], pt=[s.alloc("bpt%d" % i, 512 * 2, BF16) for i in range(4)],
               rd=[s.alloc("brd%d" % i, 512 * 4) for i in range(2)])
    oring = [s.alloc("bor%d" % i, 512 * 2, BF16) for i in range(3)]
    oi = 0
    ui = 0
    for hh in range(4):
        mk = s.mark()
        k_ap, k_b = s.alloc("bk", SV * 2, BF16)
        kown = [self.db("kshT", i) for i in range(4)]
        vown = [self.db("vsh", i) for i in range(4)]
        s.dma(k_ap[:, 0:SO], d["kg"][hh * 128:(hh + 1) * 128, :], k_b, reads=self.rd("kg"), pwrites=[k_b])
        s.dma(k_ap[:, SO:SV], d["kshT"][hh * 128:(hh + 1) * 128, :], k_b, reads=kown, pwrites=[k_b])
        v1, v1_b = s.alloc("bv1", SV * 2, BF16)
        v4, v4_b = s.alloc("bv4", SV * 2, BF16)
        v16, v16_b = s.alloc("bv16", SV * 2, BF16)
        for pi_, (vsrc, rds) in enumerate(((d["vg"][0:SO, :], self.rd("vg")), (d["vsh"], vown))):
            vcol = vsrc[:, hh * 128:(hh + 1) * 128]
            for q in range(2):
                s.dma(sub3(v1, (pi_ * 16 + q * 8) * 128, 128, 8, 1, 128), dview(vsrc, q * 1024, 1024, hh * 128, 128), v1_b,
                      reads=rds, pwrites=[v1_b])
            r4 = vcol.rearrange("(jt p r) c -> r p jt c", p=128, r=4)
            for rho in range(4):
                s.dma(sub3(v4, rho * 1024 + pi_ * 4 * 128, 128, 4, 1, 128), r4[rho], v4_b, reads=rds, pwrites=[v4_b])
            r16 = vcol.rearrange("(jt p r) c -> r p jt c", p=128, r=16)
            for rho in range(16):
                s.dma(sub3(v16, rho * 256 + pi_ * 128, 128, 1, 1, 128), r16[rho], v16_b, reads=rds, pwrites=[v16_b])
        q_ap, q_b = s.alloc("bq", 3 * SO * 2, BF16)
        for g in range(3):
            s.dma(q_ap[:, g * SO:(g + 1) * SO], d["qbT"][(g * 4 + hh) * 128:(g * 4 + hh + 1) * 128, :], q_b,
                  reads=[self.db("qbT", i) for i in range(4)], pwrites=[q_b])
        accS, accS_b = s.alloc("bacc", SO * 4)
        denS, denS_b = s.alloc("bden", SO * 4)

        def flush(bacc, bden, n, oa, od, first):
            if first:
                s.add("dve", lambda e, o=oa, i=s.psum[bacc][:, 0:n]: e.tensor_copy(o, i), reads=[s.psbuf[bacc]], pwrites=[accS_b])
                s.add("dve", lambda e, o=od, i=s.psum[bden][:, 0:n]: e.tensor_copy(o, i), reads=[s.psbuf[bden]], pwrites=[denS_b])
            else:
                s.add("dve", lambda e, o=oa, i=s.psum[bacc][:, 0:n]: e.tensor_tensor(o, o, i, ALU.add), reads=[s.psbuf[bacc], accS_b], pwrites=[accS_b])
                s.add("dve", lambda e, o=od, i=s.psum[bden][:, 0:n]: e.tensor_tensor(o, o, i, ALU.add), reads=[s.psbuf[bden], denS_b], pwrites=[denS_b])

        units = []

        def mkpost(bacc, bden, n, oa, od, first):
            return lambda: flush(bacc, bden, n, oa, od, first)
        for qt in range(4):
            t0v = SO + qt * 512
            bacc, bden = (3, 4) if (ui % 2 == 0) else (5, 6)
            ui += 1
            offs = [-128, 0, 128, 256, 384]
            for i, o_ in enumerate(offs):
                jt = (t0v + o_) // 128
                if qt == 0 and o_ < 0:
                    m_ap, m_b = mdil0, mdil0_b
                else:
                    m_ap, m_b = mdil[:, i * 512:(i + 1) * 512], mdil_b
                units.append(dict(klhs=k_ap[:, jt * 128:(jt + 1) * 128], krd=[k_b], qrhs=q_ap[:, qt * 512:(qt + 1) * 512], qrd=[q_b],
                                  extras=[(self.ident, m_ap, [self.ident_b, m_b])], vlhs=v1[:, jt * 128:(jt + 1) * 128], vrd=[v1_b],
                                  bacc=bacc, bden=bden, first=(i == 0), last=(i == 4),
                                  post=(mkpost(bacc, bden, 512, accS[:, qt * 512:(qt + 1) * 512], denS[:, qt * 512:(qt + 1) * 512], True) if i == 4 else None)))
        for rho in range(4):
            bacc, bden = (3, 4) if (ui % 2 == 0) else (5, 6)
            ui += 1
            qr = q_ap[:, SO + rho:SO + SO:4]
            offs = [-128, 0, 128, 256, 384]
            for i, o_ in enumerate(offs):
                ju0 = 512 + o_
                if o_ < 0:
                    m_ap, m_b = mdil0, mdil0_b
                else:
                    m_ap, m_b = mdil[:, i * 512:(i + 1) * 512], mdil_b
                kl = k_ap[:, rho + 4 * ju0:rho + 4 * (ju0 + 127) + 1:4]
                jtu = ju0 // 128
                units.append(dict(klhs=kl, krd=[k_b], qrhs=qr, qrd=[q_b], extras=[(self.ident, m_ap, [self.ident_b, m_b])],
                                  vlhs=v4[:, rho * 1024 + jtu * 128:rho * 1024 + (jtu + 1) * 128], vrd=[v4_b],
                                  bacc=bacc, bden=bden, first=(i == 0), last=(i == 4),
                                  post=(mkpost(bacc, bden, 512, accS[:, rho:SO:4], denS[:, rho:SO:4], False) if i == 4 else None)))
        for rho in range(16):
            bacc, bden = (3, 4) if (ui % 2 == 0) else (5, 6)
            ui += 1
            qr = q_ap[:, 2 * SO + rho:3 * SO:16]
            for i, o_ in enumerate([-128, 0]):
                ju0 = 128 + o_
                if o_ < 0:
                    m_ap, m_b = mdil0[:, 0:128], mdil0_b
                else:
                    m_ap, m_b = mdil[:, 512:512 + 128], mdil_b
                kl = k_ap[:, rho + 16 * ju0:rho + 16 * (ju0 + 127) + 1:16]
                jtu = ju0 // 128
                units.append(dict(klhs=kl, krd=[k_b], qrhs=qr, qrd=[q_b], extras=[(self.ident, m_ap, [self.ident_b, m_b])],
                                  vlhs=v16[:, rho * 256 + jtu * 128:rho * 256 + (jtu + 1) * 128], vrd=[v16_b],
                                  bacc=bacc, bden=bden, first=(i == 0), last=(i == 1), n=128,
                                  post=(mkpost(bacc, bden, 128, accS[:, rho:SO:16], denS[:, rho:SO:16], False) if i == 1 else None)))
        self.run_units(ctx, units)
        s.add("dve", lambda e: e.tensor_scalar(denS, denS, TINY, None, ALU.max), reads=[denS_b], writes=[denS_b])
        s.add("dve", lambda e: e.reciprocal(denS, denS), reads=[denS_b], writes=[denS_b])
        for qt in range(4):
            o_ap, o_b = oring[oi % 3]
            oi += 1
            s.add("dve", lambda e, o=o_ap, a=accS[:, qt * 512:(qt + 1) * 512], b=denS[:, qt * 512:(qt + 1) * 512]: e.tensor_tensor(o, a, b, ALU.mult),
                  reads=[accS_b, denS_b], writes=[o_b])
            s.dma(d["oT"][hh * 128:(hh + 1) * 128, qt * 512:(qt + 1) * 512], o_ap, o_b, reads=[o_b], pwrites=[self.db("oT", qt)])
        s.release(mk)
    s.release(mk0)


KB.stage_attn_b = _kb_stage_attn_b


def _kb_stage_inproj_b(self, w_in, w_rot):
    kb = self
    d = self.d
    panels = []
    for hp in range(6):
        segs = [(w_in, hp * 256, 256), (w_rot, hp * 256, 256)]
        jobs = [dict(cols=[(j * 128, 128), (256 + j * 128, 128)], dst=d["qbT"], dname="qbT", row0=(hp * 2 + j) * 128,
                     epi=self._rope_epi_own()) for j in range(2)]
        panels.append(dict(segs=segs, jobs=jobs))
    segs = [(w_in, 1536, 512)]
    jobs = [dict(cols=[(j * 128, 128)], row0=j * 128, epi=kb.epi_plain_fm(d["qmT"], "qmT", None, 0)) for j in range(4)]
    panels.append(dict(segs=segs, jobs=jobs))
    self.stage_lfm(d["bnT"], "bnT", 0, SO, 16, panels, self.rope_setup(SO, SO))


KB.stage_inproj_b = _kb_stage_inproj_b


def _kb_stage_final_norm(self, src, srcname, dst):
    s = self.s
    mk = s.mark()
    fg, fg_b = s.alloc("fg", DM * 4)
    s.dma(fg, self.d["fgain"], fg_b, writes=[fg_b])
    hb = [s.alloc("fh%d" % i, DM * 4) for i in range(3)]
    ob = [s.alloc("fo%d" % i, DM * 4) for i in range(2)]
    junk_ap, junk_b = s.alloc("fjunk", DM * 2, BF16)
    st = [s.alloc("fst%d" % i, 4 * 4) for i in range(3)]
    for it in range(SO // 128):
        h_ap, h_b = hb[it % 3]
        st_ap, st_b = st[it % 3]
        o_ap, o_b = ob[it % 2]
        s.dma(h_ap, src[it * 128:(it + 1) * 128, :], h_b, reads=self.rd(srcname, it // 4), writes=[h_b])
        s.add("act", lambda e, h=h_ap, o=st_ap[:, 0:1]: e.activation(junk_ap, h, AF.Square, accum_out=o), reads=[h_b], pwrites=[junk_b, st_b])
        s.add("dve", lambda e, a=st_ap: e.tensor_scalar(a[:, 1:2], a[:, 0:1], 1.0 / 2048, EPS, ALU.mult, ALU.add), reads=[st_b], pwrites=[st_b])
        s.add("act", lambda e, a=st_ap: e.sqrt(a[:, 1:2], a[:, 1:2]), reads=[st_b], pwrites=[st_b])
        s.add("dve", lambda e, a=st_ap: e.reciprocal(a[:, 2:3], a[:, 1:2]), reads=[st_b], pwrites=[st_b])
        s.add("act", lambda e, o=o_ap, h=h_ap, sc=st_ap[:, 2:3]: e.activation(o, h, AF.Copy, scale=sc), reads=[h_b, st_b], writes=[o_b])
        s.add("dve", lambda e, o=o_ap: e.tensor_tensor(o, o, fg, ALU.mult), reads=[o_b, fg_b], writes=[o_b])
        s.dma(dst[it * 128:(it + 1) * 128, :], o_ap, o_b, reads=[o_b])
    s.release(mk)


KB.stage_final_norm = _kb_stage_final_norm


def build_phase_b(debug=()):
    nc = bass.Bass("TRN2", target_bir_lowering=False)
    kb = KB(nc)
    I = kb.inp
    h2 = I("h2in", [SO, DM])
    memb = I("memb", [256, DM])
    I("kshTv", [512, SV], BF16); I("vshv", [SV, 512], BF16)
    I("cosT", [128, SV]); I("sinT", [128, SV]); I("gains", [128, NG * 16]); I("fgain", [128, DM])
    I("c_ident", [128, 128]); I("c_ones", [128, 128])
    I("m_dil", [128, 5 * 512]); I("m_dil0", [128, 512])
    w_in = I("b_w_in", [DM, 2048]); w_rot = I("b_w_rot", [DM, 1536])
    wmkv = I("b_w_mem_kv", [DM, 1024]); wout = I("b_w_out", [1024, DM])
    wg = I("b_w_gate", [DM, DFF]); wu = I("b_w_up", [DM, DFF]); wd = I("b_w_down", [DFF, DM])

    def S(name, shape, dt):
        if name in debug:
            return kb.outp(name, shape, dt)
        return kb.scr(name, shape, dt)
    S("bnT", [DM, SO], BF16); S("qbT", [1536, SO], BF16); S("qmT", [512, SO], BF16)
    S("mkT", [512, 256], BF16); S("mv", [256, 512], BF16)
    S("oT", [1024, SO], BF16); S("h3", [SO, DM], F32); S("hnT", [DM, SO], BF16); S("hidT", [DFF, SO], BF16)
    S("h4", [SO, DM], F32)
    kb.outp("out", [SO, DM], F32)
    d = kb.d
    kb.consts(NG)
    kb.stage_norm(h2, None, SO, [4], [(d["bnT"], "bnT", 0)])
    kb.stage_inproj_b(w_in, w_rot)
    kb.stage_memkv(memb, 5, wmkv)
    kb.stage_attn_b()
    kb.stage_mem_attn(d["qmT"], "qmT", d["mkT"], d["mv"], d["oT"], "oT", 4)
    kb.stage_down(d["oT"], "oT", 8, wout, h2, None, d["h3"], "h3", TB=2048)
    kb.stage_norm(d["h3"], "h3", SO, [6], [(d["hnT"], "hnT", 0)])
    kb.stage_ffn(d["hnT"], "hnT", wg, wu, wd, d["h3"], "h3", d["h4"], "h4")
    kb.stage_final_norm(d["h4"], "h4", d["out"])
    kb.s.finalize()
    return nc, kb


def dil_consts(half):
    out = {}
    md = np.zeros((128, 5, 512), np.float32)
    for i, o_ in enumerate([-128, 0, 128, 256, 384]):
        md[:, i, :] = band_mask(o_, 128, False)
    out["m_dil"] = md.reshape(128, -1)
    out["m_dil0"] = band_mask(-128, 128, half == 0)
    return out


def weights_b(inp):
    w = {}
    w_in = inp["b_w_in"][0]
    w["b_w_in"] = w_in
    w["b_w_rot"] = rot_cols(w_in, [h * 128 for h in range(12)])
    w["b_w_mem_kv"] = inp["b_w_mem_kv"][0]; w["b_w_out"] = inp["b_w_out"][0]
    w["b_w_gate"] = inp["b_w_gate"][0]; w["b_w_up"] = inp["b_w_up"][0]; w["b_w_down"] = inp["b_w_down"][0]
    w["fgain"] = np.broadcast_to(inp["final_norm"][None, :], (128, DM))
    return {k: np.ascontiguousarray(np.asarray(v, dtype=np.float32)) for k, v in w.items()}


_PROG = {}


def kernel(**inputs):
    inp = {k: np.asarray(v) for k, v in inputs.items()}
    import ml_dtypes
    bf = ml_dtypes.bfloat16
    if "a" not in _PROG:
        _PROG["a"] = build_phase_a()
        _PROG["b"] = build_phase_b()
    nca, _ = _PROG["a"]
    ncb, _ = _PROG["b"]
    sc = shared_consts()
    wa = weights_a(inp)
    maps = []
    for c in range(8):
        m = phase_a_inputs(inp, c // 2, c % 2, sc)
        m.update(wa)
        maps.append(m)
    ra = run_bass_kernel_spmd(nca, maps, core_ids=list(range(8))).results
    del maps
    wb = weights_b(inp)
    maps = []
    for c in range(8):
        b, half = c // 2, c % 2
        m = {}
        m["h2in"] = np.ascontiguousarray(ra[c]["h2"])
        ksh = np.asarray(ra[c]["kshT"])
        vsh = np.asarray(ra[c]["vsh"])
        if half == 1:
            kprev = np.asarray(ra[c - 1]["kshT"]); vprev = np.asarray(ra[c - 1]["vsh"])
        else:
            kprev = np.zeros_like(ksh); vprev = np.zeros_like(vsh)
        m["kshTv"] = np.ascontiguousarray(np.concatenate([kprev, ksh], 1))
        m["vshv"] = np.ascontiguousarray(np.concatenate([vprev, vsh], 0))
        m["memb"] = np.ascontiguousarray(inp["mem"][b])
        cosT, sinT = rope_tabs(half)
        m["cosT"] = cosT; m["sinT"] = sinT
        m["gains"] = wa["gains"]
        m["c_ident"] = sc["c_ident"]; m["c_ones"] = sc["c_ones"]
        m.update(dil_consts(half))
        m.update(wb)
        maps.append(m)
    rb = run_bass_kernel_spmd(ncb, maps, core_ids=list(range(8))).results
    out = np.zeros((4, 4096, DM), np.float32)
    for c in range(8):
        b, half = c // 2, c % 2
        out[b, half * SO:(half + 1) * SO, :] = rb[c]["out"]
    return out


def build_fused(debug=()):
    nc = bass.Bass("TRN2", target_bir_lowering=False, num_devices=8)
    kb = KB(nc)
    I = kb.inp
    xv = I("xv", [SV, DM])
    memb = I("memb", [256, DM])
    I("cosT", [128, SV]); I("sinT", [128, SV]); I("gains", [128, NG * 16]); I("fgain", [128, DM])
    I("c_ident", [128, 128]); I("c_ones", [128, 128])
    I("m_cmp", [128, 8 * 512]); I("m_win", [128, 8 * 512]); I("m_win0", [128, 4 * 512])
    I("c_E", [64, SV]); I("c_mmap", [128, 128]); I("selM", [128, 1024]); I("selA", [128, 1024]); I("c_selmat", [36, 36 * 128])
    I("m_dil", [128, 5 * 512]); I("m_dil0", [128, 512])
    w_in = I("a_w_in", [DM, 3620]); w_rot = I("a_w_rot", [DM, 2304]); gbias = I("a_gbias", [36, 1])
    w1k = I("a_w1k", [4096, 256]); w2k = I("a_w2k", [256, 128]); pek = I("a_pekT", [128, 32])
    w1v = I("a_w1v", [4096, 256]); w2v = I("a_w2v", [256, 128]); pev = I("a_pevT", [128, 32])
    wmkv = I("a_w_mem_kv", [DM, 1024]); wout = I("a_w_out", [DM, DM])
    wg = I("a_w_gate", [DM, DFF]); wu = I("a_w_up", [DM, DFF]); wd = I("a_w_down", [DFF, DM])
    wkv = I("w_kv", [DM, 1024]); wkvr = I("w_kv_rot", [DM, 512])
    bw_in = I("b_w_in", [DM, 2048]); bw_rot = I("b_w_rot", [DM, 1536])
    bwmkv = I("b_w_mem_kv", [DM, 1024]); bwout = I("b_w_out", [1024, DM])
    bwg = I("b_w_gate", [DM, DFF]); bwu = I("b_w_up", [DM, DFF]); bwd = I("b_w_down", [DFF, DM])
    S = kb.scr
    S("xnT", [DM, SV], BF16)
    S("qT", [1536, SO], BF16); S("kcmpT", [256, SV], BF16); S("vcmpT", [256, SV], BF16)
    S("kslcT", [256, SV], BF16); S("kwinT", [256, SV], BF16); S("vsw", [SV, 512], BF16)
    S("gatesT", [36, SO], F32); S("qmT", [512, SO], BF16)
    S("kcT", [256, 256], BF16); S("vc", [512, 128], BF16)
    S("mkT", [512, 256], BF16); S("mv", [256, 512], BF16)
    S("oT", [DM, SO], BF16); S("h1", [SO, DM], F32); S("hnT", [DM, SO], BF16); S("hidT", [DFF, SO], BF16)
    S("h2", [SO, DM], F32)
    S("kvnT", [DM, SO], BF16); S("bnT", [DM, SO], BF16)
    S("kshT", [512, SO], BF16); S("vsh", [SO, 512], BF16)
    S("kg", [1024, SO], BF16); S("vg", [2 * SO, 512], BF16)
    S("qbT", [1536, SO], BF16); S("h3", [SO, DM], F32); S("h4", [SO, DM], F32)
    kb.outp("out", [SO, DM], F32)
    d = kb.d
    kb.consts(NG)
    kb.stage_norm(xv, None, SV, [0], [(d["xnT"], "xnT", 0)])
    kb.stage_inproj_a(w_in, w_rot, gbias)
    kb.stage_tm_bf16(d["xnT"], "xnT", 16, 0, SV, [(w_in, 2304, 256), (w_in, 2816, 256)], d["vsw"], "vsw")
    kb.stage_cmp(w1k, w2k, pek, w1v, w2v, pev)
    kb.stage_memkv(memb, 1, wmkv)
    kb.stage_attn_a()
    kb.stage_mem_attn(d["qmT"], "qmT", d["mkT"], d["mv"], d["oT"], "oT", 12)
    kb.stage_down(d["oT"], "oT", 16, wout, xv[SO:SV, :], None, d["h1"], "h1", TB=2048)
    kb.stage_norm(d["h1"], "h1", SO, [2], [(d["hnT"], "hnT", 0)])
    kb.stage_ffn(d["hnT"], "hnT", wg, wu, wd, d["h1"], "h1", d["h2"], "h2")
    kb.stage_norm(d["h2"], "h2", SO, [3, 4], [(d["kvnT"], "kvnT", 0), (d["bnT"], "bnT", 0)])
    kb.stage_kvshared(wkv, wkvr)
    groups = [[0, 1], [2, 3], [4, 5], [6, 7]]
    kb.s.collective(lambda e: e.collective_compute("AllGather", ALU.bypass, replica_groups=groups, ins=[d["kshT"]], outs=[d["kg"]]),
                    reads=[kb.db("kshT", i) for i in range(4)], writes=[kb.db("kg")])
    kb.s.collective(lambda e: e.collective_compute("AllGather", ALU.bypass, replica_groups=groups, ins=[d["vsh"]], outs=[d["vg"]]),
                    reads=[kb.db("vsh", i) for i in range(4)], writes=[kb.db("vg")])
    kb.stage_inproj_b(bw_in, bw_rot)
    kb.stage_memkv(memb, 5, bwmkv)
    kb.stage_attn_b()
    kb.stage_mem_attn(d["qmT"], "qmT", d["mkT"], d["mv"], d["oT"], "oT", 4)
    kb.stage_down(d["oT"], "oT", 8, bwout, d["h2"], "h2", d["h3"], "h3", TB=2048)
    kb.stage_norm(d["h3"], "h3", SO, [6], [(d["hnT"], "hnT", 0)])
    kb.stage_ffn(d["hnT"], "hnT", bwg, bwu, bwd, d["h3"], "h3", d["h4"], "h4")
    kb.stage_final_norm(d["h4"], "h4", d["out"])
    kb.s.finalize()
    return nc, kb


def kernel(**inputs):
    inp = {k: np.asarray(v) for k, v in inputs.items()}
    if "f" not in _PROG:
        _PROG["f"] = build_fused()
    nc, _ = _PROG["f"]
    sc = shared_consts()
    wa = weights_a(inp)
    wb = weights_b(inp)
    maps = []
    for c in range(8):
        b, half = c // 2, c % 2
        m = phase_a_inputs(inp, b, half, sc)
        m.update(dil_consts(half))
        m.update(wa)
        m.update(wb)
        maps.append(m)
    res = run_bass_kernel_spmd(nc, maps, core_ids=list(range(8))).results
    out = np.zeros((4, 4096, DM), np.float32)
    for c in range(8):
        b, half = c // 2, c % 2
        out[b, half * SO:(half + 1) * SO, :] = res[c]["out"]
    return out
```

```python
import numpy as np
import concourse.bass as bass
import concourse.mybir as mybir
from concourse.bass_utils import run_bass_kernel_spmd

F32 = mybir.dt.float32
BF16 = mybir.dt.bfloat16
AF = mybir.ActivationFunctionType
ALU = mybir.AluOpType
AX = mybir.AxisListType


ENGS = ("pe", "act", "dve", "pool", "sp")


class Buf:
    __slots__ = ("name", "w", "r", "sem", "ndma", "lo", "hi", "space")

    def __init__(self, name, space="sb", lo=0, hi=0):
        self.name = name
        self.w = []
        self.r = []
        self.sem = None
        self.ndma = 0
        self.lo = lo
        self.hi = hi
        self.space = space


class DSem:
    __slots__ = ("handle", "ndma", "idx", "inc")

    def __init__(self, idx, inc=16):
        self.handle = None
        self.ndma = 0
        self.idx = idx
        self.inc = inc


class Op:
    __slots__ = ("eng", "idx", "fn", "waits", "flagged", "rank", "dma_buf", "pe_group")

    def __init__(self, eng, idx, fn):
        self.eng = eng
        self.idx = idx
        self.fn = fn
        self.waits = {}
        self.flagged = False
        self.rank = 0
        self.dma_buf = None


class Sched:
    def __init__(self, nc, arena_bytes=200 * 1024):
        self.nc = nc
        self.ops = {e: [] for e in ENGS}
        self.waited = {e: {} for e in ENGS}
        self.arena_bytes = arena_bytes
        self.arena = nc.alloc_sbuf_tensor("arena", [128, arena_bytes // 4], F32)
        self.arena_top = 0
        self.live = []
        self.retired = []
        self.psum = [nc.alloc_psum_tensor("ps%d" % i, [128, 512], F32) for i in range(8)]
        self.psbuf = [Buf("ps%d" % i, "ps") for i in range(8)]
        self.nsem = 0
        self.eng_sem = {}
        self.dma_rr = 0
        self.NPOOL = 90
        self.pool = [DSem(i) for i in range(self.NPOOL)]
        self.cc_sem = DSem(1000, inc=1)

    def alloc(self, name, nbytes, dtype=F32):
        req = nbytes
        nbytes = (nbytes + 31) // 32 * 32
        lo = self.arena_top
        hi = lo + nbytes
        assert hi <= self.arena_bytes, "arena overflow %s: %d > %d" % (name, hi, self.arena_bytes)
        self.arena_top = hi
        b = Buf(name, "sb", lo, hi)
        keep = []
        for rb in self.retired:
            if rb.lo < hi and lo < rb.hi:
                b.r.extend(rb.w)
                b.r.extend(rb.r)
                if rb.lo < lo or rb.hi > hi:
                    keep.append(rb)
            else:
                keep.append(rb)
        self.retired = keep
        dd = {}
        for dep in b.r:
            if dep[0] == "e":
                k = ("e", dep[1].eng)
                if k not in dd or dd[k][1].idx < dep[1].idx:
                    dd[k] = dep
            else:
                dd[("d", dep[1].idx)] = dep
        b.r = list(dd.values())
        self.live.append(b)
        ap = self.arena[:, lo // 4:hi // 4]
        if dtype != F32:
            ap = ap.bitcast(dtype)
            ap = ap[:, 0:req // 2]
        else:
            ap = ap[:, 0:req // 4]
        return ap, b

    def mark(self):
        return (self.arena_top, len(self.live))

    def release(self, mark):
        top, n = mark
        for b in self.live[n:]:
            self.retired.append(b)
        self.live = self.live[:n]
        self.arena_top = top

    def _dep_of(self, op):
        if op.dma_buf is not None:
            return ("d", op.dma_buf)
        return ("e", op)

    def _add_wait(self, op, dep):
        if dep[0] == "e":
            p = dep[1]
            if p.eng == "pe" and op.eng == "pe":
                return
            key = ("e", p.eng)
            cur = op.waits.get(key)
            if cur is None or cur.idx < p.idx:
                op.waits[key] = p
        else:
            b = dep[1]
            key = ("d", b.idx)
            op.waits[key] = (b, b.ndma * b.inc)

    def _collect(self, op, reads, writes, pwrites):
        for b in reads:
            for d in b.w:
                self._add_wait(op, d)
        for b in writes:
            for d in b.w:
                self._add_wait(op, d)
            for d in b.r:
                self._add_wait(op, d)
        for b in pwrites:
            for d in b.r:
                self._add_wait(op, d)

    @staticmethod
    def _same(d, me):
        if d[0] != me[0]:
            return False
        if me[0] == "e":
            return d[1].eng == me[1].eng
        return d[1] is me[1]

    def _register(self, me, reads, writes, pwrites):
        for b in writes:
            b.w = [me]
            b.r = []
        for b in pwrites:
            b.w = [d for d in b.w if not self._same(d, me)]
            b.w.append(me)
        for b in reads:
            if b in writes or b in pwrites:
                continue
            b.r = [d for d in b.r if not self._same(d, me)]
            b.r.append(me)

    def add(self, eng, fn, reads=(), writes=(), pwrites=()):
        op = Op(eng, len(self.ops[eng]), fn)
        self.ops[eng].append(op)
        reads, writes, pwrites = list(reads), list(writes), list(pwrites)
        self._collect(op, reads, writes, pwrites)
        self._register(("e", op), reads, writes, pwrites)
        return op

    def dma(self, out_ap, in_ap, sem_buf, reads=(), writes=(), pwrites=(), q=None):
        if q is None:
            q = "sp"
        op = Op(q, len(self.ops[q]), lambda e, o=out_ap, i=in_ap: e.dma_start(out=o, in_=i))
        self.ops[q].append(op)
        reads, writes, pwrites = list(reads), list(writes), list(pwrites)
        self._collect(op, reads, writes, pwrites)
        if sem_buf.sem is None:
            sem_buf.sem = self.pool[self.dma_rr % self.NPOOL]
            self.dma_rr += 1
        ds = sem_buf.sem
        op.dma_buf = ds
        ds.ndma += 1
        self._register(("d", ds), reads, writes, pwrites)
        return op

    def collective(self, fn, reads=(), writes=()):
        op = Op("pool", len(self.ops["pool"]), fn)
        self.ops["pool"].append(op)
        reads, writes = list(reads), list(writes)
        self._collect(op, reads, writes, [])
        ds = self.cc_sem
        op.dma_buf = ds
        ds.ndma += 1
        self._register(("d", ds), reads, writes, [])
        return op

    def finalize(self, final_bufs=()):
        nc = self.nc
        fin = Op("sp", len(self.ops["sp"]), None)
        for ds in self.pool + [self.cc_sem]:
            if ds.ndma > 0:
                fin.waits[("d", ds.idx)] = (ds, ds.ndma * ds.inc)
        self.ops["sp"].append(fin)
        for e in ENGS:
            for op in self.ops[e]:
                for k, v in op.waits.items():
                    if k[0] == "e":
                        v.flagged = True
        for e in ENGS:
            r = 0
            for op in self.ops[e]:
                if op.flagged:
                    r += 1
                    op.rank = r
        for e in ENGS:
            if e != "sp":
                self.eng_sem[e] = nc.alloc_semaphore("sem_" + e)
        n = 0
        for ds in self.pool + [self.cc_sem]:
            if ds.ndma > 0:
                ds.handle = nc.alloc_semaphore("dsem_%d" % ds.idx)
                n += 1
        self.n_dma_sems = n
        sched = self

        def emit(e, eng):
            waited = {}
            for op in sched.ops[e]:
                for k, v in op.waits.items():
                    if k[0] == "e":
                        sem = sched.eng_sem[v.eng]
                        val = v.rank
                        wk = ("e", v.eng)
                    else:
                        sem = v[0].handle
                        val = v[1]
                        wk = k
                    if waited.get(wk, 0) >= val:
                        continue
                    waited[wk] = val
                    eng.wait_ge(sem, val)
                if op.fn is None:
                    continue
                ins = op.fn(eng)
                if op.dma_buf is not None:
                    ins.then_inc(op.dma_buf.handle, op.dma_buf.inc)
                elif op.flagged:
                    ins.then_inc(sched.eng_sem[e], 1)

        with nc.Block() as block:
            @block.tensor
            def _(eng):
                emit("pe", eng)

            @block.scalar
            def _(eng):
                emit("act", eng)

            @block.vector
            def _(eng):
                emit("dve", eng)

            @block.gpsimd
            def _(eng):
                emit("pool", eng)

            @block.sync
            def _(eng):
                emit("sp", eng)

NEG = -30000.0
EPS = 1e-6
TINY = 1e-30
SV = 4096
SO = 2048
DM = 2048
DFF = 5632
SCALE = 128 ** -0.5


def sub3(a, off, s1, n1, s2, n2):
    return bass.AP(a.tensor, a.offset + off, [list(a.ap[0]), [s1, n1], [s2, n2]])


def dview(d, r0, nr, c0, nc_):
    return d[r0:r0 + nr, c0:c0 + nc_].rearrange("(k p) n -> p k n", p=128)


class KB:
    def __init__(self, nc):
        self.nc = nc
        self.s = Sched(nc, arena_bytes=198 * 1024)
        self.d = {}
        self.dbufs = {}
        self.bank_rr = 0
        self.outs = []

    def inp(self, name, shape, dt=F32):
        self.d[name] = self.nc.dram_tensor(name, list(shape), dt, kind="ExternalInput").ap()
        return self.d[name]

    def scr(self, name, shape, dt):
        self.d[name] = self.nc.dram_tensor(name, list(shape), dt).ap()
        return self.d[name]

    def outp(self, name, shape, dt=F32):
        self.d[name] = self.nc.dram_tensor(name, list(shape), dt, kind="ExternalOutput").ap()
        return self.d[name]

    def db(self, name, i=0):
        k = (name, i)
        if k not in self.dbufs:
            self.dbufs[k] = Buf("d_%s_%s" % (name, i), "dram")
        return self.dbufs[k]

    def rd(self, name, i=0):
        if name is None:
            return []
        return [self.db(name, i)]

    def consts(self, ngain):
        s = self.s
        self.ident, self.ident_b = s.alloc("ident", 128 * 2, BF16)
        self.ones, self.ones_b = s.alloc("ones", 128 * 2, BF16)
        self.gains, self.gains_b = s.alloc("gains", ngain * 16 * 4)
        s.dma(self.ident, self.d["c_ident"], self.ident_b, writes=[self.ident_b], q="pool")
        s.dma(self.ones, self.d["c_ones"], self.ones_b, writes=[self.ones_b], q="pool")
        s.dma(self.gains, self.d["gains"], self.gains_b, writes=[self.gains_b])

    def stage_norm(self, src, srcname, ntok, gidx, dsts, src_tt0=0):
        s = self.s
        mk = s.mark()
        hb = [s.alloc("nh%d" % i, 2048 * 4) for i in range(8)]
        yb = [s.alloc("ny%d" % i, 2048 * 2, BF16) for i in range(8)]
        junk_ap, junk_b = s.alloc("njunk", 2048 * 2, BF16)
        st = [s.alloc("nst%d" % i, 12 * 4) for i in range(2)]
        ob = [[s.alloc("no%d_%d" % (g, i), 16 * 512 * 2, BF16) for i in range(2)] for g in range(len(gidx))]
        psb = [s.psum[i][:, :].bitcast(BF16) for i in range(8)]
        ident, ident_b = self.ident, self.ident_b
        ngrp = ntok // 512

        def loads(tt):
            for sub in range(4):
                h_ap, h_b = hb[(tt % 2) * 4 + sub]
                r0 = tt * 512 + sub * 128
                s.dma(h_ap, src[r0:r0 + 128, :], h_b, reads=self.rd(srcname, src_tt0 + tt), writes=[h_b])
        loads(0)
        for tt in range(ngrp):
            if tt + 1 < ngrp:
                loads(tt + 1)
            st_ap, st_b = st[tt % 2]
            for sub in range(4):
                h_ap, h_b = hb[(tt % 2) * 4 + sub]
                s.add("act", lambda e, h=h_ap, o=st_ap[:, sub:sub + 1]: e.activation(junk_ap, h, AF.Square, accum_out=o),
                      reads=[h_b], pwrites=[junk_b, st_b])
            s.add("dve", lambda e, a=st_ap: e.tensor_scalar(a[:, 4:8], a[:, 0:4], 1.0 / 2048, EPS, ALU.mult, ALU.add),
                  reads=[st_b], pwrites=[st_b])
            s.add("act", lambda e, a=st_ap: e.sqrt(a[:, 4:8], a[:, 4:8]), reads=[st_b], pwrites=[st_b])
            s.add("dve", lambda e, a=st_ap: e.reciprocal(a[:, 8:12], a[:, 4:8]), reads=[st_b], pwrites=[st_b])
            for sub in range(4):
                h_ap, h_b = hb[(tt % 2) * 4 + sub]
                y_ap, y_b = yb[(tt % 2) * 4 + sub]
                s.add("act", lambda e, y=y_ap, h=h_ap, sc=st_ap[:, 8 + sub:9 + sub]: e.activation(y, h, AF.Copy, scale=sc),
                      reads=[h_b, st_b], writes=[y_b])
                for half in range(2):
                    bank = self.bank_rr % 8
                    self.bank_rr += 1
                    for k8 in range(8):
                        kc = half * 8 + k8
                        s.add("pe", lambda e, o=psb[bank][:, k8 * 128:(k8 + 1) * 128], i=y_ap[:, kc * 128:(kc + 1) * 128]:
                              e.transpose(o, i, ident), reads=[y_b, ident_b], writes=[s.psbuf[bank]])
                    for gi, g in enumerate(gidx):
                        o_ap, o_b = ob[gi][tt % 2]
                        out3 = sub3(o_ap, half * 8 * 512 + sub * 128, 512, 8, 1, 128)
                        in0 = sub3(psb[bank], 0, 128, 8, 1, 128)
                        ga = self.gains[:, g * 16 + half * 8:g * 16 + half * 8 + 8]
                        in1 = sub3(ga, 0, 1, 8, 0, 128)
                        s.add("dve", lambda e, o=out3, a=in0, b=in1: e.tensor_tensor(o, a, b, ALU.mult),
                              reads=[s.psbuf[bank], self.gains_b], pwrites=[o_b])
            for gi in range(len(gidx)):
                dst, dname, dtt0 = dsts[gi]
                o_ap, o_b = ob[gi][tt % 2]
                for q4 in range(4):
                    s.dma(dview(dst, q4 * 512, 512, (dtt0 + tt) * 512, 512),
                          sub3(o_ap, q4 * 4 * 512, 512, 4, 1, 512), o_b, reads=[o_b], pwrites=[self.db(dname, dtt0 + tt)])
        s.release(mk)

    def load_panel(self, p_ap, p_b, segs, KC, kgrp=4, stride=512):
        s = self.s
        po = 0
        for (W, c0, n) in segs:
            for q in range(0, KC, kgrp):
                kn = min(kgrp, KC - q)
                s.dma(sub3(p_ap, q * stride + po, stride, kn, 1, n), dview(W, q * 128, kn * 128, c0, n), p_b,
                      pwrites=[p_b], q="pool")
            po += n

    def stage_lfm(self, xT, xname, tok0, ntok, KC, panels, setup):
        s = self.s
        mk = s.mark()
        nt = ntok // 512
        xs = [s.alloc("lx%d" % i, KC * 512 * 2, BF16) for i in range(nt)]
        for tt in range(nt):
            x_ap, x_b = xs[tt]
            for q in range(0, KC, 4):
                s.dma(sub3(x_ap, q * 512, 512, 4, 1, 512), dview(xT, q * 128, 512, tok0 + tt * 512, 512), x_b,
                      reads=self.rd(xname, tok0 // 512 + tt), pwrites=[x_b])
        pr = [s.alloc("lp%d" % i, KC * 512 * 2, BF16) for i in range(3)]
        ctx = setup(s)
        npan = len(panels)
        for i in range(min(2, npan)):
            self.load_panel(pr[i % 3][0], pr[i % 3][1], panels[i]["segs"], KC)
        for i, pan in enumerate(panels):
            p_ap, p_b = pr[i % 3]
            for job in pan["jobs"]:
                nb = len(job["cols"])
                for tt in range(nt):
                    x_ap, x_b = xs[tt]
                    banks = []
                    for (off, n) in job["cols"]:
                        bank = self.bank_rr % 8
                        self.bank_rr += 1
                        banks.append(bank)
                        for kc in range(KC):
                            s.add("pe", lambda e, o=s.psum[bank][0:n, :], l=p_ap[:, kc * 512 + off:kc * 512 + off + n],
                                  r=x_ap[:, kc * 512:(kc + 1) * 512], st=(kc == 0), sp=(kc == KC - 1):
                                  e.matmul(o, l, r, start=st, stop=sp),
                                  reads=[p_b, x_b], writes=[s.psbuf[bank]])
                    job["epi"](ctx, job, tt, banks)
            if i + 2 < npan:
                self.load_panel(pr[(i + 2) % 3][0], pr[(i + 2) % 3][1], panels[i + 2]["segs"], KC)
        s.release(mk)

    def stage_ltm(self, aT, aname, KC, tok0, ntok, TB, panels, setup, epi, pcols=512, nring=2):
        s = self.s
        mk = s.mark()
        ntb = TB // 512
        as_ = [s.alloc("ta%d" % i, KC * 512 * 2, BF16) for i in range(ntb)]
        pr = [s.alloc("tp%d" % i, KC * pcols * 2, BF16) for i in range(nring)]
        ctx = setup(s)
        seq = [(tb, pi) for tb in range(ntok // TB) for pi in range(len(panels))]

        def pload(k):
            tb_, pi_ = seq[k]
            self.load_panel(pr[k % nring][0], pr[k % nring][1], panels[pi_], KC, stride=pcols)
        for k in range(min(nring - 1, len(seq))):
            pload(k)
        for k, (tb, pi) in enumerate(seq):
            if pi == 0:
                for tt in range(ntb):
                    a_ap, a_b = as_[tt]
                    t0 = tok0 + tb * TB + tt * 512
                    for q in range(0, KC, 4):
                        s.dma(sub3(a_ap, q * 512, 512, 4, 1, 512), dview(aT, q * 128, 512, t0, 512), a_b,
                              reads=self.rd(aname, t0 // 512), pwrites=[a_b])
            if k + nring - 1 < len(seq):
                pload(k + nring - 1)
            p_ap, p_b = pr[k % nring]
            segs = panels[pi]
            ncol = sum(n for (_, _, n) in segs)
            for tt in range(ntb):
                a_ap, a_b = as_[tt]
                for t4 in range(4):
                    bank = self.bank_rr % 8
                    self.bank_rr += 1
                    for kc in range(KC):
                        s.add("pe", lambda e, o=s.psum[bank][:, 0:ncol], l=a_ap[:, kc * 512 + t4 * 128:kc * 512 + t4 * 128 + 128],
                              r=p_ap[:, kc * pcols:kc * pcols + ncol], st=(kc == 0), sp=(kc == KC - 1):
                              e.matmul(o, l, r, start=st, stop=sp),
                              reads=[p_b, a_b], writes=[s.psbuf[bank]])
                    epi(ctx, tok0 + tb * TB + tt * 512 + t4 * 128, pi, bank, ncol)
        s.release(mk)

    def epi_plain_fm(self, dst, dname, row0fn, tok0):
        kb = self

        def epi(ctx, job, tt, banks):
            s = kb.s
            n = job["cols"][0][1]
            o_ap, o_b = ctx["oring"][ctx["oi"] % len(ctx["oring"])]
            ctx["oi"] += 1
            bank = banks[0]
            s.add("act", lambda e, o=o_ap[0:n, :], i=s.psum[bank][0:n, :]: e.copy(o, i), reads=[s.psbuf[bank]], writes=[o_b])
            r0 = job["row0"]
            s.dma(dst[r0:r0 + n, tok0 + tt * 512:tok0 + tt * 512 + 512], o_ap[0:n, :], o_b, reads=[o_b],
                  pwrites=[kb.db(dname, (tok0 // 512) + tt)])
        return epi

    def epi_rope_fm(self, dst, dname, tok0):
        kb = self

        def epi(ctx, job, tt, banks):
            s = kb.s
            bz, br = banks
            t1, t1b = ctx["t1"][ctx["oi"] % 2]
            t2, t2b = ctx["t2"][ctx["oi"] % 2]
            o_ap, o_b = ctx["oring"][ctx["oi"] % len(ctx["oring"])]
            ctx["oi"] += 1
            cs, csb = ctx["cos"]
            sn, snb = ctx["sin"]
            s.add("dve", lambda e, o=t1, a=s.psum[bz][:, :], b=cs[:, tt * 512:(tt + 1) * 512]: e.tensor_tensor(o, a, b, ALU.mult),
                  reads=[s.psbuf[bz], csb], writes=[t1b])
            s.add("dve", lambda e, o=t2, a=s.psum[br][:, :], b=sn[:, tt * 512:(tt + 1) * 512]: e.tensor_tensor(o, a, b, ALU.mult),
                  reads=[s.psbuf[br], snb], writes=[t2b])
            s.add("pool", lambda e, o=o_ap, a=t1, b=t2: e.tensor_tensor(o, a, b, ALU.add), reads=[t1b, t2b], writes=[o_b])
            r0 = job["row0"]
            s.dma(job["dst"][r0:r0 + 128, tok0 + tt * 512:tok0 + tt * 512 + 512], o_ap, o_b, reads=[o_b],
                  pwrites=[kb.db(job["dname"], (tok0 // 512) + tt)])
        return epi

    def rope_setup(self, tok0, ntok, extra=None):
        kb = self

        def setup(s):
            ctx = {"oi": 0}
            ctx["oring"] = [s.alloc("eo%d" % i, 512 * 2, BF16) for i in range(4)]
            ctx["t1"] = [s.alloc("et1%d" % i, 512 * 4) for i in range(2)]
            ctx["t2"] = [s.alloc("et2%d" % i, 512 * 4) for i in range(2)]
            ctx["cos"] = s.alloc("ecos", ntok * 4)
            ctx["sin"] = s.alloc("esin", ntok * 4)
            s.dma(ctx["cos"][0], kb.d["cosT"][:, tok0:tok0 + ntok], ctx["cos"][1], writes=[ctx["cos"][1]])
            s.dma(ctx["sin"][0], kb.d["sinT"][:, tok0:tok0 + ntok], ctx["sin"][1], writes=[ctx["sin"][1]])
            if extra is not None:
                extra(s, ctx)
            return ctx
        return setup

    def stage_ffn(self, hnT, hnname, wg, wu, wd, h_in, h_in_name, h_out, h_out_name):
        kb = self
        hidT = self.d["hidT"]

        def setup(s):
            ctx = {"oi": 0}
            ctx["sg"] = [s.alloc("fsg%d" % i, 512 * 4) for i in range(3)]
            ctx["oring"] = [s.alloc("fo%d" % i, 512 * 2, BF16) for i in range(4)]
            return ctx

        def epi(ctx, job, tt, banks):
            s = kb.s
            bg, bu = banks
            sg, sgb = ctx["sg"][ctx["oi"] % 3]
            o_ap, o_b = ctx["oring"][ctx["oi"] % 4]
            ctx["oi"] += 1
            s.add("act", lambda e, o=sg, i=s.psum[bg][:, :]: e.activation(o, i, AF.Silu), reads=[s.psbuf[bg]], writes=[sgb])
            s.add("dve", lambda e, o=o_ap, a=s.psum[bu][:, :], b=sg: e.tensor_tensor(o, a, b, ALU.mult),
                  reads=[s.psbuf[bu], sgb], writes=[o_b])
            r0 = job["row0"]
            s.dma(hidT[r0:r0 + 128, tt * 512:(tt + 1) * 512], o_ap, o_b, reads=[o_b], pwrites=[kb.db("hidT", tt)])

        panels = []
        for pc in range(DFF // 256):
            segs = [(wg, pc * 256, 256), (wu, pc * 256, 256)]
            jobs = [dict(cols=[(j * 128, 128), (256 + j * 128, 128)], epi=epi, row0=pc * 256 + j * 128) for j in range(2)]
            panels.append(dict(segs=segs, jobs=jobs))
        self.stage_lfm(hnT, hnname, 0, SO, 16, panels, setup)
        self.stage_down(self.d["hidT"], "hidT", DFF // 128, wd, h_in, h_in_name, h_out, h_out_name, TB=1024)

    def stage_down(self, aT, aname, KC, W, h_in, h_in_name, h_out, h_out_name, TB):
        kb = self

        def setup(s):
            ctx = {"oi": 0}
            ctx["hin"] = [s.alloc("dh%d" % i, 512 * 4) for i in range(3)]
            ctx["oring"] = [s.alloc("do%d" % i, 512 * 4) for i in range(3)]
            return ctx

        PC = 256

        def epi(ctx, tok, pi, bank, ncol):
            s = kb.s
            hi, hib = ctx["hin"][ctx["oi"] % 3]
            o_ap, o_b = ctx["oring"][ctx["oi"] % 3]
            ctx["oi"] += 1
            s.dma(hi[:, 0:PC], h_in[tok:tok + 128, pi * PC:(pi + 1) * PC], hib, reads=kb.rd(h_in_name, tok // 512), writes=[hib])
            s.add("dve", lambda e, o=o_ap[:, 0:PC], a=s.psum[bank][:, 0:PC], b=hi[:, 0:PC]: e.tensor_tensor(o, a, b, ALU.add),
                  reads=[s.psbuf[bank], hib], writes=[o_b])
            s.dma(h_out[tok:tok + 128, pi * PC:(pi + 1) * PC], o_ap[:, 0:PC], o_b, reads=[o_b], pwrites=[kb.db(h_out_name, tok // 512)])

        panels = [[(W, pi * PC, PC)] for pi in range(DM // PC)]
        self.stage_ltm(aT, aname, KC, 0, SO, TB, panels, setup, epi, pcols=PC, nring=4)

    def stage_tm_bf16(self, aT, aname, KC, tok0, ntok, segs, dst, dname):
        kb = self

        def setup(s):
            return {"oi": 0, "oring": [s.alloc("vo%d" % i, 512 * 2, BF16) for i in range(4)]}

        def epi(ctx, tok, pi, bank, ncol):
            s = kb.s
            o_ap, o_b = ctx["oring"][ctx["oi"] % 4]
            ctx["oi"] += 1
            s.add("act", lambda e, o=o_ap[:, 0:ncol], i=s.psum[bank][:, 0:ncol]: e.copy(o, i), reads=[s.psbuf[bank]], writes=[o_b])
            s.dma(dst[tok:tok + 128, 0:ncol], o_ap[:, 0:ncol], o_b, reads=[o_b], pwrites=[kb.db(dname, tok // 512)])

        self.stage_ltm(aT, aname, KC, tok0, ntok, min(ntok, 2048), [segs], setup, epi)

    def unit_s(self, ctx, u):
        s = self.s
        n = u.get("n", 512)
        np_ = u.get("np_", 128)
        bS = ctx["sbanks"][ctx["si"] % len(ctx["sbanks"])]
        ctx["si"] += 1
        pt, ptb = ctx["pt"][ctx["pi"] % len(ctx["pt"])]
        ctx["pi"] += 1
        extras = u["extras"]
        ne = len(extras)
        s.add("pe", lambda e, o=s.psum[bS][0:np_, 0:n], l=u["klhs"], r=u["qrhs"], sp=(ne == 0): e.matmul(o, l, r, start=True, stop=sp),
              reads=u["krd"] + u["qrd"], writes=[s.psbuf[bS]])
        for i, (l, r, rds) in enumerate(extras):
            s.add("pe", lambda e, o=s.psum[bS][0:np_, 0:n], l=l, r=r, sp=(i == ne - 1): e.matmul(o, l, r, start=False, stop=sp),
                  reads=rds, writes=[s.psbuf[bS]])
        s.add("act", lambda e, o=pt[0:np_, 0:n], i=s.psum[bS][0:np_, 0:n]: e.activation(o, i, AF.Exp, scale=SCALE),
              reads=[s.psbuf[bS]], writes=[ptb])
        return pt, ptb

    def unit_pv(self, ctx, u, rec):
        s = self.s
        n = u.get("n", 512)
        np_ = u.get("np_", 128)
        pt, ptb = rec
        first, last = u["first"], u["last"]
        if u["vlhs"] is not None:
            s.add("pe", lambda e, o=s.psum[u["bacc"]][:, 0:n], l=u["vlhs"], r=pt[0:np_, 0:n], st=first, sp=last: e.matmul(o, l, r, start=st, stop=sp),
                  reads=u["vrd"] + [ptb], writes=[s.psbuf[u["bacc"]]])
        s.add("pe", lambda e, o=s.psum[u["bden"]][:, 0:n], l=self.ones[0:np_, :], r=pt[0:np_, 0:n], st=first, sp=last: e.matmul(o, l, r, start=st, stop=sp),
              reads=[self.ones_b, ptb], writes=[s.psbuf[u["bden"]]])
        if u.get("post") is not None:
            u["post"]()

    def run_units(self, ctx, units, skew=2):
        recs = []
        nu = len(units)
        for i in range(nu + skew):
            if i < nu:
                recs.append(self.unit_s(ctx, units[i]))
            j = i - skew
            if j >= 0:
                self.unit_pv(ctx, units[j], recs[j])
        return recs

    def attn_unit(self, ctx, klhs, krd, qrhs, qrd, extras, vlhs, vrd, bacc, bden, first, last, n=512, np_=128):
        u = dict(klhs=klhs, krd=krd, qrhs=qrhs, qrd=qrd, extras=extras, vlhs=vlhs, vrd=vrd, bacc=bacc, bden=bden,
                 first=first, last=last, n=n, np_=np_)
        rec = self.unit_s(ctx, u)
        self.unit_pv(ctx, u, rec)
        return rec

    def recip_den(self, ctx, bden, n=512):
        s = self.s
        r, rb = ctx["rd"][ctx["ri"] % len(ctx["rd"])]
        ctx["ri"] += 1
        s.add("dve", lambda e, o=r[:, 0:n], i=s.psum[bden][:, 0:n]: e.tensor_scalar(o, i, TINY, None, ALU.max),
              reads=[s.psbuf[bden]], writes=[rb])
        s.add("dve", lambda e, o=r[:, 0:n]: e.reciprocal(o, o), reads=[rb], writes=[rb])
        return r, rb

    def stage_mem_attn(self, qmT, qmname, mkT, mv, oT, oname, chunk0):
        s = self.s
        mk = s.mark()
        ctx = dict(si=0, pi=0, ri=0, sbanks=[0, 1, 2], pt=[s.alloc("mpt%d" % i, 512 * 2, BF16) for i in range(4)],
                   rd=[s.alloc("mrd%d" % i, 512 * 4) for i in range(2)])
        k_ap, k_b = s.alloc("mk", 4 * 256 * 2, BF16)
        v_ap, v_b = s.alloc("mv", 2 * 512 * 2, BF16)
        q_ap, q_b = s.alloc("mq", 4 * SO * 2, BF16)
        oring = [s.alloc("mo%d" % i, 512 * 2, BF16) for i in range(3)]
        s.dma(sub3(k_ap, 0, 256, 4, 1, 256), dview(mkT, 0, 512, 0, 256), k_b, reads=self.rd("mkT"), writes=[k_b])
        s.dma(sub3(v_ap, 0, 512, 2, 1, 512), dview(mv, 0, 256, 0, 512), v_b, reads=self.rd("mv"), writes=[v_b])
        for h in range(4):
            s.dma(q_ap[:, h * SO:(h + 1) * SO], qmT[h * 128:(h + 1) * 128, :], q_b,
                  reads=[self.db(qmname, i) for i in range(4)], pwrites=[q_b])
        units = []
        oi = 0
        for h in range(4):
            for qt in range(4):
                bacc, bden = (3, 4) if (oi % 2 == 0) else (5, 6)
                o_ap, o_b = oring[oi % 3]
                oi += 1

                def post(bacc=bacc, bden=bden, o_ap=o_ap, o_b=o_b, h=h, qt=qt):
                    r, rb = self.recip_den(ctx, bden)
                    s.add("dve", lambda e, o=o_ap, a=s.psum[bacc][:, :], b=r: e.tensor_tensor(o, a, b, ALU.mult),
                          reads=[s.psbuf[bacc], rb], writes=[o_b])
                    s.dma(oT[(chunk0 + h) * 128:(chunk0 + h + 1) * 128, qt * 512:(qt + 1) * 512], o_ap, o_b, reads=[o_b],
                          pwrites=[self.db(oname, qt)])
                for mt in range(2):
                    units.append(dict(klhs=k_ap[:, h * 256 + mt * 128:h * 256 + mt * 128 + 128], krd=[k_b],
                                      qrhs=q_ap[:, h * SO + qt * 512:h * SO + qt * 512 + 512], qrd=[q_b], extras=[],
                                      vlhs=v_ap[:, mt * 512 + h * 128:mt * 512 + h * 128 + 128], vrd=[v_b], bacc=bacc, bden=bden,
                                      first=(mt == 0), last=(mt == 1), post=(post if mt == 1 else None)))
        self.run_units(ctx, units)
        s.release(mk)

    def stage_memkv(self, mem, gi, wkv):
        kb = self
        s = self.s
        mk = s.mark()
        hb = [s.alloc("kh%d" % i, 2048 * 4) for i in range(2)]
        yb = [s.alloc("ky%d" % i, 2048 * 2, BF16) for i in range(2)]
        junk_ap, junk_b = s.alloc("kjunk", 2048 * 2, BF16)
        st_ap, st_b = s.alloc("kst", 12 * 4)
        o_ap, o_b = s.alloc("ko", 16 * 256 * 2, BF16)
        psb = [s.psum[i][:, :].bitcast(BF16) for i in range(8)]
        for sub in range(2):
            h_ap, h_b = hb[sub]
            s.dma(h_ap, mem[sub * 128:(sub + 1) * 128, :], h_b, writes=[h_b])
            s.add("act", lambda e, h=h_ap, o=st_ap[:, sub:sub + 1]: e.activation(junk_ap, h, AF.Square, accum_out=o),
                  reads=[h_b], writes=[junk_b], pwrites=[st_b])
        s.add("dve", lambda e, a=st_ap: e.tensor_scalar(a[:, 4:6], a[:, 0:2], 1.0 / 2048, EPS, ALU.mult, ALU.add), reads=[st_b], pwrites=[st_b])
        s.add("act", lambda e, a=st_ap: e.sqrt(a[:, 4:6], a[:, 4:6]), reads=[st_b], pwrites=[st_b])
        s.add("dve", lambda e, a=st_ap: e.reciprocal(a[:, 8:10], a[:, 4:6]), reads=[st_b], pwrites=[st_b])
        for sub in range(2):
            h_ap, h_b = hb[sub]
            y_ap, y_b = yb[sub]
            s.add("act", lambda e, y=y_ap, h=h_ap, sc=st_ap[:, 8 + sub:9 + sub]: e.activation(y, h, AF.Copy, scale=sc),
                  reads=[h_b, st_b], writes=[y_b])
            for half in range(2):
                bank = self.bank_rr % 8
                self.bank_rr += 1
                for k8 in range(8):
                    kc = half * 8 + k8
                    s.add("pe", lambda e, o=psb[bank][:, k8 * 128:(k8 + 1) * 128], i=y_ap[:, kc * 128:(kc + 1) * 128]:
                          e.transpose(o, i, kb.ident), reads=[y_b, kb.ident_b], writes=[s.psbuf[bank]])
                out3 = sub3(o_ap, half * 8 * 256 + sub * 128, 256, 8, 1, 128)
                in0 = sub3(psb[bank], 0, 128, 8, 1, 128)
                ga = self.gains[:, gi * 16 + half * 8:gi * 16 + half * 8 + 8]
                in1 = sub3(ga, 0, 1, 8, 0, 128)
                s.add("dve", lambda e, o=out3, a=in0, b=in1: e.tensor_tensor(o, a, b, ALU.mult),
                      reads=[s.psbuf[bank], self.gains_b], pwrites=[o_b])
        mkT = self.d["mkT"]
        mv = self.d["mv"]
        pr = [s.alloc("kp%d" % i, 16 * 512 * 2, BF16) for i in range(2)]
        oring = [s.alloc("kor%d" % i, 512 * 2, BF16) for i in range(3)]
        oi = 0
        for half in range(2):
            p_ap, p_b = pr[half]
            self.load_panel(p_ap, p_b, [(wkv, half * 512, 512)], 16)
        p_ap, p_b = pr[0]
        for h in range(4):
            bank = self.bank_rr % 8
            self.bank_rr += 1
            for kc in range(16):
                s.add("pe", lambda e, o=s.psum[bank][:, 0:256], l=p_ap[:, kc * 512 + h * 128:kc * 512 + h * 128 + 128],
                      r=o_ap[:, kc * 256:(kc + 1) * 256], st=(kc == 0), sp=(kc == 15): e.matmul(o, l, r, start=st, stop=sp),
                      reads=[p_b, o_b], writes=[s.psbuf[bank]])
            oo, oob = oring[oi % 3]
            oi += 1
            s.add("act", lambda e, o=oo[:, 0:256], i=s.psum[bank][:, 0:256]: e.copy(o, i), reads=[s.psbuf[bank]], writes=[oob])
            s.dma(mkT[h * 128:(h + 1) * 128, :], oo[:, 0:256], oob, reads=[oob], pwrites=[self.db("mkT")])
        p_ap, p_b = pr[1]
        for mt in range(2):
            bank = self.bank_rr % 8
            self.bank_rr += 1
            for kc in range(16):
                s.add("pe", lambda e, o=s.psum[bank][:, :], l=o_ap[:, kc * 256 + mt * 128:kc * 256 + mt * 128 + 128],
                      r=p_ap[:, kc * 512:(kc + 1) * 512], st=(kc == 0), sp=(kc == 15): e.matmul(o, l, r, start=st, stop=sp),
                      reads=[p_b, o_b], writes=[s.psbuf[bank]])
            oo, oob = oring[oi % 3]
            oi += 1
            s.add("act", lambda e, o=oo, i=s.psum[bank][:, :]: e.copy(o, i), reads=[s.psbuf[bank]], writes=[oob])
            s.dma(mv[mt * 128:(mt + 1) * 128, :], oo, oob, reads=[oob], pwrites=[self.db("mv")])
        s.release(mk)

    def stage_inproj_a(self, w_in, w_rot, gbias):
        kb = self
        d = self.d
        xT = d["xnT"]
        for tok0, own in ((0, False), (SO, True)):
            epi_rope = self.epi_rope_fm(None, None, tok0)
            epi_plain = self.epi_plain_fm(None, None, None, tok0)

            def mkplain(dst, dname):
                return kb.epi_plain_fm(dst, dname, None, tok0)

            def gate_extra(s, ctx):
                ctx["gb"] = s.alloc("egb", 4)
                s.dma(ctx["gb"][0][0:36, :], gbias, ctx["gb"][1], writes=[ctx["gb"][1]])
                ctx["go"] = [s.alloc("ego%d" % i, 512 * 4) for i in range(2)]

            def epi_gate(ctx, job, tt, banks):
                s = kb.s
                o_ap, o_b = ctx["go"][tt % 2]
                gb, gbb = ctx["gb"]
                bank = banks[0]
                s.add("act", lambda e, o=o_ap[0:36, :], i=s.psum[bank][0:36, :], b=gb[0:36, 0:1]: e.activation(o, i, AF.Sigmoid, bias=b),
                      reads=[s.psbuf[bank], gbb], writes=[o_b])
                s.dma(d["gatesT"][:, tt * 512:(tt + 1) * 512], o_ap[0:36, :], o_b, reads=[o_b], pwrites=[kb.db("gatesT", tt)])

            panels = []
            otok = tok0 - SO

            def ropejob(off, roff, dst, dname, row0):
                return dict(cols=[(off, 128), (roff, 128)], epi=kb.epi_rope_fm(None, None, tok0 if dst is not d["qT"] else 0),
                            dst=dst, dname=dname, row0=row0)
            if own:
                for hp in range(6):
                    segs = [(w_in, hp * 256, 256), (w_rot, hp * 256, 256)]
                    jobs = []
                    for j in range(2):
                        jb = dict(cols=[(j * 128, 128), (256 + j * 128, 128)], dst=d["qT"], dname="qT", row0=(hp * 2 + j) * 128)
                        jb["epi"] = self._rope_epi_own()
                        jobs.append(jb)
                    panels.append(dict(segs=segs, jobs=jobs))
            for (kcol, rcol, dst, dname) in ((1536, 1536, d["kcmpT"], "kcmpT"), (2048, 1792, d["kslcT"], "kslcT"),
                                             (2560, 2048, d["kwinT"], "kwinT")):
                segs = [(w_in, kcol, 256), (w_rot, rcol, 256)]
                jobs = []
                for g in range(2):
                    jobs.append(dict(cols=[(g * 128, 128), (256 + g * 128, 128)], dst=dst, dname=dname, row0=g * 128,
                                     epi=self._rope_epi_all(tok0)))
                panels.append(dict(segs=segs, jobs=jobs))
            segs = [(w_in, 1792, 256)]
            jobs = [dict(cols=[(g * 128, 128)], row0=g * 128, epi=mkplain(d["vcmpT"], "vcmpT")) for g in range(2)]
            panels.append(dict(segs=segs, jobs=jobs))
            if own:
                segs = [(w_in, 3072, 36), (w_in, 3108, 256)]
                jobs = [dict(cols=[(0, 36)], epi=epi_gate)]
                for j in range(2):
                    jobs.append(dict(cols=[(36 + j * 128, 128)], row0=j * 128, epi=kb.epi_plain_fm(d["qmT"], "qmT", None, 0)))
                panels.append(dict(segs=segs, jobs=jobs))
                segs = [(w_in, 3108 + 256, 256)]
                jobs = []
                for j in range(2):
                    jobs.append(dict(cols=[(j * 128, 128)], row0=(2 + j) * 128, epi=kb.epi_plain_fm(d["qmT"], "qmT", None, 0)))
                panels.append(dict(segs=segs, jobs=jobs))
            self.cur_tok0 = tok0
            self.stage_lfm(xT, "xnT", tok0, SO, 16, panels, self.rope_setup(tok0, SO, gate_extra if own else None))

    def _rope_epi_all(self, tok0):
        kb = self

        def epi(ctx, job, tt, banks):
            kb._rope_core(ctx, job, tt, banks, tok0 + tt * 512, (tok0 // 512) + tt)
        return epi

    def _rope_epi_own(self):
        kb = self

        def epi(ctx, job, tt, banks):
            kb._rope_core(ctx, job, tt, banks, tt * 512, tt)
        return epi

    def _rope_core(self, ctx, job, tt, banks, col0, dbi):
        s = self.s
        bz, br = banks
        t1, t1b = ctx["t1"][ctx["oi"] % 2]
        t2, t2b = ctx["t2"][ctx["oi"] % 2]
        o_ap, o_b = ctx["oring"][ctx["oi"] % len(ctx["oring"])]
        ctx["oi"] += 1
        cs, csb = ctx["cos"]
        sn, snb = ctx["sin"]
        s.add("dve", lambda e, o=t1, a=s.psum[bz][:, :], b=cs[:, tt * 512:(tt + 1) * 512]: e.tensor_tensor(o, a, b, ALU.mult),
              reads=[s.psbuf[bz], csb], writes=[t1b])
        s.add("dve", lambda e, o=t2, a=s.psum[br][:, :], b=sn[:, tt * 512:(tt + 1) * 512]: e.tensor_tensor(o, a, b, ALU.mult),
              reads=[s.psbuf[br], snb], writes=[t2b])
        s.add("pool", lambda e, o=o_ap, a=t1, b=t2: e.tensor_tensor(o, a, b, ALU.add), reads=[t1b, t2b], writes=[o_b])
        r0 = job["row0"]
        s.dma(job["dst"][r0:r0 + 128, col0:col0 + 512], o_ap, o_b, reads=[o_b], pwrites=[self.db(job["dname"], dbi)])

    def stage_cmp(self, w1k, w2k, pek, w1v, w2v, pev):
        s = self.s
        d = self.d
        for kv, (w1, w2, peT, srcT, sname) in enumerate(((w1k, w2k, pek, d["kcmpT"], "kcmpT"), (w1v, w2v, pev, d["vcmpT"], "vcmpT"))):
            mk = s.mark()
            w1_ap, w1_b = s.alloc("cw1", 32 * 256 * 2, BF16)
            for q in range(0, 32, 8):
                s.dma(sub3(w1_ap, q * 256, 256, 8, 1, 256), dview(w1, q * 128, 1024, 0, 256), w1_b, pwrites=[w1_b], q="pool")
            w2_ap, w2_b = s.alloc("cw2", 2 * 128 * 2, BF16)
            s.dma(sub3(w2_ap, 0, 128, 2, 1, 128), dview(w2, 0, 256, 0, 128), w2_b, writes=[w2_b], q="pool")
            pe_ap, pe_b = s.alloc("cpe", 32 * 2, BF16)
            s.dma(pe_ap, peT, pe_b, writes=[pe_b], q="pool")
            bias_ap, bias_b = s.alloc("cbias", 2 * 4)
            for hc in range(2):
                bank = self.bank_rr % 8
                self.bank_rr += 1
                for l in range(32):
                    s.add("pe", lambda e, o=s.psum[bank][:, 0:1], lh=w1_ap[:, l * 256 + hc * 128:l * 256 + hc * 128 + 128], r=pe_ap[:, l:l + 1],
                          st=(l == 0), sp=(l == 31): e.matmul(o, lh, r, start=st, stop=sp), reads=[w1_b, pe_b], writes=[s.psbuf[bank]])
                s.add("dve", lambda e, o=bias_ap[:, hc:hc + 1], i=s.psum[bank][:, 0:1]: e.tensor_copy(o, i), reads=[s.psbuf[bank]], pwrites=[bias_b])
            for g in range(2):
                k_ap, k_b = s.alloc("ck%d" % g, SV * 2, BF16)
                s.dma(k_ap, srcT[g * 128:(g + 1) * 128, :], k_b, reads=[self.db(sname, i) for i in range(8)], writes=[k_b])
                hs_ap, hs_b = s.alloc("chs%d" % g, 2 * 256 * 2, BF16)
                for hc in range(2):
                    bank = self.bank_rr % 8
                    self.bank_rr += 1
                    for l in range(32):
                        s.add("pe", lambda e, o=s.psum[bank][:, 0:255], lh=w1_ap[:, l * 256 + hc * 128:l * 256 + hc * 128 + 128],
                              r=k_ap[:, l:l + 16 * 254 + 1:16], st=(l == 0), sp=(l == 31): e.matmul(o, lh, r, start=st, stop=sp),
                              reads=[w1_b, k_b], writes=[s.psbuf[bank]])
                    s.add("act", lambda e, o=hs_ap[:, hc * 256:hc * 256 + 255], i=s.psum[bank][:, 0:255], b=bias_ap[:, hc:hc + 1]:
                          e.activation(o, i, AF.Silu, bias=b), reads=[s.psbuf[bank], bias_b], pwrites=[hs_b])
                o_ap, o_b = s.alloc("cout%d" % g, 256 * 2, BF16)
                if kv == 0:
                    bank = self.bank_rr % 8
                    self.bank_rr += 1
                    for hc in range(2):
                        s.add("pe", lambda e, o=s.psum[bank][:, 0:255], lh=w2_ap[:, hc * 128:(hc + 1) * 128], r=hs_ap[:, hc * 256:hc * 256 + 255],
                              st=(hc == 0), sp=(hc == 1): e.matmul(o, lh, r, start=st, stop=sp), reads=[w2_b, hs_b], writes=[s.psbuf[bank]])
                    s.add("pool", lambda e, o=o_ap: e.memset(o, 0.0), writes=[o_b])
                    s.add("act", lambda e, o=o_ap[:, 0:255], i=s.psum[bank][:, 0:255]: e.copy(o, i), reads=[s.psbuf[bank]], pwrites=[o_b])
                    s.dma(d["kcT"][g * 128:(g + 1) * 128, :], o_ap, o_b, reads=[o_b], pwrites=[self.db("kcT")])
                else:
                    s.add("pool", lambda e, o=o_ap: e.memset(o, 0.0), writes=[o_b])
                    for ct in range(2):
                        ncn = 128 if ct == 0 else 127
                        bank = self.bank_rr % 8
                        self.bank_rr += 1
                        for hc in range(2):
                            s.add("pe", lambda e, o=s.psum[bank][0:ncn, 0:128], lh=hs_ap[:, hc * 256 + ct * 128:hc * 256 + ct * 128 + ncn],
                                  r=w2_ap[:, hc * 128:(hc + 1) * 128], st=(hc == 0), sp=(hc == 1): e.matmul(o, lh, r, start=st, stop=sp),
                                  reads=[w2_b, hs_b], writes=[s.psbuf[bank]])
                        s.add("act", lambda e, o=o_ap[0:ncn, ct * 128:(ct + 1) * 128], i=s.psum[bank][0:ncn, 0:128]: e.copy(o, i),
                              reads=[s.psbuf[bank]], pwrites=[o_b])
                    s.dma(dview(d["vc"], g * 256, 256, 0, 128), sub3(o_ap, 0, 128, 2, 1, 128), o_b, reads=[o_b], pwrites=[self.db("vc")])
            s.release(mk)

    def stage_attn_a(self):
        s = self.s
        d = self.d
        mk0 = s.mark()
        def ld(name, src, nbytes, dt, q="sp", parts=128):
            ap, b = s.alloc(name, nbytes, dt)
            s.dma(ap[0:parts, :], src, b, writes=[b], q=q)
            return ap, b
        mcmp, mcmp_b = ld("mcmp", d["m_cmp"], 8 * 512 * 2, BF16, "pool")
        mwin, mwin_b = ld("mwin", d["m_win"], 8 * 512 * 2, BF16, "pool")
        mwin0, mwin0_b = ld("mwin0", d["m_win0"], 4 * 512 * 2, BF16, "pool")
        E, E_b = ld("E", d["c_E"], SV * 2, BF16, "pool", 64)
        mmap, mmap_b = ld("mmap", d["c_mmap"], 2 * 64 * 4, F32)
        ph = [s.alloc("aph%d" % i, 512 * 4) for i in range(2)]
        selM, selM_b = ld("selM", d["selM"], 4 * 256 * 4, F32)
        selA, selA_b = ld("selA", d["selA"], 4 * 256 * 4, F32)
        selmat, selmat_b = ld("selmat", d["c_selmat"], 36 * 128 * 4, F32, "sp", 36)
        gat, gat_b = s.alloc("gat", SO * 4)
        s.dma(gat[0:36, :], d["gatesT"], gat_b, reads=[self.db("gatesT", i) for i in range(4)], writes=[gat_b])
        ctx = dict(si=0, pi=0, ri=0, sbanks=[0, 1, 2], pt=[s.alloc("apt%d" % i, 512 * 2, BF16) for i in range(4)],
                   rd=[s.alloc("ard%d" % i, 512 * 4) for i in range(3)])
        pn = [s.alloc("apn%d" % i, 512 * 2, BF16) for i in range(4)]
        Gs = [s.alloc("aG%d" % i, 512 * 4) for i in range(3)]
        ocs = [s.alloc("aocs%d" % i, 512 * 4) for i in range(6)]
        tb = [s.alloc("atb%d" % i, 512 * 4) for i in range(4)]
        fb = [s.alloc("afb%d" % i, 512 * 4) for i in range(2)]
        oring = [s.alloc("aor%d" % i, 512 * 2, BF16) for i in range(3)]
        sc_ap, sc_b = s.alloc("asc", 256 * 4)
        m16, m16_b = s.alloc("am16", 4 * 16 * 4)
        wk, wk_b = s.alloc("awk", 256 * 4)
        selb, selb_b = s.alloc("aselb", 256 * 2, BF16)
        selbT, selbT_b = s.alloc("aselbT", 512 * 2, BF16)
        psb = [s.psum[i][:, :].bitcast(BF16) for i in range(8)]
        gi_ = 0
        oi = 0
        for g in range(2):
            mk = s.mark()
            kc_ap, kc_b = s.alloc("akc", 256 * 2, BF16)
            s.dma(kc_ap, d["kcT"][g * 128:(g + 1) * 128, :], kc_b, reads=self.rd("kcT"), writes=[kc_b])
            vc_ap, vc_b = s.alloc("avc", 256 * 2, BF16)
            s.dma(sub3(vc_ap, 0, 128, 2, 1, 128), dview(d["vc"], g * 256, 256, 0, 128), vc_b, reads=self.rd("vc"), writes=[vc_b])
            ks_ap, ks_b = s.alloc("aks", SV * 2, BF16)
            kw_ap, kw_b = s.alloc("akw", SV * 2, BF16)
            s.dma(ks_ap, d["kslcT"][g * 128:(g + 1) * 128, :], ks_b, reads=[self.db("kslcT", i) for i in range(8)], writes=[ks_b])
            s.dma(kw_ap, d["kwinT"][g * 128:(g + 1) * 128, :], kw_b, reads=[self.db("kwinT", i) for i in range(8)], writes=[kw_b])
            vs_ap, vs_b = s.alloc("avs", SV * 2, BF16)
            vw_ap, vw_b = s.alloc("avw", SV * 2, BF16)
            for q in range(4):
                s.dma(sub3(vs_ap, q * 8 * 128, 128, 8, 1, 128), dview(d["vsw"], q * 1024, 1024, g * 128, 128), vs_b,
                      reads=[self.db("vsw", i) for i in range(8)], pwrites=[vs_b])
                s.dma(sub3(vw_ap, q * 8 * 128, 128, 8, 1, 128), dview(d["vsw"], q * 1024, 1024, 256 + g * 128, 128), vw_b,
                      reads=[self.db("vsw", i) for i in range(8)], pwrites=[vw_b])
            q_ap, q_b = s.alloc("aq", 6 * SO * 2, BF16)
            for p in range(6):
                s.dma(q_ap[:, p * SO:(p + 1) * SO], d["qT"][(g * 6 + p) * 128:(g * 6 + p + 1) * 128, :], q_b,
                      reads=[self.db("qT", i) for i in range(4)], pwrites=[q_b])
            for qt in range(4):
                t0v = SO + qt * 512
                njt = (t0v + 512) // 128
                bI = 5
                for p in range(6):
                    qr = q_ap[:, p * SO + qt * 512:p * SO + qt * 512 + 512]
                    pts = []
                    for ct in range(2):
                        pt, ptb = self.attn_unit(ctx, kc_ap[:, ct * 128:(ct + 1) * 128], [kc_b], qr, [q_b],
                                                 [(self.ident, mcmp[:, (qt * 2 + ct) * 512:(qt * 2 + ct + 1) * 512], [self.ident_b, mcmp_b])],
                                                 None, [], None, 3, ct == 0, ct == 1)
                        pts.append((pt, ptb))
                    r, rb = self.recip_den(ctx, 3)
                    pns = []
                    for ct in range(2):
                        pa, pb = pn[(p * 2 + ct) % 4]
                        s.add("pool", lambda e, o=pa, a=pts[ct][0], b=r: e.tensor_tensor(o, a, b, ALU.mult),
                              reads=[pts[ct][1], rb], writes=[pb])
                        pns.append((pa, pb))
                    for ct in range(2):
                        s.add("pe", lambda e, o=s.psum[4][:, :], l=vc_ap[:, ct * 128:(ct + 1) * 128], r_=pns[ct][0], st=(ct == 0), sp=(ct == 1):
                              e.matmul(o, l, r_, start=st, stop=sp), reads=[vc_b, pns[ct][1]], writes=[s.psbuf[4]])
                    for ct in range(2):
                        if p == 0:
                            s.add("pool", lambda e, o=ph[ct][0], a=pns[ct][0]: e.tensor_copy(o, a), reads=[pns[ct][1]], writes=[ph[ct][1]])
                        else:
                            s.add("pool", lambda e, o=ph[ct][0], a=pns[ct][0]: e.tensor_tensor(o, o, a, ALU.add),
                                  reads=[pns[ct][1], ph[ct][1]], writes=[ph[ct][1]])
                    hh = g * 6 + p
                    G, Gb = Gs[gi_ % 3]
                    gi_ += 1
                    bG = 6 + (gi_ % 2)
                    s.add("pe", lambda e, o=s.psum[bG][:, :], l=selmat[0:36, (hh * 3) * 128:(hh * 3 + 1) * 128], r_=gat[0:36, qt * 512:(qt + 1) * 512]:
                          e.matmul(o, l, r_, start=True, stop=True), reads=[selmat_b, gat_b], writes=[s.psbuf[bG]])
                    s.add("act", lambda e, o=G, i=s.psum[bG][:, :]: e.copy(o, i), reads=[s.psbuf[bG]], writes=[Gb])
                    s.add("dve", lambda e, o=ocs[p][0], a=s.psum[4][:, :], b=G: e.tensor_tensor(o, a, b, ALU.mult),
                          reads=[s.psbuf[4], Gb], writes=[ocs[p][1]])
                for qs in range(4):
                    for ct in range(2):
                        s.add("pe", lambda e, o=s.psum[bI][:, qs * 64:(qs + 1) * 64], l=ph[ct][0][:, qs * 128:(qs + 1) * 128],
                              r_=mmap[:, ct * 64:(ct + 1) * 64], st=(ct == 0), sp=(ct == 1):
                              e.matmul(o, l, r_, start=st, stop=sp), reads=[ph[ct][1], mmap_b], writes=[s.psbuf[bI]])
                s.add("dve", lambda e, o=sc_ap, a=s.psum[bI][:, 0:256], b=selM[:, qt * 256:(qt + 1) * 256]: e.tensor_tensor(o, a, b, ALU.mult),
                      reads=[s.psbuf[bI], selM_b], writes=[sc_b])
                s.add("dve", lambda e, o=sc_ap, b=selA[:, qt * 256:(qt + 1) * 256]: e.tensor_tensor(o, o, b, ALU.add),
                      reads=[sc_b, selA_b], writes=[sc_b])
                for qs in range(4):
                    scq = sc_ap[:, qs * 64:(qs + 1) * 64]
                    mm = m16[:, qs * 16:(qs + 1) * 16]
                    s.add("dve", lambda e, o=mm[:, 0:8], i=scq: e.max(o, i), reads=[sc_b], pwrites=[m16_b])
                    s.add("dve", lambda e, o=wk[:, qs * 64:(qs + 1) * 64], m=mm[:, 0:8], i=scq: e.match_replace(o, m, i, -3e9),
                          reads=[sc_b, m16_b], pwrites=[wk_b])
                    s.add("dve", lambda e, o=mm[:, 8:16], i=wk[:, qs * 64:(qs + 1) * 64]: e.max(o, i), reads=[wk_b, m16_b], pwrites=[m16_b])
                    s.add("dve", lambda e, o=mm[:, 15:16]: e.tensor_scalar(o, o, -5e8, None, ALU.max), reads=[m16_b], pwrites=[m16_b])
                    s.add("dve", lambda e, o=selb[:, qs * 64:(qs + 1) * 64], i=scq, t=mm[:, 15:16]: e.tensor_scalar(o, i, t, NEG, ALU.is_lt, ALU.mult),
                          reads=[sc_b, m16_b], pwrites=[selb_b])
                for qs in range(4):
                    s.add("pe", lambda e, o=psb[7][0:64, qs * 128:(qs + 1) * 128], i=selb[:, qs * 64:(qs + 1) * 64]: e.transpose(o, i, self.ident),
                          reads=[selb_b, self.ident_b], writes=[s.psbuf[7]])
                s.add("act", lambda e, o=selbT[0:64, :], i=psb[7][0:64, 0:512]: e.copy(o, i), reads=[s.psbuf[7]], writes=[selbT_b])
                for p in range(6):
                    qr = q_ap[:, p * SO + qt * 512:p * SO + qt * 512 + 512]
                    hh = g * 6 + p
                    units = []
                    for jt in range(njt):
                        ex = [(E[0:64, jt * 128:(jt + 1) * 128], selbT[0:64, :], [E_b, selbT_b])]
                        o_ = jt * 128 - t0v
                        if o_ >= 0:
                            mi = (o_ + 512) // 128
                            ex.append((self.ident, mwin[:, mi * 512:(mi + 1) * 512], [self.ident_b, mwin_b]))
                        units.append(dict(klhs=ks_ap[:, jt * 128:(jt + 1) * 128], krd=[ks_b], qrhs=qr, qrd=[q_b], extras=ex,
                                          vlhs=vs_ap[:, jt * 128:(jt + 1) * 128], vrd=[vs_b], bacc=3, bden=4, first=(jt == 0), last=(jt == njt - 1)))
                    jt0 = (t0v - 512) // 128
                    for jt in range(jt0, njt):
                        o_ = jt * 128 - t0v
                        mi = (o_ + 512) // 128
                        if qt == 0 and o_ < 0:
                            mt_ap, mt_b = mwin0[:, mi * 512:(mi + 1) * 512], mwin0_b
                        else:
                            mt_ap, mt_b = mwin[:, mi * 512:(mi + 1) * 512], mwin_b
                        units.append(dict(klhs=kw_ap[:, jt * 128:(jt + 1) * 128], krd=[kw_b], qrhs=qr, qrd=[q_b],
                                          extras=[(self.ident, mt_ap, [self.ident_b, mt_b])],
                                          vlhs=vw_ap[:, jt * 128:(jt + 1) * 128], vrd=[vw_b], bacc=5, bden=6, first=(jt == jt0), last=(jt == njt - 1)))
                    self.run_units(ctx, units)
                    ts = []
                    for bi, (bacc, bden) in enumerate(((3, 4), (5, 6))):
                        G, Gb = Gs[gi_ % 3]
                        gi_ += 1
                        s.add("pe", lambda e, o=s.psum[7][:, :], l=selmat[0:36, (hh * 3 + 1 + bi) * 128:(hh * 3 + 2 + bi) * 128],
                              r_=gat[0:36, qt * 512:(qt + 1) * 512]: e.matmul(o, l, r_, start=True, stop=True),
                              reads=[selmat_b, gat_b], writes=[s.psbuf[7]])
                        s.add("act", lambda e, o=G, i=s.psum[7][:, :]: e.copy(o, i), reads=[s.psbuf[7]], writes=[Gb])
                        r, rb = self.recip_den(ctx, bden)
                        f, fbb = fb[bi]
                        s.add("pool", lambda e, o=f, a=r, b=G: e.tensor_tensor(o, a, b, ALU.mult), reads=[rb, Gb], writes=[fbb])
                        t, tbb = tb[(oi * 2 + bi) % 4]
                        s.add("dve", lambda e, o=t, a=s.psum[bacc][:, :], b=f: e.tensor_tensor(o, a, b, ALU.mult),
                              reads=[s.psbuf[bacc], fbb], writes=[tbb])
                        ts.append((t, tbb))
                    o_ap, o_b = oring[oi % 3]
                    oi += 1
                    if "dbg_br" in d:
                        for bi_, (ap_, b_) in enumerate((ocs[p], ts[0], ts[1])):
                            s.dma(d["dbg_br"][bi_ * 1536 + hh * 128:bi_ * 1536 + (hh + 1) * 128, qt * 512:(qt + 1) * 512], ap_, b_, reads=[b_])
                    s.add("pool", lambda e, o=ts[0][0], a=ts[0][0], b=ocs[p][0]: e.tensor_tensor(o, a, b, ALU.add),
                          reads=[ocs[p][1], ts[0][1]], writes=[ts[0][1]])
                    s.add("pool", lambda e, o=o_ap, a=ts[0][0], b=ts[1][0]: e.tensor_tensor(o, a, b, ALU.add),
                          reads=[ts[0][1], ts[1][1]], writes=[o_b])
                    s.dma(d["oT"][hh * 128:(hh + 1) * 128, qt * 512:(qt + 1) * 512], o_ap, o_b, reads=[o_b], pwrites=[self.db("oT", qt)])
            s.release(mk)
        s.release(mk0)

    def stage_kvshared(self, w_kv, w_kv_rot):
        kb = self
        d = self.d
        panels = []
        for hp in range(2):
            segs = [(w_kv, hp * 256, 256), (w_kv_rot, hp * 256, 256)]
            jobs = [dict(cols=[(j * 128, 128), (256 + j * 128, 128)], dst=d["kshT"], dname="kshT", row0=(hp * 2 + j) * 128,
                         epi=self._rope_epi_own()) for j in range(2)]
            panels.append(dict(segs=segs, jobs=jobs))
        self.stage_lfm(d["kvnT"], "kvnT", 0, SO, 16, panels, self.rope_setup(SO, SO))
        self.stage_tm_bf16(d["kvnT"], "kvnT", 16, 0, SO, [(w_kv, 512, 512)], d["vsh"], "vsh")


NG = 8


def build_phase_a(debug=()):
    nc = bass.Bass("TRN2", target_bir_lowering=False)
    kb = KB(nc)
    I = kb.inp
    xv = I("xv", [SV, DM])
    memb = I("memb", [256, DM])
    I("cosT", [128, SV]); I("sinT", [128, SV]); I("gains", [128, NG * 16])
    I("c_ident", [128, 128]); I("c_ones", [128, 128])
    I("m_cmp", [128, 8 * 512]); I("m_win", [128, 8 * 512]); I("m_win0", [128, 4 * 512])
    I("c_E", [64, SV]); I("c_mmap", [128, 128]); I("selM", [128, 1024]); I("selA", [128, 1024]); I("c_selmat", [36, 36 * 128])
    w_in = I("a_w_in", [DM, 3620]); w_rot = I("a_w_rot", [DM, 2304]); gbias = I("a_gbias", [36, 1])
    w1k = I("a_w1k", [4096, 256]); w2k = I("a_w2k", [256, 128]); pek = I("a_pekT", [128, 32])
    w1v = I("a_w1v", [4096, 256]); w2v = I("a_w2v", [256, 128]); pev = I("a_pevT", [128, 32])
    wmkv = I("a_w_mem_kv", [DM, 1024]); wout = I("a_w_out", [DM, DM])
    wg = I("a_w_gate", [DM, DFF]); wu = I("a_w_up", [DM, DFF]); wd = I("a_w_down", [DFF, DM])
    wkv = I("w_kv", [DM, 1024]); wkvr = I("w_kv_rot", [DM, 512])

    def S(name, shape, dt):
        if name in debug:
            return kb.outp(name, shape, dt)
        return kb.scr(name, shape, dt)
    S("xnT", [DM, SV], BF16)
    S("qT", [1536, SO], BF16); S("kcmpT", [256, SV], BF16); S("vcmpT", [256, SV], BF16)
    S("kslcT", [256, SV], BF16); S("kwinT", [256, SV], BF16); S("vsw", [SV, 512], BF16)
    S("gatesT", [36, SO], F32); S("qmT", [512, SO], BF16)
    S("kcT", [256, 256], BF16); S("vc", [512, 128], BF16)
    S("mkT", [512, 256], BF16); S("mv", [256, 512], BF16)
    S("oT", [DM, SO], BF16); S("h1", [SO, DM], F32); S("hnT", [DM, SO], BF16); S("hidT", [DFF, SO], BF16)
    kb.outp("h2", [SO, DM], F32)
    S("kvnT", [DM, SO], BF16)
    kb.outp("kshT", [512, SO], BF16); kb.outp("vsh", [SO, 512], BF16)
    if "dbg_br" in debug:
        kb.outp("dbg_br", [3 * 1536, SO], F32)
    d = kb.d
    kb.consts(NG)
    stop = kb.stop_after if hasattr(kb, "stop_after") else None
    kb.stage_norm(xv, None, SV, [0], [(d["xnT"], "xnT", 0)])
    kb.stage_inproj_a(w_in, w_rot, gbias)
    kb.stage_tm_bf16(d["xnT"], "xnT", 16, 0, SV, [(w_in, 2304, 256), (w_in, 2816, 256)], d["vsw"], "vsw")
    kb.stage_cmp(w1k, w2k, pek, w1v, w2v, pev)
    kb.stage_memkv(memb, 1, wmkv)
    kb.stage_attn_a()
    kb.stage_mem_attn(d["qmT"], "qmT", d["mkT"], d["mv"], d["oT"], "oT", 12)
    kb.stage_down(d["oT"], "oT", 16, wout, xv[SO:SV, :], None, d["h1"], "h1", TB=2048)
    kb.stage_norm(d["h1"], "h1", SO, [2], [(d["hnT"], "hnT", 0)])
    kb.stage_ffn(d["hnT"], "hnT", wg, wu, wd, d["h1"], "h1", d["h2"], "h2")
    kb.stage_norm(d["h2"], "h2", SO, [3], [(d["kvnT"], "kvnT", 0)])
    kb.stage_kvshared(wkv, wkvr)
    kb.s.finalize()
    return nc, kb


def rope_tabs(half):
    inv = (1.0 / (10000.0 ** (np.arange(0, 128, 2, dtype=np.float32) / 128))).astype(np.float32)
    pos = np.arange(SV, dtype=np.float32) - (0 if half == 1 else SO)
    pos = np.maximum(pos, 0).astype(np.float32)
    ang = (pos[:, None] * inv[None, :]).astype(np.float32)
    c = np.cos(ang).astype(np.float32).T
    sn = np.sin(ang).astype(np.float32).T
    cosT = np.concatenate([c, c], 0)
    sinT = np.concatenate([-sn, sn], 0)
    return np.ascontiguousarray(cosT), np.ascontiguousarray(sinT)


def rot_cols(w, heads):
    outs = []
    for c0 in heads:
        outs.append(w[:, c0 + 64:c0 + 128])
        outs.append(w[:, c0:c0 + 64])
    return np.ascontiguousarray(np.concatenate(outs, 1))


def gain_arr(gs):
    return np.ascontiguousarray(np.concatenate([g.reshape(16, 128).T for g in gs], 1).astype(np.float32))


def band_mask(o, w, prevmask):
    jj = np.arange(128)[:, None]
    qq = np.arange(512)[None, :]
    dist = qq - jj - o
    m = np.where((dist >= 0) & (dist <= w), 0.0, NEG).astype(np.float32)
    if prevmask:
        m[:] = NEG
    return m


def attn_consts(half):
    out = {}
    m_cmp = np.zeros((128, 8, 512), np.float32)
    for qt in range(4):
        for ct in range(2):
            c = ct * 128 + np.arange(128)[:, None]
            t = SO + qt * 512 + np.arange(512)[None, :]
            valid = (16 * c + 31 <= t) & (c <= 254)
            if half == 0:
                valid &= (c >= 128)
            m_cmp[:, qt * 2 + ct, :] = np.where(valid, 0.0, NEG)
    out["m_cmp"] = m_cmp.reshape(128, -1)
    mw = np.zeros((128, 8, 512), np.float32)
    for mi in range(8):
        mw[:, mi, :] = band_mask(mi * 128 - 512, 511, False)
    out["m_win"] = mw.reshape(128, -1)
    mw0 = np.zeros((128, 4, 512), np.float32)
    for mi in range(4):
        mw0[:, mi, :] = band_mask(mi * 128 - 512, 511, half == 0)
    out["m_win0"] = mw0.reshape(128, -1)
    selM = np.zeros((128, 4, 4, 64), np.float32)
    selA = np.zeros((128, 4, 4, 64), np.float32)
    sblk = np.arange(64)[None, :]
    first = 0 if half == 1 else 32
    for qt in range(4):
        for qs in range(4):
            t = SO + qt * 512 + qs * 128 + np.arange(128)[:, None]
            cur = t // 64
            elig = (sblk * 64 <= t) & (sblk >= first)
            f0 = (sblk == first) & elig
            f1 = (sblk == cur)
            f2 = (sblk == cur - 1) & (sblk >= first)
            A = np.where(elig, 0.0, -1e9)
            A = np.where(f0, 1e9, A)
            A = np.where(f2, 2e9, A)
            A = np.where(f1, 3e9, A)
            M = (elig & ~f0 & ~f1 & ~f2).astype(np.float32)
            selM[:, qt, qs, :] = M
            selA[:, qt, qs, :] = A
    out["selM"] = selM.reshape(128, -1)
    out["selA"] = selA.reshape(128, -1)
    return out


def shared_consts():
    out = {}
    out["c_ident"] = np.eye(128, dtype=np.float32)
    out["c_ones"] = np.ones((128, 128), np.float32)
    E = np.zeros((64, SV), np.float32)
    E[np.arange(SV) // 64, np.arange(SV)] = 1.0
    out["c_E"] = E
    mm = np.zeros((2, 128, 64), np.float32)
    for c in range(255):
        for sb in range(64):
            if (16 * c < 64 * sb + 64) and (16 * c + 32 > 64 * sb):
                mm[c // 128, c % 128, sb] = 1.0
    out["c_mmap"] = np.ascontiguousarray(mm.transpose(1, 0, 2).reshape(128, 128))
    sm = np.zeros((36, 36, 128), np.float32)
    for i in range(36):
        sm[i, i, :] = 1.0
    out["c_selmat"] = sm.reshape(36, -1)
    return out


def phase_a_inputs(inp, b, half, sc):
    f = np.float32
    x = inp["x"][b]
    if half == 1:
        xv = x
    else:
        xv = np.concatenate([np.zeros((SO, DM), f), x[:SO]], 0)
    cosT, sinT = rope_tabs(half)
    m = dict(sc)
    m.update(attn_consts(half))
    m["xv"] = np.ascontiguousarray(xv)
    m["memb"] = np.ascontiguousarray(inp["mem"][b])
    m["cosT"] = cosT
    m["sinT"] = sinT
    return m


def weights_a(inp):
    w = {}
    w_in = inp["a_w_in"][0]
    w["a_w_in"] = w_in
    heads = [h * 128 for h in range(12)] + [1536 + (i * 2 + g) * 128 for i in (0, 2, 4) for g in range(2)]
    w["a_w_rot"] = rot_cols(w_in, heads)
    w["a_gbias"] = np.ascontiguousarray(inp["a_gate_bias"][0].reshape(36, 1))
    w["a_w1k"] = inp["a_cmp_w1_k"][0]; w["a_w2k"] = inp["a_cmp_w2_k"][0]
    w["a_pekT"] = np.ascontiguousarray(inp["a_cmp_pe_k"][0].T)
    w["a_w1v"] = inp["a_cmp_w1_v"][0]; w["a_w2v"] = inp["a_cmp_w2_v"][0]
    w["a_pevT"] = np.ascontiguousarray(inp["a_cmp_pe_v"][0].T)
    w["a_w_mem_kv"] = inp["a_w_mem_kv"][0]; w["a_w_out"] = inp["a_w_out"][0]
    w["a_w_gate"] = inp["a_w_gate"][0]; w["a_w_up"] = inp["a_w_up"][0]; w["a_w_down"] = inp["a_w_down"][0]
    w["w_kv"] = inp["w_kv_shared"]
    w["w_kv_rot"] = rot_cols(inp["w_kv_shared"], [h * 128 for h in range(4)])
    w["gains"] = gain_arr([inp["a_norm_attn"][0], inp["a_norm_mem"][0], inp["a_norm_ffn"][0], inp["kv_norm"],
                           inp["b_norm_attn"][0], inp["b_norm_mem"][0], inp["b_norm_ffn"][0], inp["final_norm"]])
    return {k: np.ascontiguousarray(np.asarray(v, dtype=np.float32)) for k, v in w.items()}


def _kb_stage_attn_b(self):
    s = self.s
    d = self.d
    mk0 = s.mark()
    mdil, mdil_b = s.alloc("mdil", 5 * 512 * 2, BF16)
    s.dma(mdil, d["m_dil"], mdil_b, writes=[mdil_b], q="pool")
    mdil0, mdil0_b = s.alloc("mdil0", 512 * 2, BF16)
    s.dma(mdil0, d["m_dil0"], mdil0_b, writes=[mdil0_b], q="pool")
    ctx = dict(si=0, pi=0, ri=0, sbanks=[0, 1, 2], pt=[s.alloc("bpt%d" % i, 512 * 2, BF16) for i in range(4)],
               rd=[s.alloc("brd%d" % i, 512 * 4) for i in range(2)])
    oring = [s.alloc("bor%d" % i, 512 * 2, BF16) for i in range(3)]
    oi = 0
    ui = 0
    for hh in range(4):
        mk = s.mark()
        k_ap, k_b = s.alloc("bk", SV * 2, BF16)
        kown = [self.db("kshT", i) for i in range(4)]
        vown = [self.db("vsh", i) for i in range(4)]
        s.dma(k_ap[:, 0:SO], d["kg"][hh * 128:(hh + 1) * 128, :], k_b, reads=self.rd("kg"), pwrites=[k_b])
        s.dma(k_ap[:, SO:SV], d["kshT"][hh * 128:(hh + 1) * 128, :], k_b, reads=kown, pwrites=[k_b])
        v1, v1_b = s.alloc("bv1", SV * 2, BF16)
        v4, v4_b = s.alloc("bv4", SV * 2, BF16)
        v16, v16_b = s.alloc("bv16", SV * 2, BF16)
        for pi_, (vsrc, rds) in enumerate(((d["vg"][0:SO, :], self.rd("vg")), (d["vsh"], vown))):
            vcol = vsrc[:, hh * 128:(hh + 1) * 128]
            for q in range(2):
                s.dma(sub3(v1, (pi_ * 16 + q * 8) * 128, 128, 8, 1, 128), dview(vsrc, q * 1024, 1024, hh * 128, 128), v1_b,
                      reads=rds, pwrites=[v1_b])
            r4 = vcol.rearrange("(jt p r) c -> r p jt c", p=128, r=4)
            for rho in range(4):
                s.dma(sub3(v4, rho * 1024 + pi_ * 4 * 128, 128, 4, 1, 128), r4[rho], v4_b, reads=rds, pwrites=[v4_b])
            r16 = vcol.rearrange("(jt p r) c -> r p jt c", p=128, r=16)
            for rho in range(16):
                s.dma(sub3(v16, rho * 256 + pi_ * 128, 128, 1, 1, 128), r16[rho], v16_b, reads=rds, pwrites=[v16_b])
        q_ap, q_b = s.alloc("bq", 3 * SO * 2, BF16)
        for g in range(3):
            s.dma(q_ap[:, g * SO:(g + 1) * SO], d["qbT"][(g * 4 + hh) * 128:(g * 4 + hh + 1) * 128, :], q_b,
                  reads=[self.db("qbT", i) for i in range(4)], pwrites=[q_b])
        accS, accS_b = s.alloc("bacc", SO * 4)
        denS, denS_b = s.alloc("bden", SO * 4)

        def flush(bacc, bden, n, oa, od, first):
            if first:
                s.add("dve", lambda e, o=oa, i=s.psum[bacc][:, 0:n]: e.tensor_copy(o, i), reads=[s.psbuf[bacc]], pwrites=[accS_b])
                s.add("dve", lambda e, o=od, i=s.psum[bden][:, 0:n]: e.tensor_copy(o, i), reads=[s.psbuf[bden]], pwrites=[denS_b])
            else:
                s.add("dve", lambda e, o=oa, i=s.psum[bacc][:, 0:n]: e.tensor_tensor(o, o, i, ALU.add), reads=[s.psbuf[bacc], accS_b], pwrites=[accS_b])
                s.add("dve", lambda e, o=od, i=s.psum[bden][:, 0:n]: e.tensor_tensor(o, o, i, ALU.add), reads=[s.psbuf[bden], denS_b], pwrites=[denS_b])

        units = []

        def mkpost(bacc, bden, n, oa, od, first):
            return lambda: flush(bacc, bden, n, oa, od, first)
        for qt in range(4):
            t0v = SO + qt * 512
            bacc, bden = (3, 4) if (ui % 2 == 0) else (5, 6)
            ui += 1
            offs = [-128, 0, 128, 256, 384]
            for i, o_ in enumerate(offs):
                jt = (t0v + o_) // 128
                if qt == 0 and o_ < 0:
                    m_ap, m_b = mdil0, mdil0_b
                else:
                    m_ap, m_b = mdil[:, i * 512:(i + 1) * 512], mdil_b
                units.append(dict(klhs=k_ap[:, jt * 128:(jt + 1) * 128], krd=[k_b], qrhs=q_ap[:, qt * 512:(qt + 1) * 512], qrd=[q_b],
                                  extras=[(self.ident, m_ap, [self.ident_b, m_b])], vlhs=v1[:, jt * 128:(jt + 1) * 128], vrd=[v1_b],
                                  bacc=bacc, bden=bden, first=(i == 0), last=(i == 4),
                                  post=(mkpost(bacc, bden, 512, accS[:, qt * 512:(qt + 1) * 512], denS[:, qt * 512:(qt + 1) * 512], True) if i == 4 else None)))
        for rho in range(4):
            bacc, bden = (3, 4) if (ui % 2 == 0) else (5, 6)
            ui += 1
            qr = q_ap[:, SO + rho:SO + SO:4]
            offs = [-128, 0, 128, 256, 384]
            for i, o_ in enumerate(offs):
                ju0 = 512 + o_
                if o_ < 0:
                    m_ap, m_b = mdil0, mdil0_b
                else:
                    m_ap, m_b = mdil[:, i * 512:(i + 1) * 512], mdil_b
                kl = k_ap[:, rho + 4 * ju0:rho + 4 * (ju0 + 127) + 1:4]
                jtu = ju0 // 128
                units.append(dict(klhs=kl, krd=[k_b], qrhs=qr, qrd=[q_b], extras=[(self.ident, m_ap, [self.ident_b, m_b])],
                                  vlhs=v4[:, rho * 1024 + jtu * 128:rho * 1024 + (jtu + 1) * 128], vrd=[v4_b],
                                  bacc=bacc, bden=bden, first=(i == 0), last=(i == 4),
                                  post=(mkpost(bacc, bden, 512, accS[:, rho:SO:4], denS[:, rho:SO:4], False) if i == 4 else None)))
        for rho in range(16):
            bacc, bden = (3, 4) if (ui % 2 == 0) else (5, 6)
            ui += 1
            qr = q_ap[:, 2 * SO + rho:3 * SO:16]
            for i, o_ in enumerate([-128, 0]):
                ju0 = 128 + o_
                if o_ < 0:
                    m_ap, m_b = mdil0[:, 0:128], mdil0_b
                else:
                    m_ap, m_b = mdil[:, 512:512 + 128], mdil_b
                kl = k_ap[:, rho + 16 * ju0:rho + 16 * (ju0 + 127) + 1:16]
                jtu = ju0 // 128
                units.append(dict(klhs=kl, krd=[k_b], qrhs=qr, qrd=[q_b], extras=[(self.ident, m_ap, [self.ident_b, m_b])],
                                  vlhs=v16[:, rho * 256 + jtu * 128:rho * 256 + (jtu + 1) * 128], vrd=[v16_b],
                                  bacc=bacc, bden=bden, first=(i == 0), last=(i == 1), n=128,
                                  post=(mkpost(bacc, bden, 128, accS[:, rho:SO:16], denS[:, rho:SO:16], False) if i == 1 else None)))
        self.run_units(ctx, units)
        s.add("dve", lambda e: e.tensor_scalar(denS, denS, TINY, None, ALU.max), reads=[denS_b], writes=[denS_b])
        s.add("dve", lambda e: e.reciprocal(denS, denS), reads=[denS_b], writes=[denS_b])
        for qt in range(4):
            o_ap, o_b = oring[oi % 3]
            oi += 1
            s.add("dve", lambda e, o=o_ap, a=accS[:, qt * 512:(qt + 1) * 512], b=denS[:, qt * 512:(qt + 1) * 512]: e.tensor_tensor(o, a, b, ALU.mult),
                  reads=[accS_b, denS_b], writes=[o_b])
            s.dma(d["oT"][hh * 128:(hh + 1) * 128, qt * 512:(qt + 1) * 512], o_ap, o_b, reads=[o_b], pwrites=[self.db("oT", qt)])
        s.release(mk)
    s.release(mk0)


KB.stage_attn_b = _kb_stage_attn_b


def _kb_stage_inproj_b(self, w_in, w_rot):
    kb = self
    d = self.d
    panels = []
    for hp in range(6):
        segs = [(w_in, hp * 256, 256), (w_rot, hp * 256, 256)]
        jobs = [dict(cols=[(j * 128, 128), (256 + j * 128, 128)], dst=d["qbT"], dname="qbT", row0=(hp * 2 + j) * 128,
                     epi=self._rope_epi_own()) for j in range(2)]
        panels.append(dict(segs=segs, jobs=jobs))
    segs = [(w_in, 1536, 512)]
    jobs = [dict(cols=[(j * 128, 128)], row0=j * 128, epi=kb.epi_plain_fm(d["qmT"], "qmT", None, 0)) for j in range(4)]
    panels.append(dict(segs=segs, jobs=jobs))
    self.stage_lfm(d["bnT"], "bnT", 0, SO, 16, panels, self.rope_setup(SO, SO))


KB.stage_inproj_b = _kb_stage_inproj_b


def _kb_stage_final_norm(self, src, srcname, dst):
    s = self.s
    mk = s.mark()
    fg, fg_b = s.alloc("fg", DM * 4)
    s.dma(fg, self.d["fgain"], fg_b, writes=[fg_b])
    hb = [s.alloc("fh%d" % i, DM * 4) for i in range(3)]
    ob = [s.alloc("fo%d" % i, DM * 4) for i in range(2)]
    junk_ap, junk_b = s.alloc("fjunk", DM * 2, BF16)
    st = [s.alloc("fst%d" % i, 4 * 4) for i in range(3)]
    for it in range(SO // 128):
        h_ap, h_b = hb[it % 3]
        st_ap, st_b = st[it % 3]
        o_ap, o_b = ob[it % 2]
        s.dma(h_ap, src[it * 128:(it + 1) * 128, :], h_b, reads=self.rd(srcname, it // 4), writes=[h_b])
        s.add("act", lambda e, h=h_ap, o=st_ap[:, 0:1]: e.activation(junk_ap, h, AF.Square, accum_out=o), reads=[h_b], pwrites=[junk_b, st_b])
        s.add("dve", lambda e, a=st_ap: e.tensor_scalar(a[:, 1:2], a[:, 0:1], 1.0 / 2048, EPS, ALU.mult, ALU.add), reads=[st_b], pwrites=[st_b])
        s.add("act", lambda e, a=st_ap: e.sqrt(a[:, 1:2], a[:, 1:2]), reads=[st_b], pwrites=[st_b])
        s.add("dve", lambda e, a=st_ap: e.reciprocal(a[:, 2:3], a[:, 1:2]), reads=[st_b], pwrites=[st_b])
        s.add("act", lambda e, o=o_ap, h=h_ap, sc=st_ap[:, 2:3]: e.activation(o, h, AF.Copy, scale=sc), reads=[h_b, st_b], writes=[o_b])
        s.add("dve", lambda e, o=o_ap: e.tensor_tensor(o, o, fg, ALU.mult), reads=[o_b, fg_b], writes=[o_b])
        s.dma(dst[it * 128:(it + 1) * 128, :], o_ap, o_b, reads=[o_b])
    s.release(mk)


KB.stage_final_norm = _kb_stage_final_norm


def build_phase_b(debug=()):
    nc = bass.Bass("TRN2", target_bir_lowering=False)
    kb = KB(nc)
    I = kb.inp
    h2 = I("h2in", [SO, DM])
    memb = I("memb", [256, DM])
    I("kshTv", [512, SV], BF16); I("vshv", [SV, 512], BF16)
    I("cosT", [128, SV]); I("sinT", [128, SV]); I("gains", [128, NG * 16]); I("fgain", [128, DM])
    I("c_ident", [128, 128]); I("c_ones", [128, 128])
    I("m_dil", [128, 5 * 512]); I("m_dil0", [128, 512])
    w_in = I("b_w_in", [DM, 2048]); w_rot = I("b_w_rot", [DM, 1536])
    wmkv = I("b_w_mem_kv", [DM, 1024]); wout = I("b_w_out", [1024, DM])
    wg = I("b_w_gate", [DM, DFF]); wu = I("b_w_up", [DM, DFF]); wd = I("b_w_down", [DFF, DM])

    def S(name, shape, dt):
        if name in debug:
            return kb.outp(name, shape, dt)
        return kb.scr(name, shape, dt)
    S("bnT", [DM, SO], BF16); S("qbT", [1536, SO], BF16); S("qmT", [512, SO], BF16)
    S("mkT", [512, 256], BF16); S("mv", [256, 512], BF16)
    S("oT", [1024, SO], BF16); S("h3", [SO, DM], F32); S("hnT", [DM, SO], BF16); S("hidT", [DFF, SO], BF16)
    S("h4", [SO, DM], F32)
    kb.outp("out", [SO, DM], F32)
    d = kb.d
    kb.consts(NG)
    kb.stage_norm(h2, None, SO, [4], [(d["bnT"], "bnT", 0)])
    kb.stage_inproj_b(w_in, w_rot)
    kb.stage_memkv(memb, 5, wmkv)
    kb.stage_attn_b()
    kb.stage_mem_attn(d["qmT"], "qmT", d["mkT"], d["mv"], d["oT"], "oT", 4)
    kb.stage_down(d["oT"], "oT", 8, wout, h2, None, d["h3"], "h3", TB=2048)
    kb.stage_norm(d["h3"], "h3", SO, [6], [(d["hnT"], "hnT", 0)])
    kb.stage_ffn(d["hnT"], "hnT", wg, wu, wd, d["h3"], "h3", d["h4"], "h4")
    kb.stage_final_norm(d["h4"], "h4", d["out"])
    kb.s.finalize()
    return nc, kb


def dil_consts(half):
    out = {}
    md = np.zeros((128, 5, 512), np.float32)
    for i, o_ in enumerate([-128, 0, 128, 256, 384]):
        md[:, i, :] = band_mask(o_, 128, False)
    out["m_dil"] = md.reshape(128, -1)
    out["m_dil0"] = band_mask(-128, 128, half == 0)
    return out


def weights_b(inp):
    w = {}
    w_in = inp["b_w_in"][0]
    w["b_w_in"] = w_in
    w["b_w_rot"] = rot_cols(w_in, [h * 128 for h in range(12)])
    w["b_w_mem_kv"] = inp["b_w_mem_kv"][0]; w["b_w_out"] = inp["b_w_out"][0]
    w["b_w_gate"] = inp["b_w_gate"][0]; w["b_w_up"] = inp["b_w_up"][0]; w["b_w_down"] = inp["b_w_down"][0]
    w["fgain"] = np.broadcast_to(inp["final_norm"][None, :], (128, DM))
    return {k: np.ascontiguousarray(np.asarray(v, dtype=np.float32)) for k, v in w.items()}


_PROG = {}


def kernel(**inputs):
    inp = {k: np.asarray(v) for k, v in inputs.items()}
    import ml_dtypes
    bf = ml_dtypes.bfloat16
    if "a" not in _PROG:
        _PROG["a"] = build_phase_a()
        _PROG["b"] = build_phase_b()
    nca, _ = _PROG["a"]
    ncb, _ = _PROG["b"]
    sc = shared_consts()
    wa = weights_a(inp)
    maps = []
    for c in range(8):
        m = phase_a_inputs(inp, c // 2, c % 2, sc)
        m.update(wa)
        maps.append(m)
    ra = run_bass_kernel_spmd(nca, maps, core_ids=list(range(8))).results
    del maps
    wb = weights_b(inp)
    maps = []
    for c in range(8):
        b, half = c // 2, c % 2
        m = {}
        m["h2in"] = np.ascontiguousarray(ra[c]["h2"])
        ksh = np.asarray(ra[c]["kshT"])
        vsh = np.asarray(ra[c]["vsh"])
        if half == 1:
            kprev = np.asarray(ra[c - 1]["kshT"]); vprev = np.asarray(ra[c - 1]["vsh"])
        else:
            kprev = np.zeros_like(ksh); vprev = np.zeros_like(vsh)
        m["kshTv"] = np.ascontiguousarray(np.concatenate([kprev, ksh], 1))
        m["vshv"] = np.ascontiguousarray(np.concatenate([vprev, vsh], 0))
        m["memb"] = np.ascontiguousarray(inp["mem"][b])
        cosT, sinT = rope_tabs(half)
        m["cosT"] = cosT; m["sinT"] = sinT
        m["gains"] = wa["gains"]
        m["c_ident"] = sc["c_ident"]; m["c_ones"] = sc["c_ones"]
        m.update(dil_consts(half))
        m.update(wb)
        maps.append(m)
    rb = run_bass_kernel_spmd(ncb, maps, core_ids=list(range(8))).results
    out = np.zeros((4, 4096, DM), np.float32)
    for c in range(8):
        b, half = c // 2, c % 2
        out[b, half * SO:(half + 1) * SO, :] = rb[c]["out"]
    return out


def build_fused(debug=()):
    nc = bass.Bass("TRN2", target_bir_lowering=False, num_devices=8)
    kb = KB(nc)
    I = kb.inp
    xv = I("xv", [SV, DM])
    memb = I("memb", [256, DM])
    I("cosT", [128, SV]); I("sinT", [128, SV]); I("gains", [128, NG * 16]); I("fgain", [128, DM])
    I("c_ident", [128, 128]); I("c_ones", [128, 128])
    I("m_cmp", [128, 8 * 512]); I("m_win", [128, 8 * 512]); I("m_win0", [128, 4 * 512])
    I("c_E", [64, SV]); I("c_mmap", [128, 128]); I("selM", [128, 1024]); I("selA", [128, 1024]); I("c_selmat", [36, 36 * 128])
    I("m_dil", [128, 5 * 512]); I("m_dil0", [128, 512])
    w_in = I("a_w_in", [DM, 3620]); w_rot = I("a_w_rot", [DM, 2304]); gbias = I("a_gbias", [36, 1])
    w1k = I("a_w1k", [4096, 256]); w2k = I("a_w2k", [256, 128]); pek = I("a_pekT", [128, 32])
    w1v = I("a_w1v", [4096, 256]); w2v = I("a_w2v", [256, 128]); pev = I("a_pevT", [128, 32])
    wmkv = I("a_w_mem_kv", [DM, 1024]); wout = I("a_w_out", [DM, DM])
    wg = I("a_w_gate", [DM, DFF]); wu = I("a_w_up", [DM, DFF]); wd = I("a_w_down", [DFF, DM])
    wkv = I("w_kv", [DM, 1024]); wkvr = I("w_kv_rot", [DM, 512])
    bw_in = I("b_w_in", [DM, 2048]); bw_rot = I("b_w_rot", [DM, 1536])
    bwmkv = I("b_w_mem_kv", [DM, 1024]); bwout = I("b_w_out", [1024, DM])
    bwg = I("b_w_gate", [DM, DFF]); bwu = I("b_w_up", [DM, DFF]); bwd = I("b_w_down", [DFF, DM])
    S = kb.scr
    S("xnT", [DM, SV], BF16)
    S("qT", [1536, SO], BF16); S("kcmpT", [256, SV], BF16); S("vcmpT", [256, SV], BF16)
    S("kslcT", [256, SV], BF16); S("kwinT", [256, SV], BF16); S("vsw", [SV, 512], BF16)
    S("gatesT", [36, SO], F32); S("qmT", [512, SO], BF16)
    S("kcT", [256, 256], BF16); S("vc", [512, 128], BF16)
    S("mkT", [512, 256], BF16); S("mv", [256, 512], BF16)
    S("oT", [DM, SO], BF16); S("h1", [SO, DM], F32); S("hnT", [DM, SO], BF16); S("hidT", [DFF, SO], BF16)
    S("h2", [SO, DM], F32)
    S("kvnT", [DM, SO], BF16); S("bnT", [DM, SO], BF16)
    S("kshT", [512, SO], BF16); S("vsh", [SO, 512], BF16)
    S("kg", [1024, SO], BF16); S("vg", [2 * SO, 512], BF16)
    S("qbT", [1536, SO], BF16); S("h3", [SO, DM], F32); S("h4", [SO, DM], F32)
    kb.outp("out", [SO, DM], F32)
    d = kb.d
    kb.consts(NG)
    kb.stage_norm(xv, None, SV, [0], [(d["xnT"], "xnT", 0)])
    kb.stage_inproj_a(w_in, w_rot, gbias)
    kb.stage_tm_bf16(d["xnT"], "xnT", 16, 0, SV, [(w_in, 2304, 256), (w_in, 2816, 256)], d["vsw"], "vsw")
    kb.stage_cmp(w1k, w2k, pek, w1v, w2v, pev)
    kb.stage_memkv(memb, 1, wmkv)
    kb.stage_attn_a()
    kb.stage_mem_attn(d["qmT"], "qmT", d["mkT"], d["mv"], d["oT"], "oT", 12)
    kb.stage_down(d["oT"], "oT", 16, wout, xv[SO:SV, :], None, d["h1"], "h1", TB=2048)
    kb.stage_norm(d["h1"], "h1", SO, [2], [(d["hnT"], "hnT", 0)])
    kb.stage_ffn(d["hnT"], "hnT", wg, wu, wd, d["h1"], "h1", d["h2"], "h2")
    kb.stage_norm(d["h2"], "h2", SO, [3, 4], [(d["kvnT"], "kvnT", 0), (d["bnT"], "bnT", 0)])
    kb.stage_kvshared(wkv, wkvr)
    groups = [[0, 1], [2, 3], [4, 5], [6, 7]]
    kb.s.collective(lambda e: e.collective_compute("AllGather", ALU.bypass, replica_groups=groups, ins=[d["kshT"]], outs=[d["kg"]]),
                    reads=[kb.db("kshT", i) for i in range(4)], writes=[kb.db("kg")])
    kb.s.collective(lambda e: e.collective_compute("AllGather", ALU.bypass, replica_groups=groups, ins=[d["vsh"]], outs=[d["vg"]]),
                    reads=[kb.db("vsh", i) for i in range(4)], writes=[kb.db("vg")])
    kb.stage_inproj_b(bw_in, bw_rot)
    kb.stage_memkv(memb, 5, bwmkv)
    kb.stage_attn_b()
    kb.stage_mem_attn(d["qmT"], "qmT", d["mkT"], d["mv"], d["oT"], "oT", 4)
    kb.stage_down(d["oT"], "oT", 8, bwout, d["h2"], "h2", d["h3"], "h3", TB=2048)
    kb.stage_norm(d["h3"], "h3", SO, [6], [(d["hnT"], "hnT", 0)])
    kb.stage_ffn(d["hnT"], "hnT", bwg, bwu, bwd, d["h3"], "h3", d["h4"], "h4")
    kb.stage_final_norm(d["h4"], "h4", d["out"])
    kb.s.finalize()
    return nc, kb


def kernel(**inputs):
    inp = {k: np.asarray(v) for k, v in inputs.items()}
    if "f" not in _PROG:
        _PROG["f"] = build_fused()
    nc, _ = _PROG["f"]
    sc = shared_consts()
    wa = weights_a(inp)
    wb = weights_b(inp)
    maps = []
    for c in range(8):
        b, half = c // 2, c % 2
        m = phase_a_inputs(inp, b, half, sc)
        m.update(dil_consts(half))
        m.update(wa)
        m.update(wb)
        maps.append(m)
    res = run_bass_kernel_spmd(nc, maps, core_ids=list(range(8))).results
    out = np.zeros((4, 4096, DM), np.float32)
    for c in range(8):
        b, half = c // 2, c % 2
        out[b, half * SO:(half + 1) * SO, :] = res[c]["out"]
    return out
```

```python
import numpy as np
import concourse.bass as bass
import concourse.mybir as mybir
from concourse.bass_utils import run_bass_kernel_spmd

F32 = mybir.dt.float32
BF16 = mybir.dt.bfloat16
AF = mybir.ActivationFunctionType
ALU = mybir.AluOpType
AX = mybir.AxisListType


ENGS = ("pe", "act", "dve", "pool", "sp")


class Buf:
    __slots__ = ("name", "w", "r", "sem", "ndma", "lo", "hi", "space")

    def __init__(self, name, space="sb", lo=0, hi=0):
        self.name = name
        self.w = []
        self.r = []
        self.sem = None
        self.ndma = 0
        self.lo = lo
        self.hi = hi
        self.space = space


class DSem:
    __slots__ = ("handle", "ndma", "idx", "inc")

    def __init__(self, idx, inc=16):
        self.handle = None
        self.ndma = 0
        self.idx = idx
        self.inc = inc


class Op:
    __slots__ = ("eng", "idx", "fn", "waits", "flagged", "rank", "dma_buf", "pe_group")

    def __init__(self, eng, idx, fn):
        self.eng = eng
        self.idx = idx
        self.fn = fn
        self.waits = {}
        self.flagged = False
        self.rank = 0
        self.dma_buf = None


class Sched:
    def __init__(self, nc, arena_bytes=200 * 1024):
        self.nc = nc
        self.ops = {e: [] for e in ENGS}
        self.waited = {e: {} for e in ENGS}
        self.arena_bytes = arena_bytes
        self.arena = nc.alloc_sbuf_tensor("arena", [128, arena_bytes // 4], F32)
        self.arena_top = 0
        self.live = []
        self.retired = []
        self.psum = [nc.alloc_psum_tensor("ps%d" % i, [128, 512], F32) for i in range(8)]
        self.psbuf = [Buf("ps%d" % i, "ps") for i in range(8)]
        self.nsem = 0
        self.eng_sem = {}
        self.dma_rr = 0
        self.NPOOL = 90
        self.pool = [DSem(i) for i in range(self.NPOOL)]
        self.cc_sem = DSem(1000, inc=1)

    def alloc(self, name, nbytes, dtype=F32):
        req = nbytes
        nbytes = (nbytes + 31) // 32 * 32
        lo = self.arena_top
        hi = lo + nbytes
        assert hi <= self.arena_bytes, "arena overflow %s: %d > %d" % (name, hi, self.arena_bytes)
        self.arena_top = hi
        b = Buf(name, "sb", lo, hi)
        keep = []
        for rb in self.retired:
            if rb.lo < hi and lo < rb.hi:
                b.r.extend(rb.w)
                b.r.extend(rb.r)
                if rb.lo < lo or rb.hi > hi:
                    keep.append(rb)
            else:
                keep.append(rb)
        self.retired = keep
        dd = {}
        for dep in b.r:
            if dep[0] == "e":
                k = ("e", dep[1].eng)
                if k not in dd or dd[k][1].idx < dep[1].idx:
                    dd[k] = dep
            else:
                dd[("d", dep[1].idx)] = dep
        b.r = list(dd.values())
        self.live.append(b)
        ap = self.arena[:, lo // 4:hi // 4]
        if dtype != F32:
            ap = ap.bitcast(dtype)
            ap = ap[:, 0:req // 2]
        else:
            ap = ap[:, 0:req // 4]
        return ap, b

    def mark(self):
        return (self.arena_top, len(self.live))

    def release(self, mark):
        top, n = mark
        for b in self.live[n:]:
            self.retired.append(b)
        self.live = self.live[:n]
        self.arena_top = top

    def _dep_of(self, op):
        if op.dma_buf is not None:
            return ("d", op.dma_buf)
        return ("e", op)

    def _add_wait(self, op, dep):
        if dep[0] == "e":
            p = dep[1]
            if p.eng == "pe" and op.eng == "pe":
                return
            key = ("e", p.eng)
            cur = op.waits.get(key)
            if cur is None or cur.idx < p.idx:
                op.waits[key] = p
        else:
            b = dep[1]
            key = ("d", b.idx)
            op.waits[key] = (b, b.ndma * b.inc)

    def _collect(self, op, reads, writes, pwrites):
        for b in reads:
            for d in b.w:
                self._add_wait(op, d)
        for b in writes:
            for d in b.w:
                self._add_wait(op, d)
            for d in b.r:
                self._add_wait(op, d)
        for b in pwrites:
            for d in b.r:
                self._add_wait(op, d)

    @staticmethod
    def _same(d, me):
        if d[0] != me[0]:
            return False
        if me[0] == "e":
            return d[1].eng == me[1].eng
        return d[1] is me[1]

    def _register(self, me, reads, writes, pwrites):
        for b in writes:
            b.w = [me]
            b.r = []
        for b in pwrites:
            b.w = [d for d in b.w if not self._same(d, me)]
            b.w.append(me)
        for b in reads:
            if b in writes or b in pwrites:
                continue
            b.r = [d for d in b.r if not self._same(d, me)]
            b.r.append(me)

    def add(self, eng, fn, reads=(), writes=(), pwrites=()):
        op = Op(eng, len(self.ops[eng]), fn)
        self.ops[eng].append(op)
        reads, writes, pwrites = list(reads), list(writes), list(pwrites)
        self._collect(op, reads, writes, pwrites)
        self._register(("e", op), reads, writes, pwrites)
        return op

    def dma(self, out_ap, in_ap, sem_buf, reads=(), writes=(), pwrites=(), q=None):
        if q is None:
            q = "sp"
        op = Op(q, len(self.ops[q]), lambda e, o=out_ap, i=in_ap: e.dma_start(out=o, in_=i))
        self.ops[q].append(op)
        reads, writes, pwrites = list(reads), list(writes), list(pwrites)
        self._collect(op, reads, writes, pwrites)
        if sem_buf.sem is None:
            sem_buf.sem = self.pool[self.dma_rr % self.NPOOL]
            self.dma_rr += 1
        ds = sem_buf.sem
        op.dma_buf = ds
        ds.ndma += 1
        self._register(("d", ds), reads, writes, pwrites)
        return op

    def collective(self, fn, reads=(), writes=()):
        op = Op("pool", len(self.ops["pool"]), fn)
        self.ops["pool"].append(op)
        reads, writes = list(reads), list(writes)
        self._collect(op, reads, writes, [])
        ds = self.cc_sem
        op.dma_buf = ds
        ds.ndma += 1
        self._register(("d", ds), reads, writes, [])
        return op

    def finalize(self, final_bufs=()):
        nc = self.nc
        fin = Op("sp", len(self.ops["sp"]), None)
        for ds in self.pool + [self.cc_sem]:
            if ds.ndma > 0:
                fin.waits[("d", ds.idx)] = (ds, ds.ndma * ds.inc)
        self.ops["sp"].append(fin)
        for e in ENGS:
            for op in self.ops[e]:
                for k, v in op.waits.items():
                    if k[0] == "e":
                        v.flagged = True
        for e in ENGS:
            r = 0
            for op in self.ops[e]:
                if op.flagged:
                    r += 1
                    op.rank = r
        for e in ENGS:
            if e != "sp":
                self.eng_sem[e] = nc.alloc_semaphore("sem_" + e)
        n = 0
        for ds in self.pool + [self.cc_sem]:
            if ds.ndma > 0:
                ds.handle = nc.alloc_semaphore("dsem_%d" % ds.idx)
                n += 1
        self.n_dma_sems = n
        sched = self

        def emit(e, eng):
            waited = {}
            for op in sched.ops[e]:
                for k, v in op.waits.items():
                    if k[0] == "e":
                        sem = sched.eng_sem[v.eng]
                        val = v.rank
                        wk = ("e", v.eng)
                    else:
                        sem = v[0].handle
                        val = v[1]
                        wk = k
                    if waited.get(wk, 0) >= val:
                        continue
                    waited[wk] = val
                    eng.wait_ge(sem, val)
                if op.fn is None:
                    continue
                ins = op.fn(eng)
                if op.dma_buf is not None:
                    ins.then_inc(op.dma_buf.handle, op.dma_buf.inc)
                elif op.flagged:
                    ins.then_inc(sched.eng_sem[e], 1)

        with nc.Block() as block:
            @block.tensor
            def _(eng):
                emit("pe", eng)

            @block.scalar
            def _(eng):
                emit("act", eng)

            @block.vector
            def _(eng):
                emit("dve", eng)

            @block.gpsimd
            def _(eng):
                emit("pool", eng)

            @block.sync
            def _(eng):
                emit("sp", eng)

NEG = -30000.0
EPS = 1e-6
TINY = 1e-30
SV = 4096
SO = 2048
DM = 2048
DFF = 5632
SCALE = 128 ** -0.5


def sub3(a, off, s1, n1, s2, n2):
    return bass.AP(a.tensor, a.offset + off, [list(a.ap[0]), [s1, n1], [s2, n2]])


def dview(d, r0, nr, c0, nc_):
    return d[r0:r0 + nr, c0:c0 + nc_].rearrange("(k p) n -> p k n", p=128)


class KB:
    def __init__(self, nc):
        self.nc = nc
        self.s = Sched(nc, arena_bytes=198 * 1024)
        self.d = {}
        self.dbufs = {}
        self.bank_rr = 0
        self.outs = []

    def inp(self, name, shape, dt=F32):
        self.d[name] = self.nc.dram_tensor(name, list(shape), dt, kind="ExternalInput").ap()
        return self.d[name]

    def scr(self, name, shape, dt):
        self.d[name] = self.nc.dram_tensor(name, list(shape), dt).ap()
        return self.d[name]

    def outp(self, name, shape, dt=F32):
        self.d[name] = self.nc.dram_tensor(name, list(shape), dt, kind="ExternalOutput").ap()
        return self.d[name]

    def db(self, name, i=0):
        k = (name, i)
        if k not in self.dbufs:
            self.dbufs[k] = Buf("d_%s_%s" % (name, i), "dram")
        return self.dbufs[k]

    def rd(self, name, i=0):
        if name is None:
            return []
        return [self.db(name, i)]

    def consts(self, ngain):
        s = self.s
        self.ident, self.ident_b = s.alloc("ident", 128 * 2, BF16)
        self.ones, self.ones_b = s.alloc("ones", 128 * 2, BF16)
        self.gains, self.gains_b = s.alloc("gains", ngain * 16 * 4)
        s.dma(self.ident, self.d["c_ident"], self.ident_b, writes=[self.ident_b], q="pool")
        s.dma(self.ones, self.d["c_ones"], self.ones_b, writes=[self.ones_b], q="pool")
        s.dma(self.gains, self.d["gains"], self.gains_b, writes=[self.gains_b])

    def stage_norm(self, src, srcname, ntok, gidx, dsts, src_tt0=0):
        s = self.s
        mk = s.mark()
        hb = [s.alloc("nh%d" % i, 2048 * 4) for i in range(8)]
        yb = [s.alloc("ny%d" % i, 2048 * 2, BF16) for i in range(8)]
        junk_ap, junk_b = s.alloc("njunk", 2048 * 2, BF16)
        st = [s.alloc("nst%d" % i, 12 * 4) for i in range(2)]
        ob = [[s.alloc("no%d_%d" % (g, i), 16 * 512 * 2, BF16) for i in range(2)] for g in range(len(gidx))]
        psb = [s.psum[i][:, :].bitcast(BF16) for i in range(8)]
        ident, ident_b = self.ident, self.ident_b
        ngrp = ntok // 512

        def loads(tt):
            for sub in range(4):
                h_ap, h_b = hb[(tt % 2) * 4 + sub]
                r0 = tt * 512 + sub * 128
                s.dma(h_ap, src[r0:r0 + 128, :], h_b, reads=self.rd(srcname, src_tt0 + tt), writes=[h_b])
        loads(0)
        for tt in range(ngrp):
            if tt + 1 < ngrp:
                loads(tt + 1)
            st_ap, st_b = st[tt % 2]
            for sub in range(4):
                h_ap, h_b = hb[(tt % 2) * 4 + sub]
                s.add("act", lambda e, h=h_ap, o=st_ap[:, sub:sub + 1]: e.activation(junk_ap, h, AF.Square, accum_out=o),
                      reads=[h_b], pwrites=[junk_b, st_b])
            s.add("dve", lambda e, a=st_ap: e.tensor_scalar(a[:, 4:8], a[:, 0:4], 1.0 / 2048, EPS, ALU.mult, ALU.add),
                  reads=[st_b], pwrites=[st_b])
            s.add("act", lambda e, a=st_ap: e.sqrt(a[:, 4:8], a[:, 4:8]), reads=[st_b], pwrites=[st_b])
            s.add("dve", lambda e, a=st_ap: e.reciprocal(a[:, 8:12], a[:, 4:8]), reads=[st_b], pwrites=[st_b])
            for sub in range(4):
                h_ap, h_b = hb[(tt % 2) * 4 + sub]
                y_ap, y_b = yb[(tt % 2) * 4 + sub]
                s.add("act", lambda e, y=y_ap, h=h_ap, sc=st_ap[:, 8 + sub:9 + sub]: e.activation(y, h, AF.Copy, scale=sc),
                      reads=[h_b, st_b], writes=[y_b])
                for half in range(2):
                    bank = self.bank_rr % 8
                    self.bank_rr += 1
                    for k8 in range(8):
                        kc = half * 8 + k8
                        s.add("pe", lambda e, o=psb[bank][:, k8 * 128:(k8 + 1) * 128], i=y_ap[:, kc * 128:(kc + 1) * 128]:
                              e.transpose(o, i, ident), reads=[y_b, ident_b], writes=[s.psbuf[bank]])
                    for gi, g in enumerate(gidx):
                        o_ap, o_b = ob[gi][tt % 2]
                        out3 = sub3(o_ap, half * 8 * 512 + sub * 128, 512, 8, 1, 128)
                        in0 = sub3(psb[bank], 0, 128, 8, 1, 128)
                        ga = self.gains[:, g * 16 + half * 8:g * 16 + half * 8 + 8]
                        in1 = sub3(ga, 0, 1, 8, 0, 128)
                        s.add("dve", lambda e, o=out3, a=in0, b=in1: e.tensor_tensor(o, a, b, ALU.mult),
                              reads=[s.psbuf[bank], self.gains_b], pwrites=[o_b])
            for gi in range(len(gidx)):
                dst, dname, dtt0 = dsts[gi]
                o_ap, o_b = ob[gi][tt % 2]
                for q4 in range(4):
                    s.dma(dview(dst, q4 * 512, 512, (dtt0 + tt) * 512, 512),
                          sub3(o_ap, q4 * 4 * 512, 512, 4, 1, 512), o_b, reads=[o_b], pwrites=[self.db(dname, dtt0 + tt)])
        s.release(mk)

    def load_panel(self, p_ap, p_b, segs, KC, kgrp=4, stride=512):
        s = self.s
        po = 0
        for (W, c0, n) in segs:
            for q in range(0, KC, kgrp):
                kn = min(kgrp, KC - q)
                s.dma(sub3(p_ap, q * stride + po, stride, kn, 1, n), dview(W, q * 128, kn * 128, c0, n), p_b,
                      pwrites=[p_b], q="pool")
            po += n

    def stage_lfm(self, xT, xname, tok0, ntok, KC, panels, setup):
        s = self.s
        mk = s.mark()
        nt = ntok // 512
        xs = [s.alloc("lx%d" % i, KC * 512 * 2, BF16) for i in range(nt)]
        for tt in range(nt):
            x_ap, x_b = xs[tt]
            for q in range(0, KC, 4):
                s.dma(sub3(x_ap, q * 512, 512, 4, 1, 512), dview(xT, q * 128, 512, tok0 + tt * 512, 512), x_b,
                      reads=self.rd(xname, tok0 // 512 + tt), pwrites=[x_b])
        pr = [s.alloc("lp%d" % i, KC * 512 * 2, BF16) for i in range(3)]
        ctx = setup(s)
        npan = len(panels)
        for i in range(min(2, npan)):
            self.load_panel(pr[i % 3][0], pr[i % 3][1], panels[i]["segs"], KC)
        for i, pan in enumerate(panels):
            p_ap, p_b = pr[i % 3]
            for job in pan["jobs"]:
                nb = len(job["cols"])
                for tt in range(nt):
                    x_ap, x_b = xs[tt]
                    banks = []
                    for (off, n) in job["cols"]:
                        bank = self.bank_rr % 8
                        self.bank_rr += 1
                        banks.append(bank)
                        for kc in range(KC):
                            s.add("pe", lambda e, o=s.psum[bank][0:n, :], l=p_ap[:, kc * 512 + off:kc * 512 + off + n],
                                  r=x_ap[:, kc * 512:(kc + 1) * 512], st=(kc == 0), sp=(kc == KC - 1):
                                  e.matmul(o, l, r, start=st, stop=sp),
                                  reads=[p_b, x_b], writes=[s.psbuf[bank]])
                    job["epi"](ctx, job, tt, banks)
            if i + 2 < npan:
                self.load_panel(pr[(i + 2) % 3][0], pr[(i + 2) % 3][1], panels[i + 2]["segs"], KC)
        s.release(mk)

    def stage_ltm(self, aT, aname, KC, tok0, ntok, TB, panels, setup, epi, pcols=512, nring=2):
        s = self.s
        mk = s.mark()
        ntb = TB // 512
        as_ = [s.alloc("ta%d" % i, KC * 512 * 2, BF16) for i in range(ntb)]
        pr = [s.alloc("tp%d" % i, KC * pcols * 2, BF16) for i in range(nring)]
        ctx = setup(s)
        seq = [(tb, pi) for tb in range(ntok // TB) for pi in range(len(panels))]

        def pload(k):
            tb_, pi_ = seq[k]
            self.load_panel(pr[k % nring][0], pr[k % nring][1], panels[pi_], KC, stride=pcols)
        for k in range(min(nring - 1, len(seq))):
            pload(k)
        for k, (tb, pi) in enumerate(seq):
            if pi == 0:
                for tt in range(ntb):
                    a_ap, a_b = as_[tt]
                    t0 = tok0 + tb * TB + tt * 512
                    for q in range(0, KC, 4):
                        kn = min(4, KC - q)
                        s.dma(sub3(a_ap, q * 512, 512, kn, 1, 512), dview(aT, q * 128, kn * 128, t0, 512), a_b,
                              reads=self.rd(aname, t0 // 512), pwrites=[a_b])
            if k + nring - 1 < len(seq):
                pload(k + nring - 1)
            p_ap, p_b = pr[k % nring]
            segs = panels[pi]
            ncol = sum(n for (_, _, n) in segs)
            for tt in range(ntb):
                a_ap, a_b = as_[tt]
                for t4 in range(4):
                    bank = self.bank_rr % 8
                    self.bank_rr += 1
                    for kc in range(KC):
                        s.add("pe", lambda e, o=s.psum[bank][:, 0:ncol], l=a_ap[:, kc * 512 + t4 * 128:kc * 512 + t4 * 128 + 128],
                              r=p_ap[:, kc * pcols:kc * pcols + ncol], st=(kc == 0), sp=(kc == KC - 1):
                              e.matmul(o, l, r, start=st, stop=sp),
                              reads=[p_b, a_b], writes=[s.psbuf[bank]])
                    epi(ctx, tok0 + tb * TB + tt * 512 + t4 * 128, pi, bank, ncol)
        s.release(mk)

    def epi_plain_fm(self, dst, dname, row0fn, tok0):
        kb = self

        def epi(ctx, job, tt, banks):
            s = kb.s
            n = job["cols"][0][1]
            o_ap, o_b = ctx["oring"][ctx["oi"] % len(ctx["oring"])]
            ctx["oi"] += 1
            bank = banks[0]
            s.add("act", lambda e, o=o_ap[0:n, :], i=s.psum[bank][0:n, :]: e.copy(o, i), reads=[s.psbuf[bank]], writes=[o_b])
            r0 = job["row0"]
            s.dma(dst[r0:r0 + n, tok0 + tt * 512:tok0 + tt * 512 + 512], o_ap[0:n, :], o_b, reads=[o_b],
                  pwrites=[kb.db(dname, (tok0 // 512) + tt)])
        return epi

    def epi_rope_fm(self, dst, dname, tok0):
        kb = self

        def epi(ctx, job, tt, banks):
            s = kb.s
            bz, br = banks
            t1, t1b = ctx["t1"][ctx["oi"] % 2]
            t2, t2b = ctx["t2"][ctx["oi"] % 2]
            o_ap, o_b = ctx["oring"][ctx["oi"] % len(ctx["oring"])]
            ctx["oi"] += 1
            cs, csb = ctx["cos"]
            sn, snb = ctx["sin"]
            s.add("dve", lambda e, o=t1, a=s.psum[bz][:, :], b=cs[:, tt * 512:(tt + 1) * 512]: e.tensor_tensor(o, a, b, ALU.mult),
                  reads=[s.psbuf[bz], csb], writes=[t1b])
            s.add("dve", lambda e, o=t2, a=s.psum[br][:, :], b=sn[:, tt * 512:(tt + 1) * 512]: e.tensor_tensor(o, a, b, ALU.mult),
                  reads=[s.psbuf[br], snb], writes=[t2b])
            s.add("pool", lambda e, o=o_ap, a=t1, b=t2: e.tensor_tensor(o, a, b, ALU.add), reads=[t1b, t2b], writes=[o_b])
            r0 = job["row0"]
            s.dma(job["dst"][r0:r0 + 128, tok0 + tt * 512:tok0 + tt * 512 + 512], o_ap, o_b, reads=[o_b],
                  pwrites=[kb.db(job["dname"], (tok0 // 512) + tt)])
        return epi

    def rope_setup(self, tok0, ntok, extra=None):
        kb = self

        def setup(s):
            ctx = {"oi": 0}
            ctx["oring"] = [s.alloc("eo%d" % i, 512 * 2, BF16) for i in range(4)]
            ctx["t1"] = [s.alloc("et1%d" % i, 512 * 4) for i in range(2)]
            ctx["t2"] = [s.alloc("et2%d" % i, 512 * 4) for i in range(2)]
            ctx["cos"] = s.alloc("ecos", ntok * 4)
            ctx["sin"] = s.alloc("esin", ntok * 4)
            s.dma(ctx["cos"][0], kb.d["cosT"][:, tok0:tok0 + ntok], ctx["cos"][1], writes=[ctx["cos"][1]])
            s.dma(ctx["sin"][0], kb.d["sinT"][:, tok0:tok0 + ntok], ctx["sin"][1], writes=[ctx["sin"][1]])
            if extra is not None:
                extra(s, ctx)
            return ctx
        return setup

    def stage_ffn(self, hnT, hnname, wg, wu, wd, h_in, h_in_name, h_out, h_out_name):
        kb = self
        hidT = self.d["hidT"]

        def setup(s):
            ctx = {"oi": 0}
            ctx["sg"] = [s.alloc("fsg%d" % i, 512 * 4) for i in range(3)]
            ctx["oring"] = [s.alloc("fo%d" % i, 512 * 2, BF16) for i in range(4)]
            return ctx

        def epi(ctx, job, tt, banks):
            s = kb.s
            bg, bu = banks
            sg, sgb = ctx["sg"][ctx["oi"] % 3]
            o_ap, o_b = ctx["oring"][ctx["oi"] % 4]
            ctx["oi"] += 1
            s.add("act", lambda e, o=sg, i=s.psum[bg][:, :]: e.activation(o, i, AF.Silu), reads=[s.psbuf[bg]], writes=[sgb])
            s.add("dve", lambda e, o=o_ap, a=s.psum[bu][:, :], b=sg: e.tensor_tensor(o, a, b, ALU.mult),
                  reads=[s.psbuf[bu], sgb], writes=[o_b])
            r0 = job["row0"]
            s.dma(hidT[r0:r0 + 128, tt * 512:(tt + 1) * 512], o_ap, o_b, reads=[o_b], pwrites=[kb.db("hidT", tt)])

        panels = []
        for pc in range(DFF // 256):
            segs = [(wg, pc * 256, 256), (wu, pc * 256, 256)]
            jobs = [dict(cols=[(j * 128, 128), (256 + j * 128, 128)], epi=epi, row0=pc * 256 + j * 128) for j in range(2)]
            panels.append(dict(segs=segs, jobs=jobs))
        self.stage_lfm(hnT, hnname, 0, SO, 16, panels, setup)
        hk = DFF // 2
        self.stage_down(self.d["hidT"][0:hk, :], "hidT", hk // 128, wd[0:hk, :], h_in, h_in_name, self.d["htmp"], "htmp",
                        TB=2048, PC=512, nring=3)
        self.stage_down(self.d["hidT"][hk:DFF, :], "hidT", hk // 128, wd[hk:DFF, :], self.d["htmp"], "htmp", h_out, h_out_name,
                        TB=2048, PC=512, nring=3)

    def stage_down(self, aT, aname, KC, W, h_in, h_in_name, h_out, h_out_name, TB, PC=512, nring=2):
        kb = self

        def setup(s):
            ctx = {"oi": 0}
            ctx["hin"] = [s.alloc("dh%d" % i, 512 * 4) for i in range(3)]
            ctx["oring"] = [s.alloc("do%d" % i, 512 * 4) for i in range(3)]
            return ctx

        def epi(ctx, tok, pi, bank, ncol):
            s = kb.s
            hi, hib = ctx["hin"][ctx["oi"] % 3]
            o_ap, o_b = ctx["oring"][ctx["oi"] % 3]
            ctx["oi"] += 1
            s.dma(hi[:, 0:PC], h_in[tok:tok + 128, pi * PC:(pi + 1) * PC], hib, reads=kb.rd(h_in_name, tok // 512), writes=[hib])
            s.add("dve", lambda e, o=o_ap[:, 0:PC], a=s.psum[bank][:, 0:PC], b=hi[:, 0:PC]: e.tensor_tensor(o, a, b, ALU.add),
                  reads=[s.psbuf[bank], hib], writes=[o_b])
            s.dma(h_out[tok:tok + 128, pi * PC:(pi + 1) * PC], o_ap[:, 0:PC], o_b, reads=[o_b], pwrites=[kb.db(h_out_name, tok // 512)])

        panels = [[(W, pi * PC, PC)] for pi in range(DM // PC)]
        self.stage_ltm(aT, aname, KC, 0, SO, TB, panels, setup, epi, pcols=PC, nring=nring)

    def stage_tm_bf16(self, aT, aname, KC, tok0, ntok, segs, dst, dname):
        kb = self

        def setup(s):
            return {"oi": 0, "oring": [s.alloc("vo%d" % i, 512 * 2, BF16) for i in range(4)]}

        def epi(ctx, tok, pi, bank, ncol):
            s = kb.s
            o_ap, o_b = ctx["oring"][ctx["oi"] % 4]
            ctx["oi"] += 1
            s.add("act", lambda e, o=o_ap[:, 0:ncol], i=s.psum[bank][:, 0:ncol]: e.copy(o, i), reads=[s.psbuf[bank]], writes=[o_b])
            s.dma(dst[tok:tok + 128, 0:ncol], o_ap[:, 0:ncol], o_b, reads=[o_b], pwrites=[kb.db(dname, tok // 512)])

        self.stage_ltm(aT, aname, KC, tok0, ntok, min(ntok, 2048), [segs], setup, epi)

    def unit_s(self, ctx, u):
        s = self.s
        n = u.get("n", 512)
        np_ = u.get("np_", 128)
        bS = ctx["sbanks"][ctx["si"] % len(ctx["sbanks"])]
        ctx["si"] += 1
        pt, ptb = ctx["pt"][ctx["pi"] % len(ctx["pt"])]
        ctx["pi"] += 1
        extras = u["extras"]
        ne = len(extras)
        s.add("pe", lambda e, o=s.psum[bS][0:np_, 0:n], l=u["klhs"], r=u["qrhs"], sp=(ne == 0): e.matmul(o, l, r, start=True, stop=sp),
              reads=u["krd"] + u["qrd"], writes=[s.psbuf[bS]])
        for i, (l, r, rds) in enumerate(extras):
            s.add("pe", lambda e, o=s.psum[bS][0:np_, 0:n], l=l, r=r, sp=(i == ne - 1): e.matmul(o, l, r, start=False, stop=sp),
                  reads=rds, writes=[s.psbuf[bS]])
        s.add("act", lambda e, o=pt[0:np_, 0:n], i=s.psum[bS][0:np_, 0:n]: e.activation(o, i, AF.Exp, scale=SCALE),
              reads=[s.psbuf[bS]], writes=[ptb])
        return pt, ptb

    def unit_pv(self, ctx, u, rec):
        s = self.s
        n = u.get("n", 512)
        np_ = u.get("np_", 128)
        pt, ptb = rec
        first, last = u["first"], u["last"]
        if u["vlhs"] is not None:
            s.add("pe", lambda e, o=s.psum[u["bacc"]][:, 0:n], l=u["vlhs"], r=pt[0:np_, 0:n], st=first, sp=last: e.matmul(o, l, r, start=st, stop=sp),
                  reads=u["vrd"] + [ptb], writes=[s.psbuf[u["bacc"]]])
        s.add("pe", lambda e, o=s.psum[u["bden"]][:, 0:n], l=self.ones[0:np_, :], r=pt[0:np_, 0:n], st=first, sp=last: e.matmul(o, l, r, start=st, stop=sp),
              reads=[self.ones_b, ptb], writes=[s.psbuf[u["bden"]]])
        if u.get("post") is not None:
            u["post"]()

    def run_units(self, ctx, units, skew=2):
        recs = []
        nu = len(units)
        for i in range(nu + skew):
            if i < nu:
                recs.append(self.unit_s(ctx, units[i]))
            j = i - skew
            if j >= 0:
                self.unit_pv(ctx, units[j], recs[j])
        return recs

    def attn_unit(self, ctx, klhs, krd, qrhs, qrd, extras, vlhs, vrd, bacc, bden, first, last, n=512, np_=128):
        u = dict(klhs=klhs, krd=krd, qrhs=qrhs, qrd=qrd, extras=extras, vlhs=vlhs, vrd=vrd, bacc=bacc, bden=bden,
                 first=first, last=last, n=n, np_=np_)
        rec = self.unit_s(ctx, u)
        self.unit_pv(ctx, u, rec)
        return rec

    def recip_den(self, ctx, bden, n=512):
        s = self.s
        r, rb = ctx["rd"][ctx["ri"] % len(ctx["rd"])]
        ctx["ri"] += 1
        s.add("dve", lambda e, o=r[:, 0:n], i=s.psum[bden][:, 0:n]: e.tensor_scalar(o, i, TINY, None, ALU.max),
              reads=[s.psbuf[bden]], writes=[rb])
        s.add("dve", lambda e, o=r[:, 0:n]: e.reciprocal(o, o), reads=[rb], writes=[rb])
        return r, rb

    def stage_mem_attn(self, qmT, qmname, mkT, mv, oT, oname, chunk0):
        s = self.s
        mk = s.mark()
        ctx = dict(si=0, pi=0, ri=0, sbanks=[0, 1, 2], pt=[s.alloc("mpt%d" % i, 512 * 2, BF16) for i in range(4)],
                   rd=[s.alloc("mrd%d" % i, 512 * 4) for i in range(2)])
        k_ap, k_b = s.alloc("mk", 4 * 256 * 2, BF16)
        v_ap, v_b = s.alloc("mv", 2 * 512 * 2, BF16)
        q_ap, q_b = s.alloc("mq", 4 * SO * 2, BF16)
        oring = [s.alloc("mo%d" % i, 512 * 2, BF16) for i in range(3)]
        s.dma(sub3(k_ap, 0, 256, 4, 1, 256), dview(mkT, 0, 512, 0, 256), k_b, reads=self.rd("mkT"), writes=[k_b])
        s.dma(sub3(v_ap, 0, 512, 2, 1, 512), dview(mv, 0, 256, 0, 512), v_b, reads=self.rd("mv"), writes=[v_b])
        for h in range(4):
            s.dma(q_ap[:, h * SO:(h + 1) * SO], qmT[h * 128:(h + 1) * 128, :], q_b,
                  reads=[self.db(qmname, i) for i in range(4)], pwrites=[q_b])
        units = []
        oi = 0
        for h in range(4):
            for qt in range(4):
                bacc, bden = (3, 4) if (oi % 2 == 0) else (5, 6)
                o_ap, o_b = oring[oi % 3]
                oi += 1

                def post(bacc=bacc, bden=bden, o_ap=o_ap, o_b=o_b, h=h, qt=qt):
                    r, rb = self.recip_den(ctx, bden)
                    s.add("dve", lambda e, o=o_ap, a=s.psum[bacc][:, :], b=r: e.tensor_tensor(o, a, b, ALU.mult),
                          reads=[s.psbuf[bacc], rb], writes=[o_b])
                    s.dma(oT[(chunk0 + h) * 128:(chunk0 + h + 1) * 128, qt * 512:(qt + 1) * 512], o_ap, o_b, reads=[o_b],
                          pwrites=[self.db(oname, qt)])
                for mt in range(2):
                    units.append(dict(klhs=k_ap[:, h * 256 + mt * 128:h * 256 + mt * 128 + 128], krd=[k_b],
                                      qrhs=q_ap[:, h * SO + qt * 512:h * SO + qt * 512 + 512], qrd=[q_b], extras=[],
                                      vlhs=v_ap[:, mt * 512 + h * 128:mt * 512 + h * 128 + 128], vrd=[v_b], bacc=bacc, bden=bden,
                                      first=(mt == 0), last=(mt == 1), post=(post if mt == 1 else None)))
        self.run_units(ctx, units)
        s.release(mk)

    def stage_memkv(self, mem, gi, wkv):
        kb = self
        s = self.s
        mk = s.mark()
        hb = [s.alloc("kh%d" % i, 2048 * 4) for i in range(2)]
        yb = [s.alloc("ky%d" % i, 2048 * 2, BF16) for i in range(2)]
        junk_ap, junk_b = s.alloc("kjunk", 2048 * 2, BF16)
        st_ap, st_b = s.alloc("kst", 12 * 4)
        o_ap, o_b = s.alloc("ko", 16 * 256 * 2, BF16)
        psb = [s.psum[i][:, :].bitcast(BF16) for i in range(8)]
        for sub in range(2):
            h_ap, h_b = hb[sub]
            s.dma(h_ap, mem[sub * 128:(sub + 1) * 128, :], h_b, writes=[h_b])
            s.add("act", lambda e, h=h_ap, o=st_ap[:, sub:sub + 1]: e.activation(junk_ap, h, AF.Square, accum_out=o),
                  reads=[h_b], writes=[junk_b], pwrites=[st_b])
        s.add("dve", lambda e, a=st_ap: e.tensor_scalar(a[:, 4:6], a[:, 0:2], 1.0 / 2048, EPS, ALU.mult, ALU.add), reads=[st_b], pwrites=[st_b])
        s.add("act", lambda e, a=st_ap: e.sqrt(a[:, 4:6], a[:, 4:6]), reads=[st_b], pwrites=[st_b])
        s.add("dve", lambda e, a=st_ap: e.reciprocal(a[:, 8:10], a[:, 4:6]), reads=[st_b], pwrites=[st_b])
        for sub in range(2):
            h_ap, h_b = hb[sub]
            y_ap, y_b = yb[sub]
            s.add("act", lambda e, y=y_ap, h=h_ap, sc=st_ap[:, 8 + sub:9 + sub]: e.activation(y, h, AF.Copy, scale=sc),
                  reads=[h_b, st_b], writes=[y_b])
            for half in range(2):
                bank = self.bank_rr % 8
                self.bank_rr += 1
                for k8 in range(8):
                    kc = half * 8 + k8
                    s.add("pe", lambda e, o=psb[bank][:, k8 * 128:(k8 + 1) * 128], i=y_ap[:, kc * 128:(kc + 1) * 128]:
                          e.transpose(o, i, kb.ident), reads=[y_b, kb.ident_b], writes=[s.psbuf[bank]])
                out3 = sub3(o_ap, half * 8 * 256 + sub * 128, 256, 8, 1, 128)
                in0 = sub3(psb[bank], 0, 128, 8, 1, 128)
                ga = self.gains[:, gi * 16 + half * 8:gi * 16 + half * 8 + 8]
                in1 = sub3(ga, 0, 1, 8, 0, 128)
                s.add("dve", lambda e, o=out3, a=in0, b=in1: e.tensor_tensor(o, a, b, ALU.mult),
                      reads=[s.psbuf[bank], self.gains_b], pwrites=[o_b])
        mkT = self.d["mkT"]
        mv = self.d["mv"]
        pr = [s.alloc("kp%d" % i, 16 * 512 * 2, BF16) for i in range(2)]
        oring = [s.alloc("kor%d" % i, 512 * 2, BF16) for i in range(3)]
        oi = 0
        for half in range(2):
            p_ap, p_b = pr[half]
            self.load_panel(p_ap, p_b, [(wkv, half * 512, 512)], 16)
        p_ap, p_b = pr[0]
        for h in range(4):
            bank = self.bank_rr % 8
            self.bank_rr += 1
            for kc in range(16):
                s.add("pe", lambda e, o=s.psum[bank][:, 0:256], l=p_ap[:, kc * 512 + h * 128:kc * 512 + h * 128 + 128],
                      r=o_ap[:, kc * 256:(kc + 1) * 256], st=(kc == 0), sp=(kc == 15): e.matmul(o, l, r, start=st, stop=sp),
                      reads=[p_b, o_b], writes=[s.psbuf[bank]])
            oo, oob = oring[oi % 3]
            oi += 1
            s.add("act", lambda e, o=oo[:, 0:256], i=s.psum[bank][:, 0:256]: e.copy(o, i), reads=[s.psbuf[bank]], writes=[oob])
            s.dma(mkT[h * 128:(h + 1) * 128, :], oo[:, 0:256], oob, reads=[oob], pwrites=[self.db("mkT")])
        p_ap, p_b = pr[1]
        for mt in range(2):
            bank = self.bank_rr % 8
            self.bank_rr += 1
            for kc in range(16):
                s.add("pe", lambda e, o=s.psum[bank][:, :], l=o_ap[:, kc * 256 + mt * 128:kc * 256 + mt * 128 + 128],
                      r=p_ap[:, kc * 512:(kc + 1) * 512], st=(kc == 0), sp=(kc == 15): e.matmul(o, l, r, start=st, stop=sp),
                      reads=[p_b, o_b], writes=[s.psbuf[bank]])
            oo, oob = oring[oi % 3]
            oi += 1
            s.add("act", lambda e, o=oo, i=s.psum[bank][:, :]: e.copy(o, i), reads=[s.psbuf[bank]], writes=[oob])
            s.dma(mv[mt * 128:(mt + 1) * 128, :], oo, oob, reads=[oob], pwrites=[self.db("mv")])
        s.release(mk)

    def stage_inproj_a(self, w_in, w_rot, gbias):
        kb = self
        d = self.d
        xT = d["xnT"]
        for tok0, own in ((0, False), (SO, True)):
            epi_rope = self.epi_rope_fm(None, None, tok0)
            epi_plain = self.epi_plain_fm(None, None, None, tok0)

            def mkplain(dst, dname):
                return kb.epi_plain_fm(dst, dname, None, tok0)

            def gate_extra(s, ctx):
                ctx["gb"] = s.alloc("egb", 4)
                s.dma(ctx["gb"][0][0:36, :], gbias, ctx["gb"][1], writes=[ctx["gb"][1]])
                ctx["go"] = [s.alloc("ego%d" % i, 512 * 4) for i in range(2)]

            def epi_gate(ctx, job, tt, banks):
                s = kb.s
                o_ap, o_b = ctx["go"][tt % 2]
                gb, gbb = ctx["gb"]
                bank = banks[0]
                s.add("act", lambda e, o=o_ap[0:36, :], i=s.psum[bank][0:36, :], b=gb[0:36, 0:1]: e.activation(o, i, AF.Sigmoid, bias=b),
                      reads=[s.psbuf[bank], gbb], writes=[o_b])
                s.dma(d["gatesT"][:, tt * 512:(tt + 1) * 512], o_ap[0:36, :], o_b, reads=[o_b], pwrites=[kb.db("gatesT", tt)])

            panels = []
            otok = tok0 - SO

            def ropejob(off, roff, dst, dname, row0):
                return dict(cols=[(off, 128), (roff, 128)], epi=kb.epi_rope_fm(None, None, tok0 if dst is not d["qT"] else 0),
                            dst=dst, dname=dname, row0=row0)
            if own:
                for hp in range(6):
                    segs = [(w_in, hp * 256, 256), (w_rot, hp * 256, 256)]
                    jobs = []
                    for j in range(2):
                        jb = dict(cols=[(j * 128, 128), (256 + j * 128, 128)], dst=d["qT"], dname="qT", row0=(hp * 2 + j) * 128)
                        jb["epi"] = self._rope_epi_own()
                        jobs.append(jb)
                    panels.append(dict(segs=segs, jobs=jobs))
            for (kcol, rcol, dst, dname) in ((1536, 1536, d["kcmpT"], "kcmpT"), (2048, 1792, d["kslcT"], "kslcT"),
                                             (2560, 2048, d["kwinT"], "kwinT")):
                segs = [(w_in, kcol, 256), (w_rot, rcol, 256)]
                jobs = []
                for g in range(2):
                    jobs.append(dict(cols=[(g * 128, 128), (256 + g * 128, 128)], dst=dst, dname=dname, row0=g * 128,
                                     epi=self._rope_epi_all(tok0)))
                panels.append(dict(segs=segs, jobs=jobs))
            segs = [(w_in, 1792, 256)]
            jobs = [dict(cols=[(g * 128, 128)], row0=g * 128, epi=mkplain(d["vcmpT"], "vcmpT")) for g in range(2)]
            panels.append(dict(segs=segs, jobs=jobs))
            if own:
                segs = [(w_in, 3072, 36), (w_in, 3108, 256)]
                jobs = [dict(cols=[(0, 36)], epi=epi_gate)]
                for j in range(2):
                    jobs.append(dict(cols=[(36 + j * 128, 128)], row0=j * 128, epi=kb.epi_plain_fm(d["qmT"], "qmT", None, 0)))
                panels.append(dict(segs=segs, jobs=jobs))
                segs = [(w_in, 3108 + 256, 256)]
                jobs = []
                for j in range(2):
                    jobs.append(dict(cols=[(j * 128, 128)], row0=(2 + j) * 128, epi=kb.epi_plain_fm(d["qmT"], "qmT", None, 0)))
                panels.append(dict(segs=segs, jobs=jobs))
            self.cur_tok0 = tok0
            self.stage_lfm(xT, "xnT", tok0, SO, 16, panels, self.rope_setup(tok0, SO, gate_extra if own else None))

    def _rope_epi_all(self, tok0):
        kb = self

        def epi(ctx, job, tt, banks):
            kb._rope_core(ctx, job, tt, banks, tok0 + tt * 512, (tok0 // 512) + tt)
        return epi

    def _rope_epi_own(self):
        kb = self

        def epi(ctx, job, tt, banks):
            kb._rope_core(ctx, job, tt, banks, tt * 512, tt)
        return epi

    def _rope_core(self, ctx, job, tt, banks, col0, dbi):
        s = self.s
        bz, br = banks
        t1, t1b = ctx["t1"][ctx["oi"] % 2]
        t2, t2b = ctx["t2"][ctx["oi"] % 2]
        o_ap, o_b = ctx["oring"][ctx["oi"] % len(ctx["oring"])]
        ctx["oi"] += 1
        cs, csb = ctx["cos"]
        sn, snb = ctx["sin"]
        s.add("dve", lambda e, o=t1, a=s.psum[bz][:, :], b=cs[:, tt * 512:(tt + 1) * 512]: e.tensor_tensor(o, a, b, ALU.mult),
              reads=[s.psbuf[bz], csb], writes=[t1b])
        s.add("dve", lambda e, o=t2, a=s.psum[br][:, :], b=sn[:, tt * 512:(tt + 1) * 512]: e.tensor_tensor(o, a, b, ALU.mult),
              reads=[s.psbuf[br], snb], writes=[t2b])
        s.add("pool", lambda e, o=o_ap, a=t1, b=t2: e.tensor_tensor(o, a, b, ALU.add), reads=[t1b, t2b], writes=[o_b])
        r0 = job["row0"]
        s.dma(job["dst"][r0:r0 + 128, col0:col0 + 512], o_ap, o_b, reads=[o_b], pwrites=[self.db(job["dname"], dbi)])

    def stage_cmp(self, w1k, w2k, pek, w1v, w2v, pev):
        s = self.s
        d = self.d
        for kv, (w1, w2, peT, srcT, sname) in enumerate(((w1k, w2k, pek, d["kcmpT"], "kcmpT"), (w1v, w2v, pev, d["vcmpT"], "vcmpT"))):
            mk = s.mark()
            w1_ap, w1_b = s.alloc("cw1", 32 * 256 * 2, BF16)
            for q in range(0, 32, 8):
                s.dma(sub3(w1_ap, q * 256, 256, 8, 1, 256), dview(w1, q * 128, 1024, 0, 256), w1_b, pwrites=[w1_b], q="pool")
            w2_ap, w2_b = s.alloc("cw2", 2 * 128 * 2, BF16)
            s.dma(sub3(w2_ap, 0, 128, 2, 1, 128), dview(w2, 0, 256, 0, 128), w2_b, writes=[w2_b], q="pool")
            pe_ap, pe_b = s.alloc("cpe", 32 * 2, BF16)
            s.dma(pe_ap, peT, pe_b, writes=[pe_b], q="pool")
            bias_ap, bias_b = s.alloc("cbias", 2 * 4)
            for hc in range(2):
                bank = self.bank_rr % 8
                self.bank_rr += 1
                for l in range(32):
                    s.add("pe", lambda e, o=s.psum[bank][:, 0:1], lh=w1_ap[:, l * 256 + hc * 128:l * 256 + hc * 128 + 128], r=pe_ap[:, l:l + 1],
                          st=(l == 0), sp=(l == 31): e.matmul(o, lh, r, start=st, stop=sp), reads=[w1_b, pe_b], writes=[s.psbuf[bank]])
                s.add("dve", lambda e, o=bias_ap[:, hc:hc + 1], i=s.psum[bank][:, 0:1]: e.tensor_copy(o, i), reads=[s.psbuf[bank]], pwrites=[bias_b])
            for g in range(2):
                k_ap, k_b = s.alloc("ck%d" % g, SV * 2, BF16)
                s.dma(k_ap, srcT[g * 128:(g + 1) * 128, :], k_b, reads=[self.db(sname, i) for i in range(8)], writes=[k_b])
                hs_ap, hs_b = s.alloc("chs%d" % g, 2 * 256 * 2, BF16)
                for hc in range(2):
                    bank = self.bank_rr % 8
                    self.bank_rr += 1
                    for l in range(32):
                        s.add("pe", lambda e, o=s.psum[bank][:, 0:255], lh=w1_ap[:, l * 256 + hc * 128:l * 256 + hc * 128 + 128],
                              r=k_ap[:, l:l + 16 * 254 + 1:16], st=(l == 0), sp=(l == 31): e.matmul(o, lh, r, start=st, stop=sp),
                              reads=[w1_b, k_b], writes=[s.psbuf[bank]])
                    s.add("act", lambda e, o=hs_ap[:, hc * 256:hc * 256 + 255], i=s.psum[bank][:, 0:255], b=bias_ap[:, hc:hc + 1]:
                          e.activation(o, i, AF.Silu, bias=b), reads=[s.psbuf[bank], bias_b], pwrites=[hs_b])
                o_ap, o_b = s.alloc("cout%d" % g, 256 * 2, BF16)
                if kv == 0:
                    bank = self.bank_rr % 8
                    self.bank_rr += 1
                    for hc in range(2):
                        s.add("pe", lambda e, o=s.psum[bank][:, 0:255], lh=w2_ap[:, hc * 128:(hc + 1) * 128], r=hs_ap[:, hc * 256:hc * 256 + 255],
                              st=(hc == 0), sp=(hc == 1): e.matmul(o, lh, r, start=st, stop=sp), reads=[w2_b, hs_b], writes=[s.psbuf[bank]])
                    s.add("pool", lambda e, o=o_ap: e.memset(o, 0.0), writes=[o_b])
                    s.add("act", lambda e, o=o_ap[:, 0:255], i=s.psum[bank][:, 0:255]: e.copy(o, i), reads=[s.psbuf[bank]], pwrites=[o_b])
                    s.dma(d["kcT"][g * 128:(g + 1) * 128, :], o_ap, o_b, reads=[o_b], pwrites=[self.db("kcT")])
                else:
                    s.add("pool", lambda e, o=o_ap: e.memset(o, 0.0), writes=[o_b])
                    for ct in range(2):
                        ncn = 128 if ct == 0 else 127
                        bank = self.bank_rr % 8
                        self.bank_rr += 1
                        for hc in range(2):
                            s.add("pe", lambda e, o=s.psum[bank][0:ncn, 0:128], lh=hs_ap[:, hc * 256 + ct * 128:hc * 256 + ct * 128 + ncn],
                                  r=w2_ap[:, hc * 128:(hc + 1) * 128], st=(hc == 0), sp=(hc == 1): e.matmul(o, lh, r, start=st, stop=sp),
                                  reads=[w2_b, hs_b], writes=[s.psbuf[bank]])
                        s.add("act", lambda e, o=o_ap[0:ncn, ct * 128:(ct + 1) * 128], i=s.psum[bank][0:ncn, 0:128]: e.copy(o, i),
                              reads=[s.psbuf[bank]], pwrites=[o_b])
                    s.dma(dview(d["vc"], g * 256, 256, 0, 128), sub3(o_ap, 0, 128, 2, 1, 128), o_b, reads=[o_b], pwrites=[self.db("vc")])
            s.release(mk)

    def stage_attn_a(self):
        s = self.s
        d = self.d
        mk0 = s.mark()
        def ld(name, src, nbytes, dt, q="sp", parts=128):
            ap, b = s.alloc(name, nbytes, dt)
            s.dma(ap[0:parts, :], src, b, writes=[b], q=q)
            return ap, b
        mcmp, mcmp_b = ld("mcmp", d["m_cmp"], 8 * 512 * 2, BF16, "pool")
        mwin, mwin_b = ld("mwin", d["m_win"], 8 * 512 * 2, BF16, "pool")
        mwin0, mwin0_b = ld("mwin0", d["m_win0"], 4 * 512 * 2, BF16, "pool")
        E, E_b = ld("E", d["c_E"], SV * 2, BF16, "pool", 64)
        mmap, mmap_b = ld("mmap", d["c_mmap"], 2 * 64 * 4, F32)
        ph = [s.alloc("aph%d" % i, 512 * 4) for i in range(2)]
        selM, selM_b = ld("selM", d["selM"], 4 * 256 * 4, F32)
        selA, selA_b = ld("selA", d["selA"], 4 * 256 * 4, F32)
        selmat, selmat_b = ld("selmat", d["c_selmat"], 36 * 128 * 4, F32, "sp", 36)
        gat, gat_b = s.alloc("gat", SO * 4)
        s.dma(gat[0:36, :], d["gatesT"], gat_b, reads=[self.db("gatesT", i) for i in range(4)], writes=[gat_b])
        ctx = dict(si=0, pi=0, ri=0, sbanks=[0, 1, 2], pt=[s.alloc("apt%d" % i, 512 * 2, BF16) for i in range(4)],
                   rd=[s.alloc("ard%d" % i, 512 * 4) for i in range(3)])
        pn = [s.alloc("apn%d" % i, 512 * 2, BF16) for i in range(4)]
        Gs = [s.alloc("aG%d" % i, 512 * 4) for i in range(3)]
        ocs = [s.alloc("aocs%d" % i, 512 * 4) for i in range(6)]
        tb = [s.alloc("atb%d" % i, 512 * 4) for i in range(4)]
        fb = [s.alloc("afb%d" % i, 512 * 4) for i in range(2)]
        oring = [s.alloc("aor%d" % i, 512 * 2, BF16) for i in range(3)]
        sc_ap, sc_b = s.alloc("asc", 256 * 4)
        m16, m16_b = s.alloc("am16", 4 * 16 * 4)
        wk, wk_b = s.alloc("awk", 256 * 4)
        selb, selb_b = s.alloc("aselb", 256 * 2, BF16)
        selbT, selbT_b = s.alloc("aselbT", 512 * 2, BF16)
        psb = [s.psum[i][:, :].bitcast(BF16) for i in range(8)]
        gi_ = 0
        oi = 0
        for g in range(2):
            mk = s.mark()
            kc_ap, kc_b = s.alloc("akc", 256 * 2, BF16)
            s.dma(kc_ap, d["kcT"][g * 128:(g + 1) * 128, :], kc_b, reads=self.rd("kcT"), writes=[kc_b])
            vc_ap, vc_b = s.alloc("avc", 256 * 2, BF16)
            s.dma(sub3(vc_ap, 0, 128, 2, 1, 128), dview(d["vc"], g * 256, 256, 0, 128), vc_b, reads=self.rd("vc"), writes=[vc_b])
            ks_ap, ks_b = s.alloc("aks", SV * 2, BF16)
            kw_ap, kw_b = s.alloc("akw", SV * 2, BF16)
            s.dma(ks_ap, d["kslcT"][g * 128:(g + 1) * 128, :], ks_b, reads=[self.db("kslcT", i) for i in range(8)], writes=[ks_b])
            s.dma(kw_ap, d["kwinT"][g * 128:(g + 1) * 128, :], kw_b, reads=[self.db("kwinT", i) for i in range(8)], writes=[kw_b])
            vs_ap, vs_b = s.alloc("avs", SV * 2, BF16)
            vw_ap, vw_b = s.alloc("avw", SV * 2, BF16)
            for q in range(4):
                s.dma(sub3(vs_ap, q * 8 * 128, 128, 8, 1, 128), dview(d["vsw"], q * 1024, 1024, g * 128, 128), vs_b,
                      reads=[self.db("vsw", i) for i in range(8)], pwrites=[vs_b])
                s.dma(sub3(vw_ap, q * 8 * 128, 128, 8, 1, 128), dview(d["vsw"], q * 1024, 1024, 256 + g * 128, 128), vw_b,
                      reads=[self.db("vsw", i) for i in range(8)], pwrites=[vw_b])
            q_ap, q_b = s.alloc("aq", 6 * SO * 2, BF16)
            for p in range(6):
                s.dma(q_ap[:, p * SO:(p + 1) * SO], d["qT"][(g * 6 + p) * 128:(g * 6 + p + 1) * 128, :], q_b,
                      reads=[self.db("qT", i) for i in range(4)], pwrites=[q_b])
            for qt in range(4):
                t0v = SO + qt * 512
                njt = (t0v + 512) // 128
                bI = 5
                for p in range(6):
                    qr = q_ap[:, p * SO + qt * 512:p * SO + qt * 512 + 512]
                    pts = []
                    for ct in range(2):
                        pt, ptb = self.attn_unit(ctx, kc_ap[:, ct * 128:(ct + 1) * 128], [kc_b], qr, [q_b],
                                                 [(self.ident, mcmp[:, (qt * 2 + ct) * 512:(qt * 2 + ct + 1) * 512], [self.ident_b, mcmp_b])],
                                                 None, [], None, 3, ct == 0, ct == 1)
                        pts.append((pt, ptb))
                    r, rb = self.recip_den(ctx, 3)
                    pns = []
                    for ct in range(2):
                        pa, pb = pn[(p * 2 + ct) % 4]
                        s.add("pool", lambda e, o=pa, a=pts[ct][0], b=r: e.tensor_tensor(o, a, b, ALU.mult),
                              reads=[pts[ct][1], rb], writes=[pb])
                        pns.append((pa, pb))
                    for ct in range(2):
                        s.add("pe", lambda e, o=s.psum[4][:, :], l=vc_ap[:, ct * 128:(ct + 1) * 128], r_=pns[ct][0], st=(ct == 0), sp=(ct == 1):
                              e.matmul(o, l, r_, start=st, stop=sp), reads=[vc_b, pns[ct][1]], writes=[s.psbuf[4]])
                    for ct in range(2):
                        if p == 0:
                            s.add("pool", lambda e, o=ph[ct][0], a=pns[ct][0]: e.tensor_copy(o, a), reads=[pns[ct][1]], writes=[ph[ct][1]])
                        else:
                            s.add("pool", lambda e, o=ph[ct][0], a=pns[ct][0]: e.tensor_tensor(o, o, a, ALU.add),
                                  reads=[pns[ct][1], ph[ct][1]], writes=[ph[ct][1]])
                    hh = g * 6 + p
                    G, Gb = Gs[gi_ % 3]
                    gi_ += 1
                    bG = 6 + (gi_ % 2)
                    s.add("pe", lambda e, o=s.psum[bG][:, :], l=selmat[0:36, (hh * 3) * 128:(hh * 3 + 1) * 128], r_=gat[0:36, qt * 512:(qt + 1) * 512]:
                          e.matmul(o, l, r_, start=True, stop=True), reads=[selmat_b, gat_b], writes=[s.psbuf[bG]])
                    s.add("act", lambda e, o=G, i=s.psum[bG][:, :]: e.copy(o, i), reads=[s.psbuf[bG]], writes=[Gb])
                    s.add("dve", lambda e, o=ocs[p][0], a=s.psum[4][:, :], b=G: e.tensor_tensor(o, a, b, ALU.mult),
                          reads=[s.psbuf[4], Gb], writes=[ocs[p][1]])
                for qs in range(4):
                    for ct in range(2):
                        s.add("pe", lambda e, o=s.psum[bI][:, qs * 64:(qs + 1) * 64], l=ph[ct][0][:, qs * 128:(qs + 1) * 128],
                              r_=mmap[:, ct * 64:(ct + 1) * 64], st=(ct == 0), sp=(ct == 1):
                              e.matmul(o, l, r_, start=st, stop=sp), reads=[ph[ct][1], mmap_b], writes=[s.psbuf[bI]])
                s.add("dve", lambda e, o=sc_ap, a=s.psum[bI][:, 0:256], b=selM[:, qt * 256:(qt + 1) * 256]: e.tensor_tensor(o, a, b, ALU.mult),
                      reads=[s.psbuf[bI], selM_b], writes=[sc_b])
                s.add("dve", lambda e, o=sc_ap, b=selA[:, qt * 256:(qt + 1) * 256]: e.tensor_tensor(o, o, b, ALU.add),
                      reads=[sc_b, selA_b], writes=[sc_b])
                for qs in range(4):
                    scq = sc_ap[:, qs * 64:(qs + 1) * 64]
                    mm = m16[:, qs * 16:(qs + 1) * 16]
                    s.add("dve", lambda e, o=mm[:, 0:8], i=scq: e.max(o, i), reads=[sc_b], pwrites=[m16_b])
                    s.add("dve", lambda e, o=wk[:, qs * 64:(qs + 1) * 64], m=mm[:, 0:8], i=scq: e.match_replace(o, m, i, -3e9),
                          reads=[sc_b, m16_b], pwrites=[wk_b])
                    s.add("dve", lambda e, o=mm[:, 8:16], i=wk[:, qs * 64:(qs + 1) * 64]: e.max(o, i), reads=[wk_b, m16_b], pwrites=[m16_b])
                    s.add("dve", lambda e, o=mm[:, 15:16]: e.tensor_scalar(o, o, -5e8, None, ALU.max), reads=[m16_b], pwrites=[m16_b])
                    s.add("dve", lambda e, o=selb[:, qs * 64:(qs + 1) * 64], i=scq, t=mm[:, 15:16]: e.tensor_scalar(o, i, t, NEG, ALU.is_lt, ALU.mult),
                          reads=[sc_b, m16_b], pwrites=[selb_b])
                for qs in range(4):
                    s.add("pe", lambda e, o=psb[7][0:64, qs * 128:(qs + 1) * 128], i=selb[:, qs * 64:(qs + 1) * 64]: e.transpose(o, i, self.ident),
                          reads=[selb_b, self.ident_b], writes=[s.psbuf[7]])
                s.add("act", lambda e, o=selbT[0:64, :], i=psb[7][0:64, 0:512]: e.copy(o, i), reads=[s.psbuf[7]], writes=[selbT_b])
                for p in range(6):
                    qr = q_ap[:, p * SO + qt * 512:p * SO + qt * 512 + 512]
                    hh = g * 6 + p
                    units = []
                    for jt in range(njt):
                        ex = [(E[0:64, jt * 128:(jt + 1) * 128], selbT[0:64, :], [E_b, selbT_b])]
                        o_ = jt * 128 - t0v
                        if o_ >= 0:
                            mi = (o_ + 512) // 128
                            ex.append((self.ident, mwin[:, mi * 512:(mi + 1) * 512], [self.ident_b, mwin_b]))
                        units.append(dict(klhs=ks_ap[:, jt * 128:(jt + 1) * 128], krd=[ks_b], qrhs=qr, qrd=[q_b], extras=ex,
                                          vlhs=vs_ap[:, jt * 128:(jt + 1) * 128], vrd=[vs_b], bacc=3, bden=4, first=(jt == 0), last=(jt == njt - 1)))
                    jt0 = (t0v - 512) // 128
                    for jt in range(jt0, njt):
                        o_ = jt * 128 - t0v
                        mi = (o_ + 512) // 128
                        if qt == 0 and o_ < 0:
                            mt_ap, mt_b = mwin0[:, mi * 512:(mi + 1) * 512], mwin0_b
                        else:
                            mt_ap, mt_b = mwin[:, mi * 512:(mi + 1) * 512], mwin_b
                        units.append(dict(klhs=kw_ap[:, jt * 128:(jt + 1) * 128], krd=[kw_b], qrhs=qr, qrd=[q_b],
                                          extras=[(self.ident, mt_ap, [self.ident_b, mt_b])],
                                          vlhs=vw_ap[:, jt * 128:(jt + 1) * 128], vrd=[vw_b], bacc=5, bden=6, first=(jt == jt0), last=(jt == njt - 1)))
                    self.run_units(ctx, units)
                    ts = []
                    for bi, (bacc, bden) in enumerate(((3, 4), (5, 6))):
                        G, Gb = Gs[gi_ % 3]
                        gi_ += 1
                        s.add("pe", lambda e, o=s.psum[7][:, :], l=selmat[0:36, (hh * 3 + 1 + bi) * 128:(hh * 3 + 2 + bi) * 128],
                              r_=gat[0:36, qt * 512:(qt + 1) * 512]: e.matmul(o, l, r_, start=True, stop=True),
                              reads=[selmat_b, gat_b], writes=[s.psbuf[7]])
                        s.add("act", lambda e, o=G, i=s.psum[7][:, :]: e.copy(o, i), reads=[s.psbuf[7]], writes=[Gb])
                        r, rb = self.recip_den(ctx, bden)
                        f, fbb = fb[bi]
                        s.add("pool", lambda e, o=f, a=r, b=G: e.tensor_tensor(o, a, b, ALU.mult), reads=[rb, Gb], writes=[fbb])
                        t, tbb = tb[(oi * 2 + bi) % 4]
                        s.add("dve", lambda e, o=t, a=s.psum[bacc][:, :], b=f: e.tensor_tensor(o, a, b, ALU.mult),
                              reads=[s.psbuf[bacc], fbb], writes=[tbb])
                        ts.append((t, tbb))
                    o_ap, o_b = oring[oi % 3]
                    oi += 1
                    if "dbg_br" in d:
                        for bi_, (ap_, b_) in enumerate((ocs[p], ts[0], ts[1])):
                            s.dma(d["dbg_br"][bi_ * 1536 + hh * 128:bi_ * 1536 + (hh + 1) * 128, qt * 512:(qt + 1) * 512], ap_, b_, reads=[b_])
                    s.add("pool", lambda e, o=ts[0][0], a=ts[0][0], b=ocs[p][0]: e.tensor_tensor(o, a, b, ALU.add),
                          reads=[ocs[p][1], ts[0][1]], writes=[ts[0][1]])
                    s.add("pool", lambda e, o=o_ap, a=ts[0][0], b=ts[1][0]: e.tensor_tensor(o, a, b, ALU.add),
                          reads=[ts[0][1], ts[1][1]], writes=[o_b])
                    s.dma(d["oT"][hh * 128:(hh + 1) * 128, qt * 512:(qt + 1) * 512], o_ap, o_b, reads=[o_b], pwrites=[self.db("oT", qt)])
            s.release(mk)
        s.release(mk0)

    def stage_kvshared(self, w_kv, w_kv_rot):
        kb = self
        d = self.d
        panels = []
        for hp in range(2):
            segs = [(w_kv, hp * 256, 256), (w_kv_rot, hp * 256, 256)]
            jobs = [dict(cols=[(j * 128, 128), (256 + j * 128, 128)], dst=d["kshT"], dname="kshT", row0=(hp * 2 + j) * 128,
                         epi=self._rope_epi_own()) for j in range(2)]
            panels.append(dict(segs=segs, jobs=jobs))
        self.stage_lfm(d["kvnT"], "kvnT", 0, SO, 16, panels, self.rope_setup(SO, SO))
        self.stage_tm_bf16(d["kvnT"], "kvnT", 16, 0, SO, [(w_kv, 512, 512)], d["vsh"], "vsh")


NG = 8


def build_phase_a(debug=()):
    nc = bass.Bass("TRN2", target_bir_lowering=False)
    kb = KB(nc)
    I = kb.inp
    xv = I("xv", [SV, DM])
    memb = I("memb", [256, DM])
    I("cosT", [128, SV]); I("sinT", [128, SV]); I("gains", [128, NG * 16])
    I("c_ident", [128, 128]); I("c_ones", [128, 128])
    I("m_cmp", [128, 8 * 512]); I("m_win", [128, 8 * 512]); I("m_win0", [128, 4 * 512])
    I("c_E", [64, SV]); I("c_mmap", [128, 128]); I("selM", [128, 1024]); I("selA", [128, 1024]); I("c_selmat", [36, 36 * 128])
    w_in = I("a_w_in", [DM, 3620]); w_rot = I("a_w_rot", [DM, 2304]); gbias = I("a_gbias", [36, 1])
    w1k = I("a_w1k", [4096, 256]); w2k = I("a_w2k", [256, 128]); pek = I("a_pekT", [128, 32])
    w1v = I("a_w1v", [4096, 256]); w2v = I("a_w2v", [256, 128]); pev = I("a_pevT", [128, 32])
    wmkv = I("a_w_mem_kv", [DM, 1024]); wout = I("a_w_out", [DM, DM])
    wg = I("a_w_gate", [DM, DFF]); wu = I("a_w_up", [DM, DFF]); wd = I("a_w_down", [DFF, DM])
    wkv = I("w_kv", [DM, 1024]); wkvr = I("w_kv_rot", [DM, 512])

    def S(name, shape, dt):
        if name in debug:
            return kb.outp(name, shape, dt)
        return kb.scr(name, shape, dt)
    S("xnT", [DM, SV], BF16)
    S("qT", [1536, SO], BF16); S("kcmpT", [256, SV], BF16); S("vcmpT", [256, SV], BF16)
    S("kslcT", [256, SV], BF16); S("kwinT", [256, SV], BF16); S("vsw", [SV, 512], BF16)
    S("gatesT", [36, SO], F32); S("qmT", [512, SO], BF16)
    S("kcT", [256, 256], BF16); S("vc", [512, 128], BF16)
    S("mkT", [512, 256], BF16); S("mv", [256, 512], BF16)
    S("oT", [DM, SO], BF16); S("h1", [SO, DM], F32); S("hnT", [DM, SO], BF16); S("hidT", [DFF, SO], BF16)
    kb.outp("h2", [SO, DM], F32); S("htmp", [SO, DM], F32)
    S("kvnT", [DM, SO], BF16)
    kb.outp("kshT", [512, SO], BF16); kb.outp("vsh", [SO, 512], BF16)
    if "dbg_br" in debug:
        kb.outp("dbg_br", [3 * 1536, SO], F32)
    d = kb.d
    kb.consts(NG)
    stop = kb.stop_after if hasattr(kb, "stop_after") else None
    kb.stage_norm(xv, None, SV, [0], [(d["xnT"], "xnT", 0)])
    kb.stage_inproj_a(w_in, w_rot, gbias)
    kb.stage_tm_bf16(d["xnT"], "xnT", 16, 0, SV, [(w_in, 2304, 256), (w_in, 2816, 256)], d["vsw"], "vsw")
    kb.stage_cmp(w1k, w2k, pek, w1v, w2v, pev)
    kb.stage_memkv(memb, 1, wmkv)
    kb.stage_attn_a()
    kb.stage_mem_attn(d["qmT"], "qmT", d["mkT"], d["mv"], d["oT"], "oT", 12)
    kb.stage_down(d["oT"], "oT", 16, wout, xv[SO:SV, :], None, d["h1"], "h1", TB=2048)
    kb.stage_norm(d["h1"], "h1", SO, [2], [(d["hnT"], "hnT", 0)])
    kb.stage_ffn(d["hnT"], "hnT", wg, wu, wd, d["h1"], "h1", d["h2"], "h2")
    kb.stage_norm(d["h2"], "h2", SO, [3], [(d["kvnT"], "kvnT", 0)])
    kb.stage_kvshared(wkv, wkvr)
    kb.s.finalize()
    return nc, kb


def rope_tabs(half):
    inv = (1.0 / (10000.0 ** (np.arange(0, 128, 2, dtype=np.float32) / 128))).astype(np.float32)
    pos = np.arange(SV, dtype=np.float32) - (0 if half == 1 else SO)
    pos = np.maximum(pos, 0).astype(np.float32)
    ang = (pos[:, None] * inv[None, :]).astype(np.float32)
    c = np.cos(ang).astype(np.float32).T
    sn = np.sin(ang).astype(np.float32).T
    cosT = np.concatenate([c, c], 0)
    sinT = np.concatenate([-sn, sn], 0)
    return np.ascontiguousarray(cosT), np.ascontiguousarray(sinT)


def rot_cols(w, heads):
    outs = []
    for c0 in heads:
        outs.append(w[:, c0 + 64:c0 + 128])
        outs.append(w[:, c0:c0 + 64])
    return np.ascontiguousarray(np.concatenate(outs, 1))


def gain_arr(gs):
    return np.ascontiguousarray(np.concatenate([g.reshape(16, 128).T for g in gs], 1).astype(np.float32))


def band_mask(o, w, prevmask):
    jj = np.arange(128)[:, None]
    qq = np.arange(512)[None, :]
    dist = qq - jj - o
    m = np.where((dist >= 0) & (dist <= w), 0.0, NEG).astype(np.float32)
    if prevmask:
        m[:] = NEG
    return m


def attn_consts(half):
    out = {}
    m_cmp = np.zeros((128, 8, 512), np.float32)
    for qt in range(4):
        for ct in range(2):
            c = ct * 128 + np.arange(128)[:, None]
            t = SO + qt * 512 + np.arange(512)[None, :]
            valid = (16 * c + 31 <= t) & (c <= 254)
            if half == 0:
                valid &= (c >= 128)
            m_cmp[:, qt * 2 + ct, :] = np.where(valid, 0.0, NEG)
    out["m_cmp"] = m_cmp.reshape(128, -1)
    mw = np.zeros((128, 8, 512), np.float32)
    for mi in range(8):
        mw[:, mi, :] = band_mask(mi * 128 - 512, 511, False)
    out["m_win"] = mw.reshape(128, -1)
    mw0 = np.zeros((128, 4, 512), np.float32)
    for mi in range(4):
        mw0[:, mi, :] = band_mask(mi * 128 - 512, 511, half == 0)
    out["m_win0"] = mw0.reshape(128, -1)
    selM = np.zeros((128, 4, 4, 64), np.float32)
    selA = np.zeros((128, 4, 4, 64), np.float32)
    sblk = np.arange(64)[None, :]
    first = 0 if half == 1 else 32
    for qt in range(4):
        for qs in range(4):
            t = SO + qt * 512 + qs * 128 + np.arange(128)[:, None]
            cur = t // 64
            elig = (sblk * 64 <= t) & (sblk >= first)
            f0 = (sblk == first) & elig
            f1 = (sblk == cur)
            f2 = (sblk == cur - 1) & (sblk >= first)
            A = np.where(elig, 0.0, -1e9)
            A = np.where(f0, 1e9, A)
            A = np.where(f2, 2e9, A)
            A = np.where(f1, 3e9, A)
            M = (elig & ~f0 & ~f1 & ~f2).astype(np.float32)
            selM[:, qt, qs, :] = M
            selA[:, qt, qs, :] = A
    out["selM"] = selM.reshape(128, -1)
    out["selA"] = selA.reshape(128, -1)
    return out


def shared_consts():
    out = {}
    out["c_ident"] = np.eye(128, dtype=np.float32)
    out["c_ones"] = np.ones((128, 128), np.float32)
    E = np.zeros((64, SV), np.float32)
    E[np.arange(SV) // 64, np.arange(SV)] = 1.0
    out["c_E"] = E
    mm = np.zeros((2, 128, 64), np.float32)
    for c in range(255):
        for sb in range(64):
            if (16 * c < 64 * sb + 64) and (16 * c + 32 > 64 * sb):
                mm[c // 128, c % 128, sb] = 1.0
    out["c_mmap"] = np.ascontiguousarray(mm.transpose(1, 0, 2).reshape(128, 128))
    sm = np.zeros((36, 36, 128), np.float32)
    for i in range(36):
        sm[i, i, :] = 1.0
    out["c_selmat"] = sm.reshape(36, -1)
    return out


def phase_a_inputs(inp, b, half, sc):
    f = np.float32
    x = inp["x"][b]
    if half == 1:
        xv = x
    else:
        xv = np.concatenate([np.zeros((SO, DM), f), x[:SO]], 0)
    cosT, sinT = rope_tabs(half)
    m = dict(sc)
    m.update(attn_consts(half))
    m["xv"] = np.ascontiguousarray(xv)
    m["memb"] = np.ascontiguousarray(inp["mem"][b])
    m["cosT"] = cosT
    m["sinT"] = sinT
    return m


def weights_a(inp):
    w = {}
    w_in = inp["a_w_in"][0]
    w["a_w_in"] = w_in
    heads = [h * 128 for h in range(12)] + [1536 + (i * 2 + g) * 128 for i in (0, 2, 4) for g in range(2)]
    w["a_w_rot"] = rot_cols(w_in, heads)
    w["a_gbias"] = np.ascontiguousarray(inp["a_gate_bias"][0].reshape(36, 1))
    w["a_w1k"] = inp["a_cmp_w1_k"][0]; w["a_w2k"] = inp["a_cmp_w2_k"][0]
    w["a_pekT"] = np.ascontiguousarray(inp["a_cmp_pe_k"][0].T)
    w["a_w1v"] = inp["a_cmp_w1_v"][0]; w["a_w2v"] = inp["a_cmp_w2_v"][0]
    w["a_pevT"] = np.ascontiguousarray(inp["a_cmp_pe_v"][0].T)
    w["a_w_mem_kv"] = inp["a_w_mem_kv"][0]; w["a_w_out"] = inp["a_w_out"][0]
    w["a_w_gate"] = inp["a_w_gate"][0]; w["a_w_up"] = inp["a_w_up"][0]; w["a_w_down"] = inp["a_w_down"][0]
    w["w_kv"] = inp["w_kv_shared"]
    w["w_kv_rot"] = rot_cols(inp["w_kv_shared"], [h * 128 for h in range(4)])
    w["gains"] = gain_arr([inp["a_norm_attn"][0], inp["a_norm_mem"][0], inp["a_norm_ffn"][0], inp["kv_norm"],
                           inp["b_norm_attn"][0], inp["b_norm_mem"][0], inp["b_norm_ffn"][0], inp["final_norm"]])
    return {k: np.ascontiguousarray(np.asarray(v, dtype=np.float32)) for k, v in w.items()}


def _kb_stage_attn_b(self):
    s = self.s
    d = self.d
    mk0 = s.mark()
    mdil, mdil_b = s.alloc("mdil", 5 * 512 * 2, BF16)
    s.dma(mdil, d["m_dil"], mdil_b, writes=[mdil_b], q="pool")
    mdil0, mdil0_b = s.alloc("mdil0", 512 * 2, BF16)
    s.dma(mdil0, d["m_dil0"], mdil0_b, writes=[mdil0_b], q="pool")
    ctx = dict(si=0, pi=0, ri=0, sbanks=[0, 1, 2], pt=[s.alloc("bpt%d" % i, 512 * 2, BF16) for i in range(4)],
               rd=[s.alloc("brd%d" % i, 512 * 4) for i in range(2)])
    oring = [s.alloc("bor%d" % i, 512 * 2, BF16) for i in range(3)]
    oi = 0
    ui = 0
    for hh in range(4):
        mk = s.mark()
        k_ap, k_b = s.alloc("bk", SV * 2, BF16)
        kown = [self.db("kshT", i) for i in range(4)]
        vown = [self.db("vsh", i) for i in range(4)]
        s.dma(k_ap[:, 0:SO], d["kg"][hh * 128:(hh + 1) * 128, :], k_b, reads=self.rd("kg"), pwrites=[k_b])
        s.dma(k_ap[:, SO:SV], d["kshT"][hh * 128:(hh + 1) * 128, :], k_b, reads=kown, pwrites=[k_b])
        v1, v1_b = s.alloc("bv1", SV * 2, BF16)
        v4, v4_b = s.alloc("bv4", SV * 2, BF16)
        v16, v16_b = s.alloc("bv16", SV * 2, BF16)
        for pi_, (vsrc, rds) in enumerate(((d["vg"][0:SO, :], self.rd("vg")), (d["vsh"], vown))):
            vcol = vsrc[:, hh * 128:(hh + 1) * 128]
            for q in range(2):
                s.dma(sub3(v1, (pi_ * 16 + q * 8) * 128, 128, 8, 1, 128), dview(vsrc, q * 1024, 1024, hh * 128, 128), v1_b,
                      reads=rds, pwrites=[v1_b])
            r4 = vcol.rearrange("(jt p r) c -> r p jt c", p=128, r=4)
            for rho in range(4):
                s.dma(sub3(v4, rho * 1024 + pi_ * 4 * 128, 128, 4, 1, 128), r4[rho], v4_b, reads=rds, pwrites=[v4_b])
            r16 = vcol.rearrange("(jt p r) c -> r p jt c", p=128, r=16)
            for rho in range(16):
                s.dma(sub3(v16, rho * 256 + pi_ * 128, 128, 1, 1, 128), r16[rho], v16_b, reads=rds, pwrites=[v16_b])
        q_ap, q_b = s.alloc("bq", 3 * SO * 2, BF16)
        for g in range(3):
            s.dma(q_ap[:, g * SO:(g + 1) * SO], d["qbT"][(g * 4 + hh) * 128:(g * 4 + hh + 1) * 128, :], q_b,
                  reads=[self.db("qbT", i) for i in range(4)], pwrites=[q_b])
        accS, accS_b = s.alloc("bacc", SO * 4)
        denS, denS_b = s.alloc("bden", SO * 4)

        def flush(bacc, bden, n, oa, od, first):
            if first:
                s.add("dve", lambda e, o=oa, i=s.psum[bacc][:, 0:n]: e.tensor_copy(o, i), reads=[s.psbuf[bacc]], pwrites=[accS_b])
                s.add("dve", lambda e, o=od, i=s.psum[bden][:, 0:n]: e.tensor_copy(o, i), reads=[s.psbuf[bden]], pwrites=[denS_b])
            else:
                s.add("dve", lambda e, o=oa, i=s.psum[bacc][:, 0:n]: e.tensor_tensor(o, o, i, ALU.add), reads=[s.psbuf[bacc], accS_b], pwrites=[accS_b])
                s.add("dve", lambda e, o=od, i=s.psum[bden][:, 0:n]: e.tensor_tensor(o, o, i, ALU.add), reads=[s.psbuf[bden], denS_b], pwrites=[denS_b])

        units = []

        def mkpost(bacc, bden, n, oa, od, first):
            return lambda: flush(bacc, bden, n, oa, od, first)
        for qt in range(4):
            t0v = SO + qt * 512
            bacc, bden = (3, 4) if (ui % 2 == 0) else (5, 6)
            ui += 1
            offs = [-128, 0, 128, 256, 384]
            for i, o_ in enumerate(offs):
                jt = (t0v + o_) // 128
                if qt == 0 and o_ < 0:
                    m_ap, m_b = mdil0, mdil0_b
                else:
                    m_ap, m_b = mdil[:, i * 512:(i + 1) * 512], mdil_b
                units.append(dict(klhs=k_ap[:, jt * 128:(jt + 1) * 128], krd=[k_b], qrhs=q_ap[:, qt * 512:(qt + 1) * 512], qrd=[q_b],
                                  extras=[(self.ident, m_ap, [self.ident_b, m_b])], vlhs=v1[:, jt * 128:(jt + 1) * 128], vrd=[v1_b],
                                  bacc=bacc, bden=bden, first=(i == 0), last=(i == 4),
                                  post=(mkpost(bacc, bden, 512, accS[:, qt * 512:(qt + 1) * 512], denS[:, qt * 512:(qt + 1) * 512], True) if i == 4 else None)))
        for rho in range(4):
            bacc, bden = (3, 4) if (ui % 2 == 0) else (5, 6)
            ui += 1
            qr = q_ap[:, SO + rho:SO + SO:4]
            offs = [-128, 0, 128, 256, 384]
            for i, o_ in enumerate(offs):
                ju0 = 512 + o_
                if o_ < 0:
                    m_ap, m_b = mdil0, mdil0_b
                else:
                    m_ap, m_b = mdil[:, i * 512:(i + 1) * 512], mdil_b
                kl = k_ap[:, rho + 4 * ju0:rho + 4 * (ju0 + 127) + 1:4]
                jtu = ju0 // 128
                units.append(dict(klhs=kl, krd=[k_b], qrhs=qr, qrd=[q_b], extras=[(self.ident, m_ap, [self.ident_b, m_b])],
                                  vlhs=v4[:, rho * 1024 + jtu * 128:rho * 1024 + (jtu + 1) * 128], vrd=[v4_b],
                                  bacc=bacc, bden=bden, first=(i == 0), last=(i == 4),
                                  post=(mkpost(bacc, bden, 512, accS[:, rho:SO:4], denS[:, rho:SO:4], False) if i == 4 else None)))
        for rho in range(16):
            bacc, bden = (3, 4) if (ui % 2 == 0) else (5, 6)
            ui += 1
            qr = q_ap[:, 2 * SO + rho:3 * SO:16]
            for i, o_ in enumerate([-128, 0]):
                ju0 = 128 + o_
                if o_ < 0:
                    m_ap, m_b = mdil0[:, 0:128], mdil0_b
                else:
                    m_ap, m_b = mdil[:, 512:512 + 128], mdil_b
                kl = k_ap[:, rho + 16 * ju0:rho + 16 * (ju0 + 127) + 1:16]
                jtu = ju0 // 128
                units.append(dict(klhs=kl, krd=[k_b], qrhs=qr, qrd=[q_b], extras=[(self.ident, m_ap, [self.ident_b, m_b])],
                                  vlhs=v16[:, rho * 256 + jtu * 128:rho * 256 + (jtu + 1) * 128], vrd=[v16_b],
                                  bacc=bacc, bden=bden, first=(i == 0), last=(i == 1), n=128,
                                  post=(mkpost(bacc, bden, 128, accS[:, rho:SO:16], denS[:, rho:SO:16], False) if i == 1 else None)))
        self.run_units(ctx, units)
        s.add("dve", lambda e: e.tensor_scalar(denS, denS, TINY, None, ALU.max), reads=[denS_b], writes=[denS_b])
        s.add("dve", lambda e: e.reciprocal(denS, denS), reads=[denS_b], writes=[denS_b])
        for qt in range(4):
            o_ap, o_b = oring[oi % 3]
            oi += 1
            s.add("dve", lambda e, o=o_ap, a=accS[:, qt * 512:(qt + 1) * 512], b=denS[:, qt * 512:(qt + 1) * 512]: e.tensor_tensor(o, a, b, ALU.mult),
                  reads=[accS_b, denS_b], writes=[o_b])
            s.dma(d["oT"][hh * 128:(hh + 1) * 128, qt * 512:(qt + 1) * 512], o_ap, o_b, reads=[o_b], pwrites=[self.db("oT", qt)])
        s.release(mk)
    s.release(mk0)


KB.stage_attn_b = _kb_stage_attn_b


def _kb_stage_inproj_b(self, w_in, w_rot):
    kb = self
    d = self.d
    panels = []
    for hp in range(6):
        segs = [(w_in, hp * 256, 256), (w_rot, hp * 256, 256)]
        jobs = [dict(cols=[(j * 128, 128), (256 + j * 128, 128)], dst=d["qbT"], dname="qbT", row0=(hp * 2 + j) * 128,
                     epi=self._rope_epi_own()) for j in range(2)]
        panels.append(dict(segs=segs, jobs=jobs))
    segs = [(w_in, 1536, 512)]
    jobs = [dict(cols=[(j * 128, 128)], row0=j * 128, epi=kb.epi_plain_fm(d["qmT"], "qmT", None, 0)) for j in range(4)]
    panels.append(dict(segs=segs, jobs=jobs))
    self.stage_lfm(d["bnT"], "bnT", 0, SO, 16, panels, self.rope_setup(SO, SO))


KB.stage_inproj_b = _kb_stage_inproj_b


def _kb_stage_final_norm(self, src, srcname, dst):
    s = self.s
    mk = s.mark()
    fg, fg_b = s.alloc("fg", DM * 4)
    s.dma(fg, self.d["fgain"], fg_b, writes=[fg_b])
    hb = [s.alloc("fh%d" % i, DM * 4) for i in range(3)]
    ob = [s.alloc("fo%d" % i, DM * 4) for i in range(2)]
    junk_ap, junk_b = s.alloc("fjunk", DM * 2, BF16)
    st = [s.alloc("fst%d" % i, 4 * 4) for i in range(3)]
    for it in range(SO // 128):
        h_ap, h_b = hb[it % 3]
        st_ap, st_b = st[it % 3]
        o_ap, o_b = ob[it % 2]
        s.dma(h_ap, src[it * 128:(it + 1) * 128, :], h_b, reads=self.rd(srcname, it // 4), writes=[h_b])
        s.add("act", lambda e, h=h_ap, o=st_ap[:, 0:1]: e.activation(junk_ap, h, AF.Square, accum_out=o), reads=[h_b], pwrites=[junk_b, st_b])
        s.add("dve", lambda e, a=st_ap: e.tensor_scalar(a[:, 1:2], a[:, 0:1], 1.0 / 2048, EPS, ALU.mult, ALU.add), reads=[st_b], pwrites=[st_b])
        s.add("act", lambda e, a=st_ap: e.sqrt(a[:, 1:2], a[:, 1:2]), reads=[st_b], pwrites=[st_b])
        s.add("dve", lambda e, a=st_ap: e.reciprocal(a[:, 2:3], a[:, 1:2]), reads=[st_b], pwrites=[st_b])
        s.add("act", lambda e, o=o_ap, h=h_ap, sc=st_ap[:, 2:3]: e.activation(o, h, AF.Copy, scale=sc), reads=[h_b, st_b], writes=[o_b])
        s.add("dve", lambda e, o=o_ap: e.tensor_tensor(o, o, fg, ALU.mult), reads=[o_b, fg_b], writes=[o_b])
        s.dma(dst[it * 128:(it + 1) * 128, :], o_ap, o_b, reads=[o_b])
    s.release(mk)


KB.stage_final_norm = _kb_stage_final_norm


def build_phase_b(debug=()):
    nc = bass.Bass("TRN2", target_bir_lowering=False)
    kb = KB(nc)
    I = kb.inp
    h2 = I("h2in", [SO, DM])
    memb = I("memb", [256, DM])
    I("kshTv", [512, SV], BF16); I("vshv", [SV, 512], BF16)
    I("cosT", [128, SV]); I("sinT", [128, SV]); I("gains", [128, NG * 16]); I("fgain", [128, DM])
    I("c_ident", [128, 128]); I("c_ones", [128, 128])
    I("m_dil", [128, 5 * 512]); I("m_dil0", [128, 512])
    w_in = I("b_w_in", [DM, 2048]); w_rot = I("b_w_rot", [DM, 1536])
    wmkv = I("b_w_mem_kv", [DM, 1024]); wout = I("b_w_out", [1024, DM])
    wg = I("b_w_gate", [DM, DFF]); wu = I("b_w_up", [DM, DFF]); wd = I("b_w_down", [DFF, DM])

    def S(name, shape, dt):
        if name in debug:
            return kb.outp(name, shape, dt)
        return kb.scr(name, shape, dt)
    S("bnT", [DM, SO], BF16); S("qbT", [1536, SO], BF16); S("qmT", [512, SO], BF16)
    S("mkT", [512, 256], BF16); S("mv", [256, 512], BF16)
    S("oT", [1024, SO], BF16); S("h3", [SO, DM], F32); S("hnT", [DM, SO], BF16); S("hidT", [DFF, SO], BF16)
    S("h4", [SO, DM], F32)
    kb.outp("out", [SO, DM], F32)
    d = kb.d
    kb.consts(NG)
    kb.stage_norm(h2, None, SO, [4], [(d["bnT"], "bnT", 0)])
    kb.stage_inproj_b(w_in, w_rot)
    kb.stage_memkv(memb, 5, wmkv)
    kb.stage_attn_b()
    kb.stage_mem_attn(d["qmT"], "qmT", d["mkT"], d["mv"], d["oT"], "oT", 4)
    kb.stage_down(d["oT"], "oT", 8, wout, h2, None, d["h3"], "h3", TB=2048)
    kb.stage_norm(d["h3"], "h3", SO, [6], [(d["hnT"], "hnT", 0)])
    kb.stage_ffn(d["hnT"], "hnT", wg, wu, wd, d["h3"], "h3", d["h4"], "h4")
    kb.stage_final_norm(d["h4"], "h4", d["out"])
    kb.s.finalize()
    return nc, kb


def dil_consts(half):
    out = {}
    md = np.zeros((128, 5, 512), np.float32)
    for i, o_ in enumerate([-128, 0, 128, 256, 384]):
        md[:, i, :] = band_mask(o_, 128, False)
    out["m_dil"] = md.reshape(128, -1)
    out["m_dil0"] = band_mask(-128, 128, half == 0)
    return out


def weights_b(inp):
    w = {}
    w_in = inp["b_w_in"][0]
    w["b_w_in"] = w_in
    w["b_w_rot"] = rot_cols(w_in, [h * 128 for h in range(12)])
    w["b_w_mem_kv"] = inp["b_w_mem_kv"][0]; w["b_w_out"] = inp["b_w_out"][0]
    w["b_w_gate"] = inp["b_w_gate"][0]; w["b_w_up"] = inp["b_w_up"][0]; w["b_w_down"] = inp["b_w_down"][0]
    w["fgain"] = np.broadcast_to(inp["final_norm"][None, :], (128, DM))
    return {k: np.ascontiguousarray(np.asarray(v, dtype=np.float32)) for k, v in w.items()}


_PROG = {}


def kernel(**inputs):
    inp = {k: np.asarray(v) for k, v in inputs.items()}
    import ml_dtypes
    bf = ml_dtypes.bfloat16
    if "a" not in _PROG:
        _PROG["a"] = build_phase_a()
        _PROG["b"] = build_phase_b()
    nca, _ = _PROG["a"]
    ncb, _ = _PROG["b"]
    sc = shared_consts()
    wa = weights_a(inp)
    maps = []
    for c in range(8):
        m = phase_a_inputs(inp, c // 2, c % 2, sc)
        m.update(wa)
        maps.append(m)
    ra = run_bass_kernel_spmd(nca, maps, core_ids=list(range(8))).results
    del maps
    wb = weights_b(inp)
    maps = []
    for c in range(8):
        b, half = c // 2, c % 2
        m = {}
        m["h2in"] = np.ascontiguousarray(ra[c]["h2"])
        ksh = np.asarray(ra[c]["kshT"])
        vsh = np.asarray(ra[c]["vsh"])
        if half == 1:
            kprev = np.asarray(ra[c - 1]["kshT"]); vprev = np.asarray(ra[c - 1]["vsh"])
        else:
            kprev = np.zeros_like(ksh); vprev = np.zeros_like(vsh)
        m["kshTv"] = np.ascontiguousarray(np.concatenate([kprev, ksh], 1))
        m["vshv"] = np.ascontiguousarray(np.concatenate([vprev, vsh], 0))
        m["memb"] = np.ascontiguousarray(inp["mem"][b])
        cosT, sinT = rope_tabs(half)
        m["cosT"] = cosT; m["sinT"] = sinT
        m["gains"] = wa["gains"]
        m["c_ident"] = sc["c_ident"]; m["c_ones"] = sc["c_ones"]
        m.update(dil_consts(half))
        m.update(wb)
        maps.append(m)
    rb = run_bass_kernel_spmd(ncb, maps, core_ids=list(range(8))).results
    out = np.zeros((4, 4096, DM), np.float32)
    for c in range(8):
        b, half = c // 2, c % 2
        out[b, half * SO:(half + 1) * SO, :] = rb[c]["out"]
    return out


def build_fused(debug=()):
    nc = bass.Bass("TRN2", target_bir_lowering=False, num_devices=8)
    kb = KB(nc)
    I = kb.inp
    xv = I("xv", [SV, DM])
    memb = I("memb", [256, DM])
    I("cosT", [128, SV]); I("sinT", [128, SV]); I("gains", [128, NG * 16]); I("fgain", [128, DM])
    I("c_ident", [128, 128]); I("c_ones", [128, 128])
    I("m_cmp", [128, 8 * 512]); I("m_win", [128, 8 * 512]); I("m_win0", [128, 4 * 512])
    I("c_E", [64, SV]); I("c_mmap", [128, 128]); I("selM", [128, 1024]); I("selA", [128, 1024]); I("c_selmat", [36, 36 * 128])
    I("m_dil", [128, 5 * 512]); I("m_dil0", [128, 512])
    w_in = I("a_w_in", [DM, 3620]); w_rot = I("a_w_rot", [DM, 2304]); gbias = I("a_gbias", [36, 1])
    w1k = I("a_w1k", [4096, 256]); w2k = I("a_w2k", [256, 128]); pek = I("a_pekT", [128, 32])
    w1v = I("a_w1v", [4096, 256]); w2v = I("a_w2v", [256, 128]); pev = I("a_pevT", [128, 32])
    wmkv = I("a_w_mem_kv", [DM, 1024]); wout = I("a_w_out", [DM, DM])
    wg = I("a_w_gate", [DM, DFF]); wu = I("a_w_up", [DM, DFF]); wd = I("a_w_down", [DFF, DM])
    wkv = I("w_kv", [DM, 1024]); wkvr = I("w_kv_rot", [DM, 512])
    bw_in = I("b_w_in", [DM, 2048]); bw_rot = I("b_w_rot", [DM, 1536])
    bwmkv = I("b_w_mem_kv", [DM, 1024]); bwout = I("b_w_out", [1024, DM])
    bwg = I("b_w_gate", [DM, DFF]); bwu = I("b_w_up", [DM, DFF]); bwd = I("b_w_down", [DFF, DM])
    S = kb.scr
    S("xnT", [DM, SV], BF16)
    S("qT", [1536, SO], BF16); S("kcmpT", [256, SV], BF16); S("vcmpT", [256, SV], BF16)
    S("kslcT", [256, SV], BF16); S("kwinT", [256, SV], BF16); S("vsw", [SV, 512], BF16)
    S("gatesT", [36, SO], F32); S("qmT", [512, SO], BF16)
    S("kcT", [256, 256], BF16); S("vc", [512, 128], BF16)
    S("mkT", [512, 256], BF16); S("mv", [256, 512], BF16)
    S("oT", [DM, SO], BF16); S("h1", [SO, DM], F32); S("hnT", [DM, SO], BF16); S("hidT", [DFF, SO], BF16)
    S("h2", [SO, DM], F32); S("htmp", [SO, DM], F32)
    S("kvnT", [DM, SO], BF16); S("bnT", [DM, SO], BF16)
    S("kshT", [512, SO], BF16); S("vsh", [SO, 512], BF16)
    S("kg", [1024, SO], BF16); S("vg", [2 * SO, 512], BF16)
    S("qbT", [1536, SO], BF16); S("h3", [SO, DM], F32); S("h4", [SO, DM], F32)
    kb.outp("out", [SO, DM], F32)
    d = kb.d
    kb.consts(NG)
    kb.stage_norm(xv, None, SV, [0], [(d["xnT"], "xnT", 0)])
    kb.stage_inproj_a(w_in, w_rot, gbias)
    kb.stage_tm_bf16(d["xnT"], "xnT", 16, 0, SV, [(w_in, 2304, 256), (w_in, 2816, 256)], d["vsw"], "vsw")
    kb.stage_cmp(w1k, w2k, pek, w1v, w2v, pev)
    kb.stage_memkv(memb, 1, wmkv)
    kb.stage_attn_a()
    kb.stage_mem_attn(d["qmT"], "qmT", d["mkT"], d["mv"], d["oT"], "oT", 12)
    kb.stage_down(d["oT"], "oT", 16, wout, xv[SO:SV, :], None, d["h1"], "h1", TB=2048)
    kb.stage_norm(d["h1"], "h1", SO, [2], [(d["hnT"], "hnT", 0)])
    kb.stage_ffn(d["hnT"], "hnT", wg, wu, wd, d["h1"], "h1", d["h2"], "h2")
    kb.stage_norm(d["h2"], "h2", SO, [3, 4], [(d["kvnT"], "kvnT", 0), (d["bnT"], "bnT", 0)])
    kb.stage_kvshared(wkv, wkvr)
    groups = [[0, 1], [2, 3], [4, 5], [6, 7]]
    kb.s.collective(lambda e: e.collective_compute("AllGather", ALU.bypass, replica_groups=groups, ins=[d["kshT"]], outs=[d["kg"]]),
                    reads=[kb.db("kshT", i) for i in range(4)], writes=[kb.db("kg")])
    kb.s.collective(lambda e: e.collective_compute("AllGather", ALU.bypass, replica_groups=groups, ins=[d["vsh"]], outs=[d["vg"]]),
                    reads=[kb.db("vsh", i) for i in range(4)], writes=[kb.db("vg")])
    kb.stage_inproj_b(bw_in, bw_rot)
    kb.stage_memkv(memb, 5, bwmkv)
    kb.stage_attn_b()
    kb.stage_mem_attn(d["qmT"], "qmT", d["mkT"], d["mv"], d["oT"], "oT", 4)
    kb.stage_down(d["oT"], "oT", 8, bwout, d["h2"], "h2", d["h3"], "h3", TB=2048)
    kb.stage_norm(d["h3"], "h3", SO, [6], [(d["hnT"], "hnT", 0)])
    kb.stage_ffn(d["hnT"], "hnT", bwg, bwu, bwd, d["h3"], "h3", d["h4"], "h4")
    kb.stage_final_norm(d["h4"], "h4", d["out"])
    kb.s.finalize()
    return nc, kb


def kernel(**inputs):
    inp = {k: np.asarray(v) for k, v in inputs.items()}
    if "f" not in _PROG:
        _PROG["f"] = build_fused()
    nc, _ = _PROG["f"]
    sc = shared_consts()
    wa = weights_a(inp)
    wb = weights_b(inp)
    maps = []
    for c in range(8):
        b, half = c // 2, c % 2
        m = phase_a_inputs(inp, b, half, sc)
        m.update(dil_consts(half))
        m.update(wa)
        m.update(wb)
        maps.append(m)
    res = run_bass_kernel_spmd(nc, maps, core_ids=list(range(8))).results
    out = np.zeros((4, 4096, DM), np.float32)
    for c in range(8):
        b, half = c // 2, c % 2
        out[b, half * SO:(half + 1) * SO, :] = res[c]["out"]
    return out
```

```python
import numpy as np
import concourse.bass as bass
import concourse.mybir as mybir
from concourse.bass_utils import run_bass_kernel_spmd

F32 = mybir.dt.float32
BF16 = mybir.dt.bfloat16
AF = mybir.ActivationFunctionType
ALU = mybir.AluOpType
AX = mybir.AxisListType


ENGS = ("pe", "act", "dve", "pool", "sp")


class Buf:
    __slots__ = ("name", "w", "r", "sem", "ndma", "lo", "hi", "space")

    def __init__(self, name, space="sb", lo=0, hi=0):
        self.name = name
        self.w = []
        self.r = []
        self.sem = None
        self.ndma = 0
        self.lo = lo
        self.hi = hi
        self.space = space


class DSem:
    __slots__ = ("handle", "ndma", "idx", "inc")

    def __init__(self, idx, inc=16):
        self.handle = None
        self.ndma = 0
        self.idx = idx
        self.inc = inc


class Op:
    __slots__ = ("eng", "idx", "fn", "waits", "flagged", "rank", "dma_buf", "pe_group")

    def __init__(self, eng, idx, fn):
        self.eng = eng
        self.idx = idx
        self.fn = fn
        self.waits = {}
        self.flagged = False
        self.rank = 0
        self.dma_buf = None


class Sched:
    def __init__(self, nc, arena_bytes=200 * 1024):
        self.nc = nc
        self.ops = {e: [] for e in ENGS}
        self.waited = {e: {} for e in ENGS}
        self.arena_bytes = arena_bytes
        self.arena = nc.alloc_sbuf_tensor("arena", [128, arena_bytes // 4], F32)
        self.arena_top = 0
        self.live = []
        self.retired = []
        self.psum = [nc.alloc_psum_tensor("ps%d" % i, [128, 512], F32) for i in range(8)]
        self.psbuf = [Buf("ps%d" % i, "ps") for i in range(8)]
        self.nsem = 0
        self.eng_sem = {}
        self.dma_rr = 0
        self.NPOOL = 90
        self.pool = [DSem(i) for i in range(self.NPOOL)]
        self.cc_sem = DSem(1000, inc=1)

    def alloc(self, name, nbytes, dtype=F32):
        req = nbytes
        nbytes = (nbytes + 31) // 32 * 32
        lo = self.arena_top
        hi = lo + nbytes
        assert hi <= self.arena_bytes, "arena overflow %s: %d > %d" % (name, hi, self.arena_bytes)
        self.arena_top = hi
        b = Buf(name, "sb", lo, hi)
        keep = []
        for rb in self.retired:
            if rb.lo < hi and lo < rb.hi:
                b.r.extend(rb.w)
                b.r.extend(rb.r)
                if rb.lo < lo or rb.hi > hi:
                    keep.append(rb)
            else:
                keep.append(rb)
        self.retired = keep
        dd = {}
        for dep in b.r:
            if dep[0] == "e":
                k = ("e", dep[1].eng)
                if k not in dd or dd[k][1].idx < dep[1].idx:
                    dd[k] = dep
            else:
                dd[("d", dep[1].idx)] = dep
        b.r = list(dd.values())
        self.live.append(b)
        ap = self.arena[:, lo // 4:hi // 4]
        if dtype != F32:
            ap = ap.bitcast(dtype)
            ap = ap[:, 0:req // 2]
        else:
            ap = ap[:, 0:req // 4]
        return ap, b

    def mark(self):
        return (self.arena_top, len(self.live))

    def release(self, mark):
        top, n = mark
        for b in self.live[n:]:
            self.retired.append(b)
        self.live = self.live[:n]
        self.arena_top = top

    def _dep_of(self, op):
        if op.dma_buf is not None:
            return ("d", op.dma_buf)
        return ("e", op)

    def _add_wait(self, op, dep):
        if dep[0] == "e":
            p = dep[1]
            if p.eng == "pe" and op.eng == "pe":
                return
            key = ("e", p.eng)
            cur = op.waits.get(key)
            if cur is None or cur.idx < p.idx:
                op.waits[key] = p
        else:
            b = dep[1]
            key = ("d", b.idx)
            op.waits[key] = (b, b.ndma * b.inc)

    def _collect(self, op, reads, writes, pwrites):
        for b in reads:
            for d in b.w:
                self._add_wait(op, d)
        for b in writes:
            for d in b.w:
                self._add_wait(op, d)
            for d in b.r:
                self._add_wait(op, d)
        for b in pwrites:
            for d in b.r:
                self._add_wait(op, d)

    @staticmethod
    def _same(d, me):
        if d[0] != me[0]:
            return False
        if me[0] == "e":
            return d[1].eng == me[1].eng
        return d[1] is me[1]

    def _register(self, me, reads, writes, pwrites):
        for b in writes:
            b.w = [me]
            b.r = []
        for b in pwrites:
            b.w = [d for d in b.w if not self._same(d, me)]
            b.w.append(me)
        for b in reads:
            if b in writes or b in pwrites:
                continue
            b.r = [d for d in b.r if not self._same(d, me)]
            b.r.append(me)

    def add(self, eng, fn, reads=(), writes=(), pwrites=()):
        op = Op(eng, len(self.ops[eng]), fn)
        self.ops[eng].append(op)
        reads, writes, pwrites = list(reads), list(writes), list(pwrites)
        self._collect(op, reads, writes, pwrites)
        self._register(("e", op), reads, writes, pwrites)
        return op

    def dma(self, out_ap, in_ap, sem_buf, reads=(), writes=(), pwrites=(), q=None):
        if q is None:
            q = "sp"
        op = Op(q, len(self.ops[q]), lambda e, o=out_ap, i=in_ap: e.dma_start(out=o, in_=i))
        self.ops[q].append(op)
        reads, writes, pwrites = list(reads), list(writes), list(pwrites)
        self._collect(op, reads, writes, pwrites)
        if sem_buf.sem is None:
            sem_buf.sem = self.pool[self.dma_rr % self.NPOOL]
            self.dma_rr += 1
        ds = sem_buf.sem
        op.dma_buf = ds
        ds.ndma += 1
        self._register(("d", ds), reads, writes, pwrites)
        return op

    def collective(self, fn, reads=(), writes=()):
        op = Op("pool", len(self.ops["pool"]), fn)
        self.ops["pool"].append(op)
        reads, writes = list(reads), list(writes)
        self._collect(op, reads, writes, [])
        ds = self.cc_sem
        op.dma_buf = ds
        ds.ndma += 1
        self._register(("d", ds), reads, writes, [])
        return op

    def finalize(self, final_bufs=()):
        nc = self.nc
        fin = Op("sp", len(self.ops["sp"]), None)
        for ds in self.pool + [self.cc_sem]:
            if ds.ndma > 0:
                fin.waits[("d", ds.idx)] = (ds, ds.ndma * ds.inc)
        self.ops["sp"].append(fin)
        for e in ENGS:
            for op in self.ops[e]:
                for k, v in op.waits.items():
                    if k[0] == "e":
                        v.flagged = True
        for e in ENGS:
            r = 0
            for op in self.ops[e]:
                if op.flagged:
                    r += 1
                    op.rank = r
        for e in ENGS:
            if e != "sp":
                self.eng_sem[e] = nc.alloc_semaphore("sem_" + e)
        n = 0
        for ds in self.pool + [self.cc_sem]:
            if ds.ndma > 0:
                ds.handle = nc.alloc_semaphore("dsem_%d" % ds.idx)
                n += 1
        self.n_dma_sems = n
        sched = self

        def emit(e, eng):
            waited = {}
            for op in sched.ops[e]:
                for k, v in op.waits.items():
                    if k[0] == "e":
                        sem = sched.eng_sem[v.eng]
                        val = v.rank
                        wk = ("e", v.eng)
                    else:
                        sem = v[0].handle
                        val = v[1]
                        wk = k
                    if waited.get(wk, 0) >= val:
                        continue
                    waited[wk] = val
                    eng.wait_ge(sem, val)
                if op.fn is None:
                    continue
                ins = op.fn(eng)
                if op.dma_buf is not None:
                    ins.then_inc(op.dma_buf.handle, op.dma_buf.inc)
                elif op.flagged:
                    ins.then_inc(sched.eng_sem[e], 1)

        with nc.Block() as block:
            @block.tensor
            def _(eng):
                emit("pe", eng)

            @block.scalar
            def _(eng):
                emit("act", eng)

            @block.vector
            def _(eng):
                emit("dve", eng)

            @block.gpsimd
            def _(eng):
                emit("pool", eng)

            @block.sync
            def _(eng):
                emit("sp", eng)

NEG = -30000.0
EPS = 1e-6
TINY = 1e-30
SV = 4096
SO = 2048
DM = 2048
DFF = 5632
SCALE = 128 ** -0.5


def sub3(a, off, s1, n1, s2, n2):
    return bass.AP(a.tensor, a.offset + off, [list(a.ap[0]), [s1, n1], [s2, n2]])


def dview(d, r0, nr, c0, nc_):
    return d[r0:r0 + nr, c0:c0 + nc_].rearrange("(k p) n -> p k n", p=128)


class KB:
    def __init__(self, nc):
        self.nc = nc
        self.s = Sched(nc, arena_bytes=198 * 1024)
        self.d = {}
        self.dbufs = {}
        self.bank_rr = 0
        self.outs = []

    def inp(self, name, shape, dt=F32):
        self.d[name] = self.nc.dram_tensor(name, list(shape), dt, kind="ExternalInput").ap()
        return self.d[name]

    def scr(self, name, shape, dt):
        self.d[name] = self.nc.dram_tensor(name, list(shape), dt).ap()
        return self.d[name]

    def outp(self, name, shape, dt=F32):
        self.d[name] = self.nc.dram_tensor(name, list(shape), dt, kind="ExternalOutput").ap()
        return self.d[name]

    def db(self, name, i=0):
        k = (name, i)
        if k not in self.dbufs:
            self.dbufs[k] = Buf("d_%s_%s" % (name, i), "dram")
        return self.dbufs[k]

    def rd(self, name, i=0):
        if name is None:
            return []
        return [self.db(name, i)]

    def consts(self, ngain):
        s = self.s
        self.ident, self.ident_b = s.alloc("ident", 128 * 2, BF16)
        self.ones, self.ones_b = s.alloc("ones", 128 * 2, BF16)
        self.gains, self.gains_b = s.alloc("gains", ngain * 16 * 4)
        s.dma(self.ident, self.d["c_ident"], self.ident_b, writes=[self.ident_b], q="pool")
        s.dma(self.ones, self.d["c_ones"], self.ones_b, writes=[self.ones_b], q="pool")
        s.dma(self.gains, self.d["gains"], self.gains_b, writes=[self.gains_b])

    def stage_norm(self, src, srcname, ntok, gidx, dsts, src_tt0=0):
        s = self.s
        mk = s.mark()
        hb = [s.alloc("nh%d" % i, 2048 * 4) for i in range(8)]
        yb = [s.alloc("ny%d" % i, 2048 * 2, BF16) for i in range(8)]
        junk_ap, junk_b = s.alloc("njunk", 2048 * 2, BF16)
        st = [s.alloc("nst%d" % i, 12 * 4) for i in range(2)]
        ob = [[s.alloc("no%d_%d" % (g, i), 16 * 512 * 2, BF16) for i in range(2)] for g in range(len(gidx))]
        psb = [s.psum[i][:, :].bitcast(BF16) for i in range(8)]
        ident, ident_b = self.ident, self.ident_b
        ngrp = ntok // 512

        def loads(tt):
            for sub in range(4):
                h_ap, h_b = hb[(tt % 2) * 4 + sub]
                r0 = tt * 512 + sub * 128
                s.dma(h_ap, src[r0:r0 + 128, :], h_b, reads=self.rd(srcname, src_tt0 + tt), writes=[h_b])
        loads(0)
        for tt in range(ngrp):
            if tt + 1 < ngrp:
                loads(tt + 1)
            st_ap, st_b = st[tt % 2]
            for sub in range(4):
                h_ap, h_b = hb[(tt % 2) * 4 + sub]
                s.add("act", lambda e, h=h_ap, o=st_ap[:, sub:sub + 1]: e.activation(junk_ap, h, AF.Square, accum_out=o),
                      reads=[h_b], pwrites=[junk_b, st_b])
            s.add("dve", lambda e, a=st_ap: e.tensor_scalar(a[:, 4:8], a[:, 0:4], 1.0 / 2048, EPS, ALU.mult, ALU.add),
                  reads=[st_b], pwrites=[st_b])
            s.add("act", lambda e, a=st_ap: e.sqrt(a[:, 4:8], a[:, 4:8]), reads=[st_b], pwrites=[st_b])
            s.add("dve", lambda e, a=st_ap: e.reciprocal(a[:, 8:12], a[:, 4:8]), reads=[st_b], pwrites=[st_b])
            for sub in range(4):
                h_ap, h_b = hb[(tt % 2) * 4 + sub]
                y_ap, y_b = yb[(tt % 2) * 4 + sub]
                s.add("act", lambda e, y=y_ap, h=h_ap, sc=st_ap[:, 8 + sub:9 + sub]: e.activation(y, h, AF.Copy, scale=sc),
                      reads=[h_b, st_b], writes=[y_b])
                for half in range(2):
                    bank = self.bank_rr % 8
                    self.bank_rr += 1
                    for k8 in range(8):
                        kc = half * 8 + k8
                        s.add("pe", lambda e, o=psb[bank][:, k8 * 128:(k8 + 1) * 128], i=y_ap[:, kc * 128:(kc + 1) * 128]:
                              e.transpose(o, i, ident), reads=[y_b, ident_b], writes=[s.psbuf[bank]])
                    for gi, g in enumerate(gidx):
                        o_ap, o_b = ob[gi][tt % 2]
                        out3 = sub3(o_ap, half * 8 * 512 + sub * 128, 512, 8, 1, 128)
                        in0 = sub3(psb[bank], 0, 128, 8, 1, 128)
                        ga = self.gains[:, g * 16 + half * 8:g * 16 + half * 8 + 8]
                        in1 = sub3(ga, 0, 1, 8, 0, 128)
                        s.add("dve", lambda e, o=out3, a=in0, b=in1: e.tensor_tensor(o, a, b, ALU.mult),
                              reads=[s.psbuf[bank], self.gains_b], pwrites=[o_b])
            for gi in range(len(gidx)):
                dst, dname, dtt0 = dsts[gi]
                o_ap, o_b = ob[gi][tt % 2]
                for q4 in range(4):
                    s.dma(dview(dst, q4 * 512, 512, (dtt0 + tt) * 512, 512),
                          sub3(o_ap, q4 * 4 * 512, 512, 4, 1, 512), o_b, reads=[o_b], pwrites=[self.db(dname, dtt0 + tt)])
        s.release(mk)

    def load_panel(self, p_ap, p_b, segs, KC, kgrp=4, stride=512):
        s = self.s
        po = 0
        for (W, c0, n) in segs:
            for q in range(0, KC, kgrp):
                kn = min(kgrp, KC - q)
                s.dma(sub3(p_ap, q * stride + po, stride, kn, 1, n), dview(W, q * 128, kn * 128, c0, n), p_b,
                      pwrites=[p_b], q="pool")
            po += n

    def stage_lfm(self, xT, xname, tok0, ntok, KC, panels, setup):
        s = self.s
        mk = s.mark()
        nt = ntok // 512
        xs = [s.alloc("lx%d" % i, KC * 512 * 2, BF16) for i in range(nt)]
        for tt in range(nt):
            x_ap, x_b = xs[tt]
            for q in range(0, KC, 4):
                s.dma(sub3(x_ap, q * 512, 512, 4, 1, 512), dview(xT, q * 128, 512, tok0 + tt * 512, 512), x_b,
                      reads=self.rd(xname, tok0 // 512 + tt), pwrites=[x_b])
        pr = [s.alloc("lp%d" % i, KC * 512 * 2, BF16) for i in range(3)]
        ctx = setup(s)
        npan = len(panels)
        for i in range(min(2, npan)):
            self.load_panel(pr[i % 3][0], pr[i % 3][1], panels[i]["segs"], KC)
        for i, pan in enumerate(panels):
            p_ap, p_b = pr[i % 3]
            for job in pan["jobs"]:
                nb = len(job["cols"])
                for tt in range(nt):
                    x_ap, x_b = xs[tt]
                    banks = []
                    for (off, n) in job["cols"]:
                        bank = self.bank_rr % 8
                        self.bank_rr += 1
                        banks.append(bank)
                        for kc in range(KC):
                            s.add("pe", lambda e, o=s.psum[bank][0:n, :], l=p_ap[:, kc * 512 + off:kc * 512 + off + n],
                                  r=x_ap[:, kc * 512:(kc + 1) * 512], st=(kc == 0), sp=(kc == KC - 1):
                                  e.matmul(o, l, r, start=st, stop=sp),
                                  reads=[p_b, x_b], writes=[s.psbuf[bank]])
                    job["epi"](ctx, job, tt, banks)
            if i + 2 < npan:
                self.load_panel(pr[(i + 2) % 3][0], pr[(i + 2) % 3][1], panels[i + 2]["segs"], KC)
        s.release(mk)

    def stage_ltm(self, aT, aname, KC, tok0, ntok, TB, panels, setup, epi, pcols=512, nring=2):
        s = self.s
        mk = s.mark()
        ntb = TB // 512
        as_ = [s.alloc("ta%d" % i, KC * 512 * 2, BF16) for i in range(ntb)]
        pr = [s.alloc("tp%d" % i, KC * pcols * 2, BF16) for i in range(nring)]
        ctx = setup(s)
        seq = [(tb, pi) for tb in range(ntok // TB) for pi in range(len(panels))]

        def pload(k):
            tb_, pi_ = seq[k]
            self.load_panel(pr[k % nring][0], pr[k % nring][1], panels[pi_], KC, stride=pcols)
        for k in range(min(nring - 1, len(seq))):
            pload(k)
        for k, (tb, pi) in enumerate(seq):
            if pi == 0:
                for tt in range(ntb):
                    a_ap, a_b = as_[tt]
                    t0 = tok0 + tb * TB + tt * 512
                    for q in range(0, KC, 4):
                        kn = min(4, KC - q)
                        s.dma(sub3(a_ap, q * 512, 512, kn, 1, 512), dview(aT, q * 128, kn * 128, t0, 512), a_b,
                              reads=self.rd(aname, t0 // 512), pwrites=[a_b])
            if k + nring - 1 < len(seq):
                pload(k + nring - 1)
            p_ap, p_b = pr[k % nring]
            segs = panels[pi]
            ncol = sum(n for (_, _, n) in segs)
            for tt in range(ntb):
                a_ap, a_b = as_[tt]
                for t4 in range(4):
                    bank = self.bank_rr % 8
                    self.bank_rr += 1
                    for kc in range(KC):
                        s.add("pe", lambda e, o=s.psum[bank][:, 0:ncol], l=a_ap[:, kc * 512 + t4 * 128:kc * 512 + t4 * 128 + 128],
                              r=p_ap[:, kc * pcols:kc * pcols + ncol], st=(kc == 0), sp=(kc == KC - 1):
                              e.matmul(o, l, r, start=st, stop=sp),
                              reads=[p_b, a_b], writes=[s.psbuf[bank]])
                    epi(ctx, tok0 + tb * TB + tt * 512 + t4 * 128, pi, bank, ncol)
        s.release(mk)

    def epi_plain_fm(self, dst, dname, row0fn, tok0):
        kb = self

        def epi(ctx, job, tt, banks):
            s = kb.s
            n = job["cols"][0][1]
            o_ap, o_b = ctx["oring"][ctx["oi"] % len(ctx["oring"])]
            ctx["oi"] += 1
            bank = banks[0]
            s.add("act", lambda e, o=o_ap[0:n, :], i=s.psum[bank][0:n, :]: e.copy(o, i), reads=[s.psbuf[bank]], writes=[o_b])
            r0 = job["row0"]
            s.dma(dst[r0:r0 + n, tok0 + tt * 512:tok0 + tt * 512 + 512], o_ap[0:n, :], o_b, reads=[o_b],
                  pwrites=[kb.db(dname, (tok0 // 512) + tt)])
        return epi

    def epi_rope_fm(self, dst, dname, tok0):
        kb = self

        def epi(ctx, job, tt, banks):
            s = kb.s
            bz, br = banks
            t1, t1b = ctx["t1"][ctx["oi"] % 2]
            t2, t2b = ctx["t2"][ctx["oi"] % 2]
            o_ap, o_b = ctx["oring"][ctx["oi"] % len(ctx["oring"])]
            ctx["oi"] += 1
            cs, csb = ctx["cos"]
            sn, snb = ctx["sin"]
            s.add("dve", lambda e, o=t1, a=s.psum[bz][:, :], b=cs[:, tt * 512:(tt + 1) * 512]: e.tensor_tensor(o, a, b, ALU.mult),
                  reads=[s.psbuf[bz], csb], writes=[t1b])
            s.add("dve", lambda e, o=t2, a=s.psum[br][:, :], b=sn[:, tt * 512:(tt + 1) * 512]: e.tensor_tensor(o, a, b, ALU.mult),
                  reads=[s.psbuf[br], snb], writes=[t2b])
            s.add("pool", lambda e, o=o_ap, a=t1, b=t2: e.tensor_tensor(o, a, b, ALU.add), reads=[t1b, t2b], writes=[o_b])
            r0 = job["row0"]
            s.dma(job["dst"][r0:r0 + 128, tok0 + tt * 512:tok0 + tt * 512 + 512], o_ap, o_b, reads=[o_b],
                  pwrites=[kb.db(job["dname"], (tok0 // 512) + tt)])
        return epi

    def rope_setup(self, tok0, ntok, extra=None):
        kb = self

        def setup(s):
            ctx = {"oi": 0}
            ctx["oring"] = [s.alloc("eo%d" % i, 512 * 2, BF16) for i in range(4)]
            ctx["t1"] = [s.alloc("et1%d" % i, 512 * 4) for i in range(2)]
            ctx["t2"] = [s.alloc("et2%d" % i, 512 * 4) for i in range(2)]
            ctx["cos"] = s.alloc("ecos", ntok * 4)
            ctx["sin"] = s.alloc("esin", ntok * 4)
            s.dma(ctx["cos"][0], kb.d["cosT"][:, tok0:tok0 + ntok], ctx["cos"][1], writes=[ctx["cos"][1]])
            s.dma(ctx["sin"][0], kb.d["sinT"][:, tok0:tok0 + ntok], ctx["sin"][1], writes=[ctx["sin"][1]])
            if extra is not None:
                extra(s, ctx)
            return ctx
        return setup

    def stage_ffn(self, hnT, hnname, wg, wu, wd, h_in, h_in_name, h_out, h_out_name):
        kb = self
        hidT = self.d["hidT"]

        def setup(s):
            ctx = {"oi": 0}
            ctx["sg"] = [s.alloc("fsg%d" % i, 512 * 4) for i in range(3)]
            ctx["oring"] = [s.alloc("fo%d" % i, 512 * 2, BF16) for i in range(4)]
            return ctx

        def epi(ctx, job, tt, banks):
            s = kb.s
            bg, bu = banks
            sg, sgb = ctx["sg"][ctx["oi"] % 3]
            o_ap, o_b = ctx["oring"][ctx["oi"] % 4]
            ctx["oi"] += 1
            s.add("act", lambda e, o=sg, i=s.psum[bg][:, :]: e.activation(o, i, AF.Silu), reads=[s.psbuf[bg]], writes=[sgb])
            s.add("dve", lambda e, o=o_ap, a=s.psum[bu][:, :], b=sg: e.tensor_tensor(o, a, b, ALU.mult),
                  reads=[s.psbuf[bu], sgb], writes=[o_b])
            r0 = job["row0"]
            s.dma(hidT[r0:r0 + 128, tt * 512:(tt + 1) * 512], o_ap, o_b, reads=[o_b], pwrites=[kb.db("hidT", tt)])

        panels = []
        for pc in range(DFF // 256):
            segs = [(wg, pc * 256, 256), (wu, pc * 256, 256)]
            jobs = [dict(cols=[(j * 128, 128), (256 + j * 128, 128)], epi=epi, row0=pc * 256 + j * 128) for j in range(2)]
            panels.append(dict(segs=segs, jobs=jobs))
        self.stage_lfm(hnT, hnname, 0, SO, 16, panels, setup)
        hk = DFF // 2
        self.stage_down(self.d["hidT"][0:hk, :], "hidT", hk // 128, wd[0:hk, :], h_in, h_in_name, self.d["htmp"], "htmp",
                        TB=2048, PC=512, nring=3)
        self.stage_down(self.d["hidT"][hk:DFF, :], "hidT", hk // 128, wd[hk:DFF, :], self.d["htmp"], "htmp", h_out, h_out_name,
                        TB=2048, PC=512, nring=3)

    def stage_down(self, aT, aname, KC, W, h_in, h_in_name, h_out, h_out_name, TB, PC=512, nring=2):
        kb = self

        def setup(s):
            ctx = {"oi": 0}
            ctx["hin"] = [s.alloc("dh%d" % i, 512 * 4) for i in range(3)]
            ctx["oring"] = [s.alloc("do%d" % i, 512 * 4) for i in range(3)]
            return ctx

        def epi(ctx, tok, pi, bank, ncol):
            s = kb.s
            hi, hib = ctx["hin"][ctx["oi"] % 3]
            o_ap, o_b = ctx["oring"][ctx["oi"] % 3]
            ctx["oi"] += 1
            s.dma(hi[:, 0:PC], h_in[tok:tok + 128, pi * PC:(pi + 1) * PC], hib, reads=kb.rd(h_in_name, tok // 512), writes=[hib])
            s.add("dve", lambda e, o=o_ap[:, 0:PC], a=s.psum[bank][:, 0:PC], b=hi[:, 0:PC]: e.tensor_tensor(o, a, b, ALU.add),
                  reads=[s.psbuf[bank], hib], writes=[o_b])
            s.dma(h_out[tok:tok + 128, pi * PC:(pi + 1) * PC], o_ap[:, 0:PC], o_b, reads=[o_b], pwrites=[kb.db(h_out_name, tok // 512)])

        panels = [[(W, pi * PC, PC)] for pi in range(DM // PC)]
        self.stage_ltm(aT, aname, KC, 0, SO, TB, panels, setup, epi, pcols=PC, nring=nring)

    def stage_tm_bf16(self, aT, aname, KC, tok0, ntok, segs, dst, dname):
        kb = self

        def setup(s):
            return {"oi": 0, "oring": [s.alloc("vo%d" % i, 512 * 2, BF16) for i in range(4)]}

        def epi(ctx, tok, pi, bank, ncol):
            s = kb.s
            o_ap, o_b = ctx["oring"][ctx["oi"] % 4]
            ctx["oi"] += 1
            s.add("act", lambda e, o=o_ap[:, 0:ncol], i=s.psum[bank][:, 0:ncol]: e.copy(o, i), reads=[s.psbuf[bank]], writes=[o_b])
            s.dma(dst[tok:tok + 128, 0:ncol], o_ap[:, 0:ncol], o_b, reads=[o_b], pwrites=[kb.db(dname, tok // 512)])

        self.stage_ltm(aT, aname, KC, tok0, ntok, min(ntok, 2048), [segs], setup, epi)

    def unit_s(self, ctx, u):
        s = self.s
        n = u.get("n", 512)
        np_ = u.get("np_", 128)
        bS = ctx["sbanks"][ctx["si"] % len(ctx["sbanks"])]
        ctx["si"] += 1
        pt, ptb = ctx["pt"][ctx["pi"] % len(ctx["pt"])]
        ctx["pi"] += 1
        extras = u["extras"]
        ne = len(extras)
        s.add("pe", lambda e, o=s.psum[bS][0:np_, 0:n], l=u["klhs"], r=u["qrhs"], sp=(ne == 0): e.matmul(o, l, r, start=True, stop=sp),
              reads=u["krd"] + u["qrd"], writes=[s.psbuf[bS]])
        for i, (l, r, rds) in enumerate(extras):
            s.add("pe", lambda e, o=s.psum[bS][0:np_, 0:n], l=l, r=r, sp=(i == ne - 1): e.matmul(o, l, r, start=False, stop=sp),
                  reads=rds, writes=[s.psbuf[bS]])
        s.add("act", lambda e, o=pt[0:np_, 0:n], i=s.psum[bS][0:np_, 0:n]: e.activation(o, i, AF.Exp, scale=SCALE),
              reads=[s.psbuf[bS]], writes=[ptb])
        return pt, ptb

    def unit_pv(self, ctx, u, rec):
        s = self.s
        n = u.get("n", 512)
        np_ = u.get("np_", 128)
        pt, ptb = rec
        first, last = u["first"], u["last"]
        if u["vlhs"] is not None:
            s.add("pe", lambda e, o=s.psum[u["bacc"]][:, 0:n], l=u["vlhs"], r=pt[0:np_, 0:n], st=first, sp=last: e.matmul(o, l, r, start=st, stop=sp),
                  reads=u["vrd"] + [ptb], writes=[s.psbuf[u["bacc"]]])
        s.add("pe", lambda e, o=s.psum[u["bden"]][:, 0:n], l=self.ones[0:np_, :], r=pt[0:np_, 0:n], st=first, sp=last: e.matmul(o, l, r, start=st, stop=sp),
              reads=[self.ones_b, ptb], writes=[s.psbuf[u["bden"]]])
        if u.get("post") is not None:
            u["post"]()

    def run_units(self, ctx, units, skew=2):
        recs = []
        nu = len(units)
        for i in range(nu + skew):
            if i < nu:
                recs.append(self.unit_s(ctx, units[i]))
            j = i - skew
            if j >= 0:
                self.unit_pv(ctx, units[j], recs[j])
        return recs

    def attn_unit(self, ctx, klhs, krd, qrhs, qrd, extras, vlhs, vrd, bacc, bden, first, last, n=512, np_=128):
        u = dict(klhs=klhs, krd=krd, qrhs=qrhs, qrd=qrd, extras=extras, vlhs=vlhs, vrd=vrd, bacc=bacc, bden=bden,
                 first=first, last=last, n=n, np_=np_)
        rec = self.unit_s(ctx, u)
        self.unit_pv(ctx, u, rec)
        return rec

    def recip_den(self, ctx, bden, n=512):
        s = self.s
        r, rb = ctx["rd"][ctx["ri"] % len(ctx["rd"])]
        ctx["ri"] += 1
        s.add("dve", lambda e, o=r[:, 0:n], i=s.psum[bden][:, 0:n]: e.tensor_scalar(o, i, TINY, None, ALU.max),
              reads=[s.psbuf[bden]], writes=[rb])
        s.add("dve", lambda e, o=r[:, 0:n]: e.reciprocal(o, o), reads=[rb], writes=[rb])
        return r, rb

    def stage_mem_attn(self, qmT, qmname, mkT, mv, oT, oname, chunk0, nk="mkT", nv="mv"):
        s = self.s
        mk = s.mark()
        ctx = dict(si=0, pi=0, ri=0, sbanks=[0, 1, 2], pt=[s.alloc("mpt%d" % i, 512 * 2, BF16) for i in range(4)],
                   rd=[s.alloc("mrd%d" % i, 512 * 4) for i in range(2)])
        k_ap, k_b = s.alloc("mk", 4 * 256 * 2, BF16)
        v_ap, v_b = s.alloc("mv", 2 * 512 * 2, BF16)
        q_ap, q_b = s.alloc("mq", 4 * SO * 2, BF16)
        oring = [s.alloc("mo%d" % i, 512 * 2, BF16) for i in range(3)]
        s.dma(sub3(k_ap, 0, 256, 4, 1, 256), dview(mkT, 0, 512, 0, 256), k_b, reads=self.rd(nk), writes=[k_b])
        s.dma(sub3(v_ap, 0, 512, 2, 1, 512), dview(mv, 0, 256, 0, 512), v_b, reads=self.rd(nv), writes=[v_b])
        for h in range(4):
            s.dma(q_ap[:, h * SO:(h + 1) * SO], qmT[h * 128:(h + 1) * 128, :], q_b,
                  reads=[self.db(qmname, i) for i in range(4)], pwrites=[q_b])
        units = []
        oi = 0
        for h in range(4):
            for qt in range(4):
                bacc, bden = (3, 4) if (oi % 2 == 0) else (5, 6)
                o_ap, o_b = oring[oi % 3]
                oi += 1

                def post(bacc=bacc, bden=bden, o_ap=o_ap, o_b=o_b, h=h, qt=qt):
                    r, rb = self.recip_den(ctx, bden)
                    s.add("dve", lambda e, o=o_ap, a=s.psum[bacc][:, :], b=r: e.tensor_tensor(o, a, b, ALU.mult),
                          reads=[s.psbuf[bacc], rb], writes=[o_b])
                    s.dma(oT[(chunk0 + h) * 128:(chunk0 + h + 1) * 128, qt * 512:(qt + 1) * 512], o_ap, o_b, reads=[o_b],
                          pwrites=[self.db(oname, qt)])
                for mt in range(2):
                    units.append(dict(klhs=k_ap[:, h * 256 + mt * 128:h * 256 + mt * 128 + 128], krd=[k_b],
                                      qrhs=q_ap[:, h * SO + qt * 512:h * SO + qt * 512 + 512], qrd=[q_b], extras=[],
                                      vlhs=v_ap[:, mt * 512 + h * 128:mt * 512 + h * 128 + 128], vrd=[v_b], bacc=bacc, bden=bden,
                                      first=(mt == 0), last=(mt == 1), post=(post if mt == 1 else None)))
        self.run_units(ctx, units)
        s.release(mk)

    def stage_memkv(self, mem, gi, wkv, nk="mkT", nv="mv"):
        kb = self
        s = self.s
        mk = s.mark()
        hb = [s.alloc("kh%d" % i, 2048 * 4) for i in range(2)]
        yb = [s.alloc("ky%d" % i, 2048 * 2, BF16) for i in range(2)]
        junk_ap, junk_b = s.alloc("kjunk", 2048 * 2, BF16)
        st_ap, st_b = s.alloc("kst", 12 * 4)
        o_ap, o_b = s.alloc("ko", 16 * 256 * 2, BF16)
        psb = [s.psum[i][:, :].bitcast(BF16) for i in range(8)]
        for sub in range(2):
            h_ap, h_b = hb[sub]
            s.dma(h_ap, mem[sub * 128:(sub + 1) * 128, :], h_b, writes=[h_b])
            s.add("act", lambda e, h=h_ap, o=st_ap[:, sub:sub + 1]: e.activation(junk_ap, h, AF.Square, accum_out=o),
                  reads=[h_b], writes=[junk_b], pwrites=[st_b])
        s.add("dve", lambda e, a=st_ap: e.tensor_scalar(a[:, 4:6], a[:, 0:2], 1.0 / 2048, EPS, ALU.mult, ALU.add), reads=[st_b], pwrites=[st_b])
        s.add("act", lambda e, a=st_ap: e.sqrt(a[:, 4:6], a[:, 4:6]), reads=[st_b], pwrites=[st_b])
        s.add("dve", lambda e, a=st_ap: e.reciprocal(a[:, 8:10], a[:, 4:6]), reads=[st_b], pwrites=[st_b])
        for sub in range(2):
            h_ap, h_b = hb[sub]
            y_ap, y_b = yb[sub]
            s.add("act", lambda e, y=y_ap, h=h_ap, sc=st_ap[:, 8 + sub:9 + sub]: e.activation(y, h, AF.Copy, scale=sc),
                  reads=[h_b, st_b], writes=[y_b])
            for half in range(2):
                bank = self.bank_rr % 8
                self.bank_rr += 1
                for k8 in range(8):
                    kc = half * 8 + k8
                    s.add("pe", lambda e, o=psb[bank][:, k8 * 128:(k8 + 1) * 128], i=y_ap[:, kc * 128:(kc + 1) * 128]:
                          e.transpose(o, i, kb.ident), reads=[y_b, kb.ident_b], writes=[s.psbuf[bank]])
                out3 = sub3(o_ap, half * 8 * 256 + sub * 128, 256, 8, 1, 128)
                in0 = sub3(psb[bank], 0, 128, 8, 1, 128)
                ga = self.gains[:, gi * 16 + half * 8:gi * 16 + half * 8 + 8]
                in1 = sub3(ga, 0, 1, 8, 0, 128)
                s.add("dve", lambda e, o=out3, a=in0, b=in1: e.tensor_tensor(o, a, b, ALU.mult),
                      reads=[s.psbuf[bank], self.gains_b], pwrites=[o_b])
        mkT = self.d[nk]
        mv = self.d[nv]
        pr = [s.alloc("kp%d" % i, 16 * 512 * 2, BF16) for i in range(2)]
        oring = [s.alloc("kor%d" % i, 512 * 2, BF16) for i in range(3)]
        oi = 0
        for half in range(2):
            p_ap, p_b = pr[half]
            self.load_panel(p_ap, p_b, [(wkv, half * 512, 512)], 16)
        p_ap, p_b = pr[0]
        for h in range(4):
            bank = self.bank_rr % 8
            self.bank_rr += 1
            for kc in range(16):
                s.add("pe", lambda e, o=s.psum[bank][:, 0:256], l=p_ap[:, kc * 512 + h * 128:kc * 512 + h * 128 + 128],
                      r=o_ap[:, kc * 256:(kc + 1) * 256], st=(kc == 0), sp=(kc == 15): e.matmul(o, l, r, start=st, stop=sp),
                      reads=[p_b, o_b], writes=[s.psbuf[bank]])
            oo, oob = oring[oi % 3]
            oi += 1
            s.add("act", lambda e, o=oo[:, 0:256], i=s.psum[bank][:, 0:256]: e.copy(o, i), reads=[s.psbuf[bank]], writes=[oob])
            s.dma(mkT[h * 128:(h + 1) * 128, :], oo[:, 0:256], oob, reads=[oob], pwrites=[self.db(nk)])
        p_ap, p_b = pr[1]
        for mt in range(2):
            bank = self.bank_rr % 8
            self.bank_rr += 1
            for kc in range(16):
                s.add("pe", lambda e, o=s.psum[bank][:, :], l=o_ap[:, kc * 256 + mt * 128:kc * 256 + mt * 128 + 128],
                      r=p_ap[:, kc * 512:(kc + 1) * 512], st=(kc == 0), sp=(kc == 15): e.matmul(o, l, r, start=st, stop=sp),
                      reads=[p_b, o_b], writes=[s.psbuf[bank]])
            oo, oob = oring[oi % 3]
            oi += 1
            s.add("act", lambda e, o=oo, i=s.psum[bank][:, :]: e.copy(o, i), reads=[s.psbuf[bank]], writes=[oob])
            s.dma(mv[mt * 128:(mt + 1) * 128, :], oo, oob, reads=[oob], pwrites=[self.db(nv)])
        s.release(mk)

    def stage_inproj_a(self, w_in, w_rot, gbias):
        kb = self
        d = self.d
        xT = d["xnT"]
        for tok0, own in ((0, False), (SO, True)):
            epi_rope = self.epi_rope_fm(None, None, tok0)
            epi_plain = self.epi_plain_fm(None, None, None, tok0)

            def mkplain(dst, dname):
                return kb.epi_plain_fm(dst, dname, None, tok0)

            def gate_extra(s, ctx):
                ctx["gb"] = s.alloc("egb", 4)
                s.dma(ctx["gb"][0][0:36, :], gbias, ctx["gb"][1], writes=[ctx["gb"][1]])
                ctx["go"] = [s.alloc("ego%d" % i, 512 * 4) for i in range(2)]

            def epi_gate(ctx, job, tt, banks):
                s = kb.s
                o_ap, o_b = ctx["go"][tt % 2]
                gb, gbb = ctx["gb"]
                bank = banks[0]
                s.add("act", lambda e, o=o_ap[0:36, :], i=s.psum[bank][0:36, :], b=gb[0:36, 0:1]: e.activation(o, i, AF.Sigmoid, bias=b),
                      reads=[s.psbuf[bank], gbb], writes=[o_b])
                s.dma(d["gatesT"][:, tt * 512:(tt + 1) * 512], o_ap[0:36, :], o_b, reads=[o_b], pwrites=[kb.db("gatesT", tt)])

            panels = []
            otok = tok0 - SO

            def ropejob(off, roff, dst, dname, row0):
                return dict(cols=[(off, 128), (roff, 128)], epi=kb.epi_rope_fm(None, None, tok0 if dst is not d["qT"] else 0),
                            dst=dst, dname=dname, row0=row0)
            if own:
                for hp in range(6):
                    segs = [(w_in, hp * 256, 256), (w_rot, hp * 256, 256)]
                    jobs = []
                    for j in range(2):
                        jb = dict(cols=[(j * 128, 128), (256 + j * 128, 128)], dst=d["qT"], dname="qT", row0=(hp * 2 + j) * 128)
                        jb["epi"] = self._rope_epi_own()
                        jobs.append(jb)
                    panels.append(dict(segs=segs, jobs=jobs))
            for (kcol, rcol, dst, dname) in ((1536, 1536, d["kcmpT"], "kcmpT"), (2048, 1792, d["kslcT"], "kslcT"),
                                             (2560, 2048, d["kwinT"], "kwinT")):
                segs = [(w_in, kcol, 256), (w_rot, rcol, 256)]
                jobs = []
                for g in range(2):
                    jobs.append(dict(cols=[(g * 128, 128), (256 + g * 128, 128)], dst=dst, dname=dname, row0=g * 128,
                                     epi=self._rope_epi_all(tok0)))
                panels.append(dict(segs=segs, jobs=jobs))
            segs = [(w_in, 1792, 256)]
            jobs = [dict(cols=[(g * 128, 128)], row0=g * 128, epi=mkplain(d["vcmpT"], "vcmpT")) for g in range(2)]
            panels.append(dict(segs=segs, jobs=jobs))
            if own:
                segs = [(w_in, 3072, 36), (w_in, 3108, 256)]
                jobs = [dict(cols=[(0, 36)], epi=epi_gate)]
                for j in range(2):
                    jobs.append(dict(cols=[(36 + j * 128, 128)], row0=j * 128, epi=kb.epi_plain_fm(d["qmT"], "qmT", None, 0)))
                panels.append(dict(segs=segs, jobs=jobs))
                segs = [(w_in, 3108 + 256, 256)]
                jobs = []
                for j in range(2):
                    jobs.append(dict(cols=[(j * 128, 128)], row0=(2 + j) * 128, epi=kb.epi_plain_fm(d["qmT"], "qmT", None, 0)))
                panels.append(dict(segs=segs, jobs=jobs))
            self.cur_tok0 = tok0
            self.stage_lfm(xT, "xnT", tok0, SO, 16, panels, self.rope_setup(tok0, SO, gate_extra if own else None))

    def _rope_epi_all(self, tok0):
        kb = self

        def epi(ctx, job, tt, banks):
            kb._rope_core(ctx, job, tt, banks, tok0 + tt * 512, (tok0 // 512) + tt)
        return epi

    def _rope_epi_own(self):
        kb = self

        def epi(ctx, job, tt, banks):
            kb._rope_core(ctx, job, tt, banks, tt * 512, tt)
        return epi

    def _rope_core(self, ctx, job, tt, banks, col0, dbi):
        s = self.s
        bz, br = banks
        t1, t1b = ctx["t1"][ctx["oi"] % 2]
        t2, t2b = ctx["t2"][ctx["oi"] % 2]
        o_ap, o_b = ctx["oring"][ctx["oi"] % len(ctx["oring"])]
        ctx["oi"] += 1
        cs, csb = ctx["cos"]
        sn, snb = ctx["sin"]
        s.add("dve", lambda e, o=t1, a=s.psum[bz][:, :], b=cs[:, tt * 512:(tt + 1) * 512]: e.tensor_tensor(o, a, b, ALU.mult),
              reads=[s.psbuf[bz], csb], writes=[t1b])
        s.add("dve", lambda e, o=t2, a=s.psum[br][:, :], b=sn[:, tt * 512:(tt + 1) * 512]: e.tensor_tensor(o, a, b, ALU.mult),
              reads=[s.psbuf[br], snb], writes=[t2b])
        s.add("pool", lambda e, o=o_ap, a=t1, b=t2: e.tensor_tensor(o, a, b, ALU.add), reads=[t1b, t2b], writes=[o_b])
        r0 = job["row0"]
        s.dma(job["dst"][r0:r0 + 128, col0:col0 + 512], o_ap, o_b, reads=[o_b], pwrites=[self.db(job["dname"], dbi)])

    def stage_cmp(self, w1k, w2k, pek, w1v, w2v, pev):
        s = self.s
        d = self.d
        for kv, (w1, w2, peT, srcT, sname) in enumerate(((w1k, w2k, pek, d["kcmpT"], "kcmpT"), (w1v, w2v, pev, d["vcmpT"], "vcmpT"))):
            mk = s.mark()
            w1_ap, w1_b = s.alloc("cw1", 32 * 256 * 2, BF16)
            for q in range(0, 32, 8):
                s.dma(sub3(w1_ap, q * 256, 256, 8, 1, 256), dview(w1, q * 128, 1024, 0, 256), w1_b, pwrites=[w1_b], q="pool")
            w2_ap, w2_b = s.alloc("cw2", 2 * 128 * 2, BF16)
            s.dma(sub3(w2_ap, 0, 128, 2, 1, 128), dview(w2, 0, 256, 0, 128), w2_b, writes=[w2_b], q="pool")
            pe_ap, pe_b = s.alloc("cpe", 32 * 2, BF16)
            s.dma(pe_ap, peT, pe_b, writes=[pe_b], q="pool")
            bias_ap, bias_b = s.alloc("cbias", 2 * 4)
            for hc in range(2):
                bank = self.bank_rr % 8
                self.bank_rr += 1
                for l in range(32):
                    s.add("pe", lambda e, o=s.psum[bank][:, 0:1], lh=w1_ap[:, l * 256 + hc * 128:l * 256 + hc * 128 + 128], r=pe_ap[:, l:l + 1],
                          st=(l == 0), sp=(l == 31): e.matmul(o, lh, r, start=st, stop=sp), reads=[w1_b, pe_b], writes=[s.psbuf[bank]])
                s.add("dve", lambda e, o=bias_ap[:, hc:hc + 1], i=s.psum[bank][:, 0:1]: e.tensor_copy(o, i), reads=[s.psbuf[bank]], pwrites=[bias_b])
            for g in range(2):
                k_ap, k_b = s.alloc("ck%d" % g, SV * 2, BF16)
                s.dma(k_ap, srcT[g * 128:(g + 1) * 128, :], k_b, reads=[self.db(sname, i) for i in range(8)], writes=[k_b])
                hs_ap, hs_b = s.alloc("chs%d" % g, 2 * 256 * 2, BF16)
                for hc in range(2):
                    bank = self.bank_rr % 8
                    self.bank_rr += 1
                    for l in range(32):
                        s.add("pe", lambda e, o=s.psum[bank][:, 0:255], lh=w1_ap[:, l * 256 + hc * 128:l * 256 + hc * 128 + 128],
                              r=k_ap[:, l:l + 16 * 254 + 1:16], st=(l == 0), sp=(l == 31): e.matmul(o, lh, r, start=st, stop=sp),
                              reads=[w1_b, k_b], writes=[s.psbuf[bank]])
                    s.add("act", lambda e, o=hs_ap[:, hc * 256:hc * 256 + 255], i=s.psum[bank][:, 0:255], b=bias_ap[:, hc:hc + 1]:
                          e.activation(o, i, AF.Silu, bias=b), reads=[s.psbuf[bank], bias_b], pwrites=[hs_b])
                o_ap, o_b = s.alloc("cout%d" % g, 256 * 2, BF16)
                if kv == 0:
                    bank = self.bank_rr % 8
                    self.bank_rr += 1
                    for hc in range(2):
                        s.add("pe", lambda e, o=s.psum[bank][:, 0:255], lh=w2_ap[:, hc * 128:(hc + 1) * 128], r=hs_ap[:, hc * 256:hc * 256 + 255],
                              st=(hc == 0), sp=(hc == 1): e.matmul(o, lh, r, start=st, stop=sp), reads=[w2_b, hs_b], writes=[s.psbuf[bank]])
                    s.add("pool", lambda e, o=o_ap: e.memset(o, 0.0), writes=[o_b])
                    s.add("act", lambda e, o=o_ap[:, 0:255], i=s.psum[bank][:, 0:255]: e.copy(o, i), reads=[s.psbuf[bank]], pwrites=[o_b])
                    s.dma(d["kcT"][g * 128:(g + 1) * 128, :], o_ap, o_b, reads=[o_b], pwrites=[self.db("kcT")])
                else:
                    s.add("pool", lambda e, o=o_ap: e.memset(o, 0.0), writes=[o_b])
                    for ct in range(2):
                        ncn = 128 if ct == 0 else 127
                        bank = self.bank_rr % 8
                        self.bank_rr += 1
                        for hc in range(2):
                            s.add("pe", lambda e, o=s.psum[bank][0:ncn, 0:128], lh=hs_ap[:, hc * 256 + ct * 128:hc * 256 + ct * 128 + ncn],
                                  r=w2_ap[:, hc * 128:(hc + 1) * 128], st=(hc == 0), sp=(hc == 1): e.matmul(o, lh, r, start=st, stop=sp),
                                  reads=[w2_b, hs_b], writes=[s.psbuf[bank]])
                        s.add("act", lambda e, o=o_ap[0:ncn, ct * 128:(ct + 1) * 128], i=s.psum[bank][0:ncn, 0:128]: e.copy(o, i),
                              reads=[s.psbuf[bank]], pwrites=[o_b])
                    s.dma(dview(d["vc"], g * 256, 256, 0, 128), sub3(o_ap, 0, 128, 2, 1, 128), o_b, reads=[o_b], pwrites=[self.db("vc")])
            s.release(mk)

    def stage_attn_a(self):
        s = self.s
        d = self.d
        mk0 = s.mark()
        def ld(name, src, nbytes, dt, q="sp", parts=128):
            ap, b = s.alloc(name, nbytes, dt)
            s.dma(ap[0:parts, :], src, b, writes=[b], q=q)
            return ap, b
        mcmp, mcmp_b = ld("mcmp", d["m_cmp"], 8 * 512 * 2, BF16, "pool")
        mwin, mwin_b = ld("mwin", d["m_win"], 8 * 512 * 2, BF16, "pool")
        mwin0, mwin0_b = ld("mwin0", d["m_win0"], 4 * 512 * 2, BF16, "pool")
        E, E_b = ld("E", d["c_E"], SV * 2, BF16, "pool", 64)
        mmap, mmap_b = ld("mmap", d["c_mmap"], 2 * 64 * 4, F32)
        ph = [s.alloc("aph%d" % i, 512 * 4) for i in range(2)]
        selM, selM_b = ld("selM", d["selM"], 4 * 256 * 4, F32)
        selA, selA_b = ld("selA", d["selA"], 4 * 256 * 4, F32)
        selmat, selmat_b = ld("selmat", d["c_selmat"], 36 * 128 * 4, F32, "sp", 36)
        gat, gat_b = s.alloc("gat", SO * 4)
        s.dma(gat[0:36, :], d["gatesT"], gat_b, reads=[self.db("gatesT", i) for i in range(4)], writes=[gat_b])
        ctx = dict(si=0, pi=0, ri=0, sbanks=[0, 1, 2], pt=[s.alloc("apt%d" % i, 512 * 2, BF16) for i in range(4)],
                   rd=[s.alloc("ard%d" % i, 512 * 4) for i in range(3)])
        pn = [s.alloc("apn%d" % i, 512 * 2, BF16) for i in range(4)]
        Gs = [s.alloc("aG%d" % i, 512 * 4) for i in range(3)]
        ocs = [s.alloc("aocs%d" % i, 512 * 4) for i in range(6)]
        tb = [s.alloc("atb%d" % i, 512 * 4) for i in range(4)]
        fb = [s.alloc("afb%d" % i, 512 * 4) for i in range(2)]
        oring = [s.alloc("aor%d" % i, 512 * 2, BF16) for i in range(3)]
        sc_ap, sc_b = s.alloc("asc", 256 * 4)
        m16, m16_b = s.alloc("am16", 4 * 16 * 4)
        wk, wk_b = s.alloc("awk", 256 * 4)
        selb, selb_b = s.alloc("aselb", 256 * 2, BF16)
        selbT, selbT_b = s.alloc("aselbT", 512 * 2, BF16)
        psb = [s.psum[i][:, :].bitcast(BF16) for i in range(8)]
        gi_ = 0
        oi = 0
        for g in range(2):
            mk = s.mark()
            kc_ap, kc_b = s.alloc("akc", 256 * 2, BF16)
            s.dma(kc_ap, d["kcT"][g * 128:(g + 1) * 128, :], kc_b, reads=self.rd("kcT"), writes=[kc_b])
            vc_ap, vc_b = s.alloc("avc", 256 * 2, BF16)
            s.dma(sub3(vc_ap, 0, 128, 2, 1, 128), dview(d["vc"], g * 256, 256, 0, 128), vc_b, reads=self.rd("vc"), writes=[vc_b])
            ks_ap, ks_b = s.alloc("aks", SV * 2, BF16)
            kw_ap, kw_b = s.alloc("akw", SV * 2, BF16)
            s.dma(ks_ap, d["kslcT"][g * 128:(g + 1) * 128, :], ks_b, reads=[self.db("kslcT", i) for i in range(8)], writes=[ks_b])
            s.dma(kw_ap, d["kwinT"][g * 128:(g + 1) * 128, :], kw_b, reads=[self.db("kwinT", i) for i in range(8)], writes=[kw_b])
            vs_ap, vs_b = s.alloc("avs", SV * 2, BF16)
            vw_ap, vw_b = s.alloc("avw", SV * 2, BF16)
            for q in range(4):
                s.dma(sub3(vs_ap, q * 8 * 128, 128, 8, 1, 128), dview(d["vsw"], q * 1024, 1024, g * 128, 128), vs_b,
                      reads=[self.db("vsw", i) for i in range(8)], pwrites=[vs_b])
                s.dma(sub3(vw_ap, q * 8 * 128, 128, 8, 1, 128), dview(d["vsw"], q * 1024, 1024, 256 + g * 128, 128), vw_b,
                      reads=[self.db("vsw", i) for i in range(8)], pwrites=[vw_b])
            q_ap, q_b = s.alloc("aq", 6 * SO * 2, BF16)
            for p in range(6):
                s.dma(q_ap[:, p * SO:(p + 1) * SO], d["qT"][(g * 6 + p) * 128:(g * 6 + p + 1) * 128, :], q_b,
                      reads=[self.db("qT", i) for i in range(4)], pwrites=[q_b])
            for qt in range(4):
                t0v = SO + qt * 512
                njt = (t0v + 512) // 128
                bI = 5
                for p in range(6):
                    qr = q_ap[:, p * SO + qt * 512:p * SO + qt * 512 + 512]
                    pts = []
                    for ct in range(2):
                        pt, ptb = self.attn_unit(ctx, kc_ap[:, ct * 128:(ct + 1) * 128], [kc_b], qr, [q_b],
                                                 [(self.ident, mcmp[:, (qt * 2 + ct) * 512:(qt * 2 + ct + 1) * 512], [self.ident_b, mcmp_b])],
                                                 None, [], None, 3, ct == 0, ct == 1)
                        pts.append((pt, ptb))
                    r, rb = self.recip_den(ctx, 3)
                    pns = []
                    for ct in range(2):
                        pa, pb = pn[(p * 2 + ct) % 4]
                        s.add("pool", lambda e, o=pa, a=pts[ct][0], b=r: e.tensor_tensor(o, a, b, ALU.mult),
                              reads=[pts[ct][1], rb], writes=[pb])
                        pns.append((pa, pb))
                    for ct in range(2):
                        s.add("pe", lambda e, o=s.psum[4][:, :], l=vc_ap[:, ct * 128:(ct + 1) * 128], r_=pns[ct][0], st=(ct == 0), sp=(ct == 1):
                              e.matmul(o, l, r_, start=st, stop=sp), reads=[vc_b, pns[ct][1]], writes=[s.psbuf[4]])
                    for ct in range(2):
                        if p == 0:
                            s.add("pool", lambda e, o=ph[ct][0], a=pns[ct][0]: e.tensor_copy(o, a), reads=[pns[ct][1]], writes=[ph[ct][1]])
                        else:
                            s.add("pool", lambda e, o=ph[ct][0], a=pns[ct][0]: e.tensor_tensor(o, o, a, ALU.add),
                                  reads=[pns[ct][1], ph[ct][1]], writes=[ph[ct][1]])
                    hh = g * 6 + p
                    G, Gb = Gs[gi_ % 3]
                    gi_ += 1
                    bG = 6 + (gi_ % 2)
                    s.add("pe", lambda e, o=s.psum[bG][:, :], l=selmat[0:36, (hh * 3) * 128:(hh * 3 + 1) * 128], r_=gat[0:36, qt * 512:(qt + 1) * 512]:
                          e.matmul(o, l, r_, start=True, stop=True), reads=[selmat_b, gat_b], writes=[s.psbuf[bG]])
                    s.add("act", lambda e, o=G, i=s.psum[bG][:, :]: e.copy(o, i), reads=[s.psbuf[bG]], writes=[Gb])
                    s.add("dve", lambda e, o=ocs[p][0], a=s.psum[4][:, :], b=G: e.tensor_tensor(o, a, b, ALU.mult),
                          reads=[s.psbuf[4], Gb], writes=[ocs[p][1]])
                for qs in range(4):
                    for ct in range(2):
                        s.add("pe", lambda e, o=s.psum[bI][:, qs * 64:(qs + 1) * 64], l=ph[ct][0][:, qs * 128:(qs + 1) * 128],
                              r_=mmap[:, ct * 64:(ct + 1) * 64], st=(ct == 0), sp=(ct == 1):
                              e.matmul(o, l, r_, start=st, stop=sp), reads=[ph[ct][1], mmap_b], writes=[s.psbuf[bI]])
                s.add("dve", lambda e, o=sc_ap, a=s.psum[bI][:, 0:256], b=selM[:, qt * 256:(qt + 1) * 256]: e.tensor_tensor(o, a, b, ALU.mult),
                      reads=[s.psbuf[bI], selM_b], writes=[sc_b])
                s.add("dve", lambda e, o=sc_ap, b=selA[:, qt * 256:(qt + 1) * 256]: e.tensor_tensor(o, o, b, ALU.add),
                      reads=[sc_b, selA_b], writes=[sc_b])
                for qs in range(4):
                    scq = sc_ap[:, qs * 64:(qs + 1) * 64]
                    mm = m16[:, qs * 16:(qs + 1) * 16]
                    s.add("dve", lambda e, o=mm[:, 0:8], i=scq: e.max(o, i), reads=[sc_b], pwrites=[m16_b])
                    s.add("dve", lambda e, o=wk[:, qs * 64:(qs + 1) * 64], m=mm[:, 0:8], i=scq: e.match_replace(o, m, i, -3e9),
                          reads=[sc_b, m16_b], pwrites=[wk_b])
                    s.add("dve", lambda e, o=mm[:, 8:16], i=wk[:, qs * 64:(qs + 1) * 64]: e.max(o, i), reads=[wk_b, m16_b], pwrites=[m16_b])
                    s.add("dve", lambda e, o=mm[:, 15:16]: e.tensor_scalar(o, o, -5e8, None, ALU.max), reads=[m16_b], pwrites=[m16_b])
                    s.add("dve", lambda e, o=selb[:, qs * 64:(qs + 1) * 64], i=scq, t=mm[:, 15:16]: e.tensor_scalar(o, i, t, NEG, ALU.is_lt, ALU.mult),
                          reads=[sc_b, m16_b], pwrites=[selb_b])
                for qs in range(4):
                    s.add("pe", lambda e, o=psb[7][0:64, qs * 128:(qs + 1) * 128], i=selb[:, qs * 64:(qs + 1) * 64]: e.transpose(o, i, self.ident),
                          reads=[selb_b, self.ident_b], writes=[s.psbuf[7]])
                s.add("act", lambda e, o=selbT[0:64, :], i=psb[7][0:64, 0:512]: e.copy(o, i), reads=[s.psbuf[7]], writes=[selbT_b])
                for p in range(6):
                    qr = q_ap[:, p * SO + qt * 512:p * SO + qt * 512 + 512]
                    hh = g * 6 + p
                    units = []
                    for jt in range(njt):
                        ex = [(E[0:64, jt * 128:(jt + 1) * 128], selbT[0:64, :], [E_b, selbT_b])]
                        o_ = jt * 128 - t0v
                        if o_ >= 0:
                            mi = (o_ + 512) // 128
                            ex.append((self.ident, mwin[:, mi * 512:(mi + 1) * 512], [self.ident_b, mwin_b]))
                        units.append(dict(klhs=ks_ap[:, jt * 128:(jt + 1) * 128], krd=[ks_b], qrhs=qr, qrd=[q_b], extras=ex,
                                          vlhs=vs_ap[:, jt * 128:(jt + 1) * 128], vrd=[vs_b], bacc=3, bden=4, first=(jt == 0), last=(jt == njt - 1)))
                    jt0 = (t0v - 512) // 128
                    for jt in range(jt0, njt):
                        o_ = jt * 128 - t0v
                        mi = (o_ + 512) // 128
                        if qt == 0 and o_ < 0:
                            mt_ap, mt_b = mwin0[:, mi * 512:(mi + 1) * 512], mwin0_b
                        else:
                            mt_ap, mt_b = mwin[:, mi * 512:(mi + 1) * 512], mwin_b
                        units.append(dict(klhs=kw_ap[:, jt * 128:(jt + 1) * 128], krd=[kw_b], qrhs=qr, qrd=[q_b],
                                          extras=[(self.ident, mt_ap, [self.ident_b, mt_b])],
                                          vlhs=vw_ap[:, jt * 128:(jt + 1) * 128], vrd=[vw_b], bacc=5, bden=6, first=(jt == jt0), last=(jt == njt - 1)))
                    self.run_units(ctx, units)
                    ts = []
                    for bi, (bacc, bden) in enumerate(((3, 4), (5, 6))):
                        G, Gb = Gs[gi_ % 3]
                        gi_ += 1
                        s.add("pe", lambda e, o=s.psum[7][:, :], l=selmat[0:36, (hh * 3 + 1 + bi) * 128:(hh * 3 + 2 + bi) * 128],
                              r_=gat[0:36, qt * 512:(qt + 1) * 512]: e.matmul(o, l, r_, start=True, stop=True),
                              reads=[selmat_b, gat_b], writes=[s.psbuf[7]])
                        s.add("act", lambda e, o=G, i=s.psum[7][:, :]: e.copy(o, i), reads=[s.psbuf[7]], writes=[Gb])
                        r, rb = self.recip_den(ctx, bden)
                        f, fbb = fb[bi]
                        s.add("pool", lambda e, o=f, a=r, b=G: e.tensor_tensor(o, a, b, ALU.mult), reads=[rb, Gb], writes=[fbb])
                        t, tbb = tb[(oi * 2 + bi) % 4]
                        s.add("dve", lambda e, o=t, a=s.psum[bacc][:, :], b=f: e.tensor_tensor(o, a, b, ALU.mult),
                              reads=[s.psbuf[bacc], fbb], writes=[tbb])
                        ts.append((t, tbb))
                    o_ap, o_b = oring[oi % 3]
                    oi += 1
                    if "dbg_br" in d:
                        for bi_, (ap_, b_) in enumerate((ocs[p], ts[0], ts[1])):
                            s.dma(d["dbg_br"][bi_ * 1536 + hh * 128:bi_ * 1536 + (hh + 1) * 128, qt * 512:(qt + 1) * 512], ap_, b_, reads=[b_])
                    s.add("pool", lambda e, o=ts[0][0], a=ts[0][0], b=ocs[p][0]: e.tensor_tensor(o, a, b, ALU.add),
                          reads=[ocs[p][1], ts[0][1]], writes=[ts[0][1]])
                    s.add("pool", lambda e, o=o_ap, a=ts[0][0], b=ts[1][0]: e.tensor_tensor(o, a, b, ALU.add),
                          reads=[ts[0][1], ts[1][1]], writes=[o_b])
                    s.dma(d["oT"][hh * 128:(hh + 1) * 128, qt * 512:(qt + 1) * 512], o_ap, o_b, reads=[o_b], pwrites=[self.db("oT", qt)])
            s.release(mk)
        s.release(mk0)

    def stage_kvshared(self, w_kv, w_kv_rot):
        kb = self
        d = self.d
        panels = []
        for hp in range(2):
            segs = [(w_kv, hp * 256, 256), (w_kv_rot, hp * 256, 256)]
            jobs = [dict(cols=[(j * 128, 128), (256 + j * 128, 128)], dst=d["kshT"], dname="kshT", row0=(hp * 2 + j) * 128,
                         epi=self._rope_epi_own()) for j in range(2)]
            panels.append(dict(segs=segs, jobs=jobs))
        self.stage_lfm(d["kvnT"], "kvnT", 0, SO, 16, panels, self.rope_setup(SO, SO))
        self.stage_tm_bf16(d["kvnT"], "kvnT", 16, 0, SO, [(w_kv, 512, 512)], d["vsh"], "vsh")


NG = 8


def build_phase_a(debug=()):
    nc = bass.Bass("TRN2", target_bir_lowering=False)
    kb = KB(nc)
    I = kb.inp
    xv = I("xv", [SV, DM])
    memb = I("memb", [256, DM])
    I("cosT", [128, SV]); I("sinT", [128, SV]); I("gains", [128, NG * 16])
    I("c_ident", [128, 128]); I("c_ones", [128, 128])
    I("m_cmp", [128, 8 * 512]); I("m_win", [128, 8 * 512]); I("m_win0", [128, 4 * 512])
    I("c_E", [64, SV]); I("c_mmap", [128, 128]); I("selM", [128, 1024]); I("selA", [128, 1024]); I("c_selmat", [36, 36 * 128])
    w_in = I("a_w_in", [DM, 3620]); w_rot = I("a_w_rot", [DM, 2304]); gbias = I("a_gbias", [36, 1])
    w1k = I("a_w1k", [4096, 256]); w2k = I("a_w2k", [256, 128]); pek = I("a_pekT", [128, 32])
    w1v = I("a_w1v", [4096, 256]); w2v = I("a_w2v", [256, 128]); pev = I("a_pevT", [128, 32])
    wmkv = I("a_w_mem_kv", [DM, 1024]); wout = I("a_w_out", [DM, DM])
    wg = I("a_w_gate", [DM, DFF]); wu = I("a_w_up", [DM, DFF]); wd = I("a_w_down", [DFF, DM])
    wkv = I("w_kv", [DM, 1024]); wkvr = I("w_kv_rot", [DM, 512])

    def S(name, shape, dt):
        if name in debug:
            return kb.outp(name, shape, dt)
        return kb.scr(name, shape, dt)
    S("xnT", [DM, SV], BF16)
    S("qT", [1536, SO], BF16); S("kcmpT", [256, SV], BF16); S("vcmpT", [256, SV], BF16)
    S("kslcT", [256, SV], BF16); S("kwinT", [256, SV], BF16); S("vsw", [SV, 512], BF16)
    S("gatesT", [36, SO], F32); S("qmT", [512, SO], BF16)
    S("kcT", [256, 256], BF16); S("vc", [512, 128], BF16)
    S("mkT", [512, 256], BF16); S("mv", [256, 512], BF16)
    S("oT", [DM, SO], BF16); S("h1", [SO, DM], F32); S("hnT", [DM, SO], BF16); S("hidT", [DFF, SO], BF16)
    kb.outp("h2", [SO, DM], F32); S("htmp", [SO, DM], F32)
    S("kvnT", [DM, SO], BF16)
    kb.outp("kshT", [512, SO], BF16); kb.outp("vsh", [SO, 512], BF16)
    if "dbg_br" in debug:
        kb.outp("dbg_br", [3 * 1536, SO], F32)
    d = kb.d
    kb.consts(NG)
    stop = kb.stop_after if hasattr(kb, "stop_after") else None
    kb.stage_norm(xv, None, SV, [0], [(d["xnT"], "xnT", 0)])
    kb.stage_inproj_a(w_in, w_rot, gbias)
    kb.stage_tm_bf16(d["xnT"], "xnT", 16, 0, SV, [(w_in, 2304, 256), (w_in, 2816, 256)], d["vsw"], "vsw")
    kb.stage_cmp(w1k, w2k, pek, w1v, w2v, pev)
    kb.stage_memkv(memb, 1, wmkv)
    kb.stage_attn_a()
    kb.stage_mem_attn(d["qmT"], "qmT", d["mkT"], d["mv"], d["oT"], "oT", 12)
    kb.stage_down(d["oT"], "oT", 16, wout, xv[SO:SV, :], None, d["h1"], "h1", TB=2048)
    kb.stage_norm(d["h1"], "h1", SO, [2], [(d["hnT"], "hnT", 0)])
    kb.stage_ffn(d["hnT"], "hnT", wg, wu, wd, d["h1"], "h1", d["h2"], "h2")
    kb.stage_norm(d["h2"], "h2", SO, [3], [(d["kvnT"], "kvnT", 0)])
    kb.stage_kvshared(wkv, wkvr)
    kb.s.finalize()
    return nc, kb


def rope_tabs(half):
    inv = (1.0 / (10000.0 ** (np.arange(0, 128, 2, dtype=np.float32) / 128))).astype(np.float32)
    pos = np.arange(SV, dtype=np.float32) - (0 if half == 1 else SO)
    pos = np.maximum(pos, 0).astype(np.float32)
    ang = (pos[:, None] * inv[None, :]).astype(np.float32)
    c = np.cos(ang).astype(np.float32).T
    sn = np.sin(ang).astype(np.float32).T
    cosT = np.concatenate([c, c], 0)
    sinT = np.concatenate([-sn, sn], 0)
    return np.ascontiguousarray(cosT), np.ascontiguousarray(sinT)


def rot_cols(w, heads):
    outs = []
    for c0 in heads:
        outs.append(w[:, c0 + 64:c0 + 128])
        outs.append(w[:, c0:c0 + 64])
    return np.ascontiguousarray(np.concatenate(outs, 1))


def gain_arr(gs):
    return np.ascontiguousarray(np.concatenate([g.reshape(16, 128).T for g in gs], 1).astype(np.float32))


def band_mask(o, w, prevmask):
    jj = np.arange(128)[:, None]
    qq = np.arange(512)[None, :]
    dist = qq - jj - o
    m = np.where((dist >= 0) & (dist <= w), 0.0, NEG).astype(np.float32)
    if prevmask:
        m[:] = NEG
    return m


def attn_consts(half):
    out = {}
    m_cmp = np.zeros((128, 8, 512), np.float32)
    for qt in range(4):
        for ct in range(2):
            c = ct * 128 + np.arange(128)[:, None]
            t = SO + qt * 512 + np.arange(512)[None, :]
            valid = (16 * c + 31 <= t) & (c <= 254)
            if half == 0:
                valid &= (c >= 128)
            m_cmp[:, qt * 2 + ct, :] = np.where(valid, 0.0, NEG)
    out["m_cmp"] = m_cmp.reshape(128, -1)
    mw = np.zeros((128, 8, 512), np.float32)
    for mi in range(8):
        mw[:, mi, :] = band_mask(mi * 128 - 512, 511, False)
    out["m_win"] = mw.reshape(128, -1)
    mw0 = np.zeros((128, 4, 512), np.float32)
    for mi in range(4):
        mw0[:, mi, :] = band_mask(mi * 128 - 512, 511, half == 0)
    out["m_win0"] = mw0.reshape(128, -1)
    selM = np.zeros((128, 4, 4, 64), np.float32)
    selA = np.zeros((128, 4, 4, 64), np.float32)
    sblk = np.arange(64)[None, :]
    first = 0 if half == 1 else 32
    for qt in range(4):
        for qs in range(4):
            t = SO + qt * 512 + qs * 128 + np.arange(128)[:, None]
            cur = t // 64
            elig = (sblk * 64 <= t) & (sblk >= first)
            f0 = (sblk == first) & elig
            f1 = (sblk == cur)
            f2 = (sblk == cur - 1) & (sblk >= first)
            A = np.where(elig, 0.0, -1e9)
            A = np.where(f0, 1e9, A)
            A = np.where(f2, 2e9, A)
            A = np.where(f1, 3e9, A)
            M = (elig & ~f0 & ~f1 & ~f2).astype(np.float32)
            selM[:, qt, qs, :] = M
            selA[:, qt, qs, :] = A
    out["selM"] = selM.reshape(128, -1)
    out["selA"] = selA.reshape(128, -1)
    return out


def shared_consts():
    out = {}
    out["c_ident"] = np.eye(128, dtype=np.float32)
    out["c_ones"] = np.ones((128, 128), np.float32)
    E = np.zeros((64, SV), np.float32)
    E[np.arange(SV) // 64, np.arange(SV)] = 1.0
    out["c_E"] = E
    mm = np.zeros((2, 128, 64), np.float32)
    for c in range(255):
        for sb in range(64):
            if (16 * c < 64 * sb + 64) and (16 * c + 32 > 64 * sb):
                mm[c // 128, c % 128, sb] = 1.0
    out["c_mmap"] = np.ascontiguousarray(mm.transpose(1, 0, 2).reshape(128, 128))
    sm = np.zeros((36, 36, 128), np.float32)
    for i in range(36):
        sm[i, i, :] = 1.0
    out["c_selmat"] = sm.reshape(36, -1)
    return out


def phase_a_inputs(inp, b, half, sc):
    f = np.float32
    x = inp["x"][b]
    if half == 1:
        xv = x
    else:
        xv = np.concatenate([np.zeros((SO, DM), f), x[:SO]], 0)
    cosT, sinT = rope_tabs(half)
    m = dict(sc)
    m.update(attn_consts(half))
    m["xv"] = np.ascontiguousarray(xv)
    m["memb"] = np.ascontiguousarray(inp["mem"][b])
    m["cosT"] = cosT
    m["sinT"] = sinT
    return m


def weights_a(inp):
    w = {}
    w_in = inp["a_w_in"][0]
    w["a_w_in"] = w_in
    heads = [h * 128 for h in range(12)] + [1536 + (i * 2 + g) * 128 for i in (0, 2, 4) for g in range(2)]
    w["a_w_rot"] = rot_cols(w_in, heads)
    w["a_gbias"] = np.ascontiguousarray(inp["a_gate_bias"][0].reshape(36, 1))
    w["a_w1k"] = inp["a_cmp_w1_k"][0]; w["a_w2k"] = inp["a_cmp_w2_k"][0]
    w["a_pekT"] = np.ascontiguousarray(inp["a_cmp_pe_k"][0].T)
    w["a_w1v"] = inp["a_cmp_w1_v"][0]; w["a_w2v"] = inp["a_cmp_w2_v"][0]
    w["a_pevT"] = np.ascontiguousarray(inp["a_cmp_pe_v"][0].T)
    w["a_w_mem_kv"] = inp["a_w_mem_kv"][0]; w["a_w_out"] = inp["a_w_out"][0]
    w["a_w_gate"] = inp["a_w_gate"][0]; w["a_w_up"] = inp["a_w_up"][0]; w["a_w_down"] = inp["a_w_down"][0]
    w["w_kv"] = inp["w_kv_shared"]
    w["w_kv_rot"] = rot_cols(inp["w_kv_shared"], [h * 128 for h in range(4)])
    w["gains"] = gain_arr([inp["a_norm_attn"][0], inp["a_norm_mem"][0], inp["a_norm_ffn"][0], inp["kv_norm"],
                           inp["b_norm_attn"][0], inp["b_norm_mem"][0], inp["b_norm_ffn"][0], inp["final_norm"]])
    return {k: np.ascontiguousarray(np.asarray(v, dtype=np.float32)) for k, v in w.items()}


def _kb_stage_attn_b(self):
    s = self.s
    d = self.d
    mk0 = s.mark()
    mdil, mdil_b = s.alloc("mdil", 5 * 512 * 2, BF16)
    s.dma(mdil, d["m_dil"], mdil_b, writes=[mdil_b], q="pool")
    mdil0, mdil0_b = s.alloc("mdil0", 512 * 2, BF16)
    s.dma(mdil0, d["m_dil0"], mdil0_b, writes=[mdil0_b], q="pool")
    ctx = dict(si=0, pi=0, ri=0, sbanks=[0, 1, 2], pt=[s.alloc("bpt%d" % i, 512 * 2, BF16) for i in range(4)],
               rd=[s.alloc("brd%d" % i, 512 * 4) for i in range(2)])
    oring = [s.alloc("bor%d" % i, 512 * 2, BF16) for i in range(3)]
    oi = 0
    ui = 0
    for hh in range(4):
        mk = s.mark()
        k_ap, k_b = s.alloc("bk", SV * 2, BF16)
        kown = [self.db("kshT", i) for i in range(4)]
        vown = [self.db("vsh", i) for i in range(4)]
        s.dma(k_ap[:, 0:SO], d["kg"][hh * 128:(hh + 1) * 128, :], k_b, reads=self.rd("kg"), pwrites=[k_b])
        s.dma(k_ap[:, SO:SV], d["kshT"][hh * 128:(hh + 1) * 128, :], k_b, reads=kown, pwrites=[k_b])
        q_ap, q_b = s.alloc("bq", 3 * SO * 2, BF16)
        for g in range(3):
            s.dma(q_ap[:, g * SO:(g + 1) * SO], d["qbT"][(g * 4 + hh) * 128:(g * 4 + hh + 1) * 128, :], q_b,
                  reads=[self.db("qbT", i) for i in range(4)], pwrites=[q_b])
        v1, v1_b = s.alloc("bv1", SV * 2, BF16)
        v4, v4_b = s.alloc("bv4", SV * 2, BF16)
        v16, v16_b = s.alloc("bv16", SV * 2, BF16)
        for pi_, (vsrc, rds) in enumerate(((d["vg"][0:SO, :], self.rd("vg")), (d["vsh"], vown))):
            vcol = vsrc[:, hh * 128:(hh + 1) * 128]
            for q in range(2):
                s.dma(sub3(v1, (pi_ * 16 + q * 8) * 128, 128, 8, 1, 128), dview(vsrc, q * 1024, 1024, hh * 128, 128), v1_b,
                      reads=rds, pwrites=[v1_b])
            r4 = vcol.rearrange("(jt p r) c -> r p jt c", p=128, r=4)
            for rho in range(4):
                s.dma(sub3(v4, rho * 1024 + pi_ * 4 * 128, 128, 4, 1, 128), r4[rho], v4_b, reads=rds, pwrites=[v4_b])
            r16 = vcol.rearrange("(jt p r) c -> r p jt c", p=128, r=16)
            for rho in range(16):
                s.dma(sub3(v16, rho * 256 + pi_ * 128, 128, 1, 1, 128), r16[rho], v16_b, reads=rds, pwrites=[v16_b])
        accS, accS_b = s.alloc("bacc", SO * 4)
        denS, denS_b = s.alloc("bden", SO * 4)

        def flush(bacc, bden, n, oa, od, first):
            if first:
                s.add("dve", lambda e, o=oa, i=s.psum[bacc][:, 0:n]: e.tensor_copy(o, i), reads=[s.psbuf[bacc]], pwrites=[accS_b])
                s.add("dve", lambda e, o=od, i=s.psum[bden][:, 0:n]: e.tensor_copy(o, i), reads=[s.psbuf[bden]], pwrites=[denS_b])
            else:
                s.add("dve", lambda e, o=oa, i=s.psum[bacc][:, 0:n]: e.tensor_tensor(o, o, i, ALU.add), reads=[s.psbuf[bacc], accS_b], pwrites=[accS_b])
                s.add("dve", lambda e, o=od, i=s.psum[bden][:, 0:n]: e.tensor_tensor(o, o, i, ALU.add), reads=[s.psbuf[bden], denS_b], pwrites=[denS_b])

        units = []

        def mkpost(bacc, bden, n, oa, od, first):
            return lambda: flush(bacc, bden, n, oa, od, first)
        for qt in range(4):
            t0v = SO + qt * 512
            bacc, bden = (3, 4) if (ui % 2 == 0) else (5, 6)
            ui += 1
            offs = [-128, 0, 128, 256, 384]
            for i, o_ in enumerate(offs):
                jt = (t0v + o_) // 128
                if qt == 0 and o_ < 0:
                    m_ap, m_b = mdil0, mdil0_b
                else:
                    m_ap, m_b = mdil[:, i * 512:(i + 1) * 512], mdil_b
                units.append(dict(klhs=k_ap[:, jt * 128:(jt + 1) * 128], krd=[k_b], qrhs=q_ap[:, qt * 512:(qt + 1) * 512], qrd=[q_b],
                                  extras=[(self.ident, m_ap, [self.ident_b, m_b])], vlhs=v1[:, jt * 128:(jt + 1) * 128], vrd=[v1_b],
                                  bacc=bacc, bden=bden, first=(i == 0), last=(i == 4),
                                  post=(mkpost(bacc, bden, 512, accS[:, qt * 512:(qt + 1) * 512], denS[:, qt * 512:(qt + 1) * 512], True) if i == 4 else None)))
        for rho in range(4):
            bacc, bden = (3, 4) if (ui % 2 == 0) else (5, 6)
            ui += 1
            qr = q_ap[:, SO + rho:SO + SO:4]
            offs = [-128, 0, 128, 256, 384]
            for i, o_ in enumerate(offs):
                ju0 = 512 + o_
                if o_ < 0:
                    m_ap, m_b = mdil0, mdil0_b
                else:
                    m_ap, m_b = mdil[:, i * 512:(i + 1) * 512], mdil_b
                kl = k_ap[:, rho + 4 * ju0:rho + 4 * (ju0 + 127) + 1:4]
                jtu = ju0 // 128
                units.append(dict(klhs=kl, krd=[k_b], qrhs=qr, qrd=[q_b], extras=[(self.ident, m_ap, [self.ident_b, m_b])],
                                  vlhs=v4[:, rho * 1024 + jtu * 128:rho * 1024 + (jtu + 1) * 128], vrd=[v4_b],
                                  bacc=bacc, bden=bden, first=(i == 0), last=(i == 4),
                                  post=(mkpost(bacc, bden, 512, accS[:, rho:SO:4], denS[:, rho:SO:4], False) if i == 4 else None)))
        for rho in range(16):
            bacc, bden = (3, 4) if (ui % 2 == 0) else (5, 6)
            ui += 1
            qr = q_ap[:, 2 * SO + rho:3 * SO:16]
            for i, o_ in enumerate([-128, 0]):
                ju0 = 128 + o_
                if o_ < 0:
                    m_ap, m_b = mdil0[:, 0:128], mdil0_b
                else:
                    m_ap, m_b = mdil[:, 512:512 + 128], mdil_b
                kl = k_ap[:, rho + 16 * ju0:rho + 16 * (ju0 + 127) + 1:16]
                jtu = ju0 // 128
                units.append(dict(klhs=kl, krd=[k_b], qrhs=qr, qrd=[q_b], extras=[(self.ident, m_ap, [self.ident_b, m_b])],
                                  vlhs=v16[:, rho * 256 + jtu * 128:rho * 256 + (jtu + 1) * 128], vrd=[v16_b],
                                  bacc=bacc, bden=bden, first=(i == 0), last=(i == 1), n=128,
                                  post=(mkpost(bacc, bden, 128, accS[:, rho:SO:16], denS[:, rho:SO:16], False) if i == 1 else None)))
        self.run_units(ctx, units)
        s.add("dve", lambda e: e.tensor_scalar(denS, denS, TINY, None, ALU.max), reads=[denS_b], writes=[denS_b])
        s.add("dve", lambda e: e.reciprocal(denS, denS), reads=[denS_b], writes=[denS_b])
        for qt in range(4):
            o_ap, o_b = oring[oi % 3]
            oi += 1
            s.add("dve", lambda e, o=o_ap, a=accS[:, qt * 512:(qt + 1) * 512], b=denS[:, qt * 512:(qt + 1) * 512]: e.tensor_tensor(o, a, b, ALU.mult),
                  reads=[accS_b, denS_b], writes=[o_b])
            s.dma(d["oT"][hh * 128:(hh + 1) * 128, qt * 512:(qt + 1) * 512], o_ap, o_b, reads=[o_b], pwrites=[self.db("oT", qt)])
        s.release(mk)
    s.release(mk0)


KB.stage_attn_b = _kb_stage_attn_b


def _kb_stage_inproj_b(self, w_in, w_rot):
    kb = self
    d = self.d
    panels = []
    for hp in range(6):
        segs = [(w_in, hp * 256, 256), (w_rot, hp * 256, 256)]
        jobs = [dict(cols=[(j * 128, 128), (256 + j * 128, 128)], dst=d["qbT"], dname="qbT", row0=(hp * 2 + j) * 128,
                     epi=self._rope_epi_own()) for j in range(2)]
        panels.append(dict(segs=segs, jobs=jobs))
    segs = [(w_in, 1536, 512)]
    jobs = [dict(cols=[(j * 128, 128)], row0=j * 128, epi=kb.epi_plain_fm(d["qmT"], "qmT", None, 0)) for j in range(4)]
    panels.append(dict(segs=segs, jobs=jobs))
    self.stage_lfm(d["bnT"], "bnT", 0, SO, 16, panels, self.rope_setup(SO, SO))


KB.stage_inproj_b = _kb_stage_inproj_b


def _kb_stage_final_norm(self, src, srcname, dst):
    s = self.s
    mk = s.mark()
    fg, fg_b = s.alloc("fg", DM * 4)
    s.dma(fg, self.d["fgain"], fg_b, writes=[fg_b])
    hb = [s.alloc("fh%d" % i, DM * 4) for i in range(3)]
    ob = [s.alloc("fo%d" % i, DM * 4) for i in range(2)]
    junk_ap, junk_b = s.alloc("fjunk", DM * 2, BF16)
    st = [s.alloc("fst%d" % i, 4 * 4) for i in range(3)]
    for it in range(SO // 128):
        h_ap, h_b = hb[it % 3]
        st_ap, st_b = st[it % 3]
        o_ap, o_b = ob[it % 2]
        s.dma(h_ap, src[it * 128:(it + 1) * 128, :], h_b, reads=self.rd(srcname, it // 4), writes=[h_b])
        s.add("act", lambda e, h=h_ap, o=st_ap[:, 0:1]: e.activation(junk_ap, h, AF.Square, accum_out=o), reads=[h_b], pwrites=[junk_b, st_b])
        s.add("dve", lambda e, a=st_ap: e.tensor_scalar(a[:, 1:2], a[:, 0:1], 1.0 / 2048, EPS, ALU.mult, ALU.add), reads=[st_b], pwrites=[st_b])
        s.add("act", lambda e, a=st_ap: e.sqrt(a[:, 1:2], a[:, 1:2]), reads=[st_b], pwrites=[st_b])
        s.add("dve", lambda e, a=st_ap: e.reciprocal(a[:, 2:3], a[:, 1:2]), reads=[st_b], pwrites=[st_b])
        s.add("act", lambda e, o=o_ap, h=h_ap, sc=st_ap[:, 2:3]: e.activation(o, h, AF.Copy, scale=sc), reads=[h_b, st_b], writes=[o_b])
        s.add("dve", lambda e, o=o_ap: e.tensor_tensor(o, o, fg, ALU.mult), reads=[o_b, fg_b], writes=[o_b])
        s.dma(dst[it * 128:(it + 1) * 128, :], o_ap, o_b, reads=[o_b])
    s.release(mk)


KB.stage_final_norm = _kb_stage_final_norm


def build_phase_b(debug=()):
    nc = bass.Bass("TRN2", target_bir_lowering=False)
    kb = KB(nc)
    I = kb.inp
    h2 = I("h2in", [SO, DM])
    memb = I("memb", [256, DM])
    I("kshTv", [512, SV], BF16); I("vshv", [SV, 512], BF16)
    I("cosT", [128, SV]); I("sinT", [128, SV]); I("gains", [128, NG * 16]); I("fgain", [128, DM])
    I("c_ident", [128, 128]); I("c_ones", [128, 128])
    I("m_dil", [128, 5 * 512]); I("m_dil0", [128, 512])
    w_in = I("b_w_in", [DM, 2048]); w_rot = I("b_w_rot", [DM, 1536])
    wmkv = I("b_w_mem_kv", [DM, 1024]); wout = I("b_w_out", [1024, DM])
    wg = I("b_w_gate", [DM, DFF]); wu = I("b_w_up", [DM, DFF]); wd = I("b_w_down", [DFF, DM])

    def S(name, shape, dt):
        if name in debug:
            return kb.outp(name, shape, dt)
        return kb.scr(name, shape, dt)
    S("bnT", [DM, SO], BF16); S("qbT", [1536, SO], BF16); S("qmT", [512, SO], BF16)
    S("mkT", [512, 256], BF16); S("mv", [256, 512], BF16)
    S("oT", [1024, SO], BF16); S("h3", [SO, DM], F32); S("hnT", [DM, SO], BF16); S("hidT", [DFF, SO], BF16)
    S("h4", [SO, DM], F32)
    kb.outp("out", [SO, DM], F32)
    d = kb.d
    kb.consts(NG)
    kb.stage_norm(h2, None, SO, [4], [(d["bnT"], "bnT", 0)])
    kb.stage_inproj_b(w_in, w_rot)
    kb.stage_memkv(memb, 5, wmkv)
    kb.stage_attn_b()
    kb.stage_mem_attn(d["qmT"], "qmT", d["mkT"], d["mv"], d["oT"], "oT", 4)
    kb.stage_down(d["oT"], "oT", 8, wout, h2, None, d["h3"], "h3", TB=2048)
    kb.stage_norm(d["h3"], "h3", SO, [6], [(d["hnT"], "hnT", 0)])
    kb.stage_ffn(d["hnT"], "hnT", wg, wu, wd, d["h3"], "h3", d["h4"], "h4")
    kb.stage_final_norm(d["h4"], "h4", d["out"])
    kb.s.finalize()
    return nc, kb


def dil_consts(half):
    out = {}
    md = np.zeros((128, 5, 512), np.float32)
    for i, o_ in enumerate([-128, 0, 128, 256, 384]):
        md[:, i, :] = band_mask(o_, 128, False)
    out["m_dil"] = md.reshape(128, -1)
    out["m_dil0"] = band_mask(-128, 128, half == 0)
    return out


def weights_b(inp):
    w = {}
    w_in = inp["b_w_in"][0]
    w["b_w_in"] = w_in
    w["b_w_rot"] = rot_cols(w_in, [h * 128 for h in range(12)])
    w["b_w_mem_kv"] = inp["b_w_mem_kv"][0]; w["b_w_out"] = inp["b_w_out"][0]
    w["b_w_gate"] = inp["b_w_gate"][0]; w["b_w_up"] = inp["b_w_up"][0]; w["b_w_down"] = inp["b_w_down"][0]
    w["fgain"] = np.broadcast_to(inp["final_norm"][None, :], (128, DM))
    return {k: np.ascontiguousarray(np.asarray(v, dtype=np.float32)) for k, v in w.items()}


_PROG = {}


def kernel(**inputs):
    inp = {k: np.asarray(v) for k, v in inputs.items()}
    import ml_dtypes
    bf = ml_dtypes.bfloat16
    if "a" not in _PROG:
        _PROG["a"] = build_phase_a()
        _PROG["b"] = build_phase_b()
    nca, _ = _PROG["a"]
    ncb, _ = _PROG["b"]
    sc = shared_consts()
    wa = weights_a(inp)
    maps = []
    for c in range(8):
        m = phase_a_inputs(inp, c // 2, c % 2, sc)
        m.update(wa)
        maps.append(m)
    ra = run_bass_kernel_spmd(nca, maps, core_ids=list(range(8))).results
    del maps
    wb = weights_b(inp)
    maps = []
    for c in range(8):
        b, half = c // 2, c % 2
        m = {}
        m["h2in"] = np.ascontiguousarray(ra[c]["h2"])
        ksh = np.asarray(ra[c]["kshT"])
        vsh = np.asarray(ra[c]["vsh"])
        if half == 1:
            kprev = np.asarray(ra[c - 1]["kshT"]); vprev = np.asarray(ra[c - 1]["vsh"])
        else:
            kprev = np.zeros_like(ksh); vprev = np.zeros_like(vsh)
        m["kshTv"] = np.ascontiguousarray(np.concatenate([kprev, ksh], 1))
        m["vshv"] = np.ascontiguousarray(np.concatenate([vprev, vsh], 0))
        m["memb"] = np.ascontiguousarray(inp["mem"][b])
        cosT, sinT = rope_tabs(half)
        m["cosT"] = cosT; m["sinT"] = sinT
        m["gains"] = wa["gains"]
        m["c_ident"] = sc["c_ident"]; m["c_ones"] = sc["c_ones"]
        m.update(dil_consts(half))
        m.update(wb)
        maps.append(m)
    rb = run_bass_kernel_spmd(ncb, maps, core_ids=list(range(8))).results
    out = np.zeros((4, 4096, DM), np.float32)
    for c in range(8):
        b, half = c // 2, c % 2
        out[b, half * SO:(half + 1) * SO, :] = rb[c]["out"]
    return out


def build_fused(debug=()):
    nc = bass.Bass("TRN2", target_bir_lowering=False, num_devices=8)
    kb = KB(nc)
    I = kb.inp
    xv = I("xv", [SV, DM])
    memb = I("memb", [256, DM])
    I("cosT", [128, SV]); I("sinT", [128, SV]); I("gains", [128, NG * 16]); I("fgain", [128, DM])
    I("c_ident", [128, 128]); I("c_ones", [128, 128])
    I("m_cmp", [128, 8 * 512]); I("m_win", [128, 8 * 512]); I("m_win0", [128, 4 * 512])
    I("c_E", [64, SV]); I("c_mmap", [128, 128]); I("selM", [128, 1024]); I("selA", [128, 1024]); I("c_selmat", [36, 36 * 128])
    I("m_dil", [128, 5 * 512]); I("m_dil0", [128, 512])
    w_in = I("a_w_in", [DM, 3620]); w_rot = I("a_w_rot", [DM, 2304]); gbias = I("a_gbias", [36, 1])
    w1k = I("a_w1k", [4096, 256]); w2k = I("a_w2k", [256, 128]); pek = I("a_pekT", [128, 32])
    w1v = I("a_w1v", [4096, 256]); w2v = I("a_w2v", [256, 128]); pev = I("a_pevT", [128, 32])
    wmkv = I("a_w_mem_kv", [DM, 1024]); wout = I("a_w_out", [DM, DM])
    wg = I("a_w_gate", [DM, DFF]); wu = I("a_w_up", [DM, DFF]); wd = I("a_w_down", [DFF, DM])
    wkv = I("w_kv", [DM, 1024]); wkvr = I("w_kv_rot", [DM, 512])
    bw_in = I("b_w_in", [DM, 2048]); bw_rot = I("b_w_rot", [DM, 1536])
    bwmkv = I("b_w_mem_kv", [DM, 1024]); bwout = I("b_w_out", [1024, DM])
    bwg = I("b_w_gate", [DM, DFF]); bwu = I("b_w_up", [DM, DFF]); bwd = I("b_w_down", [DFF, DM])
    S = kb.scr
    S("xnT", [DM, SV], BF16)
    S("qT", [1536, SO], BF16); S("kcmpT", [256, SV], BF16); S("vcmpT", [256, SV], BF16)
    S("kslcT", [256, SV], BF16); S("kwinT", [256, SV], BF16); S("vsw", [SV, 512], BF16)
    S("gatesT", [36, SO], F32); S("qmT", [512, SO], BF16)
    S("kcT", [256, 256], BF16); S("vc", [512, 128], BF16)
    S("mkT", [512, 256], BF16); S("mv", [256, 512], BF16); S("mkTb", [512, 256], BF16); S("mvb", [256, 512], BF16)
    S("oT", [DM, SO], BF16); S("h1", [SO, DM], F32); S("hnT", [DM, SO], BF16); S("hidT", [DFF, SO], BF16)
    S("h2", [SO, DM], F32); S("htmp", [SO, DM], F32)
    S("kvnT", [DM, SO], BF16); S("bnT", [DM, SO], BF16)
    S("kshT", [512, SO], BF16); S("vsh", [SO, 512], BF16)
    S("kg", [1024, SO], BF16); S("vg", [2 * SO, 512], BF16)
    S("qbT", [1536, SO], BF16); S("h3", [SO, DM], F32); S("h4", [SO, DM], F32)
    kb.outp("out", [SO, DM], F32)
    d = kb.d
    kb.consts(NG)
    kb.stage_norm(xv, None, SV, [0], [(d["xnT"], "xnT", 0)])
    kb.stage_memkv(memb, 1, wmkv)
    kb.stage_memkv(memb, 5, bwmkv, "mkTb", "mvb")
    kb.stage_inproj_a(w_in, w_rot, gbias)
    kb.stage_tm_bf16(d["xnT"], "xnT", 16, 0, SV, [(w_in, 2304, 256), (w_in, 2816, 256)], d["vsw"], "vsw")
    kb.stage_cmp(w1k, w2k, pek, w1v, w2v, pev)
    kb.stage_attn_a()
    kb.stage_mem_attn(d["qmT"], "qmT", d["mkT"], d["mv"], d["oT"], "oT", 12)
    kb.stage_down(d["oT"], "oT", 16, wout, xv[SO:SV, :], None, d["h1"], "h1", TB=2048)
    kb.stage_norm(d["h1"], "h1", SO, [2], [(d["hnT"], "hnT", 0)])
    kb.stage_ffn(d["hnT"], "hnT", wg, wu, wd, d["h1"], "h1", d["h2"], "h2")
    kb.stage_norm(d["h2"], "h2", SO, [3, 4], [(d["kvnT"], "kvnT", 0), (d["bnT"], "bnT", 0)])
    kb.stage_kvshared(wkv, wkvr)
    groups = [[0, 1], [2, 3], [4, 5], [6, 7]]
    kb.s.collective(lambda e: e.collective_compute("AllGather", ALU.bypass, replica_groups=groups, ins=[d["kshT"]], outs=[d["kg"]]),
                    reads=[kb.db("kshT", i) for i in range(4)], writes=[kb.db("kg")])
    kb.s.collective(lambda e: e.collective_compute("AllGather", ALU.bypass, replica_groups=groups, ins=[d["vsh"]], outs=[d["vg"]]),
                    reads=[kb.db("vsh", i) for i in range(4)], writes=[kb.db("vg")])
    kb.stage_inproj_b(bw_in, bw_rot)
    kb.stage_attn_b()
    kb.stage_mem_attn(d["qmT"], "qmT", d["mkTb"], d["mvb"], d["oT"], "oT", 4, "mkTb", "mvb")
    kb.stage_down(d["oT"], "oT", 8, bwout, d["h2"], "h2", d["h3"], "h3", TB=2048)
    kb.stage_norm(d["h3"], "h3", SO, [6], [(d["hnT"], "hnT", 0)])
    kb.stage_ffn(d["hnT"], "hnT", bwg, bwu, bwd, d["h3"], "h3", d["h4"], "h4")
    kb.stage_final_norm(d["h4"], "h4", d["out"])
    kb.s.finalize()
    return nc, kb


def kernel(**inputs):
    inp = {k: np.asarray(v) for k, v in inputs.items()}
    if "f" not in _PROG:
        _PROG["f"] = build_fused()
    nc, _ = _PROG["f"]
    sc = shared_consts()
    wa = weights_a(inp)
    wb = weights_b(inp)
    maps = []
    for c in range(8):
        b, half = c // 2, c % 2
        m = phase_a_inputs(inp, b, half, sc)
        m.update(dil_consts(half))
        m.update(wa)
        m.update(wb)
        maps.append(m)
    res = run_bass_kernel_spmd(nc, maps, core_ids=list(range(8))).results
    out = np.zeros((4, 4096, DM), np.float32)
    for c in range(8):
        b, half = c // 2, c % 2
        out[b, half * SO:(half + 1) * SO, :] = res[c]["out"]
    return out
```
